# Optimizing a Trainium2 kernel written in Bass

```python
import jax, jax.numpy as jnp
from jax import lax
import numpy as np

D_MODEL = 1024
BATCH = 16
SEQ = 4096
DEPTH = 4

CHUNK = 64
Q_BLOCK = 128
ROPE_THETA = 10000.0
EPS = 1e-6

A_HEADS = 6
A_HEAD_DIM = 64
A_WIDTH = A_HEADS * A_HEAD_DIM
IDX_HEADS = 8
IDX_DIM = 32
TOPK_MAX = 256

S5_GROUP = 16
S5_WIDTH = 256
S5_GROUPS = S5_WIDTH // S5_GROUP
S5_STATE = 64
STEP_MIN = 1e-3
STEP_MAX = 1e-1

C_HEADS = 6
C_NOPE = 64
C_ROPE = 32
C_VDIM = 64
C_QK = C_NOPE + C_ROPE
C_WIDTH = C_HEADS * C_VDIM
Q_LORA = 256
KV_LORA = 128

D_MIX = A_WIDTH + S5_WIDTH + C_WIDTH

SPLIT_SIZES = (
    A_WIDTH,
    A_HEAD_DIM,
    A_HEAD_DIM,
    IDX_HEADS * IDX_DIM,
    IDX_DIM,
    IDX_HEADS,
    A_WIDTH,
    S5_WIDTH,
    S5_WIDTH,
    Q_LORA,
    KV_LORA,
    C_ROPE,
    C_WIDTH,
)
D_IN = sum(SPLIT_SIZES)

kernel_name = "hybrid_dsa_s5_mla_parallel_heads"


def _split_points():
    return [int(v) for v in np.cumsum(SPLIT_SIZES)[:-1]]


def rms_norm(x, g):
    xf = x.astype(jnp.float32)
    y = xf * lax.rsqrt(jnp.mean(xf * xf, axis=-1, keepdims=True) + EPS)
    return (y * g.astype(jnp.float32)).astype(x.dtype)


def apply_rope(x, pos):
    d = x.shape[-1]
    half = d // 2
    inv = ROPE_THETA ** (-jnp.arange(half, dtype=jnp.float32) * 2.0 / d)
    ang = pos.astype(jnp.float32)[:, None] * inv[None, :]
    cos = jnp.cos(ang)[:, None, :]
    sin = jnp.sin(ang)[:, None, :]
    xf = x.astype(jnp.float32)
    x1, x2 = xf[..., :half], xf[..., half:]
    return jnp.concatenate([x1 * cos - x2 * sin, x2 * cos + x1 * sin], axis=-1).astype(x.dtype)


def to_blocks(t):
    b, s = t.shape[:2]
    return t.reshape((b, s // Q_BLOCK, Q_BLOCK) + t.shape[2:]).swapaxes(0, 1)


def from_blocks(t):
    nb, b, q = t.shape[:3]
    return t.swapaxes(0, 1).reshape((b, nb * q) + t.shape[3:])


def gather_rows(src, idx):
    return jax.vmap(lambda s_, i_: s_[i_])(src, idx)


def dsa_branch(qa, ka, va, iq, ik, iw, q_norm_g, k_norm_g, pos, limit):
    b, s, _ = qa.shape
    q = apply_rope(rms_norm(qa.reshape(b, s, A_HEADS, A_HEAD_DIM), q_norm_g), pos)
    k = apply_rope(rms_norm(ka.reshape(b, s, 1, A_HEAD_DIM), k_norm_g), pos)[:, :, 0]
    v = va
    iq = apply_rope(iq.reshape(b, s, IDX_HEADS, IDX_DIM), pos)
    ik = apply_rope(ik[:, :, None, :], pos)[:, :, 0].astype(jnp.float32)
    iw = iw.astype(jnp.float32) * (IDX_HEADS ** -0.5)
    k_top = min(TOPK_MAX, s // 4)
    key_pos = jnp.arange(s)

    def one_block(args):
        qb, iqb, iwb, limb = args
        sc = jnp.einsum('bthd,bsd->bths', iqb.astype(jnp.float32), ik) * (IDX_DIM ** -0.5)
        sc = jnp.einsum('bths,bth->bts', jax.nn.relu(sc), iwb)
        admissible = key_pos[None, :] < limb[:, None]
        sc = jnp.where(admissible[None], sc, -jnp.inf)
        _, idx = lax.top_k(sc, k_top)
        valid = idx < limb[None, :, None]
        k_sel = gather_rows(k, idx)
        v_sel = gather_rows(v, idx)
        att = jnp.einsum('bthd,btkd->bthk', qb, k_sel).astype(jnp.float32) * (A_HEAD_DIM ** -0.5)
        att = jnp.where(valid[:, :, None, :], att, -jnp.inf)
        p = jax.nn.softmax(att, axis=-1).astype(v_sel.dtype)
        return jnp.einsum('bthk,btkd->bthd', p, v_sel)

    out = lax.map(one_block, (to_blocks(q), to_blocks(iq), to_blocks(iw),
                              limit.reshape(s // Q_BLOCK, Q_BLOCK)))
    return from_blocks(out).reshape(b, s, A_WIDTH)


def s5_branch(u, a_re, a_im, b_re, b_im, c_re, c_im, d_skip, log_step, w_glu):
    b, s, _ = u.shape
    ug = u.reshape(b, s, S5_GROUPS, S5_GROUP).astype(jnp.float32)
    ar = a_re.astype(jnp.float32)
    ai = a_im.astype(jnp.float32)
    step = jnp.exp(log_step.astype(jnp.float32))[:, None]
    mag = jnp.exp(ar * step)
    abar_re = mag * jnp.cos(ai * step)
    abar_im = mag * jnp.sin(ai * step)
    den = ar * ar + ai * ai
    nr = abar_re - 1.0
    f_re = (nr * ar + abar_im * ai) / den
    f_im = (abar_im * ar - nr * ai) / den
    bu_re = jnp.einsum('bsgc,gpc->bsgp', ug, b_re.astype(jnp.float32))
    bu_im = jnp.einsum('bsgc,gpc->bsgp', ug, b_im.astype(jnp.float32))
    bb_re = f_re * bu_re - f_im * bu_im
    bb_im = f_re * bu_im + f_im * bu_re
    aa_re = jnp.broadcast_to(abar_re, bb_re.shape)
    aa_im = jnp.broadcast_to(abar_im, bb_im.shape)

    def combine(e1, e2):
        a1r, a1i, b1r, b1i = e1
        a2r, a2i, b2r, b2i = e2
        return (a2r * a1r - a2i * a1i,
                a2r * a1i + a2i * a1r,
                a2r * b1r - a2i * b1i + b2r,
                a2r * b1i + a2i * b1r + b2i)

    _, _, xr, xi = lax.associative_scan(combine, (aa_re, aa_im, bb_re, bb_im), axis=1)
    y = (jnp.einsum('bsgp,gcp->bsgc', xr, c_re.astype(jnp.float32))
         - jnp.einsum('bsgp,gcp->bsgc', xi, c_im.astype(jnp.float32))
         + d_skip.astype(jnp.float32) * ug)
    y = y.reshape(b, s, S5_WIDTH)
    g = jax.nn.gelu(y)
    y = g * jax.nn.sigmoid(g @ w_glu.astype(jnp.float32))
    return y.astype(u.dtype)


def mla_branch(cq, ckv, kpe, q_lora_g, kv_lora_g, w_uq, w_ukv, q_norm_g, k_norm_g, pos, limit):
    b, s, _ = cq.shape
    q = (rms_norm(cq, q_lora_g) @ w_uq).reshape(b, s, C_HEADS, C_QK)
    kv = (rms_norm(ckv, kv_lora_g) @ w_ukv).reshape(b, s, C_HEADS, C_NOPE + C_VDIM)
    k_nope, v = kv[..., :C_NOPE], kv[..., C_NOPE:]
    k_rope = jnp.broadcast_to(kpe[:, :, None, :], (b, s, C_HEADS, C_ROPE))
    k = jnp.concatenate([k_nope, k_rope], axis=-1)
    q = rms_norm(q, q_norm_g)
    k = rms_norm(k, k_norm_g)
    q = jnp.concatenate([q[..., :C_NOPE], apply_rope(q[..., C_NOPE:], pos)], axis=-1)
    k = jnp.concatenate([k[..., :C_NOPE], apply_rope(k[..., C_NOPE:], pos)], axis=-1)
    key_pos = jnp.arange(s)

    def one_block(args):
        qb, limb = args
        att = jnp.einsum('bthd,bshd->bhts', qb, k).astype(jnp.float32) * (C_QK ** -0.5)
        mask = key_pos[None, :] < limb[:, None]
        att = jnp.where(mask[None, None], att, -jnp.inf)
        p = jax.nn.softmax(att, axis=-1).astype(v.dtype)
        return jnp.einsum('bhts,bshd->bthd', p, v)

    out = lax.map(one_block, (to_blocks(q), limit.reshape(s // Q_BLOCK, Q_BLOCK)))
    return from_blocks(out).reshape(b, s, C_WIDTH)


def setup_inputs(seed: int = 0) -> dict:
    key = jax.random.key(seed)
    ks = jax.random.split(key, 24)
    f32 = jnp.float32
    nrm = lambda k, shape, sc: jax.random.normal(k, shape, f32) * sc
    gain = lambda k, shape: 1.0 + 0.05 * jax.random.normal(k, shape, f32)
    L, G, P, Cg = DEPTH, S5_GROUPS, S5_STATE, S5_GROUP
    a_re = -0.5 + 0.01 * jax.random.normal(ks[11], (L, G, P), f32)
    a_im = jnp.pi * jnp.arange(P, dtype=f32)[None, None, :] + 0.01 * jax.random.normal(ks[12], (L, G, P), f32)
    log_step = jax.random.uniform(ks[19], (L, G), f32, np.log(STEP_MIN), np.log(STEP_MAX))
    return {
        "x": jax.random.normal(ks[0], (BATCH, SEQ, D_MODEL), f32),
        "norm_g": gain(ks[1], (L, D_MODEL)),
        "w_in": nrm(ks[2], (L, D_MODEL, D_IN), D_MODEL ** -0.5),
        "attn_q_norm": gain(ks[3], (L, A_HEAD_DIM)),
        "attn_k_norm": gain(ks[4], (L, A_HEAD_DIM)),
        "mla_q_lora_norm": gain(ks[5], (L, Q_LORA)),
        "mla_kv_lora_norm": gain(ks[6], (L, KV_LORA)),
        "mla_w_uq": nrm(ks[7], (L, Q_LORA, C_HEADS * C_QK), Q_LORA ** -0.5),
        "mla_w_ukv": nrm(ks[8], (L, KV_LORA, C_HEADS * (C_NOPE + C_VDIM)), KV_LORA ** -0.5),
        "mla_q_norm": gain(ks[9], (L, C_QK)),
        "mla_k_norm": gain(ks[10], (L, C_QK)),
        "ssm_a_re": a_re,
        "ssm_a_im": a_im,
        "ssm_b_re": nrm(ks[13], (L, G, P, Cg), (2.0 * Cg) ** -0.5),
        "ssm_b_im": nrm(ks[14], (L, G, P, Cg), (2.0 * Cg) ** -0.5),
        "ssm_c_re": nrm(ks[15], (L, G, Cg, P), P ** -0.5),
        "ssm_c_im": nrm(ks[16], (L, G, Cg, P), P ** -0.5),
        "ssm_d": nrm(ks[17], (L, G, Cg), 1.0),
        "ssm_log_step": log_step,
        "ssm_w_glu": nrm(ks[18], (L, S5_WIDTH, S5_WIDTH), S5_WIDTH ** -0.5),
        "w_out": nrm(ks[20], (L, D_MIX, D_MODEL), 0.5 * D_MIX ** -0.5),
    }


def reference(x, norm_g, w_in, attn_q_norm, attn_k_norm, mla_q_lora_norm, mla_kv_lora_norm,
              mla_w_uq, mla_w_ukv, mla_q_norm, mla_k_norm, ssm_a_re, ssm_a_im, ssm_b_re,
              ssm_b_im, ssm_c_re, ssm_c_im, ssm_d, ssm_log_step, ssm_w_glu, w_out):
    s = x.shape[1]
    pos = jnp.arange(s, dtype=jnp.int32)
    limit = (pos // CHUNK + 1) * CHUNK
    points = _split_points()
    for l in range(DEPTH):
        h = rms_norm(x, norm_g[l])
        proj = h @ w_in[l]
        (qa, ka, va, iq, ik, iw, ga, u, gb, cq, ckv, kpe, gc) = jnp.split(proj, points, axis=-1)
        ya = dsa_branch(qa, ka, va, iq, ik, iw, attn_q_norm[l], attn_k_norm[l], pos, limit)
        yb = s5_branch(u, ssm_a_re[l], ssm_a_im[l], ssm_b_re[l], ssm_b_im[l], ssm_c_re[l],
                       ssm_c_im[l], ssm_d[l], ssm_log_step[l], ssm_w_glu[l])
        yc = mla_branch(cq, ckv, kpe, mla_q_lora_norm[l], mla_kv_lora_norm[l], mla_w_uq[l],
                        mla_w_ukv[l], mla_q_norm[l], mla_k_norm[l], pos, limit)
        mixed = jnp.concatenate([ya * jax.nn.silu(ga), yb * jax.nn.silu(gb),
                                 yc * jax.nn.silu(gc)], axis=-1)
        x = x + mixed @ w_out[l]
    return x
```

```python
import contextlib
import numpy as np
import ml_dtypes
import concourse.bass as bass
import concourse.mybir as mybir
from concourse.bass_utils import run_bass_kernel_spmd

F32 = mybir.dt.float32
BF16 = mybir.dt.bfloat16
AF = mybir.ActivationFunctionType
ALU = mybir.AluOpType
AX = mybir.AxisListType

D = 1024
DEPTH = 4
DIN = 2504
EPS = 1e-6
NCOL = 3240
C_IQ2 = 2728
C_QA, C_KA2, C_IQ, C_IK4, C_GA, C_U, C_GB, C_CQ, C_CKV, C_KPE, C_GC, C_VAIW = (
    0, 384, 512, 768, 896, 1280, 1536, 1792, 2048, 2176, 2272, 2656)
S_QA, S_KA, S_VA, S_IQ, S_IK, S_IW, S_GA, S_U, S_GB, S_CQ, S_CKV, S_KPE, S_GC = (
    0, 384, 448, 512, 768, 800, 808, 1192, 1448, 1704, 1960, 2088, 2120)
NEGM = -30000.0
NIT = 16
import os as _os4
DSA_MODE = int(_os4.environ.get('DSA_MODE', '0'))
import os as _os3
GV = int(_os3.environ.get('GV', '0'))
import os as _os2
GATE_DMA = int(_os2.environ.get('GATE_DMA', '2'))
import os as _os
P1CUT = int(_os.environ.get('P1CUT', '99'))
import os
PH = set(os.environ.get('PH', 'prep,s5c,p1,dsa,s5,mla,p3').split(','))


class Res:
    __slots__ = ("w", "r", "ps")

    def __init__(self, ps=False):
        self.w = {}
        self.r = {}
        self.ps = ps


class Sched:
    def __init__(self, nc, n_dma_sems=32):
        self.nc = nc
        self.engs = {}
        for name, e in (("pe", nc.tensor), ("act", nc.scalar), ("dve", nc.vector),
                        ("pool", nc.gpsimd), ("sp", nc.sync)):
            sem = nc.alloc_semaphore(name="sem_" + name)
            self.engs[name] = dict(e=e, sem=sem, cnt=0, waited={}, name=name)
        self.dma_sems = [nc.alloc_semaphore(name=f"dsem{i}") for i in range(n_dma_sems)]
        self.dma_cnt = [0] * n_dma_sems
        self.dma_i = 0
        self.nins = 0

    def _wait(self, en, deps, skip_self=False):
        E = self.engs[en]
        best = {}
        for sem, val in deps:
            k = id(sem)
            if k not in best or best[k][1] < val:
                best[k] = (sem, val)
        for k, (sem, val) in best.items():
            if skip_self and sem is E["sem"]:
                continue
            if E["waited"].get(k, 0) < val:
                E["e"].wait_ge(sem, val)
                E["waited"][k] = val

    def _deps(self, reads, writes, en=None):
        deps = []
        own = self.engs[en]["sem"] if en is not None else None
        for r in reads:
            deps.extend(r.w.values())
            if r.ps:
                deps.extend(ev for ev in r.r.values() if ev[0] is not own)
        for w in writes:
            deps.extend(w.w.values())
            deps.extend(w.r.values())
        return deps

    @staticmethod
    def _mark(ev, reads, writes):
        k = id(ev[0])
        for r in reads:
            r.r[k] = ev
        for w in writes:
            w.w[k] = ev

    def op(self, en, fn, reads=(), writes=()):
        E = self.engs[en]
        self._wait(en, self._deps(reads, writes, en), skip_self=(en == "pe"))
        ins = fn(E["e"])
        E["cnt"] += 1
        ins.then_inc(E["sem"], 1)
        ev = (E["sem"], E["cnt"])
        self._mark(ev, reads, writes)
        self.nins += 1
        return ev

    def dma(self, out, in_, reads=(), writes=(), en="sp", **kw):
        E = self.engs[en]
        i = self.dma_i % len(self.dma_sems)
        self.dma_i += 1
        sem = self.dma_sems[i]
        deps = self._deps(reads, writes)
        if self.dma_cnt[i] > 0:
            deps.append((sem, self.dma_cnt[i]))
        self._wait(en, deps)
        ins = E["e"].dma_start(out=out, in_=in_, **kw)
        self.dma_cnt[i] += 16
        ins.then_inc(sem, 16)
        ev = (sem, self.dma_cnt[i])
        self._mark(ev, reads, writes)
        self.nins += 1
        return ev

    def barrier(self):
        evs = [(E["sem"], E["cnt"]) for E in self.engs.values() if E["cnt"] > 0]
        evs += [(s, c) for s, c in zip(self.dma_sems, self.dma_cnt) if c > 0]
        for en in self.engs:
            self._wait(en, evs)


def rope_tab(S, d, rows):
    half = d // 2
    inv = (np.float32(10000.0) ** (-np.arange(half, dtype=np.float32) * np.float32(2.0) / np.float32(d))).astype(np.float32)
    ang = np.arange(S, dtype=np.float32)[:, None] * inv[None, :]
    c = np.cos(ang).astype(np.float32).T
    s = np.sin(ang).astype(np.float32).T
    idx = np.arange(rows) % half
    return c[idx], s[idx]


def make_consts(S):
    bf = ml_dtypes.bfloat16
    I = np.eye(128, dtype=np.float32)
    ones64 = np.zeros((128, 128), np.float32)
    ones64[:64, :64] = 1; ones64[64:, 64:] = 1
    ones96 = np.zeros((128, 128), np.float32); ones96[:96, :96] = 1
    ones128 = np.ones((128, 128), np.float32)

    def rot(M, hd, half, lo=0):
        R = np.zeros((128, 128), np.float32)
        for m in range(M):
            j = m % hd
            if j < lo:
                continue
            jj = j - lo
            if jj < half:
                R[m + half, m] = -1.0
            else:
                R[m - half, m] = 1.0
        return R
    R64 = rot(128, 64, 32)
    R32 = rot(128, 32, 16)
    R96 = rot(96, 96, 16, lo=64)
    nmd = np.zeros((128, 128), np.float32); nmd[:64, 64:] = NEGM
    cbf = np.concatenate([I, ones64, ones96, ones128, R64, R32, R96, nmd], axis=1).astype(bf)
    icross = np.zeros((128, 128), np.float32)
    for m in range(128):
        icross[m, (m + 64) % 128] = 1
    e65 = np.zeros((128, 64), np.float32); e65[64, :] = 1
    i16 = np.zeros((128, 16), np.float32)
    for p in range(128):
        i16[p, p % 16] = 1
    sgB = np.where(np.arange(128) < 64, -1.0, 1.0).astype(np.float32)[:, None]
    sgQ = -sgB
    epsc = np.full((128, 1), EPS, np.float32)
    pw = np.tile((2.0 ** -(np.arange(NIT + 1, dtype=np.float64) + 1.0)).astype(np.float32)[None, :], (128, 1))
    cf = np.concatenate([I, icross, e65, i16, sgB, sgQ, epsc, pw], axis=1).astype(np.float32)
    c64, s64 = rope_tab(S, 64, 128)
    c32, s32 = rope_tab(S, 32, 128)
    c96 = np.ones((128, S), np.float32); s96 = np.zeros((128, S), np.float32)
    c96[64:96] = c32[:32]; s96[64:96] = s32[:32]
    rt = np.stack([c64, s64, c32, s32, c96, s96]).astype(np.float32)
    return dict(cbf=cbf, cf=cf, rt=rt)


CB = dict(ident=0, ones64=128, ones96=256, ones128=384, R64=512, R32=640, R96=768, nmd=896)
CF = dict(ident=0, icross=128, e65=256, i16=320, sgB=336, sgQ=337, eps=338, pw=339)
NCF = 339 + NIT + 1


def build_program(S, NL, NSEQ, KTOP, dbg=None):
    nc = bass.Bass("TRN2", target_bir_lowering=False)
    NB = S // 128
    NG = S // 512
    J = S // 8
    dbg = dbg or {}

    def din(name, shape, dt=F32):
        return nc.dram_tensor(name, list(shape), dt, kind="ExternalInput").ap()

    def dscr(name, shape, dt):
        return nc.dram_tensor(name, list(shape), dt, kind="Internal").ap()

    x_in = din("x", [NSEQ, S, D])
    norm_g = din("norm_g", [DEPTH, D])
    w_in = din("w_in", [DEPTH, D, DIN])
    attn_q_norm = din("attn_q_norm", [DEPTH, 64])
    attn_k_norm = din("attn_k_norm", [DEPTH, 64])
    q_lora_g = din("mla_q_lora_norm", [DEPTH, 256])
    kv_lora_g = din("mla_kv_lora_norm", [DEPTH, 128])
    w_uq = din("mla_w_uq", [DEPTH, 256, 576])
    w_ukv = din("mla_w_ukv", [DEPTH, 128, 768])
    mla_q_norm = din("mla_q_norm", [DEPTH, 96])
    mla_k_norm = din("mla_k_norm", [DEPTH, 96])
    a_re = din("ssm_a_re", [DEPTH, 16, 64])
    a_im = din("ssm_a_im", [DEPTH, 16, 64])
    b_re = din("ssm_b_re", [DEPTH, 16, 64, 16])
    b_im = din("ssm_b_im", [DEPTH, 16, 64, 16])
    c_re = din("ssm_c_re", [DEPTH, 16, 16, 64])
    c_im = din("ssm_c_im", [DEPTH, 16, 16, 64])
    d_skip = din("ssm_d", [DEPTH, 16, 16])
    log_step = din("ssm_log_step", [DEPTH, 16])
    w_glu = din("ssm_w_glu", [DEPTH, 256, 256])
    w_out = din("w_out", [DEPTH, D, D])
    cbf_d = din("cbf", [128, 1024], BF16)
    cf_d = din("cf", [128, NCF])
    rt_d = din("rt", [6, 128, S])
    out_d = nc.dram_tensor("out", [NSEQ, S, D], F32, kind="ExternalOutput").ap()

    xs_d = dscr("xs", [NSEQ, S, D], F32)
    winp_d = dscr("winp", [128, 8, NCOL], BF16)
    wout_d = dscr("woutp", [128, 14, D], BF16)
    qT_d = dscr("qT", [3, 128, S], BF16)
    kT2_d = dscr("kT2", [128, S], BF16)
    va_d = dscr("va", [NB, 128, 65], BF16)
    iqT_d = dscr("iqT", [4, 128, S], BF16)
    ikT_d = dscr("ikT", [128, S], BF16)
    iw_d = dscr("iw", [NB, 128, 16], F32)
    gaT_d = dscr("gaT", [6, 64, S], BF16)
    gbT_d = dscr("gbT", [2, 128, S], BF16)
    gcT_d = dscr("gcT", [6, 64, S], BF16)
    U_d = dscr("U", [16, 8, 16, J], F32)
    Y_d = dscr("Y", [16, 8, 16, J], F32)
    qc_d = dscr("qc", [6, 96, S], BF16)
    kc_d = dscr("kc", [6, 96, S], BF16)
    vc_d = dscr("vc", [6, NB, 128, 65], BF16)
    mix_d = dscr("mix", [14, 128, S], BF16)
    s5c_d = dscr("s5c", [16, 128, 12, 128], F32)

    S_ = Sched(nc)
    op = S_.op
    dma = S_.dma
    RS = {}

    PSUM_KEYS = {"p1_tp", "p1_mm", "p1_aux", "at_ps", "at_o", "at_sbp", "ds_shp", "ds_scp", "s5_pp", "s5r_p", "s5p_z", "p3_pp"}

    def rs(*key):
        if key not in RS:
            RS[key] = Res(ps=(key[0] in PSUM_KEYS))
        return RS[key]

    with contextlib.ExitStack() as top:
        uid = [0]

        def sbt(st, name, shape, dt):
            uid[0] += 1
            return st.enter_context(nc.sbuf_tensor(f"sb{uid[0]}_{name}", list(shape), dt))

        def pst(st, name, shape, dt):
            uid[0] += 1
            return st.enter_context(nc.psum_tensor(f"ps{uid[0]}_{name}", list(shape), dt))

        cbf = sbt(top, "cbf", [128, 1024], BF16)
        cf = sbt(top, "cf", [128, NCF], F32)
        dma(cbf[:], cbf_d, writes=[rs("cbf")])
        dma(cf[:], cf_d, writes=[rs("cf")])
        rC = [rs("cbf"), rs("cf")]

        def cb(name, rows=128, cols=128):
            o = CB[name]
            return cbf[0:rows, o:o + cols]

        def cfa(name, rows=128, cols=128):
            o = CF[name]
            return cf[0:rows, o:o + cols]
        eps_ap = cf[:, CF["eps"]:CF["eps"] + 1]

        gsm = sbt(top, "gsm", [128, DEPTH, 16], F32)
        rg = rs("gsm")
        op("dve", lambda e: e.memset(gsm[:], 1.0), writes=[rg])
        for l in range(NL):
            dma(gsm[:, l, 0:8], norm_g[l].rearrange("(k p) -> p k", p=128), writes=[rg], allow_slow_non_contiguous=True)
            for hh in range(2):
                dma(gsm[64 * hh:64 * hh + 64, l, 8:9], attn_q_norm[l].rearrange("(p o) -> p o", o=1), writes=[rg], allow_slow_non_contiguous=True)
                dma(gsm[64 * hh:64 * hh + 64, l, 9:10], attn_k_norm[l].rearrange("(p o) -> p o", o=1), writes=[rg], allow_slow_non_contiguous=True)
            dma(gsm[:, l, 10:12], q_lora_g[l].rearrange("(k p) -> p k", p=128), writes=[rg], allow_slow_non_contiguous=True)
            dma(gsm[:, l, 12:13], kv_lora_g[l].rearrange("(p o) -> p o", o=1), writes=[rg], allow_slow_non_contiguous=True)
            dma(gsm[0:96, l, 13:14], mla_q_norm[l].rearrange("(p o) -> p o", o=1), writes=[rg], allow_slow_non_contiguous=True)
            dma(gsm[0:96, l, 14:15], mla_k_norm[l].rearrange("(p o) -> p o", o=1), writes=[rg], allow_slow_non_contiguous=True)

        def prep_weights(l):
            with contextlib.ExitStack() as st:
                stg = [sbt(st, f"stg{i}", [128, DIN], F32) for i in range(2)]
                wrow = [sbt(st, f"wrow{i}", [128, NCOL], BF16) for i in range(2)]
                for kc in range(8):
                    i = kc % 2
                    rst, rw = rs("stg", i), rs("wrow", i)
                    dma(stg[i][:], w_in[l, kc * 128:(kc + 1) * 128, :], writes=[rst])
                    g = gsm[:, l, kc:kc + 1]
                    segs = [(C_QA, S_QA, 384), (C_KA2, S_KA, 64), (C_KA2 + 64, S_KA, 64)] + [(C_IQ2 + (h_ // 2) * 128 + 64 * (h_ % 2), S_IQ + 32 * h_, 32) for h_ in range(8)] + \
                           [(C_IK4 + 32 * r, S_IK, 32) for r in range(4)] + \
                           [(C_GA, S_GA, 384), (C_U, S_U, 256), (C_GB, S_GB, 256), (C_CQ, S_CQ, 256), (C_CKV, S_CKV, 128),
                            (C_KPE + 64, S_KPE, 32), (C_GC, S_GC, 384), (C_VAIW, S_VA, 64), (C_VAIW + 64, S_IW, 8)]
                    op("pool", lambda e: e.memset(wrow[i][:, C_KPE:C_KPE + 64], 0.0), writes=[rw])
                    op("pool", lambda e: e.memset(wrow[i][:, C_IQ2:C_IQ2 + 512], 0.0), writes=[rw])
                    op("pool", lambda e: e.memset(wrow[i][:, C_IQ:C_IQ + 256], 0.0), writes=[rw])
                    for n_, (dc, sc_, w_) in enumerate(segs):
                        if n_ % 2 == 0:
                            op("dve", lambda e: e.tensor_scalar(out=wrow[i][:, dc:dc + w_], in0=stg[i][:, sc_:sc_ + w_], scalar1=g, scalar2=None, op0=ALU.mult),
                               reads=[rst, rg], writes=[rw])
                        else:
                            op("act", lambda e: e.activation(out=wrow[i][:, dc:dc + w_], in_=stg[i][:, sc_:sc_ + w_], func=AF.Copy, scale=g),
                               reads=[rst, rg], writes=[rw])
                    dma(winp_d[:, kc, :], wrow[i][:], reads=[rw], writes=[rs("winp_d")])
            with contextlib.ExitStack() as st:
                stg = [sbt(st, f"stgo{i}", [128, D], F32) for i in range(2)]
                wrow = [sbt(st, f"wrowo{i}", [128, D], BF16) for i in range(2)]
                for c in range(14):
                    i = c % 2
                    rst, rw = rs("stgo", i), rs("wrowo", i)
                    if c < 6:
                        r0, nr = 64 * c, 64
                    elif c < 8:
                        r0, nr = 384 + 128 * (c - 6), 128
                    else:
                        r0, nr = 640 + 64 * (c - 8), 64
                    dma(stg[i][0:nr, :], w_out[l, r0:r0 + nr, :], writes=[rst])
                    if c % 2 == 0:
                        op("dve", lambda e: e.tensor_copy(out=wrow[i][0:nr, :], in_=stg[i][0:nr, :]), reads=[rst], writes=[rw])
                    else:
                        op("act", lambda e: e.copy(out=wrow[i][0:nr, :], in_=stg[i][0:nr, :]), reads=[rst], writes=[rw])
                    dma(wout_d[0:nr, c, :], wrow[i][0:nr, :], reads=[rw], writes=[rs("wout_d")])
            with contextlib.ExitStack() as st:
                s1 = sbt(st, "s_uq", [128, 2, 576], F32)
                s2 = sbt(st, "s_ukv", [128, 768], F32)
                s3 = sbt(st, "s_glu", [128, 2, 256], F32)
                r1, r2, r3 = rs("s_uq"), rs("s_ukv"), rs("s_glu")
                dma(s1[:], w_uq[l].rearrange("(k p) n -> p k n", p=128), writes=[r1])
                dma(s2[:], w_ukv[l], writes=[r2])
                dma(s3[:], w_glu[l].rearrange("(k p) n -> p k n", p=128), writes=[r3])
                rw = rs("wsm")
                for k in range(2):
                    op("dve", lambda e: e.tensor_scalar(out=Wuq[:, k, :], in0=s1[:, k, :], scalar1=gsm[:, l, 10 + k:11 + k], scalar2=None, op0=ALU.mult),
                       reads=[r1, rg], writes=[rw])
                    op("dve", lambda e: e.tensor_copy(out=Wglu[:, k, :], in_=s3[:, k, :]), reads=[r3], writes=[rw])
                op("dve", lambda e: e.memset(WukT[:], 0.0), writes=[rw])
                for h in range(6):
                    op("dve", lambda e: e.tensor_scalar(out=WukT[:, h, 0:64], in0=s2[:, h * 128:h * 128 + 64], scalar1=gsm[:, l, 12:13], scalar2=None, op0=ALU.mult),
                       reads=[r2, rg], writes=[rw])
                    op("dve", lambda e: e.tensor_scalar(out=Wuv[:, h * 64:h * 64 + 64], in0=s2[:, h * 128 + 64:h * 128 + 128], scalar1=gsm[:, l, 12:13], scalar2=None, op0=ALU.mult),
                       reads=[r2, rg], writes=[rw])
            S_.barrier()

        Wuq = sbt(top, "Wuq", [128, 2, 576], BF16)
        WukT = sbt(top, "WukT", [128, 6, 96], BF16)
        Wuv = sbt(top, "Wuv", [128, 384], BF16)
        Wglu = sbt(top, "Wglu", [128, 2, 256], BF16)
        rWsm = rs("wsm")

        def phase1(l, src_ap, src_res):
            with contextlib.ExitStack() as st:
                Winp = sbt(st, "Winp", [128, 8, NCOL], BF16)
                rW = rs("Winp")
                for kc in range(8):
                    dma(Winp[:, kc, :], winp_d[:, kc, :], reads=[rs("winp_d")], writes=[rW])
                xt = sbt(st, "p1_xt", [128, 4, D], F32)
                rxt = rs("p1_xt")
                junk = sbt(st, "p1_junk", [128, D], BF16)
                xn = sbt(st, "p1_xn", [128, D], BF16)
                ssq = sbt(st, "p1_ssq", [128, 8], F32)
                hT = sbt(st, "p1_hT", [128, 8, 512], BF16)
                rhT = rs("p1_hT")
                tabs = sbt(st, "p1_tabs", [128, 6, 512], F32)
                rtab = rs("p1_tabs")
                tp = [pst(st, f"p1_tp{i}", [128, 1024], BF16) for i in range(2)]
                mm = [pst(st, f"p1_mm{i}", [128, 512], F32) for i in range(3)]
                aux = [pst(st, f"p1_aux{i}", [128, 512], F32) for i in range(3)]
                mmi = [0]
                NWK = 3
                xg = [sbt(st, f"p1_xg{i}", [128, 512], BF16) for i in range(NWK)]
                sq = [sbt(st, f"p1_sq{i}", [128, 512], BF16) for i in range(NWK)]
                sd = [sbt(st, f"p1_sd{i}", [128, 512], F32) for i in range(NWK)]
                t1 = [sbt(st, f"p1_t1{i}", [128, 512], F32) for i in range(NWK)]
                t2 = [sbt(st, f"p1_t2{i}", [128, 512], F32) for i in range(NWK)]
                ob = [sbt(st, f"p1_ob{i}", [128, 512], BF16) for i in range(NWK)]
                up = [sbt(st, f"p1_up{i}", [128, 8, 64], F32) for i in range(2)]
                cqn = sbt(st, "p1_cqn", [128, 2, 512], BF16)
                ckvn = sbt(st, "p1_ckvn", [128, 512], BF16)
                vcs = [sbt(st, f"p1_vcs{i}", [128, 6, 65], BF16) for i in range(2)]
                vas = [sbt(st, f"p1_vas{i}", [128, 65], BF16) for i in range(2)]
                iws = [sbt(st, f"p1_iws{i}", [128, 16], F32) for i in range(2)]
                for i in range(2):
                    op("pool", lambda e: e.memset(vcs[i][:], 1.0), writes=[rs("p1_vcs", i)])
                    op("pool", lambda e: e.memset(vas[i][:], 1.0), writes=[rs("p1_vas", i)])
                wk = [0]

                def proj(cols, M, rhs_fn=None):
                    k = mmi[0] % 3
                    mmi[0] += 1
                    r = rs("p1_mm", k)
                    for kc in range(8):
                        op("pe", lambda e: e.matmul(mm[k][0:M, :], lhsT=Winp[:, kc, cols:cols + M], rhs=hT[:, kc, :], start=(kc == 0), stop=(kc == 7)),
                           reads=[rW, rhT], writes=[r])
                    return mm[k], r

                def normrope(X, rX, M, gain, onesname, inv_dim, Rname, ci, dst_ap, dst_res):
                    i = wk[0] % NWK
                    wk[0] += 1
                    a0, a1 = aux[(2 * i) % 3], aux[(2 * i + 1) % 3]
                    ra0, ra1 = rs("p1_aux", (2 * i) % 3), rs("p1_aux", (2 * i + 1) % 3)
                    rxg, rsq, rsd, rt1, rt2, rob = (rs("p1_xg", i), rs("p1_sq", i), rs("p1_sd", i), rs("p1_t1", i), rs("p1_t2", i), rs("p1_ob", i))
                    Ct, St = tabs[0:M, ci, :], tabs[0:M, ci + 1, :]
                    if gain is not None:
                        op("act", lambda e: e.activation(out=xg[i][0:M, :], in_=X[0:M, :], func=AF.Copy, scale=gain), reads=[rX, rg], writes=[rxg])
                    else:
                        op("act", lambda e: e.copy(out=xg[i][0:M, :], in_=X[0:M, :]), reads=[rX], writes=[rxg])
                    op("pe", lambda e: e.matmul(a1[0:M, :], lhsT=cb(Rname, M, M), rhs=xg[i][0:M, :], start=True, stop=True), reads=[rxg] + rC, writes=[ra1])
                    if onesname is not None:
                        op("act", lambda e: e.activation(out=sq[i][0:M, :], in_=X[0:M, :], func=AF.Square), reads=[rX], writes=[rsq])
                        op("pe", lambda e: e.matmul(a0[0:M, :], lhsT=cb(onesname, M, M), rhs=sq[i][0:M, :], start=True, stop=True), reads=[rsq] + rC, writes=[ra0])
                        op("act", lambda e: e.activation(out=sd[i][0:M, :], in_=a0[0:M, :], func=AF.Sqrt, scale=inv_dim, bias=eps_ap[0:M, :]), reads=[ra0] + rC, writes=[rsd])
                        op("dve", lambda e: e.reciprocal(out=sd[i][0:M, :], in_=sd[i][0:M, :]), reads=[rsd], writes=[rsd])
                    if gain is not None:
                        op("dve", lambda e: e.scalar_tensor_tensor(out=t1[i][0:M, :], in0=X[0:M, :], scalar=gain, in1=Ct, op0=ALU.mult, op1=ALU.mult),
                           reads=[rX, rtab, rg], writes=[rt1])
                    else:
                        op("dve", lambda e: e.tensor_tensor(out=t1[i][0:M, :], in0=X[0:M, :], in1=Ct, op=ALU.mult), reads=[rX, rtab], writes=[rt1])
                    op("dve", lambda e: e.tensor_tensor(out=t2[i][0:M, :], in0=a1[0:M, :], in1=St, op=ALU.mult), reads=[ra1, rtab], writes=[rt2])
                    if onesname is not None:
                        op("dve", lambda e: e.tensor_tensor(out=t1[i][0:M, :], in0=t1[i][0:M, :], in1=t2[i][0:M, :], op=ALU.add), reads=[rt1, rt2], writes=[rt1])
                        op("dve", lambda e: e.tensor_tensor(out=ob[i][0:M, :], in0=t1[i][0:M, :], in1=sd[i][0:M, :], op=ALU.mult), reads=[rt1, rsd], writes=[rob])
                    else:
                        op("dve", lambda e: e.tensor_tensor(out=ob[i][0:M, :], in0=t1[i][0:M, :], in1=t2[i][0:M, :], op=ALU.add), reads=[rt1, rt2], writes=[rob])
                    dma(dst_ap, ob[i][0:M, :], reads=[rob], writes=[dst_res])

                def silu_out(X, rX, M, dsts):
                    i = wk[0] % NWK
                    wk[0] += 1
                    rob = rs("p1_ob", i)
                    rt1 = rs("p1_t1", i)
                    if GV == 1:
                        op("act", lambda e: e.copy(out=t1[i][0:M, :], in_=X[0:M, :]), reads=[rX], writes=[rt1])
                    else:
                        op("act", lambda e: e.activation(out=t1[i][0:M, :], in_=X[0:M, :], func=AF.Exp, scale=-1.0), reads=[rX], writes=[rt1])
                    if GV != 2:
                        op("dve", lambda e: e.tensor_scalar(out=t1[i][0:M, :], in0=t1[i][0:M, :], scalar1=1.0, scalar2=None, op0=ALU.add), reads=[rt1], writes=[rt1])
                        op("dve", lambda e: e.reciprocal(out=t1[i][0:M, :], in_=t1[i][0:M, :]), reads=[rt1], writes=[rt1])
                    if GV != 3:
                        op("dve", lambda e: e.scalar_tensor_tensor(out=ob[i][0:M, :], in0=X[0:M, :], scalar=1.0, in1=t1[i][0:M, :], op0=ALU.mult, op1=ALU.mult), reads=[rX, rt1], writes=[rob])
                    for (p0, p1, dap, dres) in dsts:
                        if GATE_DMA == 0 or (GATE_DMA == 1 and p0 != 0):
                            continue
                        dma(dap, ob[i][p0:p1, :], reads=[rob], writes=[dres])

                for tg in range(NG):
                    t0 = tg * 512
                    tsl = slice(t0, t0 + 512)
                    dma(xt[:], src_ap[t0:t0 + 512, :].rearrange("(j p) d -> p j d", p=128), reads=[src_res], writes=[rxt])
                    for ci in range(6):
                        dma(tabs[:, ci, :], rt_d[ci, :, tsl], writes=[rtab])
                    rssq, rjunk, rxn = rs("p1_ssq"), rs("p1_junk"), rs("p1_xn")
                    for j in range(4):
                        op("act", lambda e: e.activation(out=junk[:], in_=xt[:, j, :], func=AF.Square, accum_out=ssq[:, j:j + 1]), reads=[rxt], writes=[rjunk, rssq])
                    op("act", lambda e: e.activation(out=ssq[:, 4:8], in_=ssq[:, 0:4], func=AF.Sqrt, scale=1.0 / D, bias=eps_ap), reads=[rssq] + rC, writes=[rssq])
                    op("dve", lambda e: e.reciprocal(out=ssq[:, 4:8], in_=ssq[:, 4:8]), reads=[rssq], writes=[rssq])
                    for j in range(4):
                        op("dve", lambda e: e.tensor_scalar(out=xn[:], in0=xt[:, j, :], scalar1=ssq[:, 4 + j:5 + j], scalar2=None, op0=ALU.mult), reads=[rxt, rssq], writes=[rxn])
                        for half in range(2):
                            k = half
                            rtp = rs("p1_tp", k)
                            for q in range(4):
                                kc = half * 4 + q
                                op("pe", lambda e: e.transpose(out=tp[k][:, q * 128:(q + 1) * 128], in_=xn[:, kc * 128:(kc + 1) * 128], identity=cb("ident")),
                                   reads=[rxn] + rC, writes=[rtp])
                            eng = "act" if half == 0 else "dve"
                            if eng == "act":
                                op("act", lambda e: e.copy(out=hT[:, half * 4:half * 4 + 4, j * 128:(j + 1) * 128], in_=tp[k][:, 0:512].rearrange("p (q t) -> p q t", q=4)), reads=[rtp], writes=[rhT])
                            else:
                                op("dve", lambda e: e.tensor_copy(out=hT[:, half * 4:half * 4 + 4, j * 128:(j + 1) * 128], in_=tp[k][:, 0:512].rearrange("p (q t) -> p q t", q=4)), reads=[rtp], writes=[rhT])
                    if P1CUT <= 1:
                        continue
                    for c in range(3):
                        X, rX = proj(C_QA + 128 * c, 128)
                        normrope(X, rX, 128, gsm[:, l, 8:9], "ones64", 1.0 / 64, "R64", 0, qT_d[c, :, tsl], rs("qT_d"))
                    X, rX = proj(C_KA2, 128)
                    normrope(X, rX, 128, gsm[:, l, 9:10], "ones64", 1.0 / 64, "R64", 0, kT2_d[:, tsl], rs("kT2_d"))
                    if P1CUT <= 2:
                        continue
                    for c in range(4):
                        X, rX = proj(C_IQ2 + 128 * c, 128)
                        normrope(X, rX, 128, None, None, None, "R32", 2, iqT_d[c, :, tsl], rs("iqT_d"))
                    X, rX = proj(C_IK4, 128)
                    normrope(X, rX, 128, None, None, None, "R32", 2, ikT_d[:, tsl], rs("ikT_d"))
                    if P1CUT <= 3:
                        continue
                    for c in range(3):
                        X, rX = proj(C_GA + 128 * c, 128)
                        silu_out(X, rX, 128, [(0, 64, gaT_d[2 * c, :, tsl], rs("gaT_d")), (64, 128, gaT_d[2 * c + 1, :, tsl], rs("gaT_d"))])
                    for c in range(2):
                        X, rX = proj(C_GB + 128 * c, 128)
                        silu_out(X, rX, 128, [(0, 128, gbT_d[c, :, tsl], rs("gbT_d"))])
                    for c in range(3):
                        X, rX = proj(C_GC + 128 * c, 128)
                        silu_out(X, rX, 128, [(0, 64, gcT_d[2 * c, :, tsl], rs("gcT_d")), (64, 128, gcT_d[2 * c + 1, :, tsl], rs("gcT_d"))])
                    if P1CUT <= 4:
                        continue
                    for c in range(2):
                        X, rX = proj(C_U + 128 * c, 128)
                        i = c
                        rup = rs("p1_up", i)
                        op("dve", lambda e: e.tensor_copy(out=up[i][:].rearrange("p s j -> p j s"), in_=X[:].rearrange("p (j s) -> p j s", s=8)), reads=[rX], writes=[rup])
                        for g8 in range(8):
                            g = 8 * c + g8
                            for s_ in range(8):
                                dma(U_d[g, s_, :, tg * 64:(tg + 1) * 64], up[i][16 * g8:16 * g8 + 16, s_, :], reads=[rup], writes=[rs("U_d")])
                    if P1CUT <= 5:
                        continue
                    X0, rX0 = proj(C_CQ, 128)
                    X1, rX1 = proj(C_CQ + 128, 128)
                    i = wk[0] % NWK
                    wk[0] += 1
                    a0, ra0 = aux[(2 * i) % 3], rs("p1_aux", (2 * i) % 3)
                    rsq, rsd, rt1 = rs("p1_sq", i), rs("p1_sd", i), rs("p1_t1", i)
                    i2 = wk[0] % NWK
                    wk[0] += 1
                    rsq2 = rs("p1_sq", i2)
                    op("act", lambda e: e.activation(out=sq[i][:], in_=X0[:], func=AF.Square), reads=[rX0], writes=[rsq])
                    op("act", lambda e: e.activation(out=sq[i2][:], in_=X1[:], func=AF.Square), reads=[rX1], writes=[rsq2])
                    op("pe", lambda e: e.matmul(a0[:], lhsT=cb("ones128"), rhs=sq[i][:], start=True, stop=False), reads=[rsq] + rC, writes=[ra0])
                    op("pe", lambda e: e.matmul(a0[:], lhsT=cb("ones128"), rhs=sq[i2][:], start=False, stop=True), reads=[rsq2] + rC, writes=[ra0])
                    op("act", lambda e: e.activation(out=sd[i][:], in_=a0[:], func=AF.Sqrt, scale=1.0 / 256, bias=eps_ap), reads=[ra0] + rC, writes=[rsd])
                    op("dve", lambda e: e.reciprocal(out=sd[i][:], in_=sd[i][:]), reads=[rsd], writes=[rsd])
                    rcq = rs("p1_cqn")
                    op("dve", lambda e: e.tensor_tensor(out=cqn[:, 0, :], in0=X0[:], in1=sd[i][:], op=ALU.mult), reads=[rX0, rsd], writes=[rcq])
                    op("dve", lambda e: e.tensor_tensor(out=cqn[:, 1, :], in0=X1[:], in1=sd[i][:], op=ALU.mult), reads=[rX1, rsd], writes=[rcq])
                    X0, rX0 = proj(C_CKV, 128)
                    i = wk[0] % NWK
                    wk[0] += 1
                    a0, ra0 = aux[(2 * i) % 3], rs("p1_aux", (2 * i) % 3)
                    rsq, rsd = rs("p1_sq", i), rs("p1_sd", i)
                    op("act", lambda e: e.activation(out=sq[i][:], in_=X0[:], func=AF.Square), reads=[rX0], writes=[rsq])
                    op("pe", lambda e: e.matmul(a0[:], lhsT=cb("ones128"), rhs=sq[i][:], start=True, stop=True), reads=[rsq] + rC, writes=[ra0])
                    op("act", lambda e: e.activation(out=sd[i][:], in_=a0[:], func=AF.Sqrt, scale=1.0 / 128, bias=eps_ap), reads=[ra0] + rC, writes=[rsd])
                    op("dve", lambda e: e.reciprocal(out=sd[i][:], in_=sd[i][:]), reads=[rsd], writes=[rsd])
                    rckv = rs("p1_ckvn")
                    op("dve", lambda e: e.tensor_tensor(out=ckvn[:], in0=X0[:], in1=sd[i][:], op=ALU.mult), reads=[rX0, rsd], writes=[rckv])
                    if P1CUT <= 6:
                        continue
                    for h in range(6):
                        k = mmi[0] % 3
                        mmi[0] += 1
                        r = rs("p1_mm", k)
                        for c in range(2):
                            op("pe", lambda e: e.matmul(mm[k][0:96, :], lhsT=Wuq[:, c, h * 96:(h + 1) * 96], rhs=cqn[:, c, :], start=(c == 0), stop=(c == 1)),
                               reads=[rWsm, rcq], writes=[r])
                        normrope(mm[k], r, 96, gsm[0:96, l, 13:14], "ones96", 1.0 / 96, "R96", 4, qc_d[h, :, tsl], rs("qc_d"))
                    for h in range(6):
                        k = mmi[0] % 3
                        mmi[0] += 1
                        r = rs("p1_mm", k)
                        op("pe", lambda e: e.matmul(mm[k][0:96, :], lhsT=WukT[:, h, :], rhs=ckvn[:], start=True, stop=False), reads=[rWsm, rckv], writes=[r])
                        for kc in range(8):
                            op("pe", lambda e: e.matmul(mm[k][0:96, :], lhsT=Winp[:, kc, C_KPE:C_KPE + 96], rhs=hT[:, kc, :], start=False, stop=(kc == 7)),
                               reads=[rW, rhT], writes=[r])
                        normrope(mm[k], r, 96, gsm[0:96, l, 14:15], "ones96", 1.0 / 96, "R96", 4, kc_d[h, :, tsl], rs("kc_d"))
                    if P1CUT <= 7:
                        continue
                    for j in range(4):
                        tb = tg * 4 + j
                        k = mmi[0] % 3
                        mmi[0] += 1
                        r = rs("p1_mm", k)
                        op("pe", lambda e: e.matmul(mm[k][:, 0:384], lhsT=ckvn[:, j * 128:(j + 1) * 128], rhs=Wuv[:], start=True, stop=True), reads=[rWsm, rckv], writes=[r])
                        i = j % 2
                        rv = rs("p1_vcs", i)
                        op("act", lambda e: e.copy(out=vcs[i][:, :, 0:64], in_=mm[k][:, 0:384].rearrange("p (h d) -> p h d", h=6)), reads=[r], writes=[rv])
                        for h in range(6):
                            dma(vc_d[h, tb, :, :], vcs[i][:, h, :], reads=[rv], writes=[rs("vc_d")])
                        k = mmi[0] % 3
                        mmi[0] += 1
                        r = rs("p1_mm", k)
                        for kc in range(8):
                            op("pe", lambda e: e.matmul(mm[k][:, 0:72], lhsT=hT[:, kc, j * 128:(j + 1) * 128], rhs=Winp[:, kc, C_VAIW:C_VAIW + 72], start=(kc == 0), stop=(kc == 7)),
                               reads=[rW, rhT], writes=[r])
                        rva, riw = rs("p1_vas", i), rs("p1_iws", i)
                        op("act", lambda e: e.copy(out=vas[i][:, 0:64], in_=mm[k][:, 0:64]), reads=[r], writes=[rva])
                        dma(va_d[tb, :, :], vas[i][:], reads=[rva], writes=[rs("va_d")])
                        sc_ = (8.0 ** -0.5) * (32.0 ** -0.5)
                        op("act", lambda e: e.activation(out=iws[i][:, 0:8], in_=mm[k][:, 64:72], func=AF.Abs, scale=sc_), reads=[r], writes=[riw])
                        op("act", lambda e: e.activation(out=iws[i][:, 8:16], in_=mm[k][:, 64:72], func=AF.Sign), reads=[r], writes=[riw])
                        dma(iw_d[tb, :, :], iws[i][:], reads=[riw], writes=[rs("iw_d")])
            S_.barrier()

        def att_tiles(st, n_ops=2):
            T = {}
            T["att"] = [pst(st, f"at_ps{i}", [128, 512], F32) for i in range(2)]
            T["Ops"] = [pst(st, f"at_o{i}", [128, 512], F32) for i in range(n_ops)]
            T["sbp"] = pst(st, "at_sb", [128, 512], F32)
            T["PT"] = [sbt(st, f"at_pt{i}", [128, 512], BF16) for i in range(3)]
            T["Osb"] = [sbt(st, f"at_osb{i}", [65, 6, 512], F32) for i in range(2)]
            T["rcp"] = [sbt(st, f"at_rcp{i}", [64, 512], F32) for i in range(2)]
            T["yb"] = [sbt(st, f"at_yb{i}", [64, 512], BF16) for i in range(2)]
            T["gt"] = [sbt(st, f"at_gt{i}", [64, 6, 512], BF16) for i in range(2)]
            T["cnt"] = [0, 0, 0]
            return T

        def att_main(T, sb, n_heads, scale, heads, qfn, qres, nm_fn):
            att, Ops, PT, Osb = T["att"], T["Ops"], T["PT"], T["Osb"]
            cnt = T["cnt"]
            ob = sb % 2
            nkb = 4 * sb + 4
            rosb = rs("at_osb", ob)
            n_ops = len(Ops)
            steps = [(h, kb) for h in range(n_heads) for kb in range(nkb)]
            pend = None
            ois = {}
            for stp in steps + [None]:
                cur = None
                if stp is not None:
                    h, kb = stp
                    H = heads[h]
                    if kb == 0:
                        ois[h] = cnt[1] % n_ops
                        cnt[1] += 1
                    qb0 = max(kb - 4 * sb, 0)
                    qs = slice(qb0 * 128, 512)
                    ai = cnt[0] % 2
                    pi = cnt[0] % 3
                    cnt[0] += 1
                    ra, rp = rs("at_ps", ai), rs("at_pt", pi)
                    masks = []
                    for qb in range(qb0, 4):
                        m = nm_fn(4 * sb + qb, kb)
                        if m is not None:
                            masks.append((qb, m))
                    op("pe", lambda e: e.matmul(att[ai][:, qs], lhsT=H["kT"](slice(kb * 128, kb * 128 + 128)), rhs=qfn(h, qs), start=True, stop=(len(masks) == 0)),
                       reads=[H["kres"], qres], writes=[ra])
                    for mi, (qb, (map_, mres)) in enumerate(masks):
                        op("pe", lambda e: e.matmul(att[ai][:, qb * 128:(qb + 1) * 128], lhsT=map_, rhs=cb("ident"), start=False, stop=(mi == len(masks) - 1)),
                           reads=[mres] + rC, writes=[ra])
                    op("act", lambda e: e.activation(out=PT[pi][:, qs], in_=att[ai][:, qs], func=AF.Exp, scale=scale), reads=[ra], writes=[rp])
                    cur = (h, kb, pi, qs)
                if pend is not None:
                    h2, kb2, pi2, qs2 = pend
                    H2 = heads[h2]
                    oi = ois[h2]
                    rO = rs("at_o", oi)
                    op("pe", lambda e: e.matmul(Ops[oi][0:65, qs2], lhsT=H2["vaug"][:, kb2, :], rhs=PT[pi2][:, qs2], start=(kb2 == 0), stop=(kb2 == nkb - 1)),
                       reads=[rs("at_pt", pi2), H2["vres"]], writes=[rO])
                    if kb2 == nkb - 1:
                        op("act", lambda e: e.copy(out=Osb[ob][:, h2, :], in_=Ops[oi][0:65, :]), reads=[rO], writes=[rosb])
                pend = cur

        def att_norm(T, sb, n_heads, gate_d, gate_res, mix_base):
            Osb, sbp, rcp, yb, gt = T["Osb"], T["sbp"], T["rcp"], T["yb"], T["gt"]
            cnt = T["cnt"]
            ob = sb % 2
            rosb, rgt, rsb = rs("at_osb", ob), rs("at_gt", ob), rs("at_sbp")
            ssl = slice(sb * 512, (sb + 1) * 512)
            dma(gt[ob][:], gate_d[:, :, ssl].rearrange("h p t -> p h t"), reads=[rs(gate_res)], writes=[rgt])
            for h in range(n_heads):
                ri = cnt[2] % 2
                cnt[2] += 1
                rrc, ryb = rs("at_rcp", ri), rs("at_yb", ri)
                op("pe", lambda e: e.matmul(sbp[0:64, :], lhsT=cfa("e65", 65, 64), rhs=Osb[ob][:, h, :], start=True, stop=True), reads=[rosb] + rC, writes=[rsb])
                op("dve", lambda e: e.reciprocal(out=rcp[ri][:], in_=sbp[0:64, :]), reads=[rsb], writes=[rrc])
                op("dve", lambda e: e.tensor_tensor(out=rcp[ri][:], in0=rcp[ri][:], in1=Osb[ob][0:64, h, :], op=ALU.mult), reads=[rrc, rosb], writes=[rrc])
                op("dve", lambda e: e.tensor_tensor(out=yb[ri][:], in0=rcp[ri][:], in1=gt[ob][:, h, :], op=ALU.mult), reads=[rrc, rgt], writes=[ryb])
                dma(mix_d[mix_base + h, 0:64, ssl], yb[ri][:], reads=[ryb], writes=[rs("mix_d")])

        def dsa_phase():
            with contextlib.ExitStack() as st:
                kT2 = sbt(st, "ds_kT2", [128, S], BF16)
                vaug = sbt(st, "ds_vaug", [128, NB, 65], BF16)
                ikT = sbt(st, "ds_ikT", [128, S], BF16)
                rk, rv, rik = rs("ds_kT2"), rs("ds_vaug"), rs("ds_ikT")
                dma(kT2[:], kT2_d, reads=[rs("kT2_d")], writes=[rk])
                dma(vaug[:], va_d.rearrange("t p c -> p t c"), reads=[rs("va_d")], writes=[rv])
                dma(ikT[:], ikT_d, reads=[rs("ikT_d")], writes=[rik])
                NM = [sbt(st, f"ds_NM{i}", [128, 4, S], BF16) for i in range(2)]
                sc = [sbt(st, f"ds_sc{i}", [128, S], F32) for i in range(2)]
                junk = sbt(st, "ds_junk", [128, S], BF16)
                iq = [sbt(st, f"ds_iq{i}", [128, 4, 128], BF16) for i in range(2)]
                iw = [sbt(st, f"ds_iw{i}", [128, 16], F32) for i in range(2)]
                Dh = [sbt(st, f"ds_Dh{i}", [128, 8, 128], BF16) for i in range(2)]
                Th = [sbt(st, f"ds_Th{i}", [128, 512], BF16) for i in range(3)]
                bs = [sbt(st, f"ds_bs{i}", [128, 8 + NIT + 1], F32) for i in range(2)]
                shp = [pst(st, f"ds_shp{i}", [128, 512], F32) for i in range(2)]
                scp = [pst(st, f"ds_scp{i}", [128, 512], F32) for i in range(2)]
                T = att_tiles(st, n_ops=1)
                qs_t = [sbt(st, f"ds_q{i}", [128, 3, 512], BF16) for i in range(2)]
                c3 = [0, 0]

                def masks(sb):
                    for qb in range(4):
                        b = 4 * sb + qb
                        n = 128 * (b + 1)
                        if n <= KTOP:
                            continue
                        i = b % 2
                        rsc, riq, riw, rDh, rbs = rs("ds_sc", i), rs("ds_iq", i), rs("ds_iw", i), rs("ds_Dh", i), rs("ds_bs", i)
                        rNM = rs("ds_NM", sb % 2)
                        dma(iq[i][:], iqT_d[:, :, b * 128:(b + 1) * 128].rearrange("c p t -> p c t"), reads=[rs("iqT_d")], writes=[riq])
                        dma(iw[i][:], iw_d[b, :, :], reads=[rs("iw_d")], writes=[riw])
                        for h in range(8):
                            op("dve", lambda e: e.tensor_scalar(out=Dh[i][:, h, :], in0=cb("ident"), scalar1=iw[i][:, 8 + h:9 + h], scalar2=None, op0=ALU.mult),
                               reads=[riw] + rC, writes=[rDh])
                        chunks = [(c * 512, min(512, n - c * 512)) for c in range((n + 511) // 512)]
                        isteps = [(ci, h) for ci in range(len(chunks)) for h in range(8)]
                        ipend = None
                        for ist in isteps + [None]:
                            icur = None
                            if ist is not None:
                                ci, h = ist
                                k0, wc = chunks[ci]
                                hi_ = c3[0] % 2
                                ti_ = c3[0] % 3
                                c3[0] += 1
                                rsh, rth = rs("ds_shp", hi_), rs("ds_Th", ti_)
                                pb = 64 * (h % 2)
                                op("pe", lambda e: e.matmul(shp[hi_][:, 0:wc], lhsT=iq[i][pb:pb + 32, h // 2, :], rhs=ikT[pb:pb + 32, k0:k0 + wc], start=True, stop=True),
                                   reads=[riq, rik], writes=[rsh])
                                op("act", lambda e: e.activation(out=Th[ti_][:, 0:wc], in_=shp[hi_][:, 0:wc], func=AF.Relu, scale=iw[i][:, h:h + 1]), reads=[rsh, riw], writes=[rth])
                                if h == 0:
                                    c3[1] += 1
                                icur = (ci, h, ti_, c3[1] % 2)
                            if ipend is not None:
                                ci2, h2, ti2, si = ipend
                                k0, wc = chunks[ci2]
                                rscp = rs("ds_scp", si)
                                op("pe", lambda e: e.matmul(scp[si][:, 0:wc], lhsT=Dh[i][:, h2, :], rhs=Th[ti2][:, 0:wc], start=(h2 == 0), stop=(h2 == 7)), reads=[rs("ds_Th", ti2), rDh], writes=[rscp])
                                if h2 == 7:
                                    op("act", lambda e: e.copy(out=sc[i][:, k0:k0 + wc], in_=scp[si][:, 0:wc]), reads=[rscp], writes=[rsc])
                            ipend = icur
                        lo, hi, mid, cn, tt, rng = (bs[i][:, k:k + 1] for k in range(6))
                        Hc = lambda it: bs[i][:, 8 + it:9 + it]
                        op("dve", lambda e: e.tensor_reduce(out=hi, in_=sc[i][:, 0:n], axis=AX.X, op=ALU.max), reads=[rsc], writes=[rbs])
                        op("dve", lambda e: e.tensor_reduce(out=lo, in_=sc[i][:, 0:n], axis=AX.X, op=ALU.min), reads=[rsc], writes=[rbs])
                        op("dve", lambda e: e.scalar_tensor_tensor(out=rng, in0=hi, scalar=1.0, in1=lo, op0=ALU.add, op1=ALU.subtract), reads=[rbs], writes=[rbs])
                        op("dve", lambda e: e.tensor_scalar(out=bs[i][:, 8:8 + NIT + 1], in0=cf[:, CF["pw"]:CF["pw"] + NIT + 1], scalar1=rng, scalar2=None, op0=ALU.mult), reads=[rbs] + rC, writes=[rbs])
                        op("dve", lambda e: e.tensor_tensor(out=mid, in0=lo, in1=Hc(0), op=ALU.add), reads=[rbs], writes=[rbs])
                        op("dve", lambda e: e.memset(sc[i][0:64, n - 64:n], -1e30), writes=[rsc])
                        for it in range(NIT):
                            op("dve", lambda e: e.tensor_scalar(out=junk[:, 0:n], in0=sc[i][:, 0:n], scalar1=mid, scalar2=None, op0=ALU.is_ge, op1=ALU.add, accum_out=cn),
                               reads=[rsc, rbs], writes=[rbs, rs("ds_junk")])
                            op("dve", lambda e: e.tensor_scalar(out=tt, in0=cn, scalar1=float(KTOP) - 0.5, scalar2=-0.5, op0=ALU.is_ge, op1=ALU.add), reads=[rbs], writes=[rbs])
                            op("dve", lambda e: e.scalar_tensor_tensor(out=mid, in0=tt, scalar=Hc(it), in1=mid, op0=ALU.mult, op1=ALU.add), reads=[rbs], writes=[rbs])
                        op("dve", lambda e: e.tensor_tensor(out=lo, in0=mid, in1=Hc(NIT), op=ALU.subtract), reads=[rbs], writes=[rbs])
                        op("dve", lambda e: e.tensor_scalar(out=NM[sb % 2][:, qb, 0:n], in0=sc[i][:, 0:n], scalar1=lo, scalar2=NEGM, op0=ALU.is_lt, op1=ALU.mult), reads=[rsc, rbs], writes=[rNM])

                def nm_fn(b, kb):
                    n = 128 * (b + 1)
                    if n <= KTOP:
                        if kb == b:
                            return (cb("nmd"), rs("cbf"))
                        return None
                    return (NM[(b // 4) % 2][:, b % 4, kb * 128:(kb + 1) * 128], rs("ds_NM", (b // 4) % 2))

                heads = [dict(kT=(lambda ks, pb=64 * (h % 2): kT2[pb:pb + 64, ks]), kres=rk, vaug=vaug, vres=rv) for h in range(6)]
                if DSA_MODE != 2:
                    masks(0)
                for sb in range(NG):
                    if sb + 1 < NG and DSA_MODE != 2:
                        masks(sb + 1)
                    if DSA_MODE == 1:
                        continue
                    i = sb % 2
                    rq = rs("ds_q", i)
                    dma(qs_t[i][:], qT_d[:, :, sb * 512:(sb + 1) * 512].rearrange("c p t -> p c t"), reads=[rs("qT_d")], writes=[rq])
                    qfn = lambda h, qs, i=i: qs_t[i][64 * (h % 2):64 * (h % 2) + 64, h // 2, qs]
                    att_main(T, sb, 6, 64.0 ** -0.5, heads, qfn, rq, nm_fn)
                    att_norm(T, sb, 6, gaT_d, "gaT_d", 0)
            S_.barrier()

        def mla_phase():
            with contextlib.ExitStack() as st:
                kT = sbt(st, "ml_kT", [96, 6, S], BF16)
                va = sbt(st, "ml_va", [128, 6, NB, 65], BF16)
                rk, rv = rs("ml_kT"), rs("ml_va")
                for h in range(6):
                    dma(kT[:, h, :], kc_d[h], reads=[rs("kc_d")], writes=[rk])
                    dma(va[:, h, :, :], vc_d[h].rearrange("t p c -> p t c"), reads=[rs("vc_d")], writes=[rv])
                T = att_tiles(st)
                qs_t = [sbt(st, f"ml_q{i}", [96, 6, 512], BF16) for i in range(2)]

                def nm_fn(b, kb):
                    if kb == b:
                        return (cb("nmd"), rs("cbf"))
                    return None
                heads = [dict(kT=(lambda ks, h=h: kT[:, h, ks]), kres=rk, vaug=va[:, h, :, :], vres=rv) for h in range(6)]
                for sb in range(NG):
                    i = sb % 2
                    rq = rs("ml_q", i)
                    dma(qs_t[i][:], qc_d[:, :, sb * 512:(sb + 1) * 512].rearrange("h p t -> p h t"), reads=[rs("qc_d")], writes=[rq])
                    qfn = lambda h, qs, i=i: qs_t[i][:, h, qs]
                    att_main(T, sb, 6, 96.0 ** -0.5, heads, qfn, rq, nm_fn)
                    att_norm(T, sb, 6, gcT_d, "gcT_d", 8)
            S_.barrier()

        def s5_consts(l):
            with contextlib.ExitStack() as st:
                ar = sbt(st, "s5_ar", [128, 16], F32)
                ai = sbt(st, "s5_ai", [128, 16], F32)
                stp = sbt(st, "s5_stp", [128, 16], F32)
                rp = rs("s5_par")
                for hh in range(2):
                    dma(ar[64 * hh:64 * hh + 64, :], a_re[l].rearrange("g p -> p g"), writes=[rp], allow_slow_non_contiguous=True)
                    dma(ai[64 * hh:64 * hh + 64, :], a_im[l].rearrange("g p -> p g"), writes=[rp], allow_slow_non_contiguous=True)
                dma(stp[:], log_step[l:l + 1, :].broadcast_to([128, 16]), writes=[rp])
                W = sbt(st, "s5_w", [128, 24, 16], F32)
                rw = rs("s5_w")

                def wv(k):
                    return W[:, k, :]

                def dv(fn, reads=(), writes=()):
                    op("dve", fn, reads=list(reads) + [rp, rw] + rC, writes=list(writes) + [rw])
                TWO_PI = 2.0 * np.pi

                def sincos(dst, ang, shift):
                    t, n_, m = wv(20), wv(21), wv(22)
                    ti = Wi[:, 0, :]
                    dv(lambda e: e.tensor_scalar(out=t, in0=ang, scalar1=shift, scalar2=1.0 / TWO_PI, op0=ALU.add, op1=ALU.mult))
                    dv(lambda e: e.tensor_copy(out=ti, in_=t))
                    dv(lambda e: e.tensor_copy(out=n_, in_=ti))
                    dv(lambda e: e.tensor_tensor(out=t, in0=t, in1=n_, op=ALU.subtract))
                    dv(lambda e: e.tensor_scalar(out=m, in0=t, scalar1=0.5, scalar2=None, op0=ALU.is_gt))
                    dv(lambda e: e.tensor_tensor(out=t, in0=t, in1=m, op=ALU.subtract))
                    dv(lambda e: e.tensor_scalar(out=m, in0=t, scalar1=-0.5, scalar2=None, op0=ALU.is_lt))
                    dv(lambda e: e.tensor_tensor(out=t, in0=t, in1=m, op=ALU.add))
                    op("act", lambda e: e.activation(out=dst, in_=t, func=AF.Sin, scale=TWO_PI), reads=[rw], writes=[rw])
                Wi = sbt(st, "s5_wi", [128, 1, 16], mybir.dt.int32)
                op("act", lambda e: e.activation(out=stp[:], in_=stp[:], func=AF.Exp), reads=[rp], writes=[rp])
                mag, ang, abr, abi, cs, sn = wv(0), wv(1), wv(2), wv(3), wv(4), wv(5)
                dv(lambda e: e.tensor_tensor(out=mag, in0=ar[:], in1=stp[:], op=ALU.mult))
                op("act", lambda e: e.activation(out=mag, in_=mag, func=AF.Exp), reads=[rw], writes=[rw])
                dv(lambda e: e.tensor_tensor(out=ang, in0=ai[:], in1=stp[:], op=ALU.mult))
                sincos(sn, ang, 0.0)
                sincos(cs, ang, np.pi / 2)
                dv(lambda e: e.tensor_tensor(out=abr, in0=mag, in1=cs, op=ALU.mult))
                dv(lambda e: e.tensor_tensor(out=abi, in0=mag, in1=sn, op=ALU.mult))
                den, nr, fr, fi, tA, tB = wv(6), wv(7), wv(8), wv(9), wv(10), wv(11)
                dv(lambda e: e.tensor_tensor(out=den, in0=ar[:], in1=ar[:], op=ALU.mult))
                dv(lambda e: e.tensor_tensor(out=tA, in0=ai[:], in1=ai[:], op=ALU.mult))
                dv(lambda e: e.tensor_tensor(out=den, in0=den, in1=tA, op=ALU.add))
                dv(lambda e: e.reciprocal(out=den, in_=den))
                dv(lambda e: e.tensor_scalar(out=nr, in0=abr, scalar1=-1.0, scalar2=None, op0=ALU.add))
                dv(lambda e: e.tensor_tensor(out=fr, in0=nr, in1=ar[:], op=ALU.mult))
                dv(lambda e: e.tensor_tensor(out=tA, in0=abi, in1=ai[:], op=ALU.mult))
                dv(lambda e: e.tensor_tensor(out=fr, in0=fr, in1=tA, op=ALU.add))
                dv(lambda e: e.tensor_tensor(out=fr, in0=fr, in1=den, op=ALU.mult))
                dv(lambda e: e.tensor_tensor(out=fi, in0=abi, in1=ar[:], op=ALU.mult))
                dv(lambda e: e.tensor_tensor(out=tA, in0=nr, in1=ai[:], op=ALU.mult))
                dv(lambda e: e.tensor_tensor(out=fi, in0=fi, in1=tA, op=ALU.subtract))
                dv(lambda e: e.tensor_tensor(out=fi, in0=fi, in1=den, op=ALU.mult))
                PW = sbt(st, "s5_pw", [128, 9, 2, 16], F32)
                MKp = sbt(st, "s5_mkp", [128, 9, 2, 16], F32)
                FN = sbt(st, "s5_fn", [128, 8, 2, 16], F32)
                CQc = sbt(st, "s5_cqc", [128, 9, 2, 16], F32)

                def cmul(o_re, o_im, x_re, x_im, y_re, y_im):
                    dv(lambda e: e.tensor_tensor(out=tA, in0=x_re, in1=y_re, op=ALU.mult))
                    dv(lambda e: e.tensor_tensor(out=tB, in0=x_im, in1=y_im, op=ALU.mult))
                    dv(lambda e: e.tensor_tensor(out=wv(12), in0=x_re, in1=y_im, op=ALU.mult))
                    dv(lambda e: e.tensor_tensor(out=wv(13), in0=x_im, in1=y_re, op=ALU.mult))
                    dv(lambda e: e.tensor_tensor(out=o_re, in0=tA, in1=tB, op=ALU.subtract))
                    dv(lambda e: e.tensor_tensor(out=o_im, in0=wv(12), in1=wv(13), op=ALU.add))
                dv(lambda e: e.memset(PW[:, 0, 0, :], 1.0))
                dv(lambda e: e.memset(PW[:, 0, 1, :], 0.0))
                for n_ in range(1, 9):
                    cmul(PW[:, n_, 0, :], PW[:, n_, 1, :], PW[:, n_ - 1, 0, :], PW[:, n_ - 1, 1, :], abr, abi)
                dv(lambda e: e.tensor_copy(out=MKp[:, 0, :, :], in_=PW[:, 8, :, :]))
                for k in range(1, 9):
                    cmul(MKp[:, k, 0, :], MKp[:, k, 1, :], MKp[:, k - 1, 0, :], MKp[:, k - 1, 1, :], MKp[:, k - 1, 0, :], MKp[:, k - 1, 1, :])
                sgB = cf[:, CF["sgB"]:CF["sgB"] + 1]
                sgQ = cf[:, CF["sgQ"]:CF["sgQ"] + 1]
                for n_ in range(8):
                    cmul(FN[:, n_, 0, :], FN[:, n_, 1, :], PW[:, n_, 0, :], PW[:, n_, 1, :], fr, fi)
                    dv(lambda e: e.tensor_scalar(out=FN[:, n_, 1, :], in0=FN[:, n_, 1, :], scalar1=sgB, scalar2=None, op0=ALU.mult))
                for n_ in range(9):
                    dv(lambda e: e.tensor_scalar(out=CQc[:, n_, 0, :], in0=PW[:, n_, 0, :], scalar1=sgQ, scalar2=None, op0=ALU.mult))
                    dv(lambda e: e.tensor_scalar(out=CQc[:, n_, 1, :], in0=PW[:, n_, 1, :], scalar1=-1.0, scalar2=None, op0=ALU.mult))
                for k in range(9):
                    dv(lambda e: e.tensor_scalar(out=MKp[:, k, 1, :], in0=MKp[:, k, 1, :], scalar1=sgQ, scalar2=None, op0=ALU.mult))
                bx = [sbt(st, f"s5_bx{i}", [128, 16], F32) for i in range(2)]
                by = [sbt(st, f"s5_by{i}", [128, 16], F32) for i in range(2)]
                cx = [sbt(st, f"s5_cx{i}", [128, 16], F32) for i in range(2)]
                cy = [sbt(st, f"s5_cy{i}", [128, 16], F32) for i in range(2)]
                dr = [sbt(st, f"s5_dr{i}", [128, 1], F32) for i in range(2)]
                Pm = [sbt(st, f"s5_Pm{i}", [128, 128], F32) for i in range(2)]
                CQ = [sbt(st, f"s5_CQ{i}", [128, 9, 16], F32) for i in range(2)]
                Btr = [sbt(st, f"s5_Btr{i}", [128, 8, 16], F32) for i in range(2)]
                KTp = [sbt(st, f"s5_KTp{i}", [128, 15 * 16], F32) for i in range(2)]
                OUTm = [sbt(st, f"s5_OUT{i}", [128, 12, 128], F32) for i in range(2)]
                pp = [pst(st, f"s5_pp{i}", [128, 128], F32) for i in range(2)]
                for i in range(2):
                    op("pool", lambda e: e.memset(KTp[i][:], 0.0), writes=[rs("s5_KTp", i)])
                for g in range(16):
                    i = g % 2
                    rin, rPm, rCQ, rBt, rKT, rOUT, rpp = rs("s5_in", i), rs("s5_Pm", i), rs("s5_CQ", i), rs("s5_Btr", i), rs("s5_KTp", i), rs("s5_OUT", i), rs("s5_pp", i)
                    dma(bx[i][0:64, :], b_re[l, g], writes=[rin])
                    dma(bx[i][64:128, :], b_im[l, g], writes=[rin])
                    dma(by[i][0:64, :], b_im[l, g], writes=[rin])
                    dma(by[i][64:128, :], b_re[l, g], writes=[rin])
                    dma(cx[i][0:64, :], c_re[l, g].rearrange("c p -> p c"), writes=[rin], allow_slow_non_contiguous=True)
                    dma(cx[i][64:128, :], c_im[l, g].rearrange("c p -> p c"), writes=[rin], allow_slow_non_contiguous=True)
                    dma(cy[i][0:64, :], c_im[l, g].rearrange("c p -> p c"), writes=[rin], allow_slow_non_contiguous=True)
                    dma(cy[i][64:128, :], c_re[l, g].rearrange("c p -> p c"), writes=[rin], allow_slow_non_contiguous=True)
                    for s_ in range(8):
                        dma(dr[i][16 * s_:16 * s_ + 16, :], d_skip[l, g].rearrange("(c o) -> c o", o=1), writes=[rin], allow_slow_non_contiguous=True)
                    for s_ in range(8):
                        n_ = 7 - s_
                        op("dve", lambda e: e.tensor_scalar(out=Pm[i][:, 16 * s_:16 * s_ + 16], in0=bx[i][:], scalar1=FN[:, n_, 0, g:g + 1], scalar2=None, op0=ALU.mult), reads=[rin, rw], writes=[rPm])
                        op("dve", lambda e: e.scalar_tensor_tensor(out=Pm[i][:, 16 * s_:16 * s_ + 16], in0=by[i][:], scalar=FN[:, n_, 1, g:g + 1], in1=Pm[i][:, 16 * s_:16 * s_ + 16], op0=ALU.mult, op1=ALU.add),
                           reads=[rin, rw], writes=[rPm])
                    for s_ in range(8):
                        op("dve", lambda e: e.tensor_copy(out=Btr[i][:, s_, :], in_=Pm[i][:, 112:128]), reads=[rPm], writes=[rBt])
                    for n_ in range(9):
                        op("dve", lambda e: e.tensor_scalar(out=CQ[i][:, n_, :], in0=cx[i][:], scalar1=CQc[:, n_, 0, g:g + 1], scalar2=None, op0=ALU.mult), reads=[rin, rw], writes=[rCQ])
                        op("dve", lambda e: e.scalar_tensor_tensor(out=CQ[i][:, n_, :], in0=cy[i][:], scalar=CQc[:, n_, 1, g:g + 1], in1=CQ[i][:, n_, :], op0=ALU.mult, op1=ALU.add),
                           reads=[rin, rw], writes=[rCQ])
                    op("pe", lambda e: e.transpose(out=pp[i][:], in_=Pm[i][:], identity=cfa("ident")), reads=[rPm] + rC, writes=[rpp])
                    op("act", lambda e: e.copy(out=OUTm[i][:, 0, :], in_=pp[i][:]), reads=[rpp], writes=[rOUT])
                    op("dve", lambda e: e.tensor_copy(out=OUTm[i][:, 1, :], in_=CQ[i][:, 1:9, :].rearrange("p n c -> p (n c)")), reads=[rCQ], writes=[rOUT])
                    op("pe", lambda e: e.matmul(pp[i][:], lhsT=Btr[i][:].rearrange("p s c -> p (s c)"), rhs=CQ[i][:, 0:8, :].rearrange("p n c -> p (n c)"), start=True, stop=True),
                       reads=[rBt, rCQ], writes=[rpp])
                    op("act", lambda e: e.copy(out=KTp[i][:, 112:240], in_=pp[i][:]), reads=[rpp], writes=[rKT])
                    op("dve", lambda e: e.scalar_tensor_tensor(out=KTp[i][:, 112:128], in0=cfa("i16", 128, 16), scalar=dr[i][:, 0:1], in1=KTp[i][:, 112:128], op0=ALU.mult, op1=ALU.add),
                       reads=[rin, rKT] + rC, writes=[rKT])
                    for s_ in range(8):
                        dma(s5c_d[g, 16 * s_:16 * s_ + 16, 2, :], KTp[i][16 * s_:16 * s_ + 16, (7 - s_) * 16:(7 - s_) * 16 + 128], reads=[rKT], writes=[rs("s5c_d")])
                    for k in range(9):
                        op("dve", lambda e: e.tensor_scalar(out=OUTm[i][:, 3 + k, :], in0=cfa("ident"), scalar1=MKp[:, k, 0, g:g + 1], scalar2=None, op0=ALU.mult), reads=[rw] + rC, writes=[rOUT])
                        op("dve", lambda e: e.scalar_tensor_tensor(out=OUTm[i][:, 3 + k, :], in0=cfa("icross"), scalar=MKp[:, k, 1, g:g + 1], in1=OUTm[i][:, 3 + k, :], op0=ALU.mult, op1=ALU.add),
                           reads=[rw] + rC, writes=[rOUT])
                    dma(s5c_d[g, :, 0:2, :], OUTm[i][:, 0:2, :], reads=[rOUT], writes=[rs("s5c_d")])
                    dma(s5c_d[g, :, 3:12, :], OUTm[i][:, 3:12, :], reads=[rOUT], writes=[rs("s5c_d")])
            S_.barrier()

        def s5_phase():
            with contextlib.ExitStack() as st:
                Cm = [sbt(st, f"s5r_C{i}", [128, 12, 128], F32) for i in range(2)]
                Ut = [sbt(st, f"s5r_U{i}", [128, J], F32) for i in range(2)]
                St = [sbt(st, f"s5r_S{i}", [128, J], F32) for i in range(2)]
                Yt = [sbt(st, f"s5r_Y{i}", [128, J], F32) for i in range(2)]
                NJ = (J + 511) // 512
                pk = [pst(st, f"s5r_p{i}", [128, 512], F32) for i in range(4)]
                pc = [0]
                for g in range(16):
                    i = g % 2
                    rCm, rU, rSt, rY = rs("s5r_C", i), rs("s5r_U", i), rs("s5r_S", i), rs("s5r_Y", i)
                    dma(Cm[i][:], s5c_d[g], reads=[rs("s5c_d")], writes=[rCm])
                    for s_ in range(8):
                        dma(Ut[i][16 * s_:16 * s_ + 16, :], U_d[g, s_, :, :], reads=[rs("U_d")], writes=[rU])
                    for c in range(NJ):
                        cs_ = slice(c * 512, min(J, c * 512 + 512))
                        k = pc[0] % 4
                        pc[0] += 1
                        rpk = rs("s5r_p", k)
                        op("pe", lambda e: e.matmul(pk[k][:, 0:cs_.stop - cs_.start], lhsT=Cm[i][:, 0, :], rhs=Ut[i][:, cs_], start=True, stop=True), reads=[rCm, rU], writes=[rpk])
                        op("act", lambda e: e.copy(out=St[i][:, cs_], in_=pk[k][:, 0:cs_.stop - cs_.start]), reads=[rpk], writes=[rSt])
                    k_ = 0
                    while (1 << k_) < J:
                        sh = 1 << k_
                        width = J - sh
                        chunks = []
                        c0 = 0
                        while c0 < width:
                            chunks.append((c0, min(512, width - c0)))
                            c0 += 512
                        prods = []
                        for (c0, wc) in chunks:
                            k = pc[0] % 4
                            pc[0] += 1
                            rpk = rs("s5r_p", k)
                            op("pe", lambda e: e.matmul(pk[k][:, 0:wc], lhsT=Cm[i][:, 3 + k_, :], rhs=St[i][:, c0:c0 + wc], start=True, stop=True), reads=[rCm, rSt], writes=[rpk])
                            prods.append((k, rpk, c0, wc))
                            if len(prods) == 4 or (c0, wc) == chunks[-1]:
                                pass
                        for (k, rpk, c0, wc) in prods:
                            op("dve", lambda e: e.tensor_tensor(out=St[i][:, sh + c0:sh + c0 + wc], in0=pk[k][:, 0:wc], in1=St[i][:, sh + c0:sh + c0 + wc], op=ALU.add), reads=[rpk, rSt], writes=[rSt])
                        k_ += 1
                    for c in range(NJ):
                        c0 = c * 512
                        wc = min(512, J - c0)
                        k = pc[0] % 4
                        pc[0] += 1
                        rpk = rs("s5r_p", k)
                        op("pe", lambda e: e.matmul(pk[k][:, 0:wc], lhsT=Cm[i][:, 2, :], rhs=Ut[i][:, c0:c0 + wc], start=True, stop=False), reads=[rCm, rU], writes=[rpk])
                        if c0 == 0:
                            op("pe", lambda e: e.matmul(pk[k][:, 1:wc], lhsT=Cm[i][:, 1, :], rhs=St[i][:, 0:wc - 1], start=False, stop=True), reads=[rCm, rSt], writes=[rpk])
                        else:
                            op("pe", lambda e: e.matmul(pk[k][:, 0:wc], lhsT=Cm[i][:, 1, :], rhs=St[i][:, c0 - 1:c0 + wc - 1], start=False, stop=True), reads=[rCm, rSt], writes=[rpk])
                        op("act", lambda e: e.copy(out=Yt[i][:, c0:c0 + wc], in_=pk[k][:, 0:wc]), reads=[rpk], writes=[rY])
                    for s_ in range(8):
                        dma(Y_d[g, s_, :, :], Yt[i][16 * s_:16 * s_ + 16, :], reads=[rY], writes=[rs("Y_d")])
            S_.barrier()
            with contextlib.ExitStack() as st:
                yp = sbt(st, "s5p_yp", [128, 2, 8, 64], F32)
                y = sbt(st, "s5p_y", [128, 2, 512], F32)
                t = sbt(st, "s5p_t", [128, 2, 512], F32)
                gl = sbt(st, "s5p_g", [128, 2, 512], F32)
                gb16 = sbt(st, "s5p_gb", [128, 2, 512], BF16)
                sg = sbt(st, "s5p_sg", [128, 2, 512], F32)
                gate = sbt(st, "s5p_gate", [128, 2, 512], BF16)
                ob = sbt(st, "s5p_ob", [128, 2, 512], BF16)
                zp = [pst(st, f"s5p_z{i}", [128, 512], F32) for i in range(2)]
                ryp, ry, rt_, rgl, rgb, rsg, rgate, rob = (rs("s5p_" + n_) for n_ in ("yp", "y", "t", "g", "gb", "sg", "gate", "ob"))
                for tg in range(NG):
                    tsl = slice(tg * 512, tg * 512 + 512)
                    for c in range(2):
                        for g8 in range(8):
                            dma(yp[16 * g8:16 * g8 + 16, c, :, :], Y_d[8 * c + g8, :, :, tg * 64:(tg + 1) * 64].rearrange("s c j -> c s j"), reads=[rs("Y_d")], writes=[ryp])
                        dma(gate[:, c, :], gbT_d[c, :, tsl], reads=[rs("gbT_d")], writes=[rgate])
                    for c in range(2):
                        op("act", lambda e: e.copy(out=y[:, c, :].rearrange("p (j s) -> p s j", s=8), in_=yp[:, c, :, :]), reads=[ryp], writes=[ry])
                        op("act", lambda e: e.activation(out=t[:, c, :], in_=y[:, c, :], func=AF.Square), reads=[ry], writes=[rt_])
                        op("dve", lambda e: e.tensor_scalar(out=t[:, c, :], in0=t[:, c, :], scalar1=0.044715, scalar2=1.0, op0=ALU.mult, op1=ALU.add), reads=[rt_], writes=[rt_])
                        op("dve", lambda e: e.tensor_tensor(out=t[:, c, :], in0=t[:, c, :], in1=y[:, c, :], op=ALU.mult), reads=[rt_, ry], writes=[rt_])
                        op("act", lambda e: e.activation(out=t[:, c, :], in_=t[:, c, :], func=AF.Sigmoid, scale=2.0 * 0.7978845608028654), reads=[rt_], writes=[rt_])
                        op("dve", lambda e: e.tensor_tensor(out=gl[:, c, :], in0=t[:, c, :], in1=y[:, c, :], op=ALU.mult), reads=[rt_, ry], writes=[rgl])
                        op("dve", lambda e: e.tensor_copy(out=gb16[:, c, :], in_=gl[:, c, :]), reads=[rgl], writes=[rgb])
                    for co in range(2):
                        rz = rs("s5p_z", co)
                        for c in range(2):
                            op("pe", lambda e: e.matmul(zp[co][:], lhsT=Wglu[:, c, co * 128:(co + 1) * 128], rhs=gb16[:, c, :], start=(c == 0), stop=(c == 1)), reads=[rWsm, rgb], writes=[rz])
                        op("act", lambda e: e.activation(out=sg[:, co, :], in_=zp[co][:], func=AF.Sigmoid), reads=[rz], writes=[rsg])
                        op("dve", lambda e: e.tensor_tensor(out=sg[:, co, :], in0=sg[:, co, :], in1=gl[:, co, :], op=ALU.mult), reads=[rsg, rgl], writes=[rsg])
                        op("dve", lambda e: e.tensor_tensor(out=ob[:, co, :], in0=sg[:, co, :], in1=gate[:, co, :], op=ALU.mult), reads=[rsg, rgate], writes=[rob])
                        dma(mix_d[6 + co, :, tsl], ob[:, co, :], reads=[rob], writes=[rs("mix_d")])
            S_.barrier()

        def phase3(src_ap, src_res, dst_ap, dst_res):
            with contextlib.ExitStack() as st:
                Wo = sbt(st, "p3_Wo", [128, 14, D], BF16)
                rWo = rs("p3_Wo")
                for c in range(14):
                    nr = 128 if c in (6, 7) else 64
                    dma(Wo[0:nr, c, :], wout_d[0:nr, c, :], reads=[rs("wout_d")], writes=[rWo])
                mx = [sbt(st, f"p3_mx{i}", [128, 14, 512], BF16) for i in range(2)]
                xt = [sbt(st, f"p3_xt{i}", [128, 4, D], F32) for i in range(2)]
                pp = [pst(st, f"p3_pp{i}", [128, 512], F32) for i in range(4)]
                pc = 0
                for tg in range(NG):
                    i = tg % 2
                    rmx, rxt = rs("p3_mx", i), rs("p3_xt", i)
                    tsl = slice(tg * 512, tg * 512 + 512)
                    for c in range(14):
                        nr = 128 if c in (6, 7) else 64
                        dma(mx[i][0:nr, c, :], mix_d[c, 0:nr, tsl], reads=[rs("mix_d")], writes=[rmx])
                    dma(xt[i][:], src_ap[tg * 512:(tg + 1) * 512, :].rearrange("(j p) d -> p j d", p=128), reads=[src_res], writes=[rxt])
                    for j in range(4):
                        for half in range(2):
                            k = pc % 4
                            pc += 1
                            rpp = rs("p3_pp", k)
                            for c in range(14):
                                nr = 128 if c in (6, 7) else 64
                                op("pe", lambda e: e.matmul(pp[k][:], lhsT=mx[i][0:nr, c, j * 128:(j + 1) * 128], rhs=Wo[0:nr, c, half * 512:(half + 1) * 512], start=(c == 0), stop=(c == 13)),
                                   reads=[rmx, rWo], writes=[rpp])
                            op("dve", lambda e: e.tensor_tensor(out=xt[i][:, j, half * 512:(half + 1) * 512], in0=pp[k][:], in1=xt[i][:, j, half * 512:(half + 1) * 512], op=ALU.add), reads=[rpp, rxt], writes=[rxt])
                    dma(dst_ap[tg * 512:(tg + 1) * 512, :].rearrange("(j p) d -> p j d", p=128), xt[i][:], reads=[rxt], writes=[dst_res])
            S_.barrier()

        for l in range(NL):
            if "prep" in PH:
                prep_weights(l)
            if "s5c" in PH:
                s5_consts(l)
            for s in range(NSEQ):
                src = x_in[s] if l == 0 else xs_d[s]
                dst = out_d[s] if l == NL - 1 else xs_d[s]
                if "p1" in PH:
                    phase1(l, src, rs("xs", s))
                if "dsa" in PH:
                    dsa_phase()
                if "s5" in PH:
                    s5_phase()
                if "mla" in PH:
                    mla_phase()
                if "p3" in PH:
                    phase3(src, rs("xs", s), dst, rs("out_d") if l == NL - 1 else rs("xs", s))
        S_.barrier()
    return nc


xs_all = None


def _build(S, NL, NSEQ, KTOP):
    global xs_all
    return build_program(S, NL, NSEQ, KTOP)


_CACHE = {}


def kernel(**inputs):
    x = np.ascontiguousarray(np.asarray(inputs["x"], dtype=np.float32))
    B, S, _ = x.shape
    ncores = 8
    NSEQ = B // ncores
    KTOP = min(256, S // 4)
    nc = build_program(S, DEPTH, NSEQ, KTOP)
    consts = make_consts(S)
    names = ["norm_g", "w_in", "attn_q_norm", "attn_k_norm", "mla_q_lora_norm", "mla_kv_lora_norm", "mla_w_uq", "mla_w_ukv",
             "mla_q_norm", "mla_k_norm", "ssm_a_re", "ssm_a_im", "ssm_b_re", "ssm_b_im", "ssm_c_re", "ssm_c_im", "ssm_d",
             "ssm_log_step", "ssm_w_glu", "w_out"]
    shared = {n: np.ascontiguousarray(np.asarray(inputs[n], dtype=np.float32)) for n in names}
    in_maps = []
    for c in range(ncores):
        m = dict(shared)
        m["x"] = x[c * NSEQ:(c + 1) * NSEQ]
        m.update(consts)
        in_maps.append(m)
    res = run_bass_kernel_spmd(nc, in_maps, core_ids=list(range(ncores)))
    return np.concatenate([r["out"] for r in res.results], axis=0).astype(np.float32)
```

```python
import contextlib
import numpy as np
import ml_dtypes
import concourse.bass as bass
import concourse.mybir as mybir
from concourse.bass_utils import run_bass_kernel_spmd

F32 = mybir.dt.float32
BF16 = mybir.dt.bfloat16
AF = mybir.ActivationFunctionType
ALU = mybir.AluOpType
AX = mybir.AxisListType

D = 1024
DEPTH = 4
DIN = 2504
EPS = 1e-6
NCOL = 3240
C_IQ2 = 2728
C_QA, C_KA2, C_IQ, C_IK4, C_GA, C_U, C_GB, C_CQ, C_CKV, C_KPE, C_GC, C_VAIW = (
    0, 384, 512, 768, 896, 1280, 1536, 1792, 2048, 2176, 2272, 2656)
S_QA, S_KA, S_VA, S_IQ, S_IK, S_IW, S_GA, S_U, S_GB, S_CQ, S_CKV, S_KPE, S_GC = (
    0, 384, 448, 512, 768, 800, 808, 1192, 1448, 1704, 1960, 2088, 2120)
NEGM = -30000.0
NIT = 16
import os as _os5
P1ENG = _os5.environ.get('P1ENG', 'pool')
import os as _os4
DSA_MODE = int(_os4.environ.get('DSA_MODE', '0'))
import os as _os3
GV = int(_os3.environ.get('GV', '0'))
import os as _os2
GATE_DMA = int(_os2.environ.get('GATE_DMA', '2'))
import os as _os
P1CUT = int(_os.environ.get('P1CUT', '99'))
import os
PH = set(os.environ.get('PH', 'prep,s5c,p1,dsa,s5,mla,p3').split(','))


class Res:
    __slots__ = ("w", "r", "ps")

    def __init__(self, ps=False):
        self.w = {}
        self.r = {}
        self.ps = ps


class Sched:
    def __init__(self, nc, n_dma_sems=32):
        self.nc = nc
        self.engs = {}
        for name, e in (("pe", nc.tensor), ("act", nc.scalar), ("dve", nc.vector),
                        ("pool", nc.gpsimd), ("sp", nc.sync)):
            sem = nc.alloc_semaphore(name="sem_" + name)
            self.engs[name] = dict(e=e, sem=sem, cnt=0, waited={}, name=name)
        self.dma_sems = [nc.alloc_semaphore(name=f"dsem{i}") for i in range(n_dma_sems)]
        self.dma_cnt = [0] * n_dma_sems
        self.dma_i = 0
        self.nins = 0

    def _wait(self, en, deps, skip_self=False):
        E = self.engs[en]
        best = {}
        for sem, val in deps:
            k = id(sem)
            if k not in best or best[k][1] < val:
                best[k] = (sem, val)
        for k, (sem, val) in best.items():
            if skip_self and sem is E["sem"]:
                continue
            if E["waited"].get(k, 0) < val:
                E["e"].wait_ge(sem, val)
                E["waited"][k] = val

    def _deps(self, reads, writes, en=None):
        deps = []
        own = self.engs[en]["sem"] if en is not None else None
        for r in reads:
            deps.extend(r.w.values())
            if r.ps:
                deps.extend(ev for ev in r.r.values() if ev[0] is not own)
        for w in writes:
            deps.extend(w.w.values())
            deps.extend(w.r.values())
        return deps

    @staticmethod
    def _mark(ev, reads, writes):
        k = id(ev[0])
        for r in reads:
            r.r[k] = ev
        for w in writes:
            w.w[k] = ev

    def op(self, en, fn, reads=(), writes=()):
        E = self.engs[en]
        self._wait(en, self._deps(reads, writes, en), skip_self=(en == "pe"))
        ins = fn(E["e"])
        E["cnt"] += 1
        ins.then_inc(E["sem"], 1)
        ev = (E["sem"], E["cnt"])
        self._mark(ev, reads, writes)
        self.nins += 1
        return ev

    def dma(self, out, in_, reads=(), writes=(), en="sp", **kw):
        E = self.engs[en]
        i = self.dma_i % len(self.dma_sems)
        self.dma_i += 1
        sem = self.dma_sems[i]
        deps = self._deps(reads, writes)
        if self.dma_cnt[i] > 0:
            deps.append((sem, self.dma_cnt[i]))
        self._wait(en, deps)
        ins = E["e"].dma_start(out=out, in_=in_, **kw)
        self.dma_cnt[i] += 16
        ins.then_inc(sem, 16)
        ev = (sem, self.dma_cnt[i])
        self._mark(ev, reads, writes)
        self.nins += 1
        return ev

    def barrier(self):
        evs = [(E["sem"], E["cnt"]) for E in self.engs.values() if E["cnt"] > 0]
        evs += [(s, c) for s, c in zip(self.dma_sems, self.dma_cnt) if c > 0]
        for en in self.engs:
            self._wait(en, evs)


def rope_tab(S, d, rows):
    half = d // 2
    inv = (np.float32(10000.0) ** (-np.arange(half, dtype=np.float32) * np.float32(2.0) / np.float32(d))).astype(np.float32)
    ang = np.arange(S, dtype=np.float32)[:, None] * inv[None, :]
    c = np.cos(ang).astype(np.float32).T
    s = np.sin(ang).astype(np.float32).T
    idx = np.arange(rows) % half
    return c[idx], s[idx]


def make_consts(S):
    bf = ml_dtypes.bfloat16
    I = np.eye(128, dtype=np.float32)
    ones64 = np.zeros((128, 128), np.float32)
    ones64[:64, :64] = 1; ones64[64:, 64:] = 1
    ones96 = np.zeros((128, 128), np.float32); ones96[:96, :96] = 1
    ones128 = np.ones((128, 128), np.float32)

    def rot(M, hd, half, lo=0):
        R = np.zeros((128, 128), np.float32)
        for m in range(M):
            j = m % hd
            if j < lo:
                continue
            jj = j - lo
            if jj < half:
                R[m + half, m] = -1.0
            else:
                R[m - half, m] = 1.0
        return R
    R64 = rot(128, 64, 32)
    R32 = rot(128, 32, 16)
    R96 = rot(96, 96, 16, lo=64)
    nmd = np.zeros((128, 128), np.float32); nmd[:64, 64:] = NEGM
    cbf = np.concatenate([I, ones64, ones96, ones128, R64, R32, R96, nmd], axis=1).astype(bf)
    icross = np.zeros((128, 128), np.float32)
    for m in range(128):
        icross[m, (m + 64) % 128] = 1
    e65 = np.zeros((128, 64), np.float32); e65[64, :] = 1
    i16 = np.zeros((128, 16), np.float32)
    for p in range(128):
        i16[p, p % 16] = 1
    sgB = np.where(np.arange(128) < 64, -1.0, 1.0).astype(np.float32)[:, None]
    sgQ = -sgB
    epsc = np.full((128, 1), EPS, np.float32)
    pw = np.tile((2.0 ** -(np.arange(NIT + 1, dtype=np.float64) + 1.0)).astype(np.float32)[None, :], (128, 1))
    cf = np.concatenate([I, icross, e65, i16, sgB, sgQ, epsc, pw], axis=1).astype(np.float32)
    c64, s64 = rope_tab(S, 64, 128)
    c32, s32 = rope_tab(S, 32, 128)
    c96 = np.ones((128, S), np.float32); s96 = np.zeros((128, S), np.float32)
    c96[64:96] = c32[:32]; s96[64:96] = s32[:32]
    rt = np.stack([c64, s64, c32, s32, c96, s96]).astype(np.float32)
    return dict(cbf=cbf, cf=cf, rt=rt)


CB = dict(ident=0, ones64=128, ones96=256, ones128=384, R64=512, R32=640, R96=768, nmd=896)
CF = dict(ident=0, icross=128, e65=256, i16=320, sgB=336, sgQ=337, eps=338, pw=339)
NCF = 339 + NIT + 1


def build_program(S, NL, NSEQ, KTOP, dbg=None):
    nc = bass.Bass("TRN2", target_bir_lowering=False)
    NB = S // 128
    NG = S // 512
    J = S // 8
    dbg = dbg or {}

    def din(name, shape, dt=F32):
        return nc.dram_tensor(name, list(shape), dt, kind="ExternalInput").ap()

    def dscr(name, shape, dt):
        return nc.dram_tensor(name, list(shape), dt, kind="Internal").ap()

    x_in = din("x", [NSEQ, S, D])
    norm_g = din("norm_g", [DEPTH, D])
    w_in = din("w_in", [DEPTH, D, DIN])
    attn_q_norm = din("attn_q_norm", [DEPTH, 64])
    attn_k_norm = din("attn_k_norm", [DEPTH, 64])
    q_lora_g = din("mla_q_lora_norm", [DEPTH, 256])
    kv_lora_g = din("mla_kv_lora_norm", [DEPTH, 128])
    w_uq = din("mla_w_uq", [DEPTH, 256, 576])
    w_ukv = din("mla_w_ukv", [DEPTH, 128, 768])
    mla_q_norm = din("mla_q_norm", [DEPTH, 96])
    mla_k_norm = din("mla_k_norm", [DEPTH, 96])
    a_re = din("ssm_a_re", [DEPTH, 16, 64])
    a_im = din("ssm_a_im", [DEPTH, 16, 64])
    b_re = din("ssm_b_re", [DEPTH, 16, 64, 16])
    b_im = din("ssm_b_im", [DEPTH, 16, 64, 16])
    c_re = din("ssm_c_re", [DEPTH, 16, 16, 64])
    c_im = din("ssm_c_im", [DEPTH, 16, 16, 64])
    d_skip = din("ssm_d", [DEPTH, 16, 16])
    log_step = din("ssm_log_step", [DEPTH, 16])
    w_glu = din("ssm_w_glu", [DEPTH, 256, 256])
    w_out = din("w_out", [DEPTH, D, D])
    cbf_d = din("cbf", [128, 1024], BF16)
    cf_d = din("cf", [128, NCF])
    rt_d = din("rt", [6, 128, S])
    out_d = nc.dram_tensor("out", [NSEQ, S, D], F32, kind="ExternalOutput").ap()

    xs_d = dscr("xs", [NSEQ, S, D], F32)
    winp_d = dscr("winp", [128, 8, NCOL], BF16)
    wout_d = dscr("woutp", [128, 14, D], BF16)
    qT_d = dscr("qT", [3, 128, S], BF16)
    kT2_d = dscr("kT2", [128, S], BF16)
    va_d = dscr("va", [NB, 128, 65], BF16)
    iqT_d = dscr("iqT", [4, 128, S], BF16)
    ikT_d = dscr("ikT", [128, S], BF16)
    iw_d = dscr("iw", [NB, 128, 16], F32)
    gaT_d = dscr("gaT", [6, 64, S], BF16)
    gbT_d = dscr("gbT", [2, 128, S], BF16)
    gcT_d = dscr("gcT", [6, 64, S], BF16)
    U_d = dscr("U", [16, 8, 16, J], F32)
    Y_d = dscr("Y", [16, 8, 16, J], F32)
    qc_d = dscr("qc", [6, 96, S], BF16)
    kc_d = dscr("kc", [6, 96, S], BF16)
    vc_d = dscr("vc", [6, NB, 128, 65], BF16)
    mix_d = dscr("mix", [14, 128, S], BF16)
    s5c_d = dscr("s5c", [16, 128, 12, 128], F32)

    S_ = Sched(nc)
    op = S_.op
    dma = S_.dma
    RS = {}

    PSUM_KEYS = {"p1_tp", "p1_mm", "p1_aux", "at_ps", "at_o", "at_sbp", "ds_shp", "ds_scp", "s5_pp", "s5r_p", "s5p_z", "p3_pp"}

    def rs(*key):
        if key not in RS:
            RS[key] = Res(ps=(key[0] in PSUM_KEYS))
        return RS[key]

    with contextlib.ExitStack() as top:
        uid = [0]

        def sbt(st, name, shape, dt):
            uid[0] += 1
            return st.enter_context(nc.sbuf_tensor(f"sb{uid[0]}_{name}", list(shape), dt))

        def pst(st, name, shape, dt):
            uid[0] += 1
            return st.enter_context(nc.psum_tensor(f"ps{uid[0]}_{name}", list(shape), dt))

        cbf = sbt(top, "cbf", [128, 1024], BF16)
        cf = sbt(top, "cf", [128, NCF], F32)
        dma(cbf[:], cbf_d, writes=[rs("cbf")])
        dma(cf[:], cf_d, writes=[rs("cf")])
        rC = [rs("cbf"), rs("cf")]

        def cb(name, rows=128, cols=128):
            o = CB[name]
            return cbf[0:rows, o:o + cols]

        def cfa(name, rows=128, cols=128):
            o = CF[name]
            return cf[0:rows, o:o + cols]
        eps_ap = cf[:, CF["eps"]:CF["eps"] + 1]

        gsm = sbt(top, "gsm", [128, DEPTH, 16], F32)
        rg = rs("gsm")
        op("dve", lambda e: e.memset(gsm[:], 1.0), writes=[rg])
        for l in range(NL):
            dma(gsm[:, l, 0:8], norm_g[l].rearrange("(k p) -> p k", p=128), writes=[rg], allow_slow_non_contiguous=True)
            for hh in range(2):
                dma(gsm[64 * hh:64 * hh + 64, l, 8:9], attn_q_norm[l].rearrange("(p o) -> p o", o=1), writes=[rg], allow_slow_non_contiguous=True)
                dma(gsm[64 * hh:64 * hh + 64, l, 9:10], attn_k_norm[l].rearrange("(p o) -> p o", o=1), writes=[rg], allow_slow_non_contiguous=True)
            dma(gsm[:, l, 10:12], q_lora_g[l].rearrange("(k p) -> p k", p=128), writes=[rg], allow_slow_non_contiguous=True)
            dma(gsm[:, l, 12:13], kv_lora_g[l].rearrange("(p o) -> p o", o=1), writes=[rg], allow_slow_non_contiguous=True)
            dma(gsm[0:96, l, 13:14], mla_q_norm[l].rearrange("(p o) -> p o", o=1), writes=[rg], allow_slow_non_contiguous=True)
            dma(gsm[0:96, l, 14:15], mla_k_norm[l].rearrange("(p o) -> p o", o=1), writes=[rg], allow_slow_non_contiguous=True)

        def prep_weights(l):
            with contextlib.ExitStack() as st:
                stg = [sbt(st, f"stg{i}", [128, DIN], F32) for i in range(2)]
                wrow = [sbt(st, f"wrow{i}", [128, NCOL], BF16) for i in range(2)]
                for kc in range(8):
                    i = kc % 2
                    rst, rw = rs("stg", i), rs("wrow", i)
                    dma(stg[i][:], w_in[l, kc * 128:(kc + 1) * 128, :], writes=[rst])
                    g = gsm[:, l, kc:kc + 1]
                    segs = [(C_QA, S_QA, 384), (C_KA2, S_KA, 64), (C_KA2 + 64, S_KA, 64)] + [(C_IQ2 + (h_ // 2) * 128 + 64 * (h_ % 2), S_IQ + 32 * h_, 32) for h_ in range(8)] + \
                           [(C_IK4 + 32 * r, S_IK, 32) for r in range(4)] + \
                           [(C_GA, S_GA, 384), (C_U, S_U, 256), (C_GB, S_GB, 256), (C_CQ, S_CQ, 256), (C_CKV, S_CKV, 128),
                            (C_KPE + 64, S_KPE, 32), (C_GC, S_GC, 384), (C_VAIW, S_VA, 64), (C_VAIW + 64, S_IW, 8)]
                    op("pool", lambda e: e.memset(wrow[i][:, C_KPE:C_KPE + 64], 0.0), writes=[rw])
                    op("pool", lambda e: e.memset(wrow[i][:, C_IQ2:C_IQ2 + 512], 0.0), writes=[rw])
                    op("pool", lambda e: e.memset(wrow[i][:, C_IQ:C_IQ + 256], 0.0), writes=[rw])
                    for n_, (dc, sc_, w_) in enumerate(segs):
                        if n_ % 2 == 0:
                            op("dve", lambda e: e.tensor_scalar(out=wrow[i][:, dc:dc + w_], in0=stg[i][:, sc_:sc_ + w_], scalar1=g, scalar2=None, op0=ALU.mult),
                               reads=[rst, rg], writes=[rw])
                        else:
                            op("act", lambda e: e.activation(out=wrow[i][:, dc:dc + w_], in_=stg[i][:, sc_:sc_ + w_], func=AF.Copy, scale=g),
                               reads=[rst, rg], writes=[rw])
                    dma(winp_d[:, kc, :], wrow[i][:], reads=[rw], writes=[rs("winp_d")])
            with contextlib.ExitStack() as st:
                stg = [sbt(st, f"stgo{i}", [128, D], F32) for i in range(2)]
                wrow = [sbt(st, f"wrowo{i}", [128, D], BF16) for i in range(2)]
                for c in range(14):
                    i = c % 2
                    rst, rw = rs("stgo", i), rs("wrowo", i)
                    if c < 6:
                        r0, nr = 64 * c, 64
                    elif c < 8:
                        r0, nr = 384 + 128 * (c - 6), 128
                    else:
                        r0, nr = 640 + 64 * (c - 8), 64
                    dma(stg[i][0:nr, :], w_out[l, r0:r0 + nr, :], writes=[rst])
                    if c % 2 == 0:
                        op("dve", lambda e: e.tensor_copy(out=wrow[i][0:nr, :], in_=stg[i][0:nr, :]), reads=[rst], writes=[rw])
                    else:
                        op("act", lambda e: e.copy(out=wrow[i][0:nr, :], in_=stg[i][0:nr, :]), reads=[rst], writes=[rw])
                    dma(wout_d[0:nr, c, :], wrow[i][0:nr, :], reads=[rw], writes=[rs("wout_d")])
            with contextlib.ExitStack() as st:
                s1 = sbt(st, "s_uq", [128, 2, 576], F32)
                s2 = sbt(st, "s_ukv", [128, 768], F32)
                s3 = sbt(st, "s_glu", [128, 2, 256], F32)
                r1, r2, r3 = rs("s_uq"), rs("s_ukv"), rs("s_glu")
                dma(s1[:], w_uq[l].rearrange("(k p) n -> p k n", p=128), writes=[r1])
                dma(s2[:], w_ukv[l], writes=[r2])
                dma(s3[:], w_glu[l].rearrange("(k p) n -> p k n", p=128), writes=[r3])
                rw = rs("wsm")
                for k in range(2):
                    op("dve", lambda e: e.tensor_scalar(out=Wuq[:, k, :], in0=s1[:, k, :], scalar1=gsm[:, l, 10 + k:11 + k], scalar2=None, op0=ALU.mult),
                       reads=[r1, rg], writes=[rw])
                    op("dve", lambda e: e.tensor_copy(out=Wglu[:, k, :], in_=s3[:, k, :]), reads=[r3], writes=[rw])
                op("dve", lambda e: e.memset(WukT[:], 0.0), writes=[rw])
                for h in range(6):
                    op("dve", lambda e: e.tensor_scalar(out=WukT[:, h, 0:64], in0=s2[:, h * 128:h * 128 + 64], scalar1=gsm[:, l, 12:13], scalar2=None, op0=ALU.mult),
                       reads=[r2, rg], writes=[rw])
                    op("dve", lambda e: e.tensor_scalar(out=Wuv[:, h * 64:h * 64 + 64], in0=s2[:, h * 128 + 64:h * 128 + 128], scalar1=gsm[:, l, 12:13], scalar2=None, op0=ALU.mult),
                       reads=[r2, rg], writes=[rw])
            S_.barrier()

        Wuq = sbt(top, "Wuq", [128, 2, 576], BF16)
        WukT = sbt(top, "WukT", [128, 6, 96], BF16)
        Wuv = sbt(top, "Wuv", [128, 384], BF16)
        Wglu = sbt(top, "Wglu", [128, 2, 256], BF16)
        rWsm = rs("wsm")

        def phase1(l, src_ap, src_res):
            with contextlib.ExitStack() as st:
                Winp = sbt(st, "Winp", [128, 8, NCOL], BF16)
                rW = rs("Winp")
                for kc in range(8):
                    dma(Winp[:, kc, :], winp_d[:, kc, :], reads=[rs("winp_d")], writes=[rW])
                xt = sbt(st, "p1_xt", [128, 4, D], F32)
                rxt = rs("p1_xt")
                junk = sbt(st, "p1_junk", [128, D], BF16)
                xn = sbt(st, "p1_xn", [128, D], BF16)
                ssq = sbt(st, "p1_ssq", [128, 8], F32)
                hT = sbt(st, "p1_hT", [128, 8, 512], BF16)
                rhT = rs("p1_hT")
                tabs = sbt(st, "p1_tabs", [128, 6, 512], F32)
                rtab = rs("p1_tabs")
                tp = [pst(st, f"p1_tp{i}", [128, 1024], BF16) for i in range(2)]
                mm = [pst(st, f"p1_mm{i}", [128, 512], F32) for i in range(3)]
                aux = [pst(st, f"p1_aux{i}", [128, 512], F32) for i in range(3)]
                mmi = [0]
                NWK = 3
                xg = [sbt(st, f"p1_xg{i}", [128, 512], BF16) for i in range(NWK)]
                sq = [sbt(st, f"p1_sq{i}", [128, 512], BF16) for i in range(NWK)]
                sd = [sbt(st, f"p1_sd{i}", [128, 512], F32) for i in range(NWK)]
                t1 = [sbt(st, f"p1_t1{i}", [128, 512], F32) for i in range(NWK)]
                t2 = [sbt(st, f"p1_t2{i}", [128, 512], F32) for i in range(NWK)]
                ob = [sbt(st, f"p1_ob{i}", [128, 512], BF16) for i in range(NWK)]
                up = [sbt(st, f"p1_up{i}", [128, 8, 64], F32) for i in range(2)]
                cqn = sbt(st, "p1_cqn", [128, 2, 512], BF16)
                ckvn = sbt(st, "p1_ckvn", [128, 512], BF16)
                vcs = [sbt(st, f"p1_vcs{i}", [128, 6, 65], BF16) for i in range(2)]
                vas = [sbt(st, f"p1_vas{i}", [128, 65], BF16) for i in range(2)]
                iws = [sbt(st, f"p1_iws{i}", [128, 16], F32) for i in range(2)]
                for i in range(2):
                    op("pool", lambda e: e.memset(vcs[i][:], 1.0), writes=[rs("p1_vcs", i)])
                    op("pool", lambda e: e.memset(vas[i][:], 1.0), writes=[rs("p1_vas", i)])
                wk = [0]

                def proj(cols, M, rhs_fn=None):
                    k = mmi[0] % 3
                    mmi[0] += 1
                    r = rs("p1_mm", k)
                    for kc in range(8):
                        op("pe", lambda e: e.matmul(mm[k][0:M, :], lhsT=Winp[:, kc, cols:cols + M], rhs=hT[:, kc, :], start=(kc == 0), stop=(kc == 7)),
                           reads=[rW, rhT], writes=[r])
                    return mm[k], r

                def normrope(X, rX, M, gain, onesname, inv_dim, Rname, ci, dst_ap, dst_res):
                    i = wk[0] % NWK
                    wk[0] += 1
                    a0, a1 = aux[(2 * i) % 3], aux[(2 * i + 1) % 3]
                    ra0, ra1 = rs("p1_aux", (2 * i) % 3), rs("p1_aux", (2 * i + 1) % 3)
                    rxg, rsq, rsd, rt1, rt2, rob = (rs("p1_xg", i), rs("p1_sq", i), rs("p1_sd", i), rs("p1_t1", i), rs("p1_t2", i), rs("p1_ob", i))
                    Ct, St = tabs[0:M, ci, :], tabs[0:M, ci + 1, :]
                    if gain is not None:
                        op("act", lambda e: e.activation(out=xg[i][0:M, :], in_=X[0:M, :], func=AF.Copy, scale=gain), reads=[rX, rg], writes=[rxg])
                    else:
                        op("act", lambda e: e.copy(out=xg[i][0:M, :], in_=X[0:M, :]), reads=[rX], writes=[rxg])
                    op("pe", lambda e: e.matmul(a1[0:M, :], lhsT=cb(Rname, M, M), rhs=xg[i][0:M, :], start=True, stop=True), reads=[rxg] + rC, writes=[ra1])
                    if onesname is not None:
                        op("act", lambda e: e.activation(out=sq[i][0:M, :], in_=X[0:M, :], func=AF.Square), reads=[rX], writes=[rsq])
                        op("pe", lambda e: e.matmul(a0[0:M, :], lhsT=cb(onesname, M, M), rhs=sq[i][0:M, :], start=True, stop=True), reads=[rsq] + rC, writes=[ra0])
                        op("act", lambda e: e.activation(out=sd[i][0:M, :], in_=a0[0:M, :], func=AF.Sqrt, scale=inv_dim, bias=eps_ap[0:M, :]), reads=[ra0] + rC, writes=[rsd])
                        op("dve", lambda e: e.reciprocal(out=sd[i][0:M, :], in_=sd[i][0:M, :]), reads=[rsd], writes=[rsd])
                    if gain is not None:
                        op("dve", lambda e: e.scalar_tensor_tensor(out=t1[i][0:M, :], in0=X[0:M, :], scalar=gain, in1=Ct, op0=ALU.mult, op1=ALU.mult),
                           reads=[rX, rtab, rg], writes=[rt1])
                    else:
                        op("dve", lambda e: e.tensor_tensor(out=t1[i][0:M, :], in0=X[0:M, :], in1=Ct, op=ALU.mult), reads=[rX, rtab], writes=[rt1])
                    op("dve", lambda e: e.tensor_tensor(out=t2[i][0:M, :], in0=a1[0:M, :], in1=St, op=ALU.mult), reads=[ra1, rtab], writes=[rt2])
                    if onesname is not None:
                        op(P1ENG, lambda e: e.tensor_tensor(out=t1[i][0:M, :], in0=t1[i][0:M, :], in1=t2[i][0:M, :], op=ALU.add), reads=[rt1, rt2], writes=[rt1])
                        op(P1ENG, lambda e: e.tensor_tensor(out=ob[i][0:M, :], in0=t1[i][0:M, :], in1=sd[i][0:M, :], op=ALU.mult), reads=[rt1, rsd], writes=[rob])
                    else:
                        op("dve", lambda e: e.tensor_tensor(out=ob[i][0:M, :], in0=t1[i][0:M, :], in1=t2[i][0:M, :], op=ALU.add), reads=[rt1, rt2], writes=[rob])
                    dma(dst_ap, ob[i][0:M, :], reads=[rob], writes=[dst_res])

                def silu_out(X, rX, M, dsts):
                    i = wk[0] % NWK
                    wk[0] += 1
                    rob = rs("p1_ob", i)
                    rt1 = rs("p1_t1", i)
                    if GV == 1:
                        op("act", lambda e: e.copy(out=t1[i][0:M, :], in_=X[0:M, :]), reads=[rX], writes=[rt1])
                    else:
                        op("act", lambda e: e.activation(out=t1[i][0:M, :], in_=X[0:M, :], func=AF.Exp, scale=-1.0), reads=[rX], writes=[rt1])
                    if GV != 2:
                        op("dve", lambda e: e.tensor_scalar(out=t1[i][0:M, :], in0=t1[i][0:M, :], scalar1=1.0, scalar2=None, op0=ALU.add), reads=[rt1], writes=[rt1])
                        op("dve", lambda e: e.reciprocal(out=t1[i][0:M, :], in_=t1[i][0:M, :]), reads=[rt1], writes=[rt1])
                    if GV != 3:
                        op("dve", lambda e: e.scalar_tensor_tensor(out=ob[i][0:M, :], in0=X[0:M, :], scalar=1.0, in1=t1[i][0:M, :], op0=ALU.mult, op1=ALU.mult), reads=[rX, rt1], writes=[rob])
                    for (p0, p1, dap, dres) in dsts:
                        if GATE_DMA == 0 or (GATE_DMA == 1 and p0 != 0):
                            continue
                        dma(dap, ob[i][p0:p1, :], reads=[rob], writes=[dres])

                for tg in range(NG):
                    t0 = tg * 512
                    tsl = slice(t0, t0 + 512)
                    dma(xt[:], src_ap[t0:t0 + 512, :].rearrange("(j p) d -> p j d", p=128), reads=[src_res], writes=[rxt])
                    for ci in range(6):
                        dma(tabs[:, ci, :], rt_d[ci, :, tsl], writes=[rtab])
                    rssq, rjunk, rxn = rs("p1_ssq"), rs("p1_junk"), rs("p1_xn")
                    for j in range(4):
                        op("act", lambda e: e.activation(out=junk[:], in_=xt[:, j, :], func=AF.Square, accum_out=ssq[:, j:j + 1]), reads=[rxt], writes=[rjunk, rssq])
                    op("act", lambda e: e.activation(out=ssq[:, 4:8], in_=ssq[:, 0:4], func=AF.Sqrt, scale=1.0 / D, bias=eps_ap), reads=[rssq] + rC, writes=[rssq])
                    op("dve", lambda e: e.reciprocal(out=ssq[:, 4:8], in_=ssq[:, 4:8]), reads=[rssq], writes=[rssq])
                    for j in range(4):
                        op("dve", lambda e: e.tensor_scalar(out=xn[:], in0=xt[:, j, :], scalar1=ssq[:, 4 + j:5 + j], scalar2=None, op0=ALU.mult), reads=[rxt, rssq], writes=[rxn])
                        for half in range(2):
                            k = half
                            rtp = rs("p1_tp", k)
                            for q in range(4):
                                kc = half * 4 + q
                                op("pe", lambda e: e.transpose(out=tp[k][:, q * 128:(q + 1) * 128], in_=xn[:, kc * 128:(kc + 1) * 128], identity=cb("ident")),
                                   reads=[rxn] + rC, writes=[rtp])
                            eng = "act" if half == 0 else "dve"
                            if eng == "act":
                                op("act", lambda e: e.copy(out=hT[:, half * 4:half * 4 + 4, j * 128:(j + 1) * 128], in_=tp[k][:, 0:512].rearrange("p (q t) -> p q t", q=4)), reads=[rtp], writes=[rhT])
                            else:
                                op("dve", lambda e: e.tensor_copy(out=hT[:, half * 4:half * 4 + 4, j * 128:(j + 1) * 128], in_=tp[k][:, 0:512].rearrange("p (q t) -> p q t", q=4)), reads=[rtp], writes=[rhT])
                    if P1CUT <= 1:
                        continue
                    for c in range(3):
                        X, rX = proj(C_QA + 128 * c, 128)
                        normrope(X, rX, 128, gsm[:, l, 8:9], "ones64", 1.0 / 64, "R64", 0, qT_d[c, :, tsl], rs("qT_d"))
                    X, rX = proj(C_KA2, 128)
                    normrope(X, rX, 128, gsm[:, l, 9:10], "ones64", 1.0 / 64, "R64", 0, kT2_d[:, tsl], rs("kT2_d"))
                    if P1CUT <= 2:
                        continue
                    for c in range(4):
                        X, rX = proj(C_IQ2 + 128 * c, 128)
                        normrope(X, rX, 128, None, None, None, "R32", 2, iqT_d[c, :, tsl], rs("iqT_d"))
                    X, rX = proj(C_IK4, 128)
                    normrope(X, rX, 128, None, None, None, "R32", 2, ikT_d[:, tsl], rs("ikT_d"))
                    if P1CUT <= 3:
                        continue
                    for c in range(3):
                        X, rX = proj(C_GA + 128 * c, 128)
                        silu_out(X, rX, 128, [(0, 64, gaT_d[2 * c, :, tsl], rs("gaT_d")), (64, 128, gaT_d[2 * c + 1, :, tsl], rs("gaT_d"))])
                    for c in range(2):
                        X, rX = proj(C_GB + 128 * c, 128)
                        silu_out(X, rX, 128, [(0, 128, gbT_d[c, :, tsl], rs("gbT_d"))])
                    for c in range(3):
                        X, rX = proj(C_GC + 128 * c, 128)
                        silu_out(X, rX, 128, [(0, 64, gcT_d[2 * c, :, tsl], rs("gcT_d")), (64, 128, gcT_d[2 * c + 1, :, tsl], rs("gcT_d"))])
                    if P1CUT <= 4:
                        continue
                    for c in range(2):
                        X, rX = proj(C_U + 128 * c, 128)
                        i = c
                        rup = rs("p1_up", i)
                        op("dve", lambda e: e.tensor_copy(out=up[i][:].rearrange("p s j -> p j s"), in_=X[:].rearrange("p (j s) -> p j s", s=8)), reads=[rX], writes=[rup])
                        for g8 in range(8):
                            g = 8 * c + g8
                            for s_ in range(8):
                                dma(U_d[g, s_, :, tg * 64:(tg + 1) * 64], up[i][16 * g8:16 * g8 + 16, s_, :], reads=[rup], writes=[rs("U_d")])
                    if P1CUT <= 5:
                        continue
                    X0, rX0 = proj(C_CQ, 128)
                    X1, rX1 = proj(C_CQ + 128, 128)
                    i = wk[0] % NWK
                    wk[0] += 1
                    a0, ra0 = aux[(2 * i) % 3], rs("p1_aux", (2 * i) % 3)
                    rsq, rsd, rt1 = rs("p1_sq", i), rs("p1_sd", i), rs("p1_t1", i)
                    i2 = wk[0] % NWK
                    wk[0] += 1
                    rsq2 = rs("p1_sq", i2)
                    op("act", lambda e: e.activation(out=sq[i][:], in_=X0[:], func=AF.Square), reads=[rX0], writes=[rsq])
                    op("act", lambda e: e.activation(out=sq[i2][:], in_=X1[:], func=AF.Square), reads=[rX1], writes=[rsq2])
                    op("pe", lambda e: e.matmul(a0[:], lhsT=cb("ones128"), rhs=sq[i][:], start=True, stop=False), reads=[rsq] + rC, writes=[ra0])
                    op("pe", lambda e: e.matmul(a0[:], lhsT=cb("ones128"), rhs=sq[i2][:], start=False, stop=True), reads=[rsq2] + rC, writes=[ra0])
                    op("act", lambda e: e.activation(out=sd[i][:], in_=a0[:], func=AF.Sqrt, scale=1.0 / 256, bias=eps_ap), reads=[ra0] + rC, writes=[rsd])
                    op("dve", lambda e: e.reciprocal(out=sd[i][:], in_=sd[i][:]), reads=[rsd], writes=[rsd])
                    rcq = rs("p1_cqn")
                    op("dve", lambda e: e.tensor_tensor(out=cqn[:, 0, :], in0=X0[:], in1=sd[i][:], op=ALU.mult), reads=[rX0, rsd], writes=[rcq])
                    op("dve", lambda e: e.tensor_tensor(out=cqn[:, 1, :], in0=X1[:], in1=sd[i][:], op=ALU.mult), reads=[rX1, rsd], writes=[rcq])
                    X0, rX0 = proj(C_CKV, 128)
                    i = wk[0] % NWK
                    wk[0] += 1
                    a0, ra0 = aux[(2 * i) % 3], rs("p1_aux", (2 * i) % 3)
                    rsq, rsd = rs("p1_sq", i), rs("p1_sd", i)
                    op("act", lambda e: e.activation(out=sq[i][:], in_=X0[:], func=AF.Square), reads=[rX0], writes=[rsq])
                    op("pe", lambda e: e.matmul(a0[:], lhsT=cb("ones128"), rhs=sq[i][:], start=True, stop=True), reads=[rsq] + rC, writes=[ra0])
                    op("act", lambda e: e.activation(out=sd[i][:], in_=a0[:], func=AF.Sqrt, scale=1.0 / 128, bias=eps_ap), reads=[ra0] + rC, writes=[rsd])
                    op("dve", lambda e: e.reciprocal(out=sd[i][:], in_=sd[i][:]), reads=[rsd], writes=[rsd])
                    rckv = rs("p1_ckvn")
                    op("dve", lambda e: e.tensor_tensor(out=ckvn[:], in0=X0[:], in1=sd[i][:], op=ALU.mult), reads=[rX0, rsd], writes=[rckv])
                    if P1CUT <= 6:
                        continue
                    for h in range(6):
                        k = mmi[0] % 3
                        mmi[0] += 1
                        r = rs("p1_mm", k)
                        for c in range(2):
                            op("pe", lambda e: e.matmul(mm[k][0:96, :], lhsT=Wuq[:, c, h * 96:(h + 1) * 96], rhs=cqn[:, c, :], start=(c == 0), stop=(c == 1)),
                               reads=[rWsm, rcq], writes=[r])
                        normrope(mm[k], r, 96, gsm[0:96, l, 13:14], "ones96", 1.0 / 96, "R96", 4, qc_d[h, :, tsl], rs("qc_d"))
                    for h in range(6):
                        k = mmi[0] % 3
                        mmi[0] += 1
                        r = rs("p1_mm", k)
                        op("pe", lambda e: e.matmul(mm[k][0:96, :], lhsT=WukT[:, h, :], rhs=ckvn[:], start=True, stop=False), reads=[rWsm, rckv], writes=[r])
                        for kc in range(8):
                            op("pe", lambda e: e.matmul(mm[k][0:96, :], lhsT=Winp[:, kc, C_KPE:C_KPE + 96], rhs=hT[:, kc, :], start=False, stop=(kc == 7)),
                               reads=[rW, rhT], writes=[r])
                        normrope(mm[k], r, 96, gsm[0:96, l, 14:15], "ones96", 1.0 / 96, "R96", 4, kc_d[h, :, tsl], rs("kc_d"))
                    if P1CUT <= 7:
                        continue
                    for j in range(4):
                        tb = tg * 4 + j
                        k = mmi[0] % 3
                        mmi[0] += 1
                        r = rs("p1_mm", k)
                        op("pe", lambda e: e.matmul(mm[k][:, 0:384], lhsT=ckvn[:, j * 128:(j + 1) * 128], rhs=Wuv[:], start=True, stop=True), reads=[rWsm, rckv], writes=[r])
                        i = j % 2
                        rv = rs("p1_vcs", i)
                        op("act", lambda e: e.copy(out=vcs[i][:, :, 0:64], in_=mm[k][:, 0:384].rearrange("p (h d) -> p h d", h=6)), reads=[r], writes=[rv])
                        for h in range(6):
                            dma(vc_d[h, tb, :, :], vcs[i][:, h, :], reads=[rv], writes=[rs("vc_d")])
                        k = mmi[0] % 3
                        mmi[0] += 1
                        r = rs("p1_mm", k)
                        for kc in range(8):
                            op("pe", lambda e: e.matmul(mm[k][:, 0:72], lhsT=hT[:, kc, j * 128:(j + 1) * 128], rhs=Winp[:, kc, C_VAIW:C_VAIW + 72], start=(kc == 0), stop=(kc == 7)),
                               reads=[rW, rhT], writes=[r])
                        rva, riw = rs("p1_vas", i), rs("p1_iws", i)
                        op("act", lambda e: e.copy(out=vas[i][:, 0:64], in_=mm[k][:, 0:64]), reads=[r], writes=[rva])
                        dma(va_d[tb, :, :], vas[i][:], reads=[rva], writes=[rs("va_d")])
                        sc_ = (8.0 ** -0.5) * (32.0 ** -0.5)
                        op("act", lambda e: e.activation(out=iws[i][:, 0:8], in_=mm[k][:, 64:72], func=AF.Abs, scale=sc_), reads=[r], writes=[riw])
                        op("act", lambda e: e.activation(out=iws[i][:, 8:16], in_=mm[k][:, 64:72], func=AF.Sign), reads=[r], writes=[riw])
                        dma(iw_d[tb, :, :], iws[i][:], reads=[riw], writes=[rs("iw_d")])
            S_.barrier()

        def att_tiles(st, n_ops=2):
            T = {}
            T["att"] = [pst(st, f"at_ps{i}", [128, 512], F32) for i in range(2)]
            T["Ops"] = [pst(st, f"at_o{i}", [128, 512], F32) for i in range(n_ops)]
            T["sbp"] = pst(st, "at_sb", [128, 512], F32)
            T["PT"] = [sbt(st, f"at_pt{i}", [128, 512], BF16) for i in range(3)]
            T["Osb"] = [sbt(st, f"at_osb{i}", [65, 6, 512], F32) for i in range(2)]
            T["rcp"] = [sbt(st, f"at_rcp{i}", [64, 512], F32) for i in range(2)]
            T["yb"] = [sbt(st, f"at_yb{i}", [64, 512], BF16) for i in range(2)]
            T["gt"] = [sbt(st, f"at_gt{i}", [64, 6, 512], BF16) for i in range(2)]
            T["cnt"] = [0, 0, 0]
            return T

        def att_main(T, sb, n_heads, scale, heads, qfn, qres, nm_fn):
            att, Ops, PT, Osb = T["att"], T["Ops"], T["PT"], T["Osb"]
            cnt = T["cnt"]
            ob = sb % 2
            nkb = 4 * sb + 4
            rosb = rs("at_osb", ob)
            n_ops = len(Ops)
            steps = [(h, kb) for h in range(n_heads) for kb in range(nkb)]
            pend = None
            ois = {}
            for stp in steps + [None]:
                cur = None
                if stp is not None:
                    h, kb = stp
                    H = heads[h]
                    if kb == 0:
                        ois[h] = cnt[1] % n_ops
                        cnt[1] += 1
                    qb0 = max(kb - 4 * sb, 0)
                    qs = slice(qb0 * 128, 512)
                    ai = cnt[0] % 2
                    pi = cnt[0] % 3
                    cnt[0] += 1
                    ra, rp = rs("at_ps", ai), rs("at_pt", pi)
                    masks = []
                    for qb in range(qb0, 4):
                        m = nm_fn(4 * sb + qb, kb)
                        if m is not None:
                            masks.append((qb, m))
                    op("pe", lambda e: e.matmul(att[ai][:, qs], lhsT=H["kT"](slice(kb * 128, kb * 128 + 128)), rhs=qfn(h, qs), start=True, stop=(len(masks) == 0)),
                       reads=[H["kres"], qres], writes=[ra])
                    for mi, (qb, (map_, mres)) in enumerate(masks):
                        op("pe", lambda e: e.matmul(att[ai][:, qb * 128:(qb + 1) * 128], lhsT=map_, rhs=cb("ident"), start=False, stop=(mi == len(masks) - 1)),
                           reads=[mres] + rC, writes=[ra])
                    op("act", lambda e: e.activation(out=PT[pi][:, qs], in_=att[ai][:, qs], func=AF.Exp, scale=scale), reads=[ra], writes=[rp])
                    cur = (h, kb, pi, qs)
                if pend is not None:
                    h2, kb2, pi2, qs2 = pend
                    H2 = heads[h2]
                    oi = ois[h2]
                    rO = rs("at_o", oi)
                    op("pe", lambda e: e.matmul(Ops[oi][0:65, qs2], lhsT=H2["vaug"][:, kb2, :], rhs=PT[pi2][:, qs2], start=(kb2 == 0), stop=(kb2 == nkb - 1)),
                       reads=[rs("at_pt", pi2), H2["vres"]], writes=[rO])
                    if kb2 == nkb - 1:
                        op("act", lambda e: e.copy(out=Osb[ob][:, h2, :], in_=Ops[oi][0:65, :]), reads=[rO], writes=[rosb])
                pend = cur

        def att_norm(T, sb, n_heads, gate_d, gate_res, mix_base):
            Osb, sbp, rcp, yb, gt = T["Osb"], T["sbp"], T["rcp"], T["yb"], T["gt"]
            cnt = T["cnt"]
            ob = sb % 2
            rosb, rgt, rsb = rs("at_osb", ob), rs("at_gt", ob), rs("at_sbp")
            ssl = slice(sb * 512, (sb + 1) * 512)
            dma(gt[ob][:], gate_d[:, :, ssl].rearrange("h p t -> p h t"), reads=[rs(gate_res)], writes=[rgt])
            for h in range(n_heads):
                ri = cnt[2] % 2
                cnt[2] += 1
                rrc, ryb = rs("at_rcp", ri), rs("at_yb", ri)
                op("pe", lambda e: e.matmul(sbp[0:64, :], lhsT=cfa("e65", 65, 64), rhs=Osb[ob][:, h, :], start=True, stop=True), reads=[rosb] + rC, writes=[rsb])
                op("dve", lambda e: e.reciprocal(out=rcp[ri][:], in_=sbp[0:64, :]), reads=[rsb], writes=[rrc])
                op("dve", lambda e: e.tensor_tensor(out=rcp[ri][:], in0=rcp[ri][:], in1=Osb[ob][0:64, h, :], op=ALU.mult), reads=[rrc, rosb], writes=[rrc])
                op(P1ENG, lambda e: e.tensor_tensor(out=yb[ri][:], in0=rcp[ri][:], in1=gt[ob][:, h, :], op=ALU.mult), reads=[rrc, rgt], writes=[ryb])
                dma(mix_d[mix_base + h, 0:64, ssl], yb[ri][:], reads=[ryb], writes=[rs("mix_d")])

        def dsa_phase():
            with contextlib.ExitStack() as st:
                kT2 = sbt(st, "ds_kT2", [128, S], BF16)
                vaug = sbt(st, "ds_vaug", [128, NB, 65], BF16)
                ikT = sbt(st, "ds_ikT", [128, S], BF16)
                rk, rv, rik = rs("ds_kT2"), rs("ds_vaug"), rs("ds_ikT")
                dma(kT2[:], kT2_d, reads=[rs("kT2_d")], writes=[rk])
                dma(vaug[:], va_d.rearrange("t p c -> p t c"), reads=[rs("va_d")], writes=[rv])
                dma(ikT[:], ikT_d, reads=[rs("ikT_d")], writes=[rik])
                NM = [sbt(st, f"ds_NM{i}", [128, 4, S], BF16) for i in range(2)]
                sc = [sbt(st, f"ds_sc{i}", [128, S], F32) for i in range(2)]
                junk = sbt(st, "ds_junk", [128, S], BF16)
                iq = [sbt(st, f"ds_iq{i}", [128, 4, 128], BF16) for i in range(2)]
                iw = [sbt(st, f"ds_iw{i}", [128, 16], F32) for i in range(2)]
                Dh = [sbt(st, f"ds_Dh{i}", [128, 8, 128], BF16) for i in range(2)]
                Th = [sbt(st, f"ds_Th{i}", [128, 512], BF16) for i in range(3)]
                bs = [sbt(st, f"ds_bs{i}", [128, 8 + NIT + 1], F32) for i in range(2)]
                shp = [pst(st, f"ds_shp{i}", [128, 512], F32) for i in range(2)]
                scp = [pst(st, f"ds_scp{i}", [128, 512], F32) for i in range(2)]
                T = att_tiles(st, n_ops=1)
                qs_t = [sbt(st, f"ds_q{i}", [128, 3, 512], BF16) for i in range(2)]
                c3 = [0, 0]

                def masks(sb):
                    for qb in range(4):
                        b = 4 * sb + qb
                        n = 128 * (b + 1)
                        if n <= KTOP:
                            continue
                        i = b % 2
                        rsc, riq, riw, rDh, rbs = rs("ds_sc", i), rs("ds_iq", i), rs("ds_iw", i), rs("ds_Dh", i), rs("ds_bs", i)
                        rNM = rs("ds_NM", sb % 2)
                        dma(iq[i][:], iqT_d[:, :, b * 128:(b + 1) * 128].rearrange("c p t -> p c t"), reads=[rs("iqT_d")], writes=[riq])
                        dma(iw[i][:], iw_d[b, :, :], reads=[rs("iw_d")], writes=[riw])
                        for h in range(8):
                            op("dve", lambda e: e.tensor_scalar(out=Dh[i][:, h, :], in0=cb("ident"), scalar1=iw[i][:, 8 + h:9 + h], scalar2=None, op0=ALU.mult),
                               reads=[riw] + rC, writes=[rDh])
                        chunks = [(c * 512, min(512, n - c * 512)) for c in range((n + 511) // 512)]
                        isteps = [(ci, h) for ci in range(len(chunks)) for h in range(8)]
                        ipend = None
                        for ist in isteps + [None]:
                            icur = None
                            if ist is not None:
                                ci, h = ist
                                k0, wc = chunks[ci]
                                hi_ = c3[0] % 2
                                ti_ = c3[0] % 3
                                c3[0] += 1
                                rsh, rth = rs("ds_shp", hi_), rs("ds_Th", ti_)
                                pb = 64 * (h % 2)
                                op("pe", lambda e: e.matmul(shp[hi_][:, 0:wc], lhsT=iq[i][pb:pb + 32, h // 2, :], rhs=ikT[pb:pb + 32, k0:k0 + wc], start=True, stop=True),
                                   reads=[riq, rik], writes=[rsh])
                                op("act", lambda e: e.activation(out=Th[ti_][:, 0:wc], in_=shp[hi_][:, 0:wc], func=AF.Relu, scale=iw[i][:, h:h + 1]), reads=[rsh, riw], writes=[rth])
                                if h == 0:
                                    c3[1] += 1
                                icur = (ci, h, ti_, c3[1] % 2)
                            if ipend is not None:
                                ci2, h2, ti2, si = ipend
                                k0, wc = chunks[ci2]
                                rscp = rs("ds_scp", si)
                                op("pe", lambda e: e.matmul(scp[si][:, 0:wc], lhsT=Dh[i][:, h2, :], rhs=Th[ti2][:, 0:wc], start=(h2 == 0), stop=(h2 == 7)), reads=[rs("ds_Th", ti2), rDh], writes=[rscp])
                                if h2 == 7:
                                    op("act", lambda e: e.copy(out=sc[i][:, k0:k0 + wc], in_=scp[si][:, 0:wc]), reads=[rscp], writes=[rsc])
                            ipend = icur
                        lo, hi, mid, cn, tt, rng = (bs[i][:, k:k + 1] for k in range(6))
                        Hc = lambda it: bs[i][:, 8 + it:9 + it]
                        op("dve", lambda e: e.tensor_reduce(out=hi, in_=sc[i][:, 0:n], axis=AX.X, op=ALU.max), reads=[rsc], writes=[rbs])
                        op("dve", lambda e: e.tensor_reduce(out=lo, in_=sc[i][:, 0:n], axis=AX.X, op=ALU.min), reads=[rsc], writes=[rbs])
                        op("dve", lambda e: e.scalar_tensor_tensor(out=rng, in0=hi, scalar=1.0, in1=lo, op0=ALU.add, op1=ALU.subtract), reads=[rbs], writes=[rbs])
                        op("dve", lambda e: e.tensor_scalar(out=bs[i][:, 8:8 + NIT + 1], in0=cf[:, CF["pw"]:CF["pw"] + NIT + 1], scalar1=rng, scalar2=None, op0=ALU.mult), reads=[rbs] + rC, writes=[rbs])
                        op("dve", lambda e: e.tensor_tensor(out=mid, in0=lo, in1=Hc(0), op=ALU.add), reads=[rbs], writes=[rbs])
                        op("dve", lambda e: e.memset(sc[i][0:64, n - 64:n], -1e30), writes=[rsc])
                        for it in range(NIT):
                            op("dve", lambda e: e.tensor_scalar(out=junk[:, 0:n], in0=sc[i][:, 0:n], scalar1=mid, scalar2=None, op0=ALU.is_ge, op1=ALU.add, accum_out=cn),
                               reads=[rsc, rbs], writes=[rbs, rs("ds_junk")])
                            op("dve", lambda e: e.tensor_scalar(out=tt, in0=cn, scalar1=float(KTOP) - 0.5, scalar2=-0.5, op0=ALU.is_ge, op1=ALU.add), reads=[rbs], writes=[rbs])
                            op("dve", lambda e: e.scalar_tensor_tensor(out=mid, in0=tt, scalar=Hc(it), in1=mid, op0=ALU.mult, op1=ALU.add), reads=[rbs], writes=[rbs])
                        op("dve", lambda e: e.tensor_tensor(out=lo, in0=mid, in1=Hc(NIT), op=ALU.subtract), reads=[rbs], writes=[rbs])
                        op("dve", lambda e: e.tensor_scalar(out=NM[sb % 2][:, qb, 0:n], in0=sc[i][:, 0:n], scalar1=lo, scalar2=NEGM, op0=ALU.is_lt, op1=ALU.mult), reads=[rsc, rbs], writes=[rNM])

                def nm_fn(b, kb):
                    n = 128 * (b + 1)
                    if n <= KTOP:
                        if kb == b:
                            return (cb("nmd"), rs("cbf"))
                        return None
                    return (NM[(b // 4) % 2][:, b % 4, kb * 128:(kb + 1) * 128], rs("ds_NM", (b // 4) % 2))

                heads = [dict(kT=(lambda ks, pb=64 * (h % 2): kT2[pb:pb + 64, ks]), kres=rk, vaug=vaug, vres=rv) for h in range(6)]
                if DSA_MODE != 2:
                    masks(0)
                for sb in range(NG):
                    if sb + 1 < NG and DSA_MODE != 2:
                        masks(sb + 1)
                    if DSA_MODE == 1:
                        continue
                    i = sb % 2
                    rq = rs("ds_q", i)
                    dma(qs_t[i][:], qT_d[:, :, sb * 512:(sb + 1) * 512].rearrange("c p t -> p c t"), reads=[rs("qT_d")], writes=[rq])
                    qfn = lambda h, qs, i=i: qs_t[i][64 * (h % 2):64 * (h % 2) + 64, h // 2, qs]
                    att_main(T, sb, 6, 64.0 ** -0.5, heads, qfn, rq, nm_fn)
                    att_norm(T, sb, 6, gaT_d, "gaT_d", 0)
            S_.barrier()

        def mla_phase():
            with contextlib.ExitStack() as st:
                kT = sbt(st, "ml_kT", [96, 6, S], BF16)
                va = sbt(st, "ml_va", [128, 6, NB, 65], BF16)
                rk, rv = rs("ml_kT"), rs("ml_va")
                for h in range(6):
                    dma(kT[:, h, :], kc_d[h], reads=[rs("kc_d")], writes=[rk])
                    dma(va[:, h, :, :], vc_d[h].rearrange("t p c -> p t c"), reads=[rs("vc_d")], writes=[rv])
                T = att_tiles(st)
                qs_t = [sbt(st, f"ml_q{i}", [96, 6, 512], BF16) for i in range(2)]

                def nm_fn(b, kb):
                    if kb == b:
                        return (cb("nmd"), rs("cbf"))
                    return None
                heads = [dict(kT=(lambda ks, h=h: kT[:, h, ks]), kres=rk, vaug=va[:, h, :, :], vres=rv) for h in range(6)]
                for sb in range(NG):
                    i = sb % 2
                    rq = rs("ml_q", i)
                    dma(qs_t[i][:], qc_d[:, :, sb * 512:(sb + 1) * 512].rearrange("h p t -> p h t"), reads=[rs("qc_d")], writes=[rq])
                    qfn = lambda h, qs, i=i: qs_t[i][:, h, qs]
                    att_main(T, sb, 6, 96.0 ** -0.5, heads, qfn, rq, nm_fn)
                    att_norm(T, sb, 6, gcT_d, "gcT_d", 8)
            S_.barrier()

        def s5_consts(l):
            with contextlib.ExitStack() as st:
                ar = sbt(st, "s5_ar", [128, 16], F32)
                ai = sbt(st, "s5_ai", [128, 16], F32)
                stp = sbt(st, "s5_stp", [128, 16], F32)
                rp = rs("s5_par")
                for hh in range(2):
                    dma(ar[64 * hh:64 * hh + 64, :], a_re[l].rearrange("g p -> p g"), writes=[rp], allow_slow_non_contiguous=True)
                    dma(ai[64 * hh:64 * hh + 64, :], a_im[l].rearrange("g p -> p g"), writes=[rp], allow_slow_non_contiguous=True)
                dma(stp[:], log_step[l:l + 1, :].broadcast_to([128, 16]), writes=[rp])
                W = sbt(st, "s5_w", [128, 24, 16], F32)
                rw = rs("s5_w")

                def wv(k):
                    return W[:, k, :]

                def dv(fn, reads=(), writes=()):
                    op("dve", fn, reads=list(reads) + [rp, rw] + rC, writes=list(writes) + [rw])
                TWO_PI = 2.0 * np.pi

                def sincos(dst, ang, shift):
                    t, n_, m = wv(20), wv(21), wv(22)
                    ti = Wi[:, 0, :]
                    dv(lambda e: e.tensor_scalar(out=t, in0=ang, scalar1=shift, scalar2=1.0 / TWO_PI, op0=ALU.add, op1=ALU.mult))
                    dv(lambda e: e.tensor_copy(out=ti, in_=t))
                    dv(lambda e: e.tensor_copy(out=n_, in_=ti))
                    dv(lambda e: e.tensor_tensor(out=t, in0=t, in1=n_, op=ALU.subtract))
                    dv(lambda e: e.tensor_scalar(out=m, in0=t, scalar1=0.5, scalar2=None, op0=ALU.is_gt))
                    dv(lambda e: e.tensor_tensor(out=t, in0=t, in1=m, op=ALU.subtract))
                    dv(lambda e: e.tensor_scalar(out=m, in0=t, scalar1=-0.5, scalar2=None, op0=ALU.is_lt))
                    dv(lambda e: e.tensor_tensor(out=t, in0=t, in1=m, op=ALU.add))
                    op("act", lambda e: e.activation(out=dst, in_=t, func=AF.Sin, scale=TWO_PI), reads=[rw], writes=[rw])
                Wi = sbt(st, "s5_wi", [128, 1, 16], mybir.dt.int32)
                op("act", lambda e: e.activation(out=stp[:], in_=stp[:], func=AF.Exp), reads=[rp], writes=[rp])
                mag, ang, abr, abi, cs, sn = wv(0), wv(1), wv(2), wv(3), wv(4), wv(5)
                dv(lambda e: e.tensor_tensor(out=mag, in0=ar[:], in1=stp[:], op=ALU.mult))
                op("act", lambda e: e.activation(out=mag, in_=mag, func=AF.Exp), reads=[rw], writes=[rw])
                dv(lambda e: e.tensor_tensor(out=ang, in0=ai[:], in1=stp[:], op=ALU.mult))
                sincos(sn, ang, 0.0)
                sincos(cs, ang, np.pi / 2)
                dv(lambda e: e.tensor_tensor(out=abr, in0=mag, in1=cs, op=ALU.mult))
                dv(lambda e: e.tensor_tensor(out=abi, in0=mag, in1=sn, op=ALU.mult))
                den, nr, fr, fi, tA, tB = wv(6), wv(7), wv(8), wv(9), wv(10), wv(11)
                dv(lambda e: e.tensor_tensor(out=den, in0=ar[:], in1=ar[:], op=ALU.mult))
                dv(lambda e: e.tensor_tensor(out=tA, in0=ai[:], in1=ai[:], op=ALU.mult))
                dv(lambda e: e.tensor_tensor(out=den, in0=den, in1=tA, op=ALU.add))
                dv(lambda e: e.reciprocal(out=den, in_=den))
                dv(lambda e: e.tensor_scalar(out=nr, in0=abr, scalar1=-1.0, scalar2=None, op0=ALU.add))
                dv(lambda e: e.tensor_tensor(out=fr, in0=nr, in1=ar[:], op=ALU.mult))
                dv(lambda e: e.tensor_tensor(out=tA, in0=abi, in1=ai[:], op=ALU.mult))
                dv(lambda e: e.tensor_tensor(out=fr, in0=fr, in1=tA, op=ALU.add))
                dv(lambda e: e.tensor_tensor(out=fr, in0=fr, in1=den, op=ALU.mult))
                dv(lambda e: e.tensor_tensor(out=fi, in0=abi, in1=ar[:], op=ALU.mult))
                dv(lambda e: e.tensor_tensor(out=tA, in0=nr, in1=ai[:], op=ALU.mult))
                dv(lambda e: e.tensor_tensor(out=fi, in0=fi, in1=tA, op=ALU.subtract))
                dv(lambda e: e.tensor_tensor(out=fi, in0=fi, in1=den, op=ALU.mult))
                PW = sbt(st, "s5_pw", [128, 9, 2, 16], F32)
                MKp = sbt(st, "s5_mkp", [128, 9, 2, 16], F32)
                FN = sbt(st, "s5_fn", [128, 8, 2, 16], F32)
                CQc = sbt(st, "s5_cqc", [128, 9, 2, 16], F32)

                def cmul(o_re, o_im, x_re, x_im, y_re, y_im):
                    dv(lambda e: e.tensor_tensor(out=tA, in0=x_re, in1=y_re, op=ALU.mult))
                    dv(lambda e: e.tensor_tensor(out=tB, in0=x_im, in1=y_im, op=ALU.mult))
                    dv(lambda e: e.tensor_tensor(out=wv(12), in0=x_re, in1=y_im, op=ALU.mult))
                    dv(lambda e: e.tensor_tensor(out=wv(13), in0=x_im, in1=y_re, op=ALU.mult))
                    dv(lambda e: e.tensor_tensor(out=o_re, in0=tA, in1=tB, op=ALU.subtract))
                    dv(lambda e: e.tensor_tensor(out=o_im, in0=wv(12), in1=wv(13), op=ALU.add))
                dv(lambda e: e.memset(PW[:, 0, 0, :], 1.0))
                dv(lambda e: e.memset(PW[:, 0, 1, :], 0.0))
                for n_ in range(1, 9):
                    cmul(PW[:, n_, 0, :], PW[:, n_, 1, :], PW[:, n_ - 1, 0, :], PW[:, n_ - 1, 1, :], abr, abi)
                dv(lambda e: e.tensor_copy(out=MKp[:, 0, :, :], in_=PW[:, 8, :, :]))
                for k in range(1, 9):
                    cmul(MKp[:, k, 0, :], MKp[:, k, 1, :], MKp[:, k - 1, 0, :], MKp[:, k - 1, 1, :], MKp[:, k - 1, 0, :], MKp[:, k - 1, 1, :])
                sgB = cf[:, CF["sgB"]:CF["sgB"] + 1]
                sgQ = cf[:, CF["sgQ"]:CF["sgQ"] + 1]
                for n_ in range(8):
                    cmul(FN[:, n_, 0, :], FN[:, n_, 1, :], PW[:, n_, 0, :], PW[:, n_, 1, :], fr, fi)
                    dv(lambda e: e.tensor_scalar(out=FN[:, n_, 1, :], in0=FN[:, n_, 1, :], scalar1=sgB, scalar2=None, op0=ALU.mult))
                for n_ in range(9):
                    dv(lambda e: e.tensor_scalar(out=CQc[:, n_, 0, :], in0=PW[:, n_, 0, :], scalar1=sgQ, scalar2=None, op0=ALU.mult))
                    dv(lambda e: e.tensor_scalar(out=CQc[:, n_, 1, :], in0=PW[:, n_, 1, :], scalar1=-1.0, scalar2=None, op0=ALU.mult))
                for k in range(9):
                    dv(lambda e: e.tensor_scalar(out=MKp[:, k, 1, :], in0=MKp[:, k, 1, :], scalar1=sgQ, scalar2=None, op0=ALU.mult))
                bx = [sbt(st, f"s5_bx{i}", [128, 16], F32) for i in range(2)]
                by = [sbt(st, f"s5_by{i}", [128, 16], F32) for i in range(2)]
                cx = [sbt(st, f"s5_cx{i}", [128, 16], F32) for i in range(2)]
                cy = [sbt(st, f"s5_cy{i}", [128, 16], F32) for i in range(2)]
                dr = [sbt(st, f"s5_dr{i}", [128, 1], F32) for i in range(2)]
                Pm = [sbt(st, f"s5_Pm{i}", [128, 128], F32) for i in range(2)]
                CQ = [sbt(st, f"s5_CQ{i}", [128, 9, 16], F32) for i in range(2)]
                Btr = [sbt(st, f"s5_Btr{i}", [128, 8, 16], F32) for i in range(2)]
                KTp = [sbt(st, f"s5_KTp{i}", [128, 15 * 16], F32) for i in range(2)]
                OUTm = [sbt(st, f"s5_OUT{i}", [128, 12, 128], F32) for i in range(2)]
                pp = [pst(st, f"s5_pp{i}", [128, 128], F32) for i in range(2)]
                for i in range(2):
                    op("pool", lambda e: e.memset(KTp[i][:], 0.0), writes=[rs("s5_KTp", i)])
                for g in range(16):
                    i = g % 2
                    rin, rPm, rCQ, rBt, rKT, rOUT, rpp = rs("s5_in", i), rs("s5_Pm", i), rs("s5_CQ", i), rs("s5_Btr", i), rs("s5_KTp", i), rs("s5_OUT", i), rs("s5_pp", i)
                    dma(bx[i][0:64, :], b_re[l, g], writes=[rin])
                    dma(bx[i][64:128, :], b_im[l, g], writes=[rin])
                    dma(by[i][0:64, :], b_im[l, g], writes=[rin])
                    dma(by[i][64:128, :], b_re[l, g], writes=[rin])
                    dma(cx[i][0:64, :], c_re[l, g].rearrange("c p -> p c"), writes=[rin], allow_slow_non_contiguous=True)
                    dma(cx[i][64:128, :], c_im[l, g].rearrange("c p -> p c"), writes=[rin], allow_slow_non_contiguous=True)
                    dma(cy[i][0:64, :], c_im[l, g].rearrange("c p -> p c"), writes=[rin], allow_slow_non_contiguous=True)
                    dma(cy[i][64:128, :], c_re[l, g].rearrange("c p -> p c"), writes=[rin], allow_slow_non_contiguous=True)
                    for s_ in range(8):
                        dma(dr[i][16 * s_:16 * s_ + 16, :], d_skip[l, g].rearrange("(c o) -> c o", o=1), writes=[rin], allow_slow_non_contiguous=True)
                    for s_ in range(8):
                        n_ = 7 - s_
                        op("dve", lambda e: e.tensor_scalar(out=Pm[i][:, 16 * s_:16 * s_ + 16], in0=bx[i][:], scalar1=FN[:, n_, 0, g:g + 1], scalar2=None, op0=ALU.mult), reads=[rin, rw], writes=[rPm])
                        op("dve", lambda e: e.scalar_tensor_tensor(out=Pm[i][:, 16 * s_:16 * s_ + 16], in0=by[i][:], scalar=FN[:, n_, 1, g:g + 1], in1=Pm[i][:, 16 * s_:16 * s_ + 16], op0=ALU.mult, op1=ALU.add),
                           reads=[rin, rw], writes=[rPm])
                    for s_ in range(8):
                        op("dve", lambda e: e.tensor_copy(out=Btr[i][:, s_, :], in_=Pm[i][:, 112:128]), reads=[rPm], writes=[rBt])
                    for n_ in range(9):
                        op("dve", lambda e: e.tensor_scalar(out=CQ[i][:, n_, :], in0=cx[i][:], scalar1=CQc[:, n_, 0, g:g + 1], scalar2=None, op0=ALU.mult), reads=[rin, rw], writes=[rCQ])
                        op("dve", lambda e: e.scalar_tensor_tensor(out=CQ[i][:, n_, :], in0=cy[i][:], scalar=CQc[:, n_, 1, g:g + 1], in1=CQ[i][:, n_, :], op0=ALU.mult, op1=ALU.add),
                           reads=[rin, rw], writes=[rCQ])
                    op("pe", lambda e: e.transpose(out=pp[i][:], in_=Pm[i][:], identity=cfa("ident")), reads=[rPm] + rC, writes=[rpp])
                    op("act", lambda e: e.copy(out=OUTm[i][:, 0, :], in_=pp[i][:]), reads=[rpp], writes=[rOUT])
                    op("dve", lambda e: e.tensor_copy(out=OUTm[i][:, 1, :], in_=CQ[i][:, 1:9, :].rearrange("p n c -> p (n c)")), reads=[rCQ], writes=[rOUT])
                    op("pe", lambda e: e.matmul(pp[i][:], lhsT=Btr[i][:].rearrange("p s c -> p (s c)"), rhs=CQ[i][:, 0:8, :].rearrange("p n c -> p (n c)"), start=True, stop=True),
                       reads=[rBt, rCQ], writes=[rpp])
                    op("act", lambda e: e.copy(out=KTp[i][:, 112:240], in_=pp[i][:]), reads=[rpp], writes=[rKT])
                    op("dve", lambda e: e.scalar_tensor_tensor(out=KTp[i][:, 112:128], in0=cfa("i16", 128, 16), scalar=dr[i][:, 0:1], in1=KTp[i][:, 112:128], op0=ALU.mult, op1=ALU.add),
                       reads=[rin, rKT] + rC, writes=[rKT])
                    for s_ in range(8):
                        dma(s5c_d[g, 16 * s_:16 * s_ + 16, 2, :], KTp[i][16 * s_:16 * s_ + 16, (7 - s_) * 16:(7 - s_) * 16 + 128], reads=[rKT], writes=[rs("s5c_d")])
                    for k in range(9):
                        op("dve", lambda e: e.tensor_scalar(out=OUTm[i][:, 3 + k, :], in0=cfa("ident"), scalar1=MKp[:, k, 0, g:g + 1], scalar2=None, op0=ALU.mult), reads=[rw] + rC, writes=[rOUT])
                        op("dve", lambda e: e.scalar_tensor_tensor(out=OUTm[i][:, 3 + k, :], in0=cfa("icross"), scalar=MKp[:, k, 1, g:g + 1], in1=OUTm[i][:, 3 + k, :], op0=ALU.mult, op1=ALU.add),
                           reads=[rw] + rC, writes=[rOUT])
                    dma(s5c_d[g, :, 0:2, :], OUTm[i][:, 0:2, :], reads=[rOUT], writes=[rs("s5c_d")])
                    dma(s5c_d[g, :, 3:12, :], OUTm[i][:, 3:12, :], reads=[rOUT], writes=[rs("s5c_d")])
            S_.barrier()

        def s5_phase():
            with contextlib.ExitStack() as st:
                Cm = [sbt(st, f"s5r_C{i}", [128, 12, 128], F32) for i in range(2)]
                Ut = [sbt(st, f"s5r_U{i}", [128, J], F32) for i in range(2)]
                St = [sbt(st, f"s5r_S{i}", [128, J], F32) for i in range(2)]
                Yt = [sbt(st, f"s5r_Y{i}", [128, J], F32) for i in range(2)]
                NJ = (J + 511) // 512
                pk = [pst(st, f"s5r_p{i}", [128, 512], F32) for i in range(4)]
                pc = [0]
                for g in range(16):
                    i = g % 2
                    rCm, rU, rSt, rY = rs("s5r_C", i), rs("s5r_U", i), rs("s5r_S", i), rs("s5r_Y", i)
                    dma(Cm[i][:], s5c_d[g], reads=[rs("s5c_d")], writes=[rCm])
                    for s_ in range(8):
                        dma(Ut[i][16 * s_:16 * s_ + 16, :], U_d[g, s_, :, :], reads=[rs("U_d")], writes=[rU])
                    for c in range(NJ):
                        cs_ = slice(c * 512, min(J, c * 512 + 512))
                        k = pc[0] % 4
                        pc[0] += 1
                        rpk = rs("s5r_p", k)
                        op("pe", lambda e: e.matmul(pk[k][:, 0:cs_.stop - cs_.start], lhsT=Cm[i][:, 0, :], rhs=Ut[i][:, cs_], start=True, stop=True), reads=[rCm, rU], writes=[rpk])
                        op("act", lambda e: e.copy(out=St[i][:, cs_], in_=pk[k][:, 0:cs_.stop - cs_.start]), reads=[rpk], writes=[rSt])
                    k_ = 0
                    while (1 << k_) < J:
                        sh = 1 << k_
                        width = J - sh
                        chunks = []
                        c0 = 0
                        while c0 < width:
                            chunks.append((c0, min(512, width - c0)))
                            c0 += 512
                        prods = []
                        for (c0, wc) in chunks:
                            k = pc[0] % 4
                            pc[0] += 1
                            rpk = rs("s5r_p", k)
                            op("pe", lambda e: e.matmul(pk[k][:, 0:wc], lhsT=Cm[i][:, 3 + k_, :], rhs=St[i][:, c0:c0 + wc], start=True, stop=True), reads=[rCm, rSt], writes=[rpk])
                            prods.append((k, rpk, c0, wc))
                            if len(prods) == 4 or (c0, wc) == chunks[-1]:
                                pass
                        for (k, rpk, c0, wc) in prods:
                            op("dve", lambda e: e.tensor_tensor(out=St[i][:, sh + c0:sh + c0 + wc], in0=pk[k][:, 0:wc], in1=St[i][:, sh + c0:sh + c0 + wc], op=ALU.add), reads=[rpk, rSt], writes=[rSt])
                        k_ += 1
                    for c in range(NJ):
                        c0 = c * 512
                        wc = min(512, J - c0)
                        k = pc[0] % 4
                        pc[0] += 1
                        rpk = rs("s5r_p", k)
                        op("pe", lambda e: e.matmul(pk[k][:, 0:wc], lhsT=Cm[i][:, 2, :], rhs=Ut[i][:, c0:c0 + wc], start=True, stop=False), reads=[rCm, rU], writes=[rpk])
                        if c0 == 0:
                            op("pe", lambda e: e.matmul(pk[k][:, 1:wc], lhsT=Cm[i][:, 1, :], rhs=St[i][:, 0:wc - 1], start=False, stop=True), reads=[rCm, rSt], writes=[rpk])
                        else:
                            op("pe", lambda e: e.matmul(pk[k][:, 0:wc], lhsT=Cm[i][:, 1, :], rhs=St[i][:, c0 - 1:c0 + wc - 1], start=False, stop=True), reads=[rCm, rSt], writes=[rpk])
                        op("act", lambda e: e.copy(out=Yt[i][:, c0:c0 + wc], in_=pk[k][:, 0:wc]), reads=[rpk], writes=[rY])
                    for s_ in range(8):
                        dma(Y_d[g, s_, :, :], Yt[i][16 * s_:16 * s_ + 16, :], reads=[rY], writes=[rs("Y_d")])
            S_.barrier()
            with contextlib.ExitStack() as st:
                yp = sbt(st, "s5p_yp", [128, 2, 8, 64], F32)
                y = sbt(st, "s5p_y", [128, 2, 512], F32)
                t = sbt(st, "s5p_t", [128, 2, 512], F32)
                gl = sbt(st, "s5p_g", [128, 2, 512], F32)
                gb16 = sbt(st, "s5p_gb", [128, 2, 512], BF16)
                sg = sbt(st, "s5p_sg", [128, 2, 512], F32)
                gate = sbt(st, "s5p_gate", [128, 2, 512], BF16)
                ob = sbt(st, "s5p_ob", [128, 2, 512], BF16)
                zp = [pst(st, f"s5p_z{i}", [128, 512], F32) for i in range(2)]
                ryp, ry, rt_, rgl, rgb, rsg, rgate, rob = (rs("s5p_" + n_) for n_ in ("yp", "y", "t", "g", "gb", "sg", "gate", "ob"))
                for tg in range(NG):
                    tsl = slice(tg * 512, tg * 512 + 512)
                    for c in range(2):
                        for g8 in range(8):
                            dma(yp[16 * g8:16 * g8 + 16, c, :, :], Y_d[8 * c + g8, :, :, tg * 64:(tg + 1) * 64].rearrange("s c j -> c s j"), reads=[rs("Y_d")], writes=[ryp])
                        dma(gate[:, c, :], gbT_d[c, :, tsl], reads=[rs("gbT_d")], writes=[rgate])
                    for c in range(2):
                        op("act", lambda e: e.copy(out=y[:, c, :].rearrange("p (j s) -> p s j", s=8), in_=yp[:, c, :, :]), reads=[ryp], writes=[ry])
                        op("act", lambda e: e.activation(out=t[:, c, :], in_=y[:, c, :], func=AF.Square), reads=[ry], writes=[rt_])
                        op("dve", lambda e: e.tensor_scalar(out=t[:, c, :], in0=t[:, c, :], scalar1=0.044715, scalar2=1.0, op0=ALU.mult, op1=ALU.add), reads=[rt_], writes=[rt_])
                        op("dve", lambda e: e.tensor_tensor(out=t[:, c, :], in0=t[:, c, :], in1=y[:, c, :], op=ALU.mult), reads=[rt_, ry], writes=[rt_])
                        op("act", lambda e: e.activation(out=t[:, c, :], in_=t[:, c, :], func=AF.Sigmoid, scale=2.0 * 0.7978845608028654), reads=[rt_], writes=[rt_])
                        op("dve", lambda e: e.tensor_tensor(out=gl[:, c, :], in0=t[:, c, :], in1=y[:, c, :], op=ALU.mult), reads=[rt_, ry], writes=[rgl])
                        op("dve", lambda e: e.tensor_copy(out=gb16[:, c, :], in_=gl[:, c, :]), reads=[rgl], writes=[rgb])
                    for co in range(2):
                        rz = rs("s5p_z", co)
                        for c in range(2):
                            op("pe", lambda e: e.matmul(zp[co][:], lhsT=Wglu[:, c, co * 128:(co + 1) * 128], rhs=gb16[:, c, :], start=(c == 0), stop=(c == 1)), reads=[rWsm, rgb], writes=[rz])
                        op("act", lambda e: e.activation(out=sg[:, co, :], in_=zp[co][:], func=AF.Sigmoid), reads=[rz], writes=[rsg])
                        op("dve", lambda e: e.tensor_tensor(out=sg[:, co, :], in0=sg[:, co, :], in1=gl[:, co, :], op=ALU.mult), reads=[rsg, rgl], writes=[rsg])
                        op("dve", lambda e: e.tensor_tensor(out=ob[:, co, :], in0=sg[:, co, :], in1=gate[:, co, :], op=ALU.mult), reads=[rsg, rgate], writes=[rob])
                        dma(mix_d[6 + co, :, tsl], ob[:, co, :], reads=[rob], writes=[rs("mix_d")])
            S_.barrier()

        def phase3(src_ap, src_res, dst_ap, dst_res):
            with contextlib.ExitStack() as st:
                Wo = sbt(st, "p3_Wo", [128, 14, D], BF16)
                rWo = rs("p3_Wo")
                for c in range(14):
                    nr = 128 if c in (6, 7) else 64
                    dma(Wo[0:nr, c, :], wout_d[0:nr, c, :], reads=[rs("wout_d")], writes=[rWo])
                mx = [sbt(st, f"p3_mx{i}", [128, 14, 512], BF16) for i in range(2)]
                xt = [sbt(st, f"p3_xt{i}", [128, 4, D], F32) for i in range(2)]
                pp = [pst(st, f"p3_pp{i}", [128, 512], F32) for i in range(4)]
                pc = 0
                for tg in range(NG):
                    i = tg % 2
                    rmx, rxt = rs("p3_mx", i), rs("p3_xt", i)
                    tsl = slice(tg * 512, tg * 512 + 512)
                    for c in range(14):
                        nr = 128 if c in (6, 7) else 64
                        dma(mx[i][0:nr, c, :], mix_d[c, 0:nr, tsl], reads=[rs("mix_d")], writes=[rmx])
                    dma(xt[i][:], src_ap[tg * 512:(tg + 1) * 512, :].rearrange("(j p) d -> p j d", p=128), reads=[src_res], writes=[rxt])
                    for j in range(4):
                        for half in range(2):
                            k = pc % 4
                            pc += 1
                            rpp = rs("p3_pp", k)
                            for c in range(14):
                                nr = 128 if c in (6, 7) else 64
                                op("pe", lambda e: e.matmul(pp[k][:], lhsT=mx[i][0:nr, c, j * 128:(j + 1) * 128], rhs=Wo[0:nr, c, half * 512:(half + 1) * 512], start=(c == 0), stop=(c == 13)),
                                   reads=[rmx, rWo], writes=[rpp])
                            op("dve", lambda e: e.tensor_tensor(out=xt[i][:, j, half * 512:(half + 1) * 512], in0=pp[k][:], in1=xt[i][:, j, half * 512:(half + 1) * 512], op=ALU.add), reads=[rpp, rxt], writes=[rxt])
                    dma(dst_ap[tg * 512:(tg + 1) * 512, :].rearrange("(j p) d -> p j d", p=128), xt[i][:], reads=[rxt], writes=[dst_res])
            S_.barrier()

        for l in range(NL):
            if "prep" in PH:
                prep_weights(l)
            if "s5c" in PH:
                s5_consts(l)
            for s in range(NSEQ):
                src = x_in[s] if l == 0 else xs_d[s]
                dst = out_d[s] if l == NL - 1 else xs_d[s]
                if "p1" in PH:
                    phase1(l, src, rs("xs", s))
                if "dsa" in PH:
                    dsa_phase()
                if "s5" in PH:
                    s5_phase()
                if "mla" in PH:
                    mla_phase()
                if "p3" in PH:
                    phase3(src, rs("xs", s), dst, rs("out_d") if l == NL - 1 else rs("xs", s))
        S_.barrier()
    return nc


xs_all = None


def _build(S, NL, NSEQ, KTOP):
    global xs_all
    return build_program(S, NL, NSEQ, KTOP)


_CACHE = {}


def kernel(**inputs):
    x = np.ascontiguousarray(np.asarray(inputs["x"], dtype=np.float32))
    B, S, _ = x.shape
    ncores = 8
    NSEQ = B // ncores
    KTOP = min(256, S // 4)
    nc = build_program(S, DEPTH, NSEQ, KTOP)
    consts = make_consts(S)
    names = ["norm_g", "w_in", "attn_q_norm", "attn_k_norm", "mla_q_lora_norm", "mla_kv_lora_norm", "mla_w_uq", "mla_w_ukv",
             "mla_q_norm", "mla_k_norm", "ssm_a_re", "ssm_a_im", "ssm_b_re", "ssm_b_im", "ssm_c_re", "ssm_c_im", "ssm_d",
             "ssm_log_step", "ssm_w_glu", "w_out"]
    shared = {n: np.ascontiguousarray(np.asarray(inputs[n], dtype=np.float32)) for n in names}
    in_maps = []
    for c in range(ncores):
        m = dict(shared)
        m["x"] = x[c * NSEQ:(c + 1) * NSEQ]
        m.update(consts)
        in_maps.append(m)
    res = run_bass_kernel_spmd(nc, in_maps, core_ids=list(range(ncores)))
    return np.concatenate([r["out"] for r in res.results], axis=0).astype(np.float32)
```

```python
import contextlib
import numpy as np
import ml_dtypes
import concourse.bass as bass
import concourse.mybir as mybir
from concourse.bass_utils import run_bass_kernel_spmd

F32 = mybir.dt.float32
BF16 = mybir.dt.bfloat16
AF = mybir.ActivationFunctionType
ALU = mybir.AluOpType
AX = mybir.AxisListType

D = 1024
DEPTH = 4
DIN = 2504
EPS = 1e-6
NCOL = 3240
C_IQ2 = 2728
C_QA, C_KA2, C_IQ, C_IK4, C_GA, C_U, C_GB, C_CQ, C_CKV, C_KPE, C_GC, C_VAIW = (
    0, 384, 512, 768, 896, 1280, 1536, 1792, 2048, 2176, 2272, 2656)
S_QA, S_KA, S_VA, S_IQ, S_IK, S_IW, S_GA, S_U, S_GB, S_CQ, S_CKV, S_KPE, S_GC = (
    0, 384, 448, 512, 768, 800, 808, 1192, 1448, 1704, 1960, 2088, 2120)
NEGM = -30000.0
NIT = 16
import os as _os5
P1ENG = _os5.environ.get('P1ENG', 'pool')
import os as _os4
DSA_MODE = int(_os4.environ.get('DSA_MODE', '0'))
import os as _os3
GV = int(_os3.environ.get('GV', '0'))
import os as _os2
GATE_DMA = int(_os2.environ.get('GATE_DMA', '2'))
import os as _os
P1CUT = int(_os.environ.get('P1CUT', '99'))
import os
PH = set(os.environ.get('PH', 'prep,s5c,p1,dsa,s5,mla,p3').split(','))


class Res:
    __slots__ = ("w", "r", "ps")

    def __init__(self, ps=False):
        self.w = {}
        self.r = {}
        self.ps = ps


class Sched:
    def __init__(self, nc, n_dma_sems=32):
        self.nc = nc
        self.engs = {}
        for name, e in (("pe", nc.tensor), ("act", nc.scalar), ("dve", nc.vector),
                        ("pool", nc.gpsimd), ("sp", nc.sync)):
            sem = nc.alloc_semaphore(name="sem_" + name)
            self.engs[name] = dict(e=e, sem=sem, cnt=0, waited={}, name=name)
        self.dma_sems = [nc.alloc_semaphore(name=f"dsem{i}") for i in range(n_dma_sems)]
        self.dma_cnt = [0] * n_dma_sems
        self.dma_i = 0
        self.nins = 0

    def _wait(self, en, deps, skip_self=False):
        E = self.engs[en]
        best = {}
        for sem, val in deps:
            k = id(sem)
            if k not in best or best[k][1] < val:
                best[k] = (sem, val)
        for k, (sem, val) in best.items():
            if skip_self and sem is E["sem"]:
                continue
            if E["waited"].get(k, 0) < val:
                E["e"].wait_ge(sem, val)
                E["waited"][k] = val

    def _deps(self, reads, writes, en=None):
        deps = []
        own = self.engs[en]["sem"] if en is not None else None
        for r in reads:
            deps.extend(r.w.values())
            if r.ps:
                deps.extend(ev for ev in r.r.values() if ev[0] is not own)
        for w in writes:
            deps.extend(w.w.values())
            deps.extend(w.r.values())
        return deps

    @staticmethod
    def _mark(ev, reads, writes):
        k = id(ev[0])
        for r in reads:
            r.r[k] = ev
        for w in writes:
            w.w[k] = ev

    def op(self, en, fn, reads=(), writes=()):
        E = self.engs[en]
        self._wait(en, self._deps(reads, writes, en), skip_self=(en == "pe"))
        ins = fn(E["e"])
        E["cnt"] += 1
        ins.then_inc(E["sem"], 1)
        ev = (E["sem"], E["cnt"])
        self._mark(ev, reads, writes)
        self.nins += 1
        return ev

    def dma(self, out, in_, reads=(), writes=(), en="sp", **kw):
        E = self.engs[en]
        i = self.dma_i % len(self.dma_sems)
        self.dma_i += 1
        sem = self.dma_sems[i]
        deps = self._deps(reads, writes)
        if self.dma_cnt[i] > 0:
            deps.append((sem, self.dma_cnt[i]))
        self._wait(en, deps)
        ins = E["e"].dma_start(out=out, in_=in_, **kw)
        self.dma_cnt[i] += 16
        ins.then_inc(sem, 16)
        ev = (sem, self.dma_cnt[i])
        self._mark(ev, reads, writes)
        self.nins += 1
        return ev

    def barrier(self):
        evs = [(E["sem"], E["cnt"]) for E in self.engs.values() if E["cnt"] > 0]
        evs += [(s, c) for s, c in zip(self.dma_sems, self.dma_cnt) if c > 0]
        for en in self.engs:
            self._wait(en, evs)


def rope_tab(S, d, rows):
    half = d // 2
    inv = (np.float32(10000.0) ** (-np.arange(half, dtype=np.float32) * np.float32(2.0) / np.float32(d))).astype(np.float32)
    ang = np.arange(S, dtype=np.float32)[:, None] * inv[None, :]
    c = np.cos(ang).astype(np.float32).T
    s = np.sin(ang).astype(np.float32).T
    idx = np.arange(rows) % half
    return c[idx], s[idx]


def make_consts(S):
    bf = ml_dtypes.bfloat16
    I = np.eye(128, dtype=np.float32)
    ones64 = np.zeros((128, 128), np.float32)
    ones64[:64, :64] = 1; ones64[64:, 64:] = 1
    ones96 = np.zeros((128, 128), np.float32); ones96[:96, :96] = 1
    ones128 = np.ones((128, 128), np.float32)

    def rot(M, hd, half, lo=0):
        R = np.zeros((128, 128), np.float32)
        for m in range(M):
            j = m % hd
            if j < lo:
                continue
            jj = j - lo
            if jj < half:
                R[m + half, m] = -1.0
            else:
                R[m - half, m] = 1.0
        return R
    R64 = rot(128, 64, 32)
    R32 = rot(128, 32, 16)
    R96 = rot(96, 96, 16, lo=64)
    nmd = np.zeros((128, 128), np.float32); nmd[:64, 64:] = NEGM
    cbf = np.concatenate([I, ones64, ones96, ones128, R64, R32, R96, nmd], axis=1).astype(bf)
    icross = np.zeros((128, 128), np.float32)
    for m in range(128):
        icross[m, (m + 64) % 128] = 1
    e65 = np.zeros((128, 64), np.float32); e65[64, :] = 1
    i16 = np.zeros((128, 16), np.float32)
    for p in range(128):
        i16[p, p % 16] = 1
    sgB = np.where(np.arange(128) < 64, -1.0, 1.0).astype(np.float32)[:, None]
    sgQ = -sgB
    epsc = np.full((128, 1), EPS, np.float32)
    pw = np.tile((2.0 ** -(np.arange(NIT + 1, dtype=np.float64) + 1.0)).astype(np.float32)[None, :], (128, 1))
    cf = np.concatenate([I, icross, e65, i16, sgB, sgQ, epsc, pw], axis=1).astype(np.float32)
    c64, s64 = rope_tab(S, 64, 128)
    c32, s32 = rope_tab(S, 32, 128)
    c96 = np.ones((128, S), np.float32); s96 = np.zeros((128, S), np.float32)
    c96[64:96] = c32[:32]; s96[64:96] = s32[:32]
    rt = np.stack([c64, s64, c32, s32, c96, s96]).astype(np.float32)
    return dict(cbf=cbf, cf=cf, rt=rt)


CB = dict(ident=0, ones64=128, ones96=256, ones128=384, R64=512, R32=640, R96=768, nmd=896)
CF = dict(ident=0, icross=128, e65=256, i16=320, sgB=336, sgQ=337, eps=338, pw=339)
NCF = 339 + NIT + 1


def build_program(S, NL, NSEQ, KTOP, dbg=None):
    nc = bass.Bass("TRN2", target_bir_lowering=False)
    NB = S // 128
    NG = S // 512
    J = S // 8
    dbg = dbg or {}

    def din(name, shape, dt=F32):
        return nc.dram_tensor(name, list(shape), dt, kind="ExternalInput").ap()

    def dscr(name, shape, dt):
        return nc.dram_tensor(name, list(shape), dt, kind="Internal").ap()

    x_in = din("x", [NSEQ, S, D])
    norm_g = din("norm_g", [DEPTH, D])
    w_in = din("w_in", [DEPTH, D, DIN])
    attn_q_norm = din("attn_q_norm", [DEPTH, 64])
    attn_k_norm = din("attn_k_norm", [DEPTH, 64])
    q_lora_g = din("mla_q_lora_norm", [DEPTH, 256])
    kv_lora_g = din("mla_kv_lora_norm", [DEPTH, 128])
    w_uq = din("mla_w_uq", [DEPTH, 256, 576])
    w_ukv = din("mla_w_ukv", [DEPTH, 128, 768])
    mla_q_norm = din("mla_q_norm", [DEPTH, 96])
    mla_k_norm = din("mla_k_norm", [DEPTH, 96])
    a_re = din("ssm_a_re", [DEPTH, 16, 64])
    a_im = din("ssm_a_im", [DEPTH, 16, 64])
    b_re = din("ssm_b_re", [DEPTH, 16, 64, 16])
    b_im = din("ssm_b_im", [DEPTH, 16, 64, 16])
    c_re = din("ssm_c_re", [DEPTH, 16, 16, 64])
    c_im = din("ssm_c_im", [DEPTH, 16, 16, 64])
    d_skip = din("ssm_d", [DEPTH, 16, 16])
    log_step = din("ssm_log_step", [DEPTH, 16])
    w_glu = din("ssm_w_glu", [DEPTH, 256, 256])
    w_out = din("w_out", [DEPTH, D, D])
    cbf_d = din("cbf", [128, 1024], BF16)
    cf_d = din("cf", [128, NCF])
    rt_d = din("rt", [6, 128, S])
    out_d = nc.dram_tensor("out", [NSEQ, S, D], F32, kind="ExternalOutput").ap()

    xs_d = dscr("xs", [NSEQ, S, D], F32)
    winp_d = dscr("winp", [128, 8, NCOL], BF16)
    wout_d = dscr("woutp", [128, 14, D], BF16)
    qT_d = dscr("qT", [3, 128, S], BF16)
    kT2_d = dscr("kT2", [128, S], BF16)
    va_d = dscr("va", [NB, 128, 65], BF16)
    iqT_d = dscr("iqT", [4, 128, S], BF16)
    ikT_d = dscr("ikT", [128, S], BF16)
    iw_d = dscr("iw", [NB, 128, 16], F32)
    gaT_d = dscr("gaT", [6, 64, S], BF16)
    gbT_d = dscr("gbT", [2, 128, S], BF16)
    gcT_d = dscr("gcT", [6, 64, S], BF16)
    U_d = dscr("U", [16, 8, 16, J], F32)
    Y_d = dscr("Y", [16, 8, 16, J], F32)
    qc_d = dscr("qc", [6, 96, S], BF16)
    kc_d = dscr("kc", [6, 96, S], BF16)
    vc_d = dscr("vc", [6, NB, 128, 65], BF16)
    mix_d = dscr("mix", [14, 128, S], BF16)
    s5c_d = dscr("s5c", [16, 128, 12, 128], F32)

    S_ = Sched(nc)
    op = S_.op
    dma = S_.dma
    RS = {}

    PSUM_KEYS = {"p1_tp", "p1_mm", "p1_aux", "at_ps", "at_o", "at_sbp", "ds_shp", "ds_scp", "s5_pp", "s5r_p", "s5p_z", "p3_pp"}

    def rs(*key):
        if key not in RS:
            RS[key] = Res(ps=(key[0] in PSUM_KEYS))
        return RS[key]

    with contextlib.ExitStack() as top:
        uid = [0]

        def sbt(st, name, shape, dt):
            uid[0] += 1
            return st.enter_context(nc.sbuf_tensor(f"sb{uid[0]}_{name}", list(shape), dt))

        def pst(st, name, shape, dt):
            uid[0] += 1
            return st.enter_context(nc.psum_tensor(f"ps{uid[0]}_{name}", list(shape), dt))

        cbf = sbt(top, "cbf", [128, 1024], BF16)
        cf = sbt(top, "cf", [128, NCF], F32)
        dma(cbf[:], cbf_d, writes=[rs("cbf")])
        dma(cf[:], cf_d, writes=[rs("cf")])
        rC = [rs("cbf"), rs("cf")]

        def cb(name, rows=128, cols=128):
            o = CB[name]
            return cbf[0:rows, o:o + cols]

        def cfa(name, rows=128, cols=128):
            o = CF[name]
            return cf[0:rows, o:o + cols]
        eps_ap = cf[:, CF["eps"]:CF["eps"] + 1]

        gsm = sbt(top, "gsm", [128, DEPTH, 16], F32)
        rg = rs("gsm")
        op("dve", lambda e: e.memset(gsm[:], 1.0), writes=[rg])
        for l in range(NL):
            dma(gsm[:, l, 0:8], norm_g[l].rearrange("(k p) -> p k", p=128), writes=[rg], allow_slow_non_contiguous=True)
            for hh in range(2):
                dma(gsm[64 * hh:64 * hh + 64, l, 8:9], attn_q_norm[l].rearrange("(p o) -> p o", o=1), writes=[rg], allow_slow_non_contiguous=True)
                dma(gsm[64 * hh:64 * hh + 64, l, 9:10], attn_k_norm[l].rearrange("(p o) -> p o", o=1), writes=[rg], allow_slow_non_contiguous=True)
            dma(gsm[:, l, 10:12], q_lora_g[l].rearrange("(k p) -> p k", p=128), writes=[rg], allow_slow_non_contiguous=True)
            dma(gsm[:, l, 12:13], kv_lora_g[l].rearrange("(p o) -> p o", o=1), writes=[rg], allow_slow_non_contiguous=True)
            dma(gsm[0:96, l, 13:14], mla_q_norm[l].rearrange("(p o) -> p o", o=1), writes=[rg], allow_slow_non_contiguous=True)
            dma(gsm[0:96, l, 14:15], mla_k_norm[l].rearrange("(p o) -> p o", o=1), writes=[rg], allow_slow_non_contiguous=True)

        def prep_weights(l):
            with contextlib.ExitStack() as st:
                stg = [sbt(st, f"stg{i}", [128, DIN], F32) for i in range(2)]
                wrow = [sbt(st, f"wrow{i}", [128, NCOL], BF16) for i in range(2)]
                for kc in range(8):
                    i = kc % 2
                    rst, rw = rs("stg", i), rs("wrow", i)
                    dma(stg[i][:], w_in[l, kc * 128:(kc + 1) * 128, :], writes=[rst])
                    g = gsm[:, l, kc:kc + 1]
                    segs = [(C_QA, S_QA, 384), (C_KA2, S_KA, 64), (C_KA2 + 64, S_KA, 64)] + [(C_IQ2 + (h_ // 2) * 128 + 64 * (h_ % 2), S_IQ + 32 * h_, 32) for h_ in range(8)] + \
                           [(C_IK4 + 32 * r, S_IK, 32) for r in range(4)] + \
                           [(C_GA, S_GA, 384), (C_U, S_U, 256), (C_GB, S_GB, 256), (C_CQ, S_CQ, 256), (C_CKV, S_CKV, 128),
                            (C_KPE + 64, S_KPE, 32), (C_GC, S_GC, 384), (C_VAIW, S_VA, 64), (C_VAIW + 64, S_IW, 8)]
                    op("pool", lambda e: e.memset(wrow[i][:, C_KPE:C_KPE + 64], 0.0), writes=[rw])
                    op("pool", lambda e: e.memset(wrow[i][:, C_IQ2:C_IQ2 + 512], 0.0), writes=[rw])
                    op("pool", lambda e: e.memset(wrow[i][:, C_IQ:C_IQ + 256], 0.0), writes=[rw])
                    for n_, (dc, sc_, w_) in enumerate(segs):
                        if n_ % 2 == 0:
                            op("dve", lambda e: e.tensor_scalar(out=wrow[i][:, dc:dc + w_], in0=stg[i][:, sc_:sc_ + w_], scalar1=g, scalar2=None, op0=ALU.mult),
                               reads=[rst, rg], writes=[rw])
                        else:
                            op("act", lambda e: e.activation(out=wrow[i][:, dc:dc + w_], in_=stg[i][:, sc_:sc_ + w_], func=AF.Copy, scale=g),
                               reads=[rst, rg], writes=[rw])
                    dma(winp_d[:, kc, :], wrow[i][:], reads=[rw], writes=[rs("winp_d")])
            with contextlib.ExitStack() as st:
                stg = [sbt(st, f"stgo{i}", [128, D], F32) for i in range(2)]
                wrow = [sbt(st, f"wrowo{i}", [128, D], BF16) for i in range(2)]
                for c in range(14):
                    i = c % 2
                    rst, rw = rs("stgo", i), rs("wrowo", i)
                    if c < 6:
                        r0, nr = 64 * c, 64
                    elif c < 8:
                        r0, nr = 384 + 128 * (c - 6), 128
                    else:
                        r0, nr = 640 + 64 * (c - 8), 64
                    dma(stg[i][0:nr, :], w_out[l, r0:r0 + nr, :], writes=[rst])
                    if c % 2 == 0:
                        op("dve", lambda e: e.tensor_copy(out=wrow[i][0:nr, :], in_=stg[i][0:nr, :]), reads=[rst], writes=[rw])
                    else:
                        op("act", lambda e: e.copy(out=wrow[i][0:nr, :], in_=stg[i][0:nr, :]), reads=[rst], writes=[rw])
                    dma(wout_d[0:nr, c, :], wrow[i][0:nr, :], reads=[rw], writes=[rs("wout_d")])
            with contextlib.ExitStack() as st:
                s1 = sbt(st, "s_uq", [128, 2, 576], F32)
                s2 = sbt(st, "s_ukv", [128, 768], F32)
                s3 = sbt(st, "s_glu", [128, 2, 256], F32)
                r1, r2, r3 = rs("s_uq"), rs("s_ukv"), rs("s_glu")
                dma(s1[:], w_uq[l].rearrange("(k p) n -> p k n", p=128), writes=[r1])
                dma(s2[:], w_ukv[l], writes=[r2])
                dma(s3[:], w_glu[l].rearrange("(k p) n -> p k n", p=128), writes=[r3])
                rw = rs("wsm")
                for k in range(2):
                    op("dve", lambda e: e.tensor_scalar(out=Wuq[:, k, :], in0=s1[:, k, :], scalar1=gsm[:, l, 10 + k:11 + k], scalar2=None, op0=ALU.mult),
                       reads=[r1, rg], writes=[rw])
                    op("dve", lambda e: e.tensor_copy(out=Wglu[:, k, :], in_=s3[:, k, :]), reads=[r3], writes=[rw])
                op("dve", lambda e: e.memset(WukT[:], 0.0), writes=[rw])
                for h in range(6):
                    op("dve", lambda e: e.tensor_scalar(out=WukT[:, h, 0:64], in0=s2[:, h * 128:h * 128 + 64], scalar1=gsm[:, l, 12:13], scalar2=None, op0=ALU.mult),
                       reads=[r2, rg], writes=[rw])
                    op("dve", lambda e: e.tensor_scalar(out=Wuv[:, h * 64:h * 64 + 64], in0=s2[:, h * 128 + 64:h * 128 + 128], scalar1=gsm[:, l, 12:13], scalar2=None, op0=ALU.mult),
                       reads=[r2, rg], writes=[rw])
            S_.barrier()

        Wuq = sbt(top, "Wuq", [128, 2, 576], BF16)
        WukT = sbt(top, "WukT", [128, 6, 96], BF16)
        Wuv = sbt(top, "Wuv", [128, 384], BF16)
        Wglu = sbt(top, "Wglu", [128, 2, 256], BF16)
        rWsm = rs("wsm")

        def phase1(l, src_ap, src_res):
            with contextlib.ExitStack() as st:
                Winp = sbt(st, "Winp", [128, 8, NCOL], BF16)
                rW = rs("Winp")
                for kc in range(8):
                    dma(Winp[:, kc, :], winp_d[:, kc, :], reads=[rs("winp_d")], writes=[rW])
                xt = sbt(st, "p1_xt", [128, 4, D], F32)
                rxt = rs("p1_xt")
                junk = sbt(st, "p1_junk", [128, D], BF16)
                xn = sbt(st, "p1_xn", [128, D], BF16)
                ssq = sbt(st, "p1_ssq", [128, 8], F32)
                hT = sbt(st, "p1_hT", [128, 8, 512], BF16)
                rhT = rs("p1_hT")
                tabs = sbt(st, "p1_tabs", [128, 6, 512], F32)
                rtab = rs("p1_tabs")
                tp = [pst(st, f"p1_tp{i}", [128, 1024], BF16) for i in range(2)]
                mm = [pst(st, f"p1_mm{i}", [128, 512], F32) for i in range(3)]
                aux = [pst(st, f"p1_aux{i}", [128, 512], F32) for i in range(3)]
                mmi = [0]
                NWK = 3
                xg = [sbt(st, f"p1_xg{i}", [128, 512], BF16) for i in range(NWK)]
                sq = [sbt(st, f"p1_sq{i}", [128, 512], BF16) for i in range(NWK)]
                sd = [sbt(st, f"p1_sd{i}", [128, 512], F32) for i in range(NWK)]
                t1 = [sbt(st, f"p1_t1{i}", [128, 512], F32) for i in range(NWK)]
                t2 = [sbt(st, f"p1_t2{i}", [128, 512], F32) for i in range(NWK)]
                ob = [sbt(st, f"p1_ob{i}", [128, 512], BF16) for i in range(NWK)]
                up = [sbt(st, f"p1_up{i}", [128, 8, 64], F32) for i in range(2)]
                cqn = sbt(st, "p1_cqn", [128, 2, 512], BF16)
                ckvn = sbt(st, "p1_ckvn", [128, 512], BF16)
                vcs = [sbt(st, f"p1_vcs{i}", [128, 6, 65], BF16) for i in range(2)]
                vas = [sbt(st, f"p1_vas{i}", [128, 65], BF16) for i in range(2)]
                iws = [sbt(st, f"p1_iws{i}", [128, 16], F32) for i in range(2)]
                for i in range(2):
                    op("pool", lambda e: e.memset(vcs[i][:], 1.0), writes=[rs("p1_vcs", i)])
                    op("pool", lambda e: e.memset(vas[i][:], 1.0), writes=[rs("p1_vas", i)])
                wk = [0]

                def proj(cols, M, rhs_fn=None):
                    k = mmi[0] % 3
                    mmi[0] += 1
                    r = rs("p1_mm", k)
                    for kc in range(8):
                        op("pe", lambda e: e.matmul(mm[k][0:M, :], lhsT=Winp[:, kc, cols:cols + M], rhs=hT[:, kc, :], start=(kc == 0), stop=(kc == 7)),
                           reads=[rW, rhT], writes=[r])
                    return mm[k], r

                def normrope(X, rX, M, gain, onesname, inv_dim, Rname, ci, dst_ap, dst_res):
                    i = wk[0] % NWK
                    wk[0] += 1
                    a0, a1 = aux[(2 * i) % 3], aux[(2 * i + 1) % 3]
                    ra0, ra1 = rs("p1_aux", (2 * i) % 3), rs("p1_aux", (2 * i + 1) % 3)
                    rxg, rsq, rsd, rt1, rt2, rob = (rs("p1_xg", i), rs("p1_sq", i), rs("p1_sd", i), rs("p1_t1", i), rs("p1_t2", i), rs("p1_ob", i))
                    Ct, St = tabs[0:M, ci, :], tabs[0:M, ci + 1, :]
                    if gain is not None:
                        op("act", lambda e: e.activation(out=xg[i][0:M, :], in_=X[0:M, :], func=AF.Copy, scale=gain), reads=[rX, rg], writes=[rxg])
                    else:
                        op("act", lambda e: e.copy(out=xg[i][0:M, :], in_=X[0:M, :]), reads=[rX], writes=[rxg])
                    op("pe", lambda e: e.matmul(a1[0:M, :], lhsT=cb(Rname, M, M), rhs=xg[i][0:M, :], start=True, stop=True), reads=[rxg] + rC, writes=[ra1])
                    if onesname is not None:
                        op("act", lambda e: e.activation(out=sq[i][0:M, :], in_=X[0:M, :], func=AF.Square), reads=[rX], writes=[rsq])
                        op("pe", lambda e: e.matmul(a0[0:M, :], lhsT=cb(onesname, M, M), rhs=sq[i][0:M, :], start=True, stop=True), reads=[rsq] + rC, writes=[ra0])
                        op("act", lambda e: e.activation(out=sd[i][0:M, :], in_=a0[0:M, :], func=AF.Ln, scale=inv_dim, bias=eps_ap[0:M, :]), reads=[ra0] + rC, writes=[rsd])
                        op("act", lambda e: e.activation(out=sd[i][0:M, :], in_=sd[i][0:M, :], func=AF.Exp, scale=-0.5), reads=[rsd], writes=[rsd])
                    if gain is not None:
                        op("dve", lambda e: e.scalar_tensor_tensor(out=t1[i][0:M, :], in0=X[0:M, :], scalar=gain, in1=Ct, op0=ALU.mult, op1=ALU.mult),
                           reads=[rX, rtab, rg], writes=[rt1])
                    else:
                        op("dve", lambda e: e.tensor_tensor(out=t1[i][0:M, :], in0=X[0:M, :], in1=Ct, op=ALU.mult), reads=[rX, rtab], writes=[rt1])
                    op("dve", lambda e: e.tensor_tensor(out=t2[i][0:M, :], in0=a1[0:M, :], in1=St, op=ALU.mult), reads=[ra1, rtab], writes=[rt2])
                    if onesname is not None:
                        op(P1ENG, lambda e: e.tensor_tensor(out=t1[i][0:M, :], in0=t1[i][0:M, :], in1=t2[i][0:M, :], op=ALU.add), reads=[rt1, rt2], writes=[rt1])
                        op(P1ENG, lambda e: e.tensor_tensor(out=ob[i][0:M, :], in0=t1[i][0:M, :], in1=sd[i][0:M, :], op=ALU.mult), reads=[rt1, rsd], writes=[rob])
                    else:
                        op("dve", lambda e: e.tensor_tensor(out=ob[i][0:M, :], in0=t1[i][0:M, :], in1=t2[i][0:M, :], op=ALU.add), reads=[rt1, rt2], writes=[rob])
                    dma(dst_ap, ob[i][0:M, :], reads=[rob], writes=[dst_res])

                def silu_out(X, rX, M, dsts):
                    i = wk[0] % NWK
                    wk[0] += 1
                    rob = rs("p1_ob", i)
                    rt1 = rs("p1_t1", i)
                    op("act", lambda e: e.activation(out=t1[i][0:M, :], in_=X[0:M, :], func=AF.Sigmoid), reads=[rX], writes=[rt1])
                    if GV != 3:
                        op("dve", lambda e: e.scalar_tensor_tensor(out=ob[i][0:M, :], in0=X[0:M, :], scalar=1.0, in1=t1[i][0:M, :], op0=ALU.mult, op1=ALU.mult), reads=[rX, rt1], writes=[rob])
                    for (p0, p1, dap, dres) in dsts:
                        if GATE_DMA == 0 or (GATE_DMA == 1 and p0 != 0):
                            continue
                        dma(dap, ob[i][p0:p1, :], reads=[rob], writes=[dres])

                for tg in range(NG):
                    t0 = tg * 512
                    tsl = slice(t0, t0 + 512)
                    dma(xt[:], src_ap[t0:t0 + 512, :].rearrange("(j p) d -> p j d", p=128), reads=[src_res], writes=[rxt])
                    for ci in range(6):
                        dma(tabs[:, ci, :], rt_d[ci, :, tsl], writes=[rtab])
                    rssq, rjunk, rxn = rs("p1_ssq"), rs("p1_junk"), rs("p1_xn")
                    for j in range(4):
                        op("act", lambda e: e.activation(out=junk[:], in_=xt[:, j, :], func=AF.Square, accum_out=ssq[:, j:j + 1]), reads=[rxt], writes=[rjunk, rssq])
                    op("act", lambda e: e.activation(out=ssq[:, 4:8], in_=ssq[:, 0:4], func=AF.Sqrt, scale=1.0 / D, bias=eps_ap), reads=[rssq] + rC, writes=[rssq])
                    op("dve", lambda e: e.reciprocal(out=ssq[:, 4:8], in_=ssq[:, 4:8]), reads=[rssq], writes=[rssq])
                    for j in range(4):
                        op("dve", lambda e: e.tensor_scalar(out=xn[:], in0=xt[:, j, :], scalar1=ssq[:, 4 + j:5 + j], scalar2=None, op0=ALU.mult), reads=[rxt, rssq], writes=[rxn])
                        for half in range(2):
                            k = half
                            rtp = rs("p1_tp", k)
                            for q in range(4):
                                kc = half * 4 + q
                                op("pe", lambda e: e.transpose(out=tp[k][:, q * 128:(q + 1) * 128], in_=xn[:, kc * 128:(kc + 1) * 128], identity=cb("ident")),
                                   reads=[rxn] + rC, writes=[rtp])
                            eng = "act" if half == 0 else "dve"
                            if eng == "act":
                                op("act", lambda e: e.copy(out=hT[:, half * 4:half * 4 + 4, j * 128:(j + 1) * 128], in_=tp[k][:, 0:512].rearrange("p (q t) -> p q t", q=4)), reads=[rtp], writes=[rhT])
                            else:
                                op("dve", lambda e: e.tensor_copy(out=hT[:, half * 4:half * 4 + 4, j * 128:(j + 1) * 128], in_=tp[k][:, 0:512].rearrange("p (q t) -> p q t", q=4)), reads=[rtp], writes=[rhT])
                    if P1CUT <= 1:
                        continue
                    for c in range(3):
                        X, rX = proj(C_QA + 128 * c, 128)
                        normrope(X, rX, 128, gsm[:, l, 8:9], "ones64", 1.0 / 64, "R64", 0, qT_d[c, :, tsl], rs("qT_d"))
                    X, rX = proj(C_KA2, 128)
                    normrope(X, rX, 128, gsm[:, l, 9:10], "ones64", 1.0 / 64, "R64", 0, kT2_d[:, tsl], rs("kT2_d"))
                    if P1CUT <= 2:
                        continue
                    for c in range(4):
                        X, rX = proj(C_IQ2 + 128 * c, 128)
                        normrope(X, rX, 128, None, None, None, "R32", 2, iqT_d[c, :, tsl], rs("iqT_d"))
                    X, rX = proj(C_IK4, 128)
                    normrope(X, rX, 128, None, None, None, "R32", 2, ikT_d[:, tsl], rs("ikT_d"))
                    if P1CUT <= 3:
                        continue
                    for c in range(3):
                        X, rX = proj(C_GA + 128 * c, 128)
                        silu_out(X, rX, 128, [(0, 64, gaT_d[2 * c, :, tsl], rs("gaT_d")), (64, 128, gaT_d[2 * c + 1, :, tsl], rs("gaT_d"))])
                    for c in range(2):
                        X, rX = proj(C_GB + 128 * c, 128)
                        silu_out(X, rX, 128, [(0, 128, gbT_d[c, :, tsl], rs("gbT_d"))])
                    for c in range(3):
                        X, rX = proj(C_GC + 128 * c, 128)
                        silu_out(X, rX, 128, [(0, 64, gcT_d[2 * c, :, tsl], rs("gcT_d")), (64, 128, gcT_d[2 * c + 1, :, tsl], rs("gcT_d"))])
                    if P1CUT <= 4:
                        continue
                    for c in range(2):
                        X, rX = proj(C_U + 128 * c, 128)
                        i = c
                        rup = rs("p1_up", i)
                        op("dve", lambda e: e.tensor_copy(out=up[i][:].rearrange("p s j -> p j s"), in_=X[:].rearrange("p (j s) -> p j s", s=8)), reads=[rX], writes=[rup])
                        for g8 in range(8):
                            g = 8 * c + g8
                            dma(U_d[g, :, :, tg * 64:(tg + 1) * 64].rearrange("s c j -> c s j"), up[i][16 * g8:16 * g8 + 16, :, :], reads=[rup], writes=[rs("U_d")])
                    if P1CUT <= 5:
                        continue
                    X0, rX0 = proj(C_CQ, 128)
                    X1, rX1 = proj(C_CQ + 128, 128)
                    i = wk[0] % NWK
                    wk[0] += 1
                    a0, ra0 = aux[(2 * i) % 3], rs("p1_aux", (2 * i) % 3)
                    rsq, rsd, rt1 = rs("p1_sq", i), rs("p1_sd", i), rs("p1_t1", i)
                    i2 = wk[0] % NWK
                    wk[0] += 1
                    rsq2 = rs("p1_sq", i2)
                    op("act", lambda e: e.activation(out=sq[i][:], in_=X0[:], func=AF.Square), reads=[rX0], writes=[rsq])
                    op("act", lambda e: e.activation(out=sq[i2][:], in_=X1[:], func=AF.Square), reads=[rX1], writes=[rsq2])
                    op("pe", lambda e: e.matmul(a0[:], lhsT=cb("ones128"), rhs=sq[i][:], start=True, stop=False), reads=[rsq] + rC, writes=[ra0])
                    op("pe", lambda e: e.matmul(a0[:], lhsT=cb("ones128"), rhs=sq[i2][:], start=False, stop=True), reads=[rsq2] + rC, writes=[ra0])
                    op("act", lambda e: e.activation(out=sd[i][:], in_=a0[:], func=AF.Ln, scale=1.0 / 256, bias=eps_ap), reads=[ra0] + rC, writes=[rsd])
                    op("act", lambda e: e.activation(out=sd[i][:], in_=sd[i][:], func=AF.Exp, scale=-0.5), reads=[rsd], writes=[rsd])
                    rcq = rs("p1_cqn")
                    op("dve", lambda e: e.tensor_tensor(out=cqn[:, 0, :], in0=X0[:], in1=sd[i][:], op=ALU.mult), reads=[rX0, rsd], writes=[rcq])
                    op("dve", lambda e: e.tensor_tensor(out=cqn[:, 1, :], in0=X1[:], in1=sd[i][:], op=ALU.mult), reads=[rX1, rsd], writes=[rcq])
                    X0, rX0 = proj(C_CKV, 128)
                    i = wk[0] % NWK
                    wk[0] += 1
                    a0, ra0 = aux[(2 * i) % 3], rs("p1_aux", (2 * i) % 3)
                    rsq, rsd = rs("p1_sq", i), rs("p1_sd", i)
                    op("act", lambda e: e.activation(out=sq[i][:], in_=X0[:], func=AF.Square), reads=[rX0], writes=[rsq])
                    op("pe", lambda e: e.matmul(a0[:], lhsT=cb("ones128"), rhs=sq[i][:], start=True, stop=True), reads=[rsq] + rC, writes=[ra0])
                    op("act", lambda e: e.activation(out=sd[i][:], in_=a0[:], func=AF.Ln, scale=1.0 / 128, bias=eps_ap), reads=[ra0] + rC, writes=[rsd])
                    op("act", lambda e: e.activation(out=sd[i][:], in_=sd[i][:], func=AF.Exp, scale=-0.5), reads=[rsd], writes=[rsd])
                    rckv = rs("p1_ckvn")
                    op("dve", lambda e: e.tensor_tensor(out=ckvn[:], in0=X0[:], in1=sd[i][:], op=ALU.mult), reads=[rX0, rsd], writes=[rckv])
                    if P1CUT <= 6:
                        continue
                    for h in range(6):
                        k = mmi[0] % 3
                        mmi[0] += 1
                        r = rs("p1_mm", k)
                        for c in range(2):
                            op("pe", lambda e: e.matmul(mm[k][0:96, :], lhsT=Wuq[:, c, h * 96:(h + 1) * 96], rhs=cqn[:, c, :], start=(c == 0), stop=(c == 1)),
                               reads=[rWsm, rcq], writes=[r])
                        normrope(mm[k], r, 96, gsm[0:96, l, 13:14], "ones96", 1.0 / 96, "R96", 4, qc_d[h, :, tsl], rs("qc_d"))
                    for h in range(6):
                        k = mmi[0] % 3
                        mmi[0] += 1
                        r = rs("p1_mm", k)
                        op("pe", lambda e: e.matmul(mm[k][0:96, :], lhsT=WukT[:, h, :], rhs=ckvn[:], start=True, stop=False), reads=[rWsm, rckv], writes=[r])
                        for kc in range(8):
                            op("pe", lambda e: e.matmul(mm[k][0:96, :], lhsT=Winp[:, kc, C_KPE:C_KPE + 96], rhs=hT[:, kc, :], start=False, stop=(kc == 7)),
                               reads=[rW, rhT], writes=[r])
                        normrope(mm[k], r, 96, gsm[0:96, l, 14:15], "ones96", 1.0 / 96, "R96", 4, kc_d[h, :, tsl], rs("kc_d"))
                    if P1CUT <= 7:
                        continue
                    for j in range(4):
                        tb = tg * 4 + j
                        k = mmi[0] % 3
                        mmi[0] += 1
                        r = rs("p1_mm", k)
                        op("pe", lambda e: e.matmul(mm[k][:, 0:384], lhsT=ckvn[:, j * 128:(j + 1) * 128], rhs=Wuv[:], start=True, stop=True), reads=[rWsm, rckv], writes=[r])
                        i = j % 2
                        rv = rs("p1_vcs", i)
                        op("act", lambda e: e.copy(out=vcs[i][:, :, 0:64], in_=mm[k][:, 0:384].rearrange("p (h d) -> p h d", h=6)), reads=[r], writes=[rv])
                        for h in range(6):
                            dma(vc_d[h, tb, :, :], vcs[i][:, h, :], reads=[rv], writes=[rs("vc_d")])
                        k = mmi[0] % 3
                        mmi[0] += 1
                        r = rs("p1_mm", k)
                        for kc in range(8):
                            op("pe", lambda e: e.matmul(mm[k][:, 0:72], lhsT=hT[:, kc, j * 128:(j + 1) * 128], rhs=Winp[:, kc, C_VAIW:C_VAIW + 72], start=(kc == 0), stop=(kc == 7)),
                               reads=[rW, rhT], writes=[r])
                        rva, riw = rs("p1_vas", i), rs("p1_iws", i)
                        op("act", lambda e: e.copy(out=vas[i][:, 0:64], in_=mm[k][:, 0:64]), reads=[r], writes=[rva])
                        dma(va_d[tb, :, :], vas[i][:], reads=[rva], writes=[rs("va_d")])
                        sc_ = (8.0 ** -0.5) * (32.0 ** -0.5)
                        op("act", lambda e: e.activation(out=iws[i][:, 0:8], in_=mm[k][:, 64:72], func=AF.Abs, scale=sc_), reads=[r], writes=[riw])
                        op("act", lambda e: e.activation(out=iws[i][:, 8:16], in_=mm[k][:, 64:72], func=AF.Sign), reads=[r], writes=[riw])
                        dma(iw_d[tb, :, :], iws[i][:], reads=[riw], writes=[rs("iw_d")])
            S_.barrier()

        def att_tiles(st, n_ops=2):
            T = {}
            T["att"] = [pst(st, f"at_ps{i}", [128, 512], F32) for i in range(2)]
            T["Ops"] = [pst(st, f"at_o{i}", [128, 512], F32) for i in range(n_ops)]
            T["sbp"] = pst(st, "at_sb", [128, 512], F32)
            T["PT"] = [sbt(st, f"at_pt{i}", [128, 512], BF16) for i in range(3)]
            T["Osb"] = [sbt(st, f"at_osb{i}", [65, 6, 512], F32) for i in range(2)]
            T["rcp"] = [sbt(st, f"at_rcp{i}", [64, 512], F32) for i in range(2)]
            T["yb"] = [sbt(st, f"at_yb{i}", [64, 512], BF16) for i in range(2)]
            T["gt"] = [sbt(st, f"at_gt{i}", [64, 6, 512], BF16) for i in range(2)]
            T["cnt"] = [0, 0, 0]
            return T

        def att_main(T, sb, n_heads, scale, heads, qfn, qres, nm_fn):
            att, Ops, PT, Osb = T["att"], T["Ops"], T["PT"], T["Osb"]
            cnt = T["cnt"]
            ob = sb % 2
            nkb = 4 * sb + 4
            rosb = rs("at_osb", ob)
            n_ops = len(Ops)
            steps = [(h, kb) for h in range(n_heads) for kb in range(nkb)]
            pend = None
            ois = {}
            for stp in steps + [None]:
                cur = None
                if stp is not None:
                    h, kb = stp
                    H = heads[h]
                    if kb == 0:
                        ois[h] = cnt[1] % n_ops
                        cnt[1] += 1
                    qb0 = max(kb - 4 * sb, 0)
                    qs = slice(qb0 * 128, 512)
                    ai = cnt[0] % 2
                    pi = cnt[0] % 3
                    cnt[0] += 1
                    ra, rp = rs("at_ps", ai), rs("at_pt", pi)
                    masks = []
                    for qb in range(qb0, 4):
                        m = nm_fn(4 * sb + qb, kb)
                        if m is not None:
                            masks.append((qb, m))
                    op("pe", lambda e: e.matmul(att[ai][:, qs], lhsT=H["kT"](slice(kb * 128, kb * 128 + 128)), rhs=qfn(h, qs), start=True, stop=(len(masks) == 0)),
                       reads=[H["kres"], qres], writes=[ra])
                    for mi, (qb, (map_, mres)) in enumerate(masks):
                        op("pe", lambda e: e.matmul(att[ai][:, qb * 128:(qb + 1) * 128], lhsT=map_, rhs=cb("ident"), start=False, stop=(mi == len(masks) - 1)),
                           reads=[mres] + rC, writes=[ra])
                    op("act", lambda e: e.activation(out=PT[pi][:, qs], in_=att[ai][:, qs], func=AF.Exp, scale=scale), reads=[ra], writes=[rp])
                    cur = (h, kb, pi, qs)
                if pend is not None:
                    h2, kb2, pi2, qs2 = pend
                    H2 = heads[h2]
                    oi = ois[h2]
                    rO = rs("at_o", oi)
                    op("pe", lambda e: e.matmul(Ops[oi][0:65, qs2], lhsT=H2["vaug"][:, kb2, :], rhs=PT[pi2][:, qs2], start=(kb2 == 0), stop=(kb2 == nkb - 1)),
                       reads=[rs("at_pt", pi2), H2["vres"]], writes=[rO])
                    if kb2 == nkb - 1:
                        op("act", lambda e: e.copy(out=Osb[ob][:, h2, :], in_=Ops[oi][0:65, :]), reads=[rO], writes=[rosb])
                pend = cur

        def att_norm(T, sb, n_heads, gate_d, gate_res, mix_base):
            Osb, sbp, rcp, yb, gt = T["Osb"], T["sbp"], T["rcp"], T["yb"], T["gt"]
            cnt = T["cnt"]
            ob = sb % 2
            rosb, rgt, rsb = rs("at_osb", ob), rs("at_gt", ob), rs("at_sbp")
            ssl = slice(sb * 512, (sb + 1) * 512)
            dma(gt[ob][:], gate_d[:, :, ssl].rearrange("h p t -> p h t"), reads=[rs(gate_res)], writes=[rgt])
            for h in range(n_heads):
                ri = cnt[2] % 2
                cnt[2] += 1
                rrc, ryb = rs("at_rcp", ri), rs("at_yb", ri)
                op("pe", lambda e: e.matmul(sbp[0:64, :], lhsT=cfa("e65", 65, 64), rhs=Osb[ob][:, h, :], start=True, stop=True), reads=[rosb] + rC, writes=[rsb])
                op("act", lambda e: e.activation(out=rcp[ri][:], in_=sbp[0:64, :], func=AF.Ln), reads=[rsb], writes=[rrc])
                op("act", lambda e: e.activation(out=rcp[ri][:], in_=rcp[ri][:], func=AF.Exp, scale=-1.0), reads=[rrc], writes=[rrc])
                op("dve", lambda e: e.tensor_tensor(out=rcp[ri][:], in0=rcp[ri][:], in1=Osb[ob][0:64, h, :], op=ALU.mult), reads=[rrc, rosb], writes=[rrc])
                op(P1ENG, lambda e: e.tensor_tensor(out=yb[ri][:], in0=rcp[ri][:], in1=gt[ob][:, h, :], op=ALU.mult), reads=[rrc, rgt], writes=[ryb])
                dma(mix_d[mix_base + h, 0:64, ssl], yb[ri][:], reads=[ryb], writes=[rs("mix_d")])

        def dsa_phase():
            with contextlib.ExitStack() as st:
                kT2 = sbt(st, "ds_kT2", [128, S], BF16)
                vaug = sbt(st, "ds_vaug", [128, NB, 65], BF16)
                ikT = sbt(st, "ds_ikT", [128, S], BF16)
                rk, rv, rik = rs("ds_kT2"), rs("ds_vaug"), rs("ds_ikT")
                dma(kT2[:], kT2_d, reads=[rs("kT2_d")], writes=[rk])
                dma(vaug[:], va_d.rearrange("t p c -> p t c"), reads=[rs("va_d")], writes=[rv])
                dma(ikT[:], ikT_d, reads=[rs("ikT_d")], writes=[rik])
                NM = [sbt(st, f"ds_NM{i}", [128, 4, S], BF16) for i in range(2)]
                sc = [sbt(st, f"ds_sc{i}", [128, S], F32) for i in range(2)]
                junk = sbt(st, "ds_junk", [128, S], BF16)
                iq = [sbt(st, f"ds_iq{i}", [128, 4, 128], BF16) for i in range(2)]
                iw = [sbt(st, f"ds_iw{i}", [128, 16], F32) for i in range(2)]
                Dh = [sbt(st, f"ds_Dh{i}", [128, 8, 128], BF16) for i in range(2)]
                Th = [sbt(st, f"ds_Th{i}", [128, 512], BF16) for i in range(3)]
                bs = [sbt(st, f"ds_bs{i}", [128, 8 + NIT + 1], F32) for i in range(2)]
                shp = [pst(st, f"ds_shp{i}", [128, 512], F32) for i in range(2)]
                scp = [pst(st, f"ds_scp{i}", [128, 512], F32) for i in range(2)]
                T = att_tiles(st, n_ops=1)
                qs_t = [sbt(st, f"ds_q{i}", [128, 3, 512], BF16) for i in range(2)]
                c3 = [0, 0]

                def masks(sb):
                    for qb in range(4):
                        b = 4 * sb + qb
                        n = 128 * (b + 1)
                        if n <= KTOP:
                            continue
                        i = b % 2
                        rsc, riq, riw, rDh, rbs = rs("ds_sc", i), rs("ds_iq", i), rs("ds_iw", i), rs("ds_Dh", i), rs("ds_bs", i)
                        rNM = rs("ds_NM", sb % 2)
                        dma(iq[i][:], iqT_d[:, :, b * 128:(b + 1) * 128].rearrange("c p t -> p c t"), reads=[rs("iqT_d")], writes=[riq])
                        dma(iw[i][:], iw_d[b, :, :], reads=[rs("iw_d")], writes=[riw])
                        for h in range(8):
                            op("dve", lambda e: e.tensor_scalar(out=Dh[i][:, h, :], in0=cb("ident"), scalar1=iw[i][:, 8 + h:9 + h], scalar2=None, op0=ALU.mult),
                               reads=[riw] + rC, writes=[rDh])
                        chunks = [(c * 512, min(512, n - c * 512)) for c in range((n + 511) // 512)]
                        isteps = [(ci, h) for ci in range(len(chunks)) for h in range(8)]
                        ipend = None
                        for ist in isteps + [None]:
                            icur = None
                            if ist is not None:
                                ci, h = ist
                                k0, wc = chunks[ci]
                                hi_ = c3[0] % 2
                                ti_ = c3[0] % 3
                                c3[0] += 1
                                rsh, rth = rs("ds_shp", hi_), rs("ds_Th", ti_)
                                pb = 64 * (h % 2)
                                op("pe", lambda e: e.matmul(shp[hi_][:, 0:wc], lhsT=iq[i][pb:pb + 32, h // 2, :], rhs=ikT[pb:pb + 32, k0:k0 + wc], start=True, stop=True),
                                   reads=[riq, rik], writes=[rsh])
                                op("act", lambda e: e.activation(out=Th[ti_][:, 0:wc], in_=shp[hi_][:, 0:wc], func=AF.Relu, scale=iw[i][:, h:h + 1]), reads=[rsh, riw], writes=[rth])
                                if h == 0:
                                    c3[1] += 1
                                icur = (ci, h, ti_, c3[1] % 2)
                            if ipend is not None:
                                ci2, h2, ti2, si = ipend
                                k0, wc = chunks[ci2]
                                rscp = rs("ds_scp", si)
                                op("pe", lambda e: e.matmul(scp[si][:, 0:wc], lhsT=Dh[i][:, h2, :], rhs=Th[ti2][:, 0:wc], start=(h2 == 0), stop=(h2 == 7)), reads=[rs("ds_Th", ti2), rDh], writes=[rscp])
                                if h2 == 7:
                                    op("act", lambda e: e.copy(out=sc[i][:, k0:k0 + wc], in_=scp[si][:, 0:wc]), reads=[rscp], writes=[rsc])
                            ipend = icur
                        lo, hi, mid, cn, tt, rng = (bs[i][:, k:k + 1] for k in range(6))
                        Hc = lambda it: bs[i][:, 8 + it:9 + it]
                        op("dve", lambda e: e.tensor_reduce(out=hi, in_=sc[i][:, 0:n], axis=AX.X, op=ALU.max), reads=[rsc], writes=[rbs])
                        op("dve", lambda e: e.tensor_reduce(out=lo, in_=sc[i][:, 0:n], axis=AX.X, op=ALU.min), reads=[rsc], writes=[rbs])
                        op("dve", lambda e: e.scalar_tensor_tensor(out=rng, in0=hi, scalar=1.0, in1=lo, op0=ALU.add, op1=ALU.subtract), reads=[rbs], writes=[rbs])
                        op("dve", lambda e: e.tensor_scalar(out=bs[i][:, 8:8 + NIT + 1], in0=cf[:, CF["pw"]:CF["pw"] + NIT + 1], scalar1=rng, scalar2=None, op0=ALU.mult), reads=[rbs] + rC, writes=[rbs])
                        op("dve", lambda e: e.tensor_tensor(out=mid, in0=lo, in1=Hc(0), op=ALU.add), reads=[rbs], writes=[rbs])
                        op("dve", lambda e: e.memset(sc[i][0:64, n - 64:n], -1e30), writes=[rsc])
                        for it in range(NIT):
                            op("dve", lambda e: e.tensor_scalar(out=junk[:, 0:n], in0=sc[i][:, 0:n], scalar1=mid, scalar2=None, op0=ALU.is_ge, op1=ALU.add, accum_out=cn),
                               reads=[rsc, rbs], writes=[rbs, rs("ds_junk")])
                            op("dve", lambda e: e.tensor_scalar(out=tt, in0=cn, scalar1=float(KTOP) - 0.5, scalar2=-0.5, op0=ALU.is_ge, op1=ALU.add), reads=[rbs], writes=[rbs])
                            op("dve", lambda e: e.scalar_tensor_tensor(out=mid, in0=tt, scalar=Hc(it), in1=mid, op0=ALU.mult, op1=ALU.add), reads=[rbs], writes=[rbs])
                        op("dve", lambda e: e.tensor_tensor(out=lo, in0=mid, in1=Hc(NIT), op=ALU.subtract), reads=[rbs], writes=[rbs])
                        op("dve", lambda e: e.tensor_scalar(out=NM[sb % 2][:, qb, 0:n], in0=sc[i][:, 0:n], scalar1=lo, scalar2=NEGM, op0=ALU.is_lt, op1=ALU.mult), reads=[rsc, rbs], writes=[rNM])

                def nm_fn(b, kb):
                    n = 128 * (b + 1)
                    if n <= KTOP:
                        if kb == b:
                            return (cb("nmd"), rs("cbf"))
                        return None
                    return (NM[(b // 4) % 2][:, b % 4, kb * 128:(kb + 1) * 128], rs("ds_NM", (b // 4) % 2))

                heads = [dict(kT=(lambda ks, pb=64 * (h % 2): kT2[pb:pb + 64, ks]), kres=rk, vaug=vaug, vres=rv) for h in range(6)]
                if DSA_MODE != 2:
                    masks(0)
                for sb in range(NG):
                    if sb + 1 < NG and DSA_MODE != 2:
                        masks(sb + 1)
                    if DSA_MODE == 1:
                        continue
                    i = sb % 2
                    rq = rs("ds_q", i)
                    dma(qs_t[i][:], qT_d[:, :, sb * 512:(sb + 1) * 512].rearrange("c p t -> p c t"), reads=[rs("qT_d")], writes=[rq])
                    qfn = lambda h, qs, i=i: qs_t[i][64 * (h % 2):64 * (h % 2) + 64, h // 2, qs]
                    att_main(T, sb, 6, 64.0 ** -0.5, heads, qfn, rq, nm_fn)
                    att_norm(T, sb, 6, gaT_d, "gaT_d", 0)
            S_.barrier()

        def mla_phase():
            with contextlib.ExitStack() as st:
                kT = sbt(st, "ml_kT", [96, 6, S], BF16)
                va = sbt(st, "ml_va", [128, 6, NB, 65], BF16)
                rk, rv = rs("ml_kT"), rs("ml_va")
                for h in range(6):
                    dma(kT[:, h, :], kc_d[h], reads=[rs("kc_d")], writes=[rk])
                    dma(va[:, h, :, :], vc_d[h].rearrange("t p c -> p t c"), reads=[rs("vc_d")], writes=[rv])
                T = att_tiles(st)
                qs_t = [sbt(st, f"ml_q{i}", [96, 6, 512], BF16) for i in range(2)]

                def nm_fn(b, kb):
                    if kb == b:
                        return (cb("nmd"), rs("cbf"))
                    return None
                heads = [dict(kT=(lambda ks, h=h: kT[:, h, ks]), kres=rk, vaug=va[:, h, :, :], vres=rv) for h in range(6)]
                for sb in range(NG):
                    i = sb % 2
                    rq = rs("ml_q", i)
                    dma(qs_t[i][:], qc_d[:, :, sb * 512:(sb + 1) * 512].rearrange("h p t -> p h t"), reads=[rs("qc_d")], writes=[rq])
                    qfn = lambda h, qs, i=i: qs_t[i][:, h, qs]
                    att_main(T, sb, 6, 96.0 ** -0.5, heads, qfn, rq, nm_fn)
                    att_norm(T, sb, 6, gcT_d, "gcT_d", 8)
            S_.barrier()

        def s5_consts(l):
            with contextlib.ExitStack() as st:
                ar = sbt(st, "s5_ar", [128, 16], F32)
                ai = sbt(st, "s5_ai", [128, 16], F32)
                stp = sbt(st, "s5_stp", [128, 16], F32)
                rp = rs("s5_par")
                for hh in range(2):
                    dma(ar[64 * hh:64 * hh + 64, :], a_re[l].rearrange("g p -> p g"), writes=[rp], allow_slow_non_contiguous=True)
                    dma(ai[64 * hh:64 * hh + 64, :], a_im[l].rearrange("g p -> p g"), writes=[rp], allow_slow_non_contiguous=True)
                dma(stp[:], log_step[l:l + 1, :].broadcast_to([128, 16]), writes=[rp])
                W = sbt(st, "s5_w", [128, 24, 16], F32)
                rw = rs("s5_w")

                def wv(k):
                    return W[:, k, :]

                def dv(fn, reads=(), writes=()):
                    op("dve", fn, reads=list(reads) + [rp, rw] + rC, writes=list(writes) + [rw])
                TWO_PI = 2.0 * np.pi

                def sincos(dst, ang, shift):
                    t, n_, m = wv(20), wv(21), wv(22)
                    ti = Wi[:, 0, :]
                    dv(lambda e: e.tensor_scalar(out=t, in0=ang, scalar1=shift, scalar2=1.0 / TWO_PI, op0=ALU.add, op1=ALU.mult))
                    dv(lambda e: e.tensor_copy(out=ti, in_=t))
                    dv(lambda e: e.tensor_copy(out=n_, in_=ti))
                    dv(lambda e: e.tensor_tensor(out=t, in0=t, in1=n_, op=ALU.subtract))
                    dv(lambda e: e.tensor_scalar(out=m, in0=t, scalar1=0.5, scalar2=None, op0=ALU.is_gt))
                    dv(lambda e: e.tensor_tensor(out=t, in0=t, in1=m, op=ALU.subtract))
                    dv(lambda e: e.tensor_scalar(out=m, in0=t, scalar1=-0.5, scalar2=None, op0=ALU.is_lt))
                    dv(lambda e: e.tensor_tensor(out=t, in0=t, in1=m, op=ALU.add))
                    op("act", lambda e: e.activation(out=dst, in_=t, func=AF.Sin, scale=TWO_PI), reads=[rw], writes=[rw])
                Wi = sbt(st, "s5_wi", [128, 1, 16], mybir.dt.int32)
                op("act", lambda e: e.activation(out=stp[:], in_=stp[:], func=AF.Exp), reads=[rp], writes=[rp])
                mag, ang, abr, abi, cs, sn = wv(0), wv(1), wv(2), wv(3), wv(4), wv(5)
                dv(lambda e: e.tensor_tensor(out=mag, in0=ar[:], in1=stp[:], op=ALU.mult))
                op("act", lambda e: e.activation(out=mag, in_=mag, func=AF.Exp), reads=[rw], writes=[rw])
                dv(lambda e: e.tensor_tensor(out=ang, in0=ai[:], in1=stp[:], op=ALU.mult))
                sincos(sn, ang, 0.0)
                sincos(cs, ang, np.pi / 2)
                dv(lambda e: e.tensor_tensor(out=abr, in0=mag, in1=cs, op=ALU.mult))
                dv(lambda e: e.tensor_tensor(out=abi, in0=mag, in1=sn, op=ALU.mult))
                den, nr, fr, fi, tA, tB = wv(6), wv(7), wv(8), wv(9), wv(10), wv(11)
                dv(lambda e: e.tensor_tensor(out=den, in0=ar[:], in1=ar[:], op=ALU.mult))
                dv(lambda e: e.tensor_tensor(out=tA, in0=ai[:], in1=ai[:], op=ALU.mult))
                dv(lambda e: e.tensor_tensor(out=den, in0=den, in1=tA, op=ALU.add))
                dv(lambda e: e.reciprocal(out=den, in_=den))
                dv(lambda e: e.tensor_scalar(out=nr, in0=abr, scalar1=-1.0, scalar2=None, op0=ALU.add))
                dv(lambda e: e.tensor_tensor(out=fr, in0=nr, in1=ar[:], op=ALU.mult))
                dv(lambda e: e.tensor_tensor(out=tA, in0=abi, in1=ai[:], op=ALU.mult))
                dv(lambda e: e.tensor_tensor(out=fr, in0=fr, in1=tA, op=ALU.add))
                dv(lambda e: e.tensor_tensor(out=fr, in0=fr, in1=den, op=ALU.mult))
                dv(lambda e: e.tensor_tensor(out=fi, in0=abi, in1=ar[:], op=ALU.mult))
                dv(lambda e: e.tensor_tensor(out=tA, in0=nr, in1=ai[:], op=ALU.mult))
                dv(lambda e: e.tensor_tensor(out=fi, in0=fi, in1=tA, op=ALU.subtract))
                dv(lambda e: e.tensor_tensor(out=fi, in0=fi, in1=den, op=ALU.mult))
                PW = sbt(st, "s5_pw", [128, 9, 2, 16], F32)
                MKp = sbt(st, "s5_mkp", [128, 9, 2, 16], F32)
                FN = sbt(st, "s5_fn", [128, 8, 2, 16], F32)
                CQc = sbt(st, "s5_cqc", [128, 9, 2, 16], F32)

                def cmul(o_re, o_im, x_re, x_im, y_re, y_im):
                    dv(lambda e: e.tensor_tensor(out=tA, in0=x_re, in1=y_re, op=ALU.mult))
                    dv(lambda e: e.tensor_tensor(out=tB, in0=x_im, in1=y_im, op=ALU.mult))
                    dv(lambda e: e.tensor_tensor(out=wv(12), in0=x_re, in1=y_im, op=ALU.mult))
                    dv(lambda e: e.tensor_tensor(out=wv(13), in0=x_im, in1=y_re, op=ALU.mult))
                    dv(lambda e: e.tensor_tensor(out=o_re, in0=tA, in1=tB, op=ALU.subtract))
                    dv(lambda e: e.tensor_tensor(out=o_im, in0=wv(12), in1=wv(13), op=ALU.add))
                dv(lambda e: e.memset(PW[:, 0, 0, :], 1.0))
                dv(lambda e: e.memset(PW[:, 0, 1, :], 0.0))
                for n_ in range(1, 9):
                    cmul(PW[:, n_, 0, :], PW[:, n_, 1, :], PW[:, n_ - 1, 0, :], PW[:, n_ - 1, 1, :], abr, abi)
                dv(lambda e: e.tensor_copy(out=MKp[:, 0, :, :], in_=PW[:, 8, :, :]))
                for k in range(1, 9):
                    cmul(MKp[:, k, 0, :], MKp[:, k, 1, :], MKp[:, k - 1, 0, :], MKp[:, k - 1, 1, :], MKp[:, k - 1, 0, :], MKp[:, k - 1, 1, :])
                sgB = cf[:, CF["sgB"]:CF["sgB"] + 1]
                sgQ = cf[:, CF["sgQ"]:CF["sgQ"] + 1]
                for n_ in range(8):
                    cmul(FN[:, n_, 0, :], FN[:, n_, 1, :], PW[:, n_, 0, :], PW[:, n_, 1, :], fr, fi)
                    dv(lambda e: e.tensor_scalar(out=FN[:, n_, 1, :], in0=FN[:, n_, 1, :], scalar1=sgB, scalar2=None, op0=ALU.mult))
                for n_ in range(9):
                    dv(lambda e: e.tensor_scalar(out=CQc[:, n_, 0, :], in0=PW[:, n_, 0, :], scalar1=sgQ, scalar2=None, op0=ALU.mult))
                    dv(lambda e: e.tensor_scalar(out=CQc[:, n_, 1, :], in0=PW[:, n_, 1, :], scalar1=-1.0, scalar2=None, op0=ALU.mult))
                for k in range(9):
                    dv(lambda e: e.tensor_scalar(out=MKp[:, k, 1, :], in0=MKp[:, k, 1, :], scalar1=sgQ, scalar2=None, op0=ALU.mult))
                bx = [sbt(st, f"s5_bx{i}", [128, 16], F32) for i in range(2)]
                by = [sbt(st, f"s5_by{i}", [128, 16], F32) for i in range(2)]
                cx = [sbt(st, f"s5_cx{i}", [128, 16], F32) for i in range(2)]
                cy = [sbt(st, f"s5_cy{i}", [128, 16], F32) for i in range(2)]
                dr = [sbt(st, f"s5_dr{i}", [128, 1], F32) for i in range(2)]
                Pm = [sbt(st, f"s5_Pm{i}", [128, 128], F32) for i in range(2)]
                CQ = [sbt(st, f"s5_CQ{i}", [128, 9, 16], F32) for i in range(2)]
                Btr = [sbt(st, f"s5_Btr{i}", [128, 8, 16], F32) for i in range(2)]
                KTp = [sbt(st, f"s5_KTp{i}", [128, 15 * 16], F32) for i in range(2)]
                OUTm = [sbt(st, f"s5_OUT{i}", [128, 12, 128], F32) for i in range(2)]
                pp = [pst(st, f"s5_pp{i}", [128, 128], F32) for i in range(2)]
                for i in range(2):
                    op("pool", lambda e: e.memset(KTp[i][:], 0.0), writes=[rs("s5_KTp", i)])
                for g in range(16):
                    i = g % 2
                    rin, rPm, rCQ, rBt, rKT, rOUT, rpp = rs("s5_in", i), rs("s5_Pm", i), rs("s5_CQ", i), rs("s5_Btr", i), rs("s5_KTp", i), rs("s5_OUT", i), rs("s5_pp", i)
                    dma(bx[i][0:64, :], b_re[l, g], writes=[rin])
                    dma(bx[i][64:128, :], b_im[l, g], writes=[rin])
                    dma(by[i][0:64, :], b_im[l, g], writes=[rin])
                    dma(by[i][64:128, :], b_re[l, g], writes=[rin])
                    dma(cx[i][0:64, :], c_re[l, g].rearrange("c p -> p c"), writes=[rin], allow_slow_non_contiguous=True)
                    dma(cx[i][64:128, :], c_im[l, g].rearrange("c p -> p c"), writes=[rin], allow_slow_non_contiguous=True)
                    dma(cy[i][0:64, :], c_im[l, g].rearrange("c p -> p c"), writes=[rin], allow_slow_non_contiguous=True)
                    dma(cy[i][64:128, :], c_re[l, g].rearrange("c p -> p c"), writes=[rin], allow_slow_non_contiguous=True)
                    for s_ in range(8):
                        dma(dr[i][16 * s_:16 * s_ + 16, :], d_skip[l, g].rearrange("(c o) -> c o", o=1), writes=[rin], allow_slow_non_contiguous=True)
                    for s_ in range(8):
                        n_ = 7 - s_
                        op("dve", lambda e: e.tensor_scalar(out=Pm[i][:, 16 * s_:16 * s_ + 16], in0=bx[i][:], scalar1=FN[:, n_, 0, g:g + 1], scalar2=None, op0=ALU.mult), reads=[rin, rw], writes=[rPm])
                        op("dve", lambda e: e.scalar_tensor_tensor(out=Pm[i][:, 16 * s_:16 * s_ + 16], in0=by[i][:], scalar=FN[:, n_, 1, g:g + 1], in1=Pm[i][:, 16 * s_:16 * s_ + 16], op0=ALU.mult, op1=ALU.add),
                           reads=[rin, rw], writes=[rPm])
                    for s_ in range(8):
                        op("dve", lambda e: e.tensor_copy(out=Btr[i][:, s_, :], in_=Pm[i][:, 112:128]), reads=[rPm], writes=[rBt])
                    for n_ in range(9):
                        op("dve", lambda e: e.tensor_scalar(out=CQ[i][:, n_, :], in0=cx[i][:], scalar1=CQc[:, n_, 0, g:g + 1], scalar2=None, op0=ALU.mult), reads=[rin, rw], writes=[rCQ])
                        op("dve", lambda e: e.scalar_tensor_tensor(out=CQ[i][:, n_, :], in0=cy[i][:], scalar=CQc[:, n_, 1, g:g + 1], in1=CQ[i][:, n_, :], op0=ALU.mult, op1=ALU.add),
                           reads=[rin, rw], writes=[rCQ])
                    op("pe", lambda e: e.transpose(out=pp[i][:], in_=Pm[i][:], identity=cfa("ident")), reads=[rPm] + rC, writes=[rpp])
                    op("act", lambda e: e.copy(out=OUTm[i][:, 0, :], in_=pp[i][:]), reads=[rpp], writes=[rOUT])
                    op("dve", lambda e: e.tensor_copy(out=OUTm[i][:, 1, :], in_=CQ[i][:, 1:9, :].rearrange("p n c -> p (n c)")), reads=[rCQ], writes=[rOUT])
                    op("pe", lambda e: e.matmul(pp[i][:], lhsT=Btr[i][:].rearrange("p s c -> p (s c)"), rhs=CQ[i][:, 0:8, :].rearrange("p n c -> p (n c)"), start=True, stop=True),
                       reads=[rBt, rCQ], writes=[rpp])
                    op("act", lambda e: e.copy(out=KTp[i][:, 112:240], in_=pp[i][:]), reads=[rpp], writes=[rKT])
                    op("dve", lambda e: e.scalar_tensor_tensor(out=KTp[i][:, 112:128], in0=cfa("i16", 128, 16), scalar=dr[i][:, 0:1], in1=KTp[i][:, 112:128], op0=ALU.mult, op1=ALU.add),
                       reads=[rin, rKT] + rC, writes=[rKT])
                    for s_ in range(8):
                        dma(s5c_d[g, 16 * s_:16 * s_ + 16, 2, :], KTp[i][16 * s_:16 * s_ + 16, (7 - s_) * 16:(7 - s_) * 16 + 128], reads=[rKT], writes=[rs("s5c_d")])
                    for k in range(9):
                        op("dve", lambda e: e.tensor_scalar(out=OUTm[i][:, 3 + k, :], in0=cfa("ident"), scalar1=MKp[:, k, 0, g:g + 1], scalar2=None, op0=ALU.mult), reads=[rw] + rC, writes=[rOUT])
                        op("dve", lambda e: e.scalar_tensor_tensor(out=OUTm[i][:, 3 + k, :], in0=cfa("icross"), scalar=MKp[:, k, 1, g:g + 1], in1=OUTm[i][:, 3 + k, :], op0=ALU.mult, op1=ALU.add),
                           reads=[rw] + rC, writes=[rOUT])
                    dma(s5c_d[g, :, 0:2, :], OUTm[i][:, 0:2, :], reads=[rOUT], writes=[rs("s5c_d")])
                    dma(s5c_d[g, :, 3:12, :], OUTm[i][:, 3:12, :], reads=[rOUT], writes=[rs("s5c_d")])
            S_.barrier()

        def s5_phase():
            with contextlib.ExitStack() as st:
                Cm = [sbt(st, f"s5r_C{i}", [128, 12, 128], F32) for i in range(2)]
                Ut = [sbt(st, f"s5r_U{i}", [128, J], F32) for i in range(2)]
                St = [sbt(st, f"s5r_S{i}", [128, J], F32) for i in range(2)]
                Yt = [sbt(st, f"s5r_Y{i}", [128, J], F32) for i in range(2)]
                NJ = (J + 511) // 512
                pk = [pst(st, f"s5r_p{i}", [128, 512], F32) for i in range(4)]
                pc = [0]
                for g in range(16):
                    i = g % 2
                    rCm, rU, rSt, rY = rs("s5r_C", i), rs("s5r_U", i), rs("s5r_S", i), rs("s5r_Y", i)
                    dma(Cm[i][:], s5c_d[g], reads=[rs("s5c_d")], writes=[rCm])
                    for s_ in range(8):
                        dma(Ut[i][16 * s_:16 * s_ + 16, :], U_d[g, s_, :, :], reads=[rs("U_d")], writes=[rU])
                    for c in range(NJ):
                        cs_ = slice(c * 512, min(J, c * 512 + 512))
                        k = pc[0] % 4
                        pc[0] += 1
                        rpk = rs("s5r_p", k)
                        op("pe", lambda e: e.matmul(pk[k][:, 0:cs_.stop - cs_.start], lhsT=Cm[i][:, 0, :], rhs=Ut[i][:, cs_], start=True, stop=True), reads=[rCm, rU], writes=[rpk])
                        op("act", lambda e: e.copy(out=St[i][:, cs_], in_=pk[k][:, 0:cs_.stop - cs_.start]), reads=[rpk], writes=[rSt])
                    k_ = 0
                    while (1 << k_) < J:
                        sh = 1 << k_
                        width = J - sh
                        chunks = []
                        c0 = 0
                        while c0 < width:
                            chunks.append((c0, min(512, width - c0)))
                            c0 += 512
                        prods = []
                        for (c0, wc) in chunks:
                            k = pc[0] % 4
                            pc[0] += 1
                            rpk = rs("s5r_p", k)
                            op("pe", lambda e: e.matmul(pk[k][:, 0:wc], lhsT=Cm[i][:, 3 + k_, :], rhs=St[i][:, c0:c0 + wc], start=True, stop=True), reads=[rCm, rSt], writes=[rpk])
                            prods.append((k, rpk, c0, wc))
                            if len(prods) == 4 or (c0, wc) == chunks[-1]:
                                pass
                        for (k, rpk, c0, wc) in prods:
                            op("dve", lambda e: e.tensor_tensor(out=St[i][:, sh + c0:sh + c0 + wc], in0=pk[k][:, 0:wc], in1=St[i][:, sh + c0:sh + c0 + wc], op=ALU.add), reads=[rpk, rSt], writes=[rSt])
                        k_ += 1
                    for c in range(NJ):
                        c0 = c * 512
                        wc = min(512, J - c0)
                        k = pc[0] % 4
                        pc[0] += 1
                        rpk = rs("s5r_p", k)
                        op("pe", lambda e: e.matmul(pk[k][:, 0:wc], lhsT=Cm[i][:, 2, :], rhs=Ut[i][:, c0:c0 + wc], start=True, stop=False), reads=[rCm, rU], writes=[rpk])
                        if c0 == 0:
                            op("pe", lambda e: e.matmul(pk[k][:, 1:wc], lhsT=Cm[i][:, 1, :], rhs=St[i][:, 0:wc - 1], start=False, stop=True), reads=[rCm, rSt], writes=[rpk])
                        else:
                            op("pe", lambda e: e.matmul(pk[k][:, 0:wc], lhsT=Cm[i][:, 1, :], rhs=St[i][:, c0 - 1:c0 + wc - 1], start=False, stop=True), reads=[rCm, rSt], writes=[rpk])
                        op("act", lambda e: e.copy(out=Yt[i][:, c0:c0 + wc], in_=pk[k][:, 0:wc]), reads=[rpk], writes=[rY])
                    for s_ in range(8):
                        dma(Y_d[g, s_, :, :], Yt[i][16 * s_:16 * s_ + 16, :], reads=[rY], writes=[rs("Y_d")])
            S_.barrier()
            with contextlib.ExitStack() as st:
                yp = sbt(st, "s5p_yp", [128, 2, 8, 64], F32)
                y = sbt(st, "s5p_y", [128, 2, 512], F32)
                t = sbt(st, "s5p_t", [128, 2, 512], F32)
                gl = sbt(st, "s5p_g", [128, 2, 512], F32)
                gb16 = sbt(st, "s5p_gb", [128, 2, 512], BF16)
                sg = sbt(st, "s5p_sg", [128, 2, 512], F32)
                gate = sbt(st, "s5p_gate", [128, 2, 512], BF16)
                ob = sbt(st, "s5p_ob", [128, 2, 512], BF16)
                zp = [pst(st, f"s5p_z{i}", [128, 512], F32) for i in range(2)]
                ryp, ry, rt_, rgl, rgb, rsg, rgate, rob = (rs("s5p_" + n_) for n_ in ("yp", "y", "t", "g", "gb", "sg", "gate", "ob"))
                for tg in range(NG):
                    tsl = slice(tg * 512, tg * 512 + 512)
                    for c in range(2):
                        for g8 in range(8):
                            dma(yp[16 * g8:16 * g8 + 16, c, :, :], Y_d[8 * c + g8, :, :, tg * 64:(tg + 1) * 64].rearrange("s c j -> c s j"), reads=[rs("Y_d")], writes=[ryp])
                        dma(gate[:, c, :], gbT_d[c, :, tsl], reads=[rs("gbT_d")], writes=[rgate])
                    for c in range(2):
                        op("act", lambda e: e.copy(out=y[:, c, :].rearrange("p (j s) -> p s j", s=8), in_=yp[:, c, :, :]), reads=[ryp], writes=[ry])
                        op("act", lambda e: e.activation(out=t[:, c, :], in_=y[:, c, :], func=AF.Square), reads=[ry], writes=[rt_])
                        op("dve", lambda e: e.tensor_scalar(out=t[:, c, :], in0=t[:, c, :], scalar1=0.044715, scalar2=1.0, op0=ALU.mult, op1=ALU.add), reads=[rt_], writes=[rt_])
                        op("dve", lambda e: e.tensor_tensor(out=t[:, c, :], in0=t[:, c, :], in1=y[:, c, :], op=ALU.mult), reads=[rt_, ry], writes=[rt_])
                        op("act", lambda e: e.activation(out=t[:, c, :], in_=t[:, c, :], func=AF.Sigmoid, scale=2.0 * 0.7978845608028654), reads=[rt_], writes=[rt_])
                        op("dve", lambda e: e.tensor_tensor(out=gl[:, c, :], in0=t[:, c, :], in1=y[:, c, :], op=ALU.mult), reads=[rt_, ry], writes=[rgl])
                        op("dve", lambda e: e.tensor_copy(out=gb16[:, c, :], in_=gl[:, c, :]), reads=[rgl], writes=[rgb])
                    for co in range(2):
                        rz = rs("s5p_z", co)
                        for c in range(2):
                            op("pe", lambda e: e.matmul(zp[co][:], lhsT=Wglu[:, c, co * 128:(co + 1) * 128], rhs=gb16[:, c, :], start=(c == 0), stop=(c == 1)), reads=[rWsm, rgb], writes=[rz])
                        op("act", lambda e: e.activation(out=sg[:, co, :], in_=zp[co][:], func=AF.Sigmoid), reads=[rz], writes=[rsg])
                        op("dve", lambda e: e.tensor_tensor(out=sg[:, co, :], in0=sg[:, co, :], in1=gl[:, co, :], op=ALU.mult), reads=[rsg, rgl], writes=[rsg])
                        op("dve", lambda e: e.tensor_tensor(out=ob[:, co, :], in0=sg[:, co, :], in1=gate[:, co, :], op=ALU.mult), reads=[rsg, rgate], writes=[rob])
                        dma(mix_d[6 + co, :, tsl], ob[:, co, :], reads=[rob], writes=[rs("mix_d")])
            S_.barrier()

        def phase3(src_ap, src_res, dst_ap, dst_res):
            with contextlib.ExitStack() as st:
                Wo = sbt(st, "p3_Wo", [128, 14, D], BF16)
                rWo = rs("p3_Wo")
                for c in range(14):
                    nr = 128 if c in (6, 7) else 64
                    dma(Wo[0:nr, c, :], wout_d[0:nr, c, :], reads=[rs("wout_d")], writes=[rWo])
                mx = [sbt(st, f"p3_mx{i}", [128, 14, 512], BF16) for i in range(2)]
                xt = [sbt(st, f"p3_xt{i}", [128, 4, D], F32) for i in range(2)]
                pp = [pst(st, f"p3_pp{i}", [128, 512], F32) for i in range(4)]
                pc = 0
                for tg in range(NG):
                    i = tg % 2
                    rmx, rxt = rs("p3_mx", i), rs("p3_xt", i)
                    tsl = slice(tg * 512, tg * 512 + 512)
                    for c in range(14):
                        nr = 128 if c in (6, 7) else 64
                        dma(mx[i][0:nr, c, :], mix_d[c, 0:nr, tsl], reads=[rs("mix_d")], writes=[rmx])
                    dma(xt[i][:], src_ap[tg * 512:(tg + 1) * 512, :].rearrange("(j p) d -> p j d", p=128), reads=[src_res], writes=[rxt])
                    for j in range(4):
                        for half in range(2):
                            k = pc % 4
                            pc += 1
                            rpp = rs("p3_pp", k)
                            for c in range(14):
                                nr = 128 if c in (6, 7) else 64
                                op("pe", lambda e: e.matmul(pp[k][:], lhsT=mx[i][0:nr, c, j * 128:(j + 1) * 128], rhs=Wo[0:nr, c, half * 512:(half + 1) * 512], start=(c == 0), stop=(c == 13)),
                                   reads=[rmx, rWo], writes=[rpp])
                            op("dve", lambda e: e.tensor_tensor(out=xt[i][:, j, half * 512:(half + 1) * 512], in0=pp[k][:], in1=xt[i][:, j, half * 512:(half + 1) * 512], op=ALU.add), reads=[rpp, rxt], writes=[rxt])
                    dma(dst_ap[tg * 512:(tg + 1) * 512, :].rearrange("(j p) d -> p j d", p=128), xt[i][:], reads=[rxt], writes=[dst_res])
            S_.barrier()

        for l in range(NL):
            if "prep" in PH:
                prep_weights(l)
            if "s5c" in PH:
                s5_consts(l)
            for s in range(NSEQ):
                src = x_in[s] if l == 0 else xs_d[s]
                dst = out_d[s] if l == NL - 1 else xs_d[s]
                if "p1" in PH:
                    phase1(l, src, rs("xs", s))
                if "dsa" in PH:
                    dsa_phase()
                if "s5" in PH:
                    s5_phase()
                if "mla" in PH:
                    mla_phase()
                if "p3" in PH:
                    phase3(src, rs("xs", s), dst, rs("out_d") if l == NL - 1 else rs("xs", s))
        S_.barrier()
    return nc


xs_all = None


def _build(S, NL, NSEQ, KTOP):
    global xs_all
    return build_program(S, NL, NSEQ, KTOP)


_CACHE = {}


def kernel(**inputs):
    x = np.ascontiguousarray(np.asarray(inputs["x"], dtype=np.float32))
    B, S, _ = x.shape
    ncores = 8
    NSEQ = B // ncores
    KTOP = min(256, S // 4)
    nc = build_program(S, DEPTH, NSEQ, KTOP)
    consts = make_consts(S)
    names = ["norm_g", "w_in", "attn_q_norm", "attn_k_norm", "mla_q_lora_norm", "mla_kv_lora_norm", "mla_w_uq", "mla_w_ukv",
             "mla_q_norm", "mla_k_norm", "ssm_a_re", "ssm_a_im", "ssm_b_re", "ssm_b_im", "ssm_c_re", "ssm_c_im", "ssm_d",
             "ssm_log_step", "ssm_w_glu", "w_out"]
    shared = {n: np.ascontiguousarray(np.asarray(inputs[n], dtype=np.float32)) for n in names}
    in_maps = []
    for c in range(ncores):
        m = dict(shared)
        m["x"] = x[c * NSEQ:(c + 1) * NSEQ]
        m.update(consts)
        in_maps.append(m)
    res = run_bass_kernel_spmd(nc, in_maps, core_ids=list(range(ncores)))
    return np.concatenate([r["out"] for r in res.results], axis=0).astype(np.float32)
```

```python
import contextlib
import numpy as np
import ml_dtypes
import concourse.bass as bass
import concourse.mybir as mybir
from concourse.bass_utils import run_bass_kernel_spmd

F32 = mybir.dt.float32
BF16 = mybir.dt.bfloat16
AF = mybir.ActivationFunctionType
ALU = mybir.AluOpType
AX = mybir.AxisListType

D = 1024
DEPTH = 4
DIN = 2504
EPS = 1e-6
NCOL = 3240
C_IQ2 = 2728
C_QA, C_KA2, C_IQ, C_IK4, C_GA, C_U, C_GB, C_CQ, C_CKV, C_KPE, C_GC, C_VAIW = (
    0, 384, 512, 768, 896, 1280, 1536, 1792, 2048, 2176, 2272, 2656)
S_QA, S_KA, S_VA, S_IQ, S_IK, S_IW, S_GA, S_U, S_GB, S_CQ, S_CKV, S_KPE, S_GC = (
    0, 384, 448, 512, 768, 800, 808, 1192, 1448, 1704, 1960, 2088, 2120)
NEGM = -30000.0
NIT = 16
import os as _os5
P1ENG = _os5.environ.get('P1ENG', 'pool')
import os as _os4
DSA_MODE = int(_os4.environ.get('DSA_MODE', '0'))
import os as _os3
GV = int(_os3.environ.get('GV', '0'))
import os as _os2
GATE_DMA = int(_os2.environ.get('GATE_DMA', '2'))
import os as _os
P1CUT = int(_os.environ.get('P1CUT', '99'))
import os
PH = set(os.environ.get('PH', 'prep,s5c,p1,dsa,s5,mla,p3').split(','))


class Res:
    __slots__ = ("w", "r", "ps")

    def __init__(self, ps=False):
        self.w = {}
        self.r = {}
        self.ps = ps


class Sched:
    def __init__(self, nc, n_dma_sems=32):
        self.nc = nc
        self.engs = {}
        for name, e in (("pe", nc.tensor), ("act", nc.scalar), ("dve", nc.vector),
                        ("pool", nc.gpsimd), ("sp", nc.sync)):
            sem = nc.alloc_semaphore(name="sem_" + name)
            self.engs[name] = dict(e=e, sem=sem, cnt=0, waited={}, name=name)
        self.dma_sems = [nc.alloc_semaphore(name=f"dsem{i}") for i in range(n_dma_sems)]
        self.dma_cnt = [0] * n_dma_sems
        self.dma_i = 0
        self.nins = 0

    def _wait(self, en, deps, skip_self=False):
        E = self.engs[en]
        best = {}
        for sem, val in deps:
            k = id(sem)
            if k not in best or best[k][1] < val:
                best[k] = (sem, val)
        for k, (sem, val) in best.items():
            if skip_self and sem is E["sem"]:
                continue
            if E["waited"].get(k, 0) < val:
                E["e"].wait_ge(sem, val)
                E["waited"][k] = val

    def _deps(self, reads, writes, en=None):
        deps = []
        own = self.engs[en]["sem"] if en is not None else None
        for r in reads:
            deps.extend(r.w.values())
            if r.ps:
                deps.extend(ev for ev in r.r.values() if ev[0] is not own)
        for w in writes:
            deps.extend(w.w.values())
            deps.extend(w.r.values())
        return deps

    @staticmethod
    def _mark(ev, reads, writes):
        k = id(ev[0])
        for r in reads:
            r.r[k] = ev
        for w in writes:
            w.w[k] = ev

    def op(self, en, fn, reads=(), writes=()):
        E = self.engs[en]
        self._wait(en, self._deps(reads, writes, en), skip_self=(en == "pe"))
        ins = fn(E["e"])
        E["cnt"] += 1
        ins.then_inc(E["sem"], 1)
        ev = (E["sem"], E["cnt"])
        self._mark(ev, reads, writes)
        self.nins += 1
        return ev

    def dma(self, out, in_, reads=(), writes=(), en="sp", **kw):
        E = self.engs[en]
        i = self.dma_i % len(self.dma_sems)
        self.dma_i += 1
        sem = self.dma_sems[i]
        deps = self._deps(reads, writes)
        if self.dma_cnt[i] > 0:
            deps.append((sem, self.dma_cnt[i]))
        self._wait(en, deps)
        ins = E["e"].dma_start(out=out, in_=in_, **kw)
        self.dma_cnt[i] += 16
        ins.then_inc(sem, 16)
        ev = (sem, self.dma_cnt[i])
        self._mark(ev, reads, writes)
        self.nins += 1
        return ev

    def barrier(self):
        evs = [(E["sem"], E["cnt"]) for E in self.engs.values() if E["cnt"] > 0]
        evs += [(s, c) for s, c in zip(self.dma_sems, self.dma_cnt) if c > 0]
        for en in self.engs:
            self._wait(en, evs)


def rope_tab(S, d, rows):
    half = d // 2
    inv = (np.float32(10000.0) ** (-np.arange(half, dtype=np.float32) * np.float32(2.0) / np.float32(d))).astype(np.float32)
    ang = np.arange(S, dtype=np.float32)[:, None] * inv[None, :]
    c = np.cos(ang).astype(np.float32).T
    s = np.sin(ang).astype(np.float32).T
    idx = np.arange(rows) % half
    return c[idx], s[idx]


def make_consts(S):
    bf = ml_dtypes.bfloat16
    I = np.eye(128, dtype=np.float32)
    ones64 = np.zeros((128, 128), np.float32)
    ones64[:64, :64] = 1; ones64[64:, 64:] = 1
    ones96 = np.zeros((128, 128), np.float32); ones96[:96, :96] = 1
    ones128 = np.ones((128, 128), np.float32)

    def rot(M, hd, half, lo=0):
        R = np.zeros((128, 128), np.float32)
        for m in range(M):
            j = m % hd
            if j < lo:
                continue
            jj = j - lo
            if jj < half:
                R[m + half, m] = -1.0
            else:
                R[m - half, m] = 1.0
        return R
    R64 = rot(128, 64, 32)
    R32 = rot(128, 32, 16)
    R96 = rot(96, 96, 16, lo=64)
    nmd = np.zeros((128, 128), np.float32); nmd[:64, 64:] = NEGM
    cbf = np.concatenate([I, ones64, ones96, ones128, R64, R32, R96, nmd], axis=1).astype(bf)
    icross = np.zeros((128, 128), np.float32)
    for m in range(128):
        icross[m, (m + 64) % 128] = 1
    e65 = np.zeros((128, 64), np.float32); e65[64, :] = 1
    i16 = np.zeros((128, 16), np.float32)
    for p in range(128):
        i16[p, p % 16] = 1
    sgB = np.where(np.arange(128) < 64, -1.0, 1.0).astype(np.float32)[:, None]
    sgQ = -sgB
    epsc = np.full((128, 1), EPS, np.float32)
    pw = np.tile((2.0 ** -(np.arange(NIT + 1, dtype=np.float64) + 1.0)).astype(np.float32)[None, :], (128, 1))
    cf = np.concatenate([I, icross, e65, i16, sgB, sgQ, epsc, pw], axis=1).astype(np.float32)
    c64, s64 = rope_tab(S, 64, 128)
    c32, s32 = rope_tab(S, 32, 128)
    c96 = np.ones((128, S), np.float32); s96 = np.zeros((128, S), np.float32)
    c96[64:96] = c32[:32]; s96[64:96] = s32[:32]
    rt = np.stack([c64, s64, c32, s32, c96, s96]).astype(np.float32)
    return dict(cbf=cbf, cf=cf, rt=rt)


CB = dict(ident=0, ones64=128, ones96=256, ones128=384, R64=512, R32=640, R96=768, nmd=896)
CF = dict(ident=0, icross=128, e65=256, i16=320, sgB=336, sgQ=337, eps=338, pw=339)
NCF = 339 + NIT + 1


def build_program(S, NL, NSEQ, KTOP, dbg=None):
    nc = bass.Bass("TRN2", target_bir_lowering=False)
    NB = S // 128
    NG = S // 512
    J = S // 8
    dbg = dbg or {}

    def din(name, shape, dt=F32):
        return nc.dram_tensor(name, list(shape), dt, kind="ExternalInput").ap()

    def dscr(name, shape, dt):
        return nc.dram_tensor(name, list(shape), dt, kind="Internal").ap()

    x_in = din("x", [NSEQ, S, D])
    norm_g = din("norm_g", [DEPTH, D])
    w_in = din("w_in", [DEPTH, D, DIN])
    attn_q_norm = din("attn_q_norm", [DEPTH, 64])
    attn_k_norm = din("attn_k_norm", [DEPTH, 64])
    q_lora_g = din("mla_q_lora_norm", [DEPTH, 256])
    kv_lora_g = din("mla_kv_lora_norm", [DEPTH, 128])
    w_uq = din("mla_w_uq", [DEPTH, 256, 576])
    w_ukv = din("mla_w_ukv", [DEPTH, 128, 768])
    mla_q_norm = din("mla_q_norm", [DEPTH, 96])
    mla_k_norm = din("mla_k_norm", [DEPTH, 96])
    a_re = din("ssm_a_re", [DEPTH, 16, 64])
    a_im = din("ssm_a_im", [DEPTH, 16, 64])
    b_re = din("ssm_b_re", [DEPTH, 16, 64, 16])
    b_im = din("ssm_b_im", [DEPTH, 16, 64, 16])
    c_re = din("ssm_c_re", [DEPTH, 16, 16, 64])
    c_im = din("ssm_c_im", [DEPTH, 16, 16, 64])
    d_skip = din("ssm_d", [DEPTH, 16, 16])
    log_step = din("ssm_log_step", [DEPTH, 16])
    w_glu = din("ssm_w_glu", [DEPTH, 256, 256])
    w_out = din("w_out", [DEPTH, D, D])
    cbf_d = din("cbf", [128, 1024], BF16)
    cf_d = din("cf", [128, NCF])
    rt_d = din("rt", [6, 128, S])
    out_d = nc.dram_tensor("out", [NSEQ, S, D], F32, kind="ExternalOutput").ap()

    xs_d = dscr("xs", [NSEQ, S, D], F32)
    winp_d = dscr("winp", [128, 8, NCOL], BF16)
    wout_d = dscr("woutp", [128, 14, D], BF16)
    qT_d = dscr("qT", [3, 128, S], BF16)
    kT2_d = dscr("kT2", [128, S], BF16)
    va_d = dscr("va", [NB, 128, 65], BF16)
    iqT_d = dscr("iqT", [4, 128, S], BF16)
    ikT_d = dscr("ikT", [128, S], BF16)
    iw_d = dscr("iw", [NB, 128, 16], F32)
    gaT_d = dscr("gaT", [6, 64, S], BF16)
    gbT_d = dscr("gbT", [2, 128, S], BF16)
    gcT_d = dscr("gcT", [6, 64, S], BF16)
    U_d = dscr("U", [16, 8, 16, J], F32)
    Y_d = dscr("Y", [16, 8, 16, J], F32)
    qc_d = dscr("qc", [6, 96, S], BF16)
    kc_d = dscr("kc", [6, 96, S], BF16)
    vc_d = dscr("vc", [6, NB, 128, 65], BF16)
    mix_d = dscr("mix", [14, 128, S], BF16)
    s5c_d = dscr("s5c", [16, 128, 12, 128], F32)

    S_ = Sched(nc)
    op = S_.op
    dma = S_.dma
    RS = {}

    PSUM_KEYS = {"p1_tp", "p1_mm", "p1_aux", "at_ps", "at_o", "at_sbp", "ds_shp", "ds_scp", "s5_pp", "s5r_p", "s5p_z", "p3_pp"}

    def rs(*key):
        if key not in RS:
            RS[key] = Res(ps=(key[0] in PSUM_KEYS))
        return RS[key]

    with contextlib.ExitStack() as top:
        uid = [0]

        def sbt(st, name, shape, dt):
            uid[0] += 1
            return st.enter_context(nc.sbuf_tensor(f"sb{uid[0]}_{name}", list(shape), dt))

        def pst(st, name, shape, dt):
            uid[0] += 1
            return st.enter_context(nc.psum_tensor(f"ps{uid[0]}_{name}", list(shape), dt))

        cbf = sbt(top, "cbf", [128, 1024], BF16)
        cf = sbt(top, "cf", [128, NCF], F32)
        dma(cbf[:], cbf_d, writes=[rs("cbf")])
        dma(cf[:], cf_d, writes=[rs("cf")])
        rC = [rs("cbf"), rs("cf")]

        def cb(name, rows=128, cols=128):
            o = CB[name]
            return cbf[0:rows, o:o + cols]

        def cfa(name, rows=128, cols=128):
            o = CF[name]
            return cf[0:rows, o:o + cols]
        eps_ap = cf[:, CF["eps"]:CF["eps"] + 1]

        gsm = sbt(top, "gsm", [128, DEPTH, 16], F32)
        rg = rs("gsm")
        op("dve", lambda e: e.memset(gsm[:], 1.0), writes=[rg])
        for l in range(NL):
            dma(gsm[:, l, 0:8], norm_g[l].rearrange("(k p) -> p k", p=128), writes=[rg], allow_slow_non_contiguous=True)
            for hh in range(2):
                dma(gsm[64 * hh:64 * hh + 64, l, 8:9], attn_q_norm[l].rearrange("(p o) -> p o", o=1), writes=[rg], allow_slow_non_contiguous=True)
                dma(gsm[64 * hh:64 * hh + 64, l, 9:10], attn_k_norm[l].rearrange("(p o) -> p o", o=1), writes=[rg], allow_slow_non_contiguous=True)
            dma(gsm[:, l, 10:12], q_lora_g[l].rearrange("(k p) -> p k", p=128), writes=[rg], allow_slow_non_contiguous=True)
            dma(gsm[:, l, 12:13], kv_lora_g[l].rearrange("(p o) -> p o", o=1), writes=[rg], allow_slow_non_contiguous=True)
            dma(gsm[0:96, l, 13:14], mla_q_norm[l].rearrange("(p o) -> p o", o=1), writes=[rg], allow_slow_non_contiguous=True)
            dma(gsm[0:96, l, 14:15], mla_k_norm[l].rearrange("(p o) -> p o", o=1), writes=[rg], allow_slow_non_contiguous=True)

        def prep_weights(l):
            with contextlib.ExitStack() as st:
                stg = [sbt(st, f"stg{i}", [128, DIN], F32) for i in range(2)]
                wrow = [sbt(st, f"wrow{i}", [128, NCOL], BF16) for i in range(2)]
                for kc in range(8):
                    i = kc % 2
                    rst, rw = rs("stg", i), rs("wrow", i)
                    dma(stg[i][:], w_in[l, kc * 128:(kc + 1) * 128, :], writes=[rst])
                    g = gsm[:, l, kc:kc + 1]
                    segs = [(C_QA, S_QA, 384), (C_KA2, S_KA, 64), (C_KA2 + 64, S_KA, 64)] + [(C_IQ2 + (h_ // 2) * 128 + 64 * (h_ % 2), S_IQ + 32 * h_, 32) for h_ in range(8)] + \
                           [(C_IK4 + 32 * r, S_IK, 32) for r in range(4)] + \
                           [(C_GA, S_GA, 384), (C_U, S_U, 256), (C_GB, S_GB, 256), (C_CQ, S_CQ, 256), (C_CKV, S_CKV, 128),
                            (C_KPE + 64, S_KPE, 32), (C_GC, S_GC, 384), (C_VAIW, S_VA, 64), (C_VAIW + 64, S_IW, 8)]
                    op("pool", lambda e: e.memset(wrow[i][:, C_KPE:C_KPE + 64], 0.0), writes=[rw])
                    op("pool", lambda e: e.memset(wrow[i][:, C_IQ2:C_IQ2 + 512], 0.0), writes=[rw])
                    op("pool", lambda e: e.memset(wrow[i][:, C_IQ:C_IQ + 256], 0.0), writes=[rw])
                    for n_, (dc, sc_, w_) in enumerate(segs):
                        if n_ % 2 == 0:
                            op("dve", lambda e: e.tensor_scalar(out=wrow[i][:, dc:dc + w_], in0=stg[i][:, sc_:sc_ + w_], scalar1=g, scalar2=None, op0=ALU.mult),
                               reads=[rst, rg], writes=[rw])
                        else:
                            op("act", lambda e: e.activation(out=wrow[i][:, dc:dc + w_], in_=stg[i][:, sc_:sc_ + w_], func=AF.Copy, scale=g),
                               reads=[rst, rg], writes=[rw])
                    dma(winp_d[:, kc, :], wrow[i][:], reads=[rw], writes=[rs("winp_d")])
            with contextlib.ExitStack() as st:
                stg = [sbt(st, f"stgo{i}", [128, D], F32) for i in range(2)]
                wrow = [sbt(st, f"wrowo{i}", [128, D], BF16) for i in range(2)]
                for c in range(14):
                    i = c % 2
                    rst, rw = rs("stgo", i), rs("wrowo", i)
                    if c < 6:
                        r0, nr = 64 * c, 64
                    elif c < 8:
                        r0, nr = 384 + 128 * (c - 6), 128
                    else:
                        r0, nr = 640 + 64 * (c - 8), 64
                    dma(stg[i][0:nr, :], w_out[l, r0:r0 + nr, :], writes=[rst])
                    if c % 2 == 0:
                        op("dve", lambda e: e.tensor_copy(out=wrow[i][0:nr, :], in_=stg[i][0:nr, :]), reads=[rst], writes=[rw])
                    else:
                        op("act", lambda e: e.copy(out=wrow[i][0:nr, :], in_=stg[i][0:nr, :]), reads=[rst], writes=[rw])
                    dma(wout_d[0:nr, c, :], wrow[i][0:nr, :], reads=[rw], writes=[rs("wout_d")])
            with contextlib.ExitStack() as st:
                s1 = sbt(st, "s_uq", [128, 2, 576], F32)
                s2 = sbt(st, "s_ukv", [128, 768], F32)
                s3 = sbt(st, "s_glu", [128, 2, 256], F32)
                r1, r2, r3 = rs("s_uq"), rs("s_ukv"), rs("s_glu")
                dma(s1[:], w_uq[l].rearrange("(k p) n -> p k n", p=128), writes=[r1])
                dma(s2[:], w_ukv[l], writes=[r2])
                dma(s3[:], w_glu[l].rearrange("(k p) n -> p k n", p=128), writes=[r3])
                rw = rs("wsm")
                for k in range(2):
                    op("dve", lambda e: e.tensor_scalar(out=Wuq[:, k, :], in0=s1[:, k, :], scalar1=gsm[:, l, 10 + k:11 + k], scalar2=None, op0=ALU.mult),
                       reads=[r1, rg], writes=[rw])
                    op("dve", lambda e: e.tensor_copy(out=Wglu[:, k, :], in_=s3[:, k, :]), reads=[r3], writes=[rw])
                op("dve", lambda e: e.memset(WukT[:], 0.0), writes=[rw])
                for h in range(6):
                    op("dve", lambda e: e.tensor_scalar(out=WukT[:, h, 0:64], in0=s2[:, h * 128:h * 128 + 64], scalar1=gsm[:, l, 12:13], scalar2=None, op0=ALU.mult),
                       reads=[r2, rg], writes=[rw])
                    op("dve", lambda e: e.tensor_scalar(out=Wuv[:, h * 64:h * 64 + 64], in0=s2[:, h * 128 + 64:h * 128 + 128], scalar1=gsm[:, l, 12:13], scalar2=None, op0=ALU.mult),
                       reads=[r2, rg], writes=[rw])
            S_.barrier()

        Wuq = sbt(top, "Wuq", [128, 2, 576], BF16)
        WukT = sbt(top, "WukT", [128, 6, 96], BF16)
        Wuv = sbt(top, "Wuv", [128, 384], BF16)
        Wglu = sbt(top, "Wglu", [128, 2, 256], BF16)
        rWsm = rs("wsm")

        def phase1(l, src_ap, src_res):
            with contextlib.ExitStack() as st:
                Winp = sbt(st, "Winp", [128, 8, NCOL], BF16)
                rW = rs("Winp")
                for kc in range(8):
                    dma(Winp[:, kc, :], winp_d[:, kc, :], reads=[rs("winp_d")], writes=[rW])
                xt = sbt(st, "p1_xt", [128, 4, D], F32)
                rxt = rs("p1_xt")
                junk = sbt(st, "p1_junk", [128, D], BF16)
                xn = sbt(st, "p1_xn", [128, D], BF16)
                ssq = sbt(st, "p1_ssq", [128, 8], F32)
                hT = sbt(st, "p1_hT", [128, 8, 512], BF16)
                rhT = rs("p1_hT")
                tabs = sbt(st, "p1_tabs", [128, 6, 512], F32)
                rtab = rs("p1_tabs")
                tp = [pst(st, f"p1_tp{i}", [128, 1024], BF16) for i in range(2)]
                mm = [pst(st, f"p1_mm{i}", [128, 512], F32) for i in range(3)]
                aux = [pst(st, f"p1_aux{i}", [128, 512], F32) for i in range(3)]
                mmi = [0]
                NWK = 3
                xg = [sbt(st, f"p1_xg{i}", [128, 512], BF16) for i in range(NWK)]
                sq = [sbt(st, f"p1_sq{i}", [128, 512], BF16) for i in range(NWK)]
                sd = [sbt(st, f"p1_sd{i}", [128, 512], F32) for i in range(NWK)]
                t1 = [sbt(st, f"p1_t1{i}", [128, 512], F32) for i in range(NWK)]
                t2 = [sbt(st, f"p1_t2{i}", [128, 512], F32) for i in range(NWK)]
                ob = [sbt(st, f"p1_ob{i}", [128, 512], BF16) for i in range(NWK)]
                up = [sbt(st, f"p1_up{i}", [128, 8, 64], F32) for i in range(2)]
                cqn = sbt(st, "p1_cqn", [128, 2, 512], BF16)
                ckvn = sbt(st, "p1_ckvn", [128, 512], BF16)
                vcs = [sbt(st, f"p1_vcs{i}", [128, 6, 65], BF16) for i in range(2)]
                vas = [sbt(st, f"p1_vas{i}", [128, 65], BF16) for i in range(2)]
                iws = [sbt(st, f"p1_iws{i}", [128, 16], F32) for i in range(2)]
                for i in range(2):
                    op("pool", lambda e: e.memset(vcs[i][:], 1.0), writes=[rs("p1_vcs", i)])
                    op("pool", lambda e: e.memset(vas[i][:], 1.0), writes=[rs("p1_vas", i)])
                wk = [0]

                def proj(cols, M, rhs_fn=None):
                    k = mmi[0] % 3
                    mmi[0] += 1
                    r = rs("p1_mm", k)
                    for kc in range(8):
                        op("pe", lambda e: e.matmul(mm[k][0:M, :], lhsT=Winp[:, kc, cols:cols + M], rhs=hT[:, kc, :], start=(kc == 0), stop=(kc == 7)),
                           reads=[rW, rhT], writes=[r])
                    return mm[k], r

                def normrope(X, rX, M, gain, onesname, inv_dim, Rname, ci, dst_ap, dst_res):
                    i = wk[0] % NWK
                    wk[0] += 1
                    a0, a1 = aux[(2 * i) % 3], aux[(2 * i + 1) % 3]
                    ra0, ra1 = rs("p1_aux", (2 * i) % 3), rs("p1_aux", (2 * i + 1) % 3)
                    rxg, rsq, rsd, rt1, rt2, rob = (rs("p1_xg", i), rs("p1_sq", i), rs("p1_sd", i), rs("p1_t1", i), rs("p1_t2", i), rs("p1_ob", i))
                    Ct, St = tabs[0:M, ci, :], tabs[0:M, ci + 1, :]
                    if gain is not None:
                        op("act", lambda e: e.activation(out=xg[i][0:M, :], in_=X[0:M, :], func=AF.Copy, scale=gain), reads=[rX, rg], writes=[rxg])
                    else:
                        op("act", lambda e: e.copy(out=xg[i][0:M, :], in_=X[0:M, :]), reads=[rX], writes=[rxg])
                    op("pe", lambda e: e.matmul(a1[0:M, :], lhsT=cb(Rname, M, M), rhs=xg[i][0:M, :], start=True, stop=True), reads=[rxg] + rC, writes=[ra1])
                    if onesname is not None:
                        op("act", lambda e: e.activation(out=sq[i][0:M, :], in_=X[0:M, :], func=AF.Square), reads=[rX], writes=[rsq])
                        op("pe", lambda e: e.matmul(a0[0:M, :], lhsT=cb(onesname, M, M), rhs=sq[i][0:M, :], start=True, stop=True), reads=[rsq] + rC, writes=[ra0])
                        op("act", lambda e: e.activation(out=sd[i][0:M, :], in_=a0[0:M, :], func=AF.Ln, scale=inv_dim, bias=eps_ap[0:M, :]), reads=[ra0] + rC, writes=[rsd])
                        op("act", lambda e: e.activation(out=sd[i][0:M, :], in_=sd[i][0:M, :], func=AF.Exp, scale=-0.5), reads=[rsd], writes=[rsd])
                    if gain is not None:
                        op("dve", lambda e: e.scalar_tensor_tensor(out=t1[i][0:M, :], in0=X[0:M, :], scalar=gain, in1=Ct, op0=ALU.mult, op1=ALU.mult),
                           reads=[rX, rtab, rg], writes=[rt1])
                    else:
                        op("dve", lambda e: e.tensor_tensor(out=t1[i][0:M, :], in0=X[0:M, :], in1=Ct, op=ALU.mult), reads=[rX, rtab], writes=[rt1])
                    op("dve", lambda e: e.tensor_tensor(out=t2[i][0:M, :], in0=a1[0:M, :], in1=St, op=ALU.mult), reads=[ra1, rtab], writes=[rt2])
                    if onesname is not None:
                        op(P1ENG, lambda e: e.tensor_tensor(out=t1[i][0:M, :], in0=t1[i][0:M, :], in1=t2[i][0:M, :], op=ALU.add), reads=[rt1, rt2], writes=[rt1])
                        op(P1ENG, lambda e: e.tensor_tensor(out=ob[i][0:M, :], in0=t1[i][0:M, :], in1=sd[i][0:M, :], op=ALU.mult), reads=[rt1, rsd], writes=[rob])
                    else:
                        op("dve", lambda e: e.tensor_tensor(out=ob[i][0:M, :], in0=t1[i][0:M, :], in1=t2[i][0:M, :], op=ALU.add), reads=[rt1, rt2], writes=[rob])
                    dma(dst_ap, ob[i][0:M, :], reads=[rob], writes=[dst_res])

                def silu_out(X, rX, M, dsts):
                    i = wk[0] % NWK
                    wk[0] += 1
                    rob = rs("p1_ob", i)
                    rt1 = rs("p1_t1", i)
                    op("act", lambda e: e.activation(out=t1[i][0:M, :], in_=X[0:M, :], func=AF.Sigmoid), reads=[rX], writes=[rt1])
                    if GV != 3:
                        op("dve", lambda e: e.scalar_tensor_tensor(out=ob[i][0:M, :], in0=X[0:M, :], scalar=1.0, in1=t1[i][0:M, :], op0=ALU.mult, op1=ALU.mult), reads=[rX, rt1], writes=[rob])
                    for (p0, p1, dap, dres) in dsts:
                        if GATE_DMA == 0 or (GATE_DMA == 1 and p0 != 0):
                            continue
                        dma(dap, ob[i][p0:p1, :], reads=[rob], writes=[dres])

                for tg in range(NG):
                    t0 = tg * 512
                    tsl = slice(t0, t0 + 512)
                    dma(xt[:], src_ap[t0:t0 + 512, :].rearrange("(j p) d -> p j d", p=128), reads=[src_res], writes=[rxt])
                    for ci in range(6):
                        dma(tabs[:, ci, :], rt_d[ci, :, tsl], writes=[rtab])
                    rssq, rjunk, rxn = rs("p1_ssq"), rs("p1_junk"), rs("p1_xn")
                    for j in range(4):
                        op("act", lambda e: e.activation(out=junk[:], in_=xt[:, j, :], func=AF.Square, accum_out=ssq[:, j:j + 1]), reads=[rxt], writes=[rjunk, rssq])
                    op("act", lambda e: e.activation(out=ssq[:, 4:8], in_=ssq[:, 0:4], func=AF.Sqrt, scale=1.0 / D, bias=eps_ap), reads=[rssq] + rC, writes=[rssq])
                    op("dve", lambda e: e.reciprocal(out=ssq[:, 4:8], in_=ssq[:, 4:8]), reads=[rssq], writes=[rssq])
                    for j in range(4):
                        op("dve", lambda e: e.tensor_scalar(out=xn[:], in0=xt[:, j, :], scalar1=ssq[:, 4 + j:5 + j], scalar2=None, op0=ALU.mult), reads=[rxt, rssq], writes=[rxn])
                        for half in range(2):
                            k = half
                            rtp = rs("p1_tp", k)
                            for q in range(4):
                                kc = half * 4 + q
                                op("pe", lambda e: e.transpose(out=tp[k][:, q * 128:(q + 1) * 128], in_=xn[:, kc * 128:(kc + 1) * 128], identity=cb("ident")),
                                   reads=[rxn] + rC, writes=[rtp])
                            eng = "act" if half == 0 else "dve"
                            if eng == "act":
                                op("act", lambda e: e.copy(out=hT[:, half * 4:half * 4 + 4, j * 128:(j + 1) * 128], in_=tp[k][:, 0:512].rearrange("p (q t) -> p q t", q=4)), reads=[rtp], writes=[rhT])
                            else:
                                op("dve", lambda e: e.tensor_copy(out=hT[:, half * 4:half * 4 + 4, j * 128:(j + 1) * 128], in_=tp[k][:, 0:512].rearrange("p (q t) -> p q t", q=4)), reads=[rtp], writes=[rhT])
                    if P1CUT <= 1:
                        continue
                    for c in range(3):
                        X, rX = proj(C_QA + 128 * c, 128)
                        normrope(X, rX, 128, gsm[:, l, 8:9], "ones64", 1.0 / 64, "R64", 0, qT_d[c, :, tsl], rs("qT_d"))
                    X, rX = proj(C_KA2, 128)
                    normrope(X, rX, 128, gsm[:, l, 9:10], "ones64", 1.0 / 64, "R64", 0, kT2_d[:, tsl], rs("kT2_d"))
                    if P1CUT <= 2:
                        continue
                    for c in range(4):
                        X, rX = proj(C_IQ2 + 128 * c, 128)
                        normrope(X, rX, 128, None, None, None, "R32", 2, iqT_d[c, :, tsl], rs("iqT_d"))
                    X, rX = proj(C_IK4, 128)
                    normrope(X, rX, 128, None, None, None, "R32", 2, ikT_d[:, tsl], rs("ikT_d"))
                    if P1CUT <= 3:
                        continue
                    for c in range(3):
                        X, rX = proj(C_GA + 128 * c, 128)
                        silu_out(X, rX, 128, [(0, 64, gaT_d[2 * c, :, tsl], rs("gaT_d")), (64, 128, gaT_d[2 * c + 1, :, tsl], rs("gaT_d"))])
                    for c in range(2):
                        X, rX = proj(C_GB + 128 * c, 128)
                        silu_out(X, rX, 128, [(0, 128, gbT_d[c, :, tsl], rs("gbT_d"))])
                    for c in range(3):
                        X, rX = proj(C_GC + 128 * c, 128)
                        silu_out(X, rX, 128, [(0, 64, gcT_d[2 * c, :, tsl], rs("gcT_d")), (64, 128, gcT_d[2 * c + 1, :, tsl], rs("gcT_d"))])
                    if P1CUT <= 4:
                        continue
                    for c in range(2):
                        X, rX = proj(C_U + 128 * c, 128)
                        i = c
                        rup = rs("p1_up", i)
                        op("dve", lambda e: e.tensor_copy(out=up[i][:].rearrange("p s j -> p j s"), in_=X[:].rearrange("p (j s) -> p j s", s=8)), reads=[rX], writes=[rup])
                        for g8 in range(8):
                            g = 8 * c + g8
                            dma(U_d[g, :, :, tg * 64:(tg + 1) * 64].rearrange("s c j -> c s j"), up[i][16 * g8:16 * g8 + 16, :, :], reads=[rup], writes=[rs("U_d")])
                    if P1CUT <= 5:
                        continue
                    X0, rX0 = proj(C_CQ, 128)
                    X1, rX1 = proj(C_CQ + 128, 128)
                    i = wk[0] % NWK
                    wk[0] += 1
                    a0, ra0 = aux[(2 * i) % 3], rs("p1_aux", (2 * i) % 3)
                    rsq, rsd, rt1 = rs("p1_sq", i), rs("p1_sd", i), rs("p1_t1", i)
                    i2 = wk[0] % NWK
                    wk[0] += 1
                    rsq2 = rs("p1_sq", i2)
                    op("act", lambda e: e.activation(out=sq[i][:], in_=X0[:], func=AF.Square), reads=[rX0], writes=[rsq])
                    op("act", lambda e: e.activation(out=sq[i2][:], in_=X1[:], func=AF.Square), reads=[rX1], writes=[rsq2])
                    op("pe", lambda e: e.matmul(a0[:], lhsT=cb("ones128"), rhs=sq[i][:], start=True, stop=False), reads=[rsq] + rC, writes=[ra0])
                    op("pe", lambda e: e.matmul(a0[:], lhsT=cb("ones128"), rhs=sq[i2][:], start=False, stop=True), reads=[rsq2] + rC, writes=[ra0])
                    op("act", lambda e: e.activation(out=sd[i][:], in_=a0[:], func=AF.Ln, scale=1.0 / 256, bias=eps_ap), reads=[ra0] + rC, writes=[rsd])
                    op("act", lambda e: e.activation(out=sd[i][:], in_=sd[i][:], func=AF.Exp, scale=-0.5), reads=[rsd], writes=[rsd])
                    rcq = rs("p1_cqn")
                    op("dve", lambda e: e.tensor_tensor(out=cqn[:, 0, :], in0=X0[:], in1=sd[i][:], op=ALU.mult), reads=[rX0, rsd], writes=[rcq])
                    op("dve", lambda e: e.tensor_tensor(out=cqn[:, 1, :], in0=X1[:], in1=sd[i][:], op=ALU.mult), reads=[rX1, rsd], writes=[rcq])
                    X0, rX0 = proj(C_CKV, 128)
                    i = wk[0] % NWK
                    wk[0] += 1
                    a0, ra0 = aux[(2 * i) % 3], rs("p1_aux", (2 * i) % 3)
                    rsq, rsd = rs("p1_sq", i), rs("p1_sd", i)
                    op("act", lambda e: e.activation(out=sq[i][:], in_=X0[:], func=AF.Square), reads=[rX0], writes=[rsq])
                    op("pe", lambda e: e.matmul(a0[:], lhsT=cb("ones128"), rhs=sq[i][:], start=True, stop=True), reads=[rsq] + rC, writes=[ra0])
                    op("act", lambda e: e.activation(out=sd[i][:], in_=a0[:], func=AF.Ln, scale=1.0 / 128, bias=eps_ap), reads=[ra0] + rC, writes=[rsd])
                    op("act", lambda e: e.activation(out=sd[i][:], in_=sd[i][:], func=AF.Exp, scale=-0.5), reads=[rsd], writes=[rsd])
                    rckv = rs("p1_ckvn")
                    op("dve", lambda e: e.tensor_tensor(out=ckvn[:], in0=X0[:], in1=sd[i][:], op=ALU.mult), reads=[rX0, rsd], writes=[rckv])
                    if P1CUT <= 6:
                        continue
                    for h in range(6):
                        k = mmi[0] % 3
                        mmi[0] += 1
                        r = rs("p1_mm", k)
                        for c in range(2):
                            op("pe", lambda e: e.matmul(mm[k][0:96, :], lhsT=Wuq[:, c, h * 96:(h + 1) * 96], rhs=cqn[:, c, :], start=(c == 0), stop=(c == 1)),
                               reads=[rWsm, rcq], writes=[r])
                        normrope(mm[k], r, 96, gsm[0:96, l, 13:14], "ones96", 1.0 / 96, "R96", 4, qc_d[h, :, tsl], rs("qc_d"))
                    for h in range(6):
                        k = mmi[0] % 3
                        mmi[0] += 1
                        r = rs("p1_mm", k)
                        op("pe", lambda e: e.matmul(mm[k][0:96, :], lhsT=WukT[:, h, :], rhs=ckvn[:], start=True, stop=False), reads=[rWsm, rckv], writes=[r])
                        for kc in range(8):
                            op("pe", lambda e: e.matmul(mm[k][0:96, :], lhsT=Winp[:, kc, C_KPE:C_KPE + 96], rhs=hT[:, kc, :], start=False, stop=(kc == 7)),
                               reads=[rW, rhT], writes=[r])
                        normrope(mm[k], r, 96, gsm[0:96, l, 14:15], "ones96", 1.0 / 96, "R96", 4, kc_d[h, :, tsl], rs("kc_d"))
                    if P1CUT <= 7:
                        continue
                    for j in range(4):
                        tb = tg * 4 + j
                        k = mmi[0] % 3
                        mmi[0] += 1
                        r = rs("p1_mm", k)
                        op("pe", lambda e: e.matmul(mm[k][:, 0:384], lhsT=ckvn[:, j * 128:(j + 1) * 128], rhs=Wuv[:], start=True, stop=True), reads=[rWsm, rckv], writes=[r])
                        i = j % 2
                        rv = rs("p1_vcs", i)
                        op("act", lambda e: e.copy(out=vcs[i][:, :, 0:64], in_=mm[k][:, 0:384].rearrange("p (h d) -> p h d", h=6)), reads=[r], writes=[rv])
                        for h in range(6):
                            dma(vc_d[h, tb, :, :], vcs[i][:, h, :], reads=[rv], writes=[rs("vc_d")])
                        k = mmi[0] % 3
                        mmi[0] += 1
                        r = rs("p1_mm", k)
                        for kc in range(8):
                            op("pe", lambda e: e.matmul(mm[k][:, 0:72], lhsT=hT[:, kc, j * 128:(j + 1) * 128], rhs=Winp[:, kc, C_VAIW:C_VAIW + 72], start=(kc == 0), stop=(kc == 7)),
                               reads=[rW, rhT], writes=[r])
                        rva, riw = rs("p1_vas", i), rs("p1_iws", i)
                        op("act", lambda e: e.copy(out=vas[i][:, 0:64], in_=mm[k][:, 0:64]), reads=[r], writes=[rva])
                        dma(va_d[tb, :, :], vas[i][:], reads=[rva], writes=[rs("va_d")])
                        sc_ = (8.0 ** -0.5) * (32.0 ** -0.5)
                        op("act", lambda e: e.activation(out=iws[i][:, 0:8], in_=mm[k][:, 64:72], func=AF.Abs, scale=sc_), reads=[r], writes=[riw])
                        op("act", lambda e: e.activation(out=iws[i][:, 8:16], in_=mm[k][:, 64:72], func=AF.Sign), reads=[r], writes=[riw])
                        dma(iw_d[tb, :, :], iws[i][:], reads=[riw], writes=[rs("iw_d")])
            S_.barrier()

        def att_tiles(st, n_ops=2):
            T = {}
            T["att"] = [pst(st, f"at_ps{i}", [128, 512], F32) for i in range(2)]
            T["Ops"] = [pst(st, f"at_o{i}", [128, 512], F32) for i in range(n_ops)]
            T["sbp"] = pst(st, "at_sb", [128, 512], F32)
            T["PT"] = [sbt(st, f"at_pt{i}", [128, 512], BF16) for i in range(3)]
            T["Osb"] = [sbt(st, f"at_osb{i}", [65, 6, 512], F32) for i in range(2)]
            T["rcp"] = [sbt(st, f"at_rcp{i}", [64, 512], F32) for i in range(2)]
            T["yb"] = [sbt(st, f"at_yb{i}", [64, 512], BF16) for i in range(2)]
            T["gt"] = [sbt(st, f"at_gt{i}", [64, 6, 512], BF16) for i in range(2)]
            T["cnt"] = [0, 0, 0]
            return T

        def att_main(T, sb, n_heads, scale, heads, qfn, qres, nm_fn):
            att, Ops, PT, Osb = T["att"], T["Ops"], T["PT"], T["Osb"]
            cnt = T["cnt"]
            ob = sb % 2
            nkb = 4 * sb + 4
            rosb = rs("at_osb", ob)
            n_ops = len(Ops)
            steps = [(h, kb) for h in range(n_heads) for kb in range(nkb)]
            pend = None
            ois = {}
            for stp in steps + [None]:
                cur = None
                if stp is not None:
                    h, kb = stp
                    H = heads[h]
                    if kb == 0:
                        ois[h] = cnt[1] % n_ops
                        cnt[1] += 1
                    qb0 = max(kb - 4 * sb, 0)
                    qs = slice(qb0 * 128, 512)
                    ai = cnt[0] % 2
                    pi = cnt[0] % 3
                    cnt[0] += 1
                    ra, rp = rs("at_ps", ai), rs("at_pt", pi)
                    masks = []
                    for qb in range(qb0, 4):
                        m = nm_fn(4 * sb + qb, kb)
                        if m is not None:
                            masks.append((qb, m))
                    op("pe", lambda e: e.matmul(att[ai][:, qs], lhsT=H["kT"](slice(kb * 128, kb * 128 + 128)), rhs=qfn(h, qs), start=True, stop=(len(masks) == 0)),
                       reads=[H["kres"], qres], writes=[ra])
                    for mi, (qb, (map_, mres)) in enumerate(masks):
                        op("pe", lambda e: e.matmul(att[ai][:, qb * 128:(qb + 1) * 128], lhsT=map_, rhs=cb("ident"), start=False, stop=(mi == len(masks) - 1)),
                           reads=[mres] + rC, writes=[ra])
                    op("act", lambda e: e.activation(out=PT[pi][:, qs], in_=att[ai][:, qs], func=AF.Exp, scale=scale), reads=[ra], writes=[rp])
                    cur = (h, kb, pi, qs)
                if pend is not None:
                    h2, kb2, pi2, qs2 = pend
                    H2 = heads[h2]
                    oi = ois[h2]
                    rO = rs("at_o", oi)
                    op("pe", lambda e: e.matmul(Ops[oi][0:65, qs2], lhsT=H2["vaug"][:, kb2, :], rhs=PT[pi2][:, qs2], start=(kb2 == 0), stop=(kb2 == nkb - 1)),
                       reads=[rs("at_pt", pi2), H2["vres"]], writes=[rO])
                    if kb2 == nkb - 1:
                        op("act", lambda e: e.copy(out=Osb[ob][:, h2, :], in_=Ops[oi][0:65, :]), reads=[rO], writes=[rosb])
                pend = cur

        def att_norm(T, sb, n_heads, gate_d, gate_res, mix_base):
            Osb, sbp, rcp, yb, gt = T["Osb"], T["sbp"], T["rcp"], T["yb"], T["gt"]
            cnt = T["cnt"]
            ob = sb % 2
            rosb, rgt, rsb = rs("at_osb", ob), rs("at_gt", ob), rs("at_sbp")
            ssl = slice(sb * 512, (sb + 1) * 512)
            dma(gt[ob][:], gate_d[:, :, ssl].rearrange("h p t -> p h t"), reads=[rs(gate_res)], writes=[rgt])
            for h in range(n_heads):
                ri = cnt[2] % 2
                cnt[2] += 1
                rrc, ryb = rs("at_rcp", ri), rs("at_yb", ri)
                op("pe", lambda e: e.matmul(sbp[0:64, :], lhsT=cfa("e65", 65, 64), rhs=Osb[ob][:, h, :], start=True, stop=True), reads=[rosb] + rC, writes=[rsb])
                op("act", lambda e: e.activation(out=rcp[ri][:], in_=sbp[0:64, :], func=AF.Ln), reads=[rsb], writes=[rrc])
                op("act", lambda e: e.activation(out=rcp[ri][:], in_=rcp[ri][:], func=AF.Exp, scale=-1.0), reads=[rrc], writes=[rrc])
                op("dve", lambda e: e.tensor_tensor(out=rcp[ri][:], in0=rcp[ri][:], in1=Osb[ob][0:64, h, :], op=ALU.mult), reads=[rrc, rosb], writes=[rrc])
                op(P1ENG, lambda e: e.tensor_tensor(out=yb[ri][:], in0=rcp[ri][:], in1=gt[ob][:, h, :], op=ALU.mult), reads=[rrc, rgt], writes=[ryb])
                dma(mix_d[mix_base + h, 0:64, ssl], yb[ri][:], reads=[ryb], writes=[rs("mix_d")])

        def dsa_phase():
            with contextlib.ExitStack() as st:
                kT2 = sbt(st, "ds_kT2", [128, S], BF16)
                vaug = sbt(st, "ds_vaug", [128, NB, 65], BF16)
                ikT = sbt(st, "ds_ikT", [128, S], BF16)
                rk, rv, rik = rs("ds_kT2"), rs("ds_vaug"), rs("ds_ikT")
                dma(kT2[:], kT2_d, reads=[rs("kT2_d")], writes=[rk])
                dma(vaug[:], va_d.rearrange("t p c -> p t c"), reads=[rs("va_d")], writes=[rv])
                dma(ikT[:], ikT_d, reads=[rs("ikT_d")], writes=[rik])
                NM = [sbt(st, f"ds_NM{i}", [128, 4, S], BF16) for i in range(2)]
                sc = [sbt(st, f"ds_sc{i}", [128, S], F32) for i in range(2)]
                junk = sbt(st, "ds_junk", [128, S], BF16)
                iq = [sbt(st, f"ds_iq{i}", [128, 4, 128], BF16) for i in range(2)]
                iw = [sbt(st, f"ds_iw{i}", [128, 16], F32) for i in range(2)]
                Dh = [sbt(st, f"ds_Dh{i}", [128, 8, 128], BF16) for i in range(2)]
                Th = [sbt(st, f"ds_Th{i}", [128, 512], BF16) for i in range(3)]
                bs = [sbt(st, f"ds_bs{i}", [128, 8 + NIT + 1], F32) for i in range(2)]
                shp = [pst(st, f"ds_shp{i}", [128, 512], F32) for i in range(2)]
                scp = [pst(st, f"ds_scp{i}", [128, 512], F32) for i in range(2)]
                T = att_tiles(st, n_ops=1)
                qs_t = [sbt(st, f"ds_q{i}", [128, 3, 512], BF16) for i in range(2)]
                c3 = [0, 0]

                def masks(sb):
                    for qb in range(4):
                        b = 4 * sb + qb
                        n = 128 * (b + 1)
                        if n <= KTOP:
                            continue
                        i = b % 2
                        rsc, riq, riw, rDh, rbs = rs("ds_sc", i), rs("ds_iq", i), rs("ds_iw", i), rs("ds_Dh", i), rs("ds_bs", i)
                        rNM = rs("ds_NM", sb % 2)
                        dma(iq[i][:], iqT_d[:, :, b * 128:(b + 1) * 128].rearrange("c p t -> p c t"), reads=[rs("iqT_d")], writes=[riq])
                        dma(iw[i][:], iw_d[b, :, :], reads=[rs("iw_d")], writes=[riw])
                        for h in range(8):
                            op("dve", lambda e: e.tensor_scalar(out=Dh[i][:, h, :], in0=cb("ident"), scalar1=iw[i][:, 8 + h:9 + h], scalar2=None, op0=ALU.mult),
                               reads=[riw] + rC, writes=[rDh])
                        chunks = [(c * 512, min(512, n - c * 512)) for c in range((n + 511) // 512)]
                        isteps = [(ci, h) for ci in range(len(chunks)) for h in range(8)]
                        ipend = None
                        for ist in isteps + [None]:
                            icur = None
                            if ist is not None:
                                ci, h = ist
                                k0, wc = chunks[ci]
                                hi_ = c3[0] % 2
                                ti_ = c3[0] % 3
                                c3[0] += 1
                                rsh, rth = rs("ds_shp", hi_), rs("ds_Th", ti_)
                                pb = 64 * (h % 2)
                                op("pe", lambda e: e.matmul(shp[hi_][:, 0:wc], lhsT=iq[i][pb:pb + 32, h // 2, :], rhs=ikT[pb:pb + 32, k0:k0 + wc], start=True, stop=True),
                                   reads=[riq, rik], writes=[rsh])
                                op("act", lambda e: e.activation(out=Th[ti_][:, 0:wc], in_=shp[hi_][:, 0:wc], func=AF.Relu, scale=iw[i][:, h:h + 1]), reads=[rsh, riw], writes=[rth])
                                if h == 0:
                                    c3[1] += 1
                                icur = (ci, h, ti_, c3[1] % 2)
                            if ipend is not None:
                                ci2, h2, ti2, si = ipend
                                k0, wc = chunks[ci2]
                                rscp = rs("ds_scp", si)
                                op("pe", lambda e: e.matmul(scp[si][:, 0:wc], lhsT=Dh[i][:, h2, :], rhs=Th[ti2][:, 0:wc], start=(h2 == 0), stop=(h2 == 7)), reads=[rs("ds_Th", ti2), rDh], writes=[rscp])
                                if h2 == 7:
                                    op("act", lambda e: e.copy(out=sc[i][:, k0:k0 + wc], in_=scp[si][:, 0:wc]), reads=[rscp], writes=[rsc])
                            ipend = icur
                        lo, hi, mid, cn, tt, rng = (bs[i][:, k:k + 1] for k in range(6))
                        Hc = lambda it: bs[i][:, 8 + it:9 + it]
                        op("dve", lambda e: e.tensor_reduce(out=hi, in_=sc[i][:, 0:n], axis=AX.X, op=ALU.max), reads=[rsc], writes=[rbs])
                        op("dve", lambda e: e.tensor_reduce(out=lo, in_=sc[i][:, 0:n], axis=AX.X, op=ALU.min), reads=[rsc], writes=[rbs])
                        op("dve", lambda e: e.scalar_tensor_tensor(out=rng, in0=hi, scalar=1.0, in1=lo, op0=ALU.add, op1=ALU.subtract), reads=[rbs], writes=[rbs])
                        op("dve", lambda e: e.tensor_scalar(out=bs[i][:, 8:8 + NIT + 1], in0=cf[:, CF["pw"]:CF["pw"] + NIT + 1], scalar1=rng, scalar2=None, op0=ALU.mult), reads=[rbs] + rC, writes=[rbs])
                        op("dve", lambda e: e.tensor_tensor(out=mid, in0=lo, in1=Hc(0), op=ALU.add), reads=[rbs], writes=[rbs])
                        op("dve", lambda e: e.memset(sc[i][0:64, n - 64:n], -1e30), writes=[rsc])
                        for it in range(NIT):
                            op("dve", lambda e: e.tensor_scalar(out=junk[:, 0:n], in0=sc[i][:, 0:n], scalar1=mid, scalar2=None, op0=ALU.is_ge, op1=ALU.add, accum_out=cn),
                               reads=[rsc, rbs], writes=[rbs, rs("ds_junk")])
                            op("dve", lambda e: e.tensor_scalar(out=tt, in0=cn, scalar1=float(KTOP) - 0.5, scalar2=-0.5, op0=ALU.is_ge, op1=ALU.add), reads=[rbs], writes=[rbs])
                            op("dve", lambda e: e.scalar_tensor_tensor(out=mid, in0=tt, scalar=Hc(it), in1=mid, op0=ALU.mult, op1=ALU.add), reads=[rbs], writes=[rbs])
                        op("dve", lambda e: e.tensor_tensor(out=lo, in0=mid, in1=Hc(NIT), op=ALU.subtract), reads=[rbs], writes=[rbs])
                        op("dve", lambda e: e.tensor_scalar(out=NM[sb % 2][:, qb, 0:n], in0=sc[i][:, 0:n], scalar1=lo, scalar2=NEGM, op0=ALU.is_lt, op1=ALU.mult), reads=[rsc, rbs], writes=[rNM])

                def nm_fn(b, kb):
                    n = 128 * (b + 1)
                    if n <= KTOP:
                        if kb == b:
                            return (cb("nmd"), rs("cbf"))
                        return None
                    return (NM[(b // 4) % 2][:, b % 4, kb * 128:(kb + 1) * 128], rs("ds_NM", (b // 4) % 2))

                heads = [dict(kT=(lambda ks, pb=64 * (h % 2): kT2[pb:pb + 64, ks]), kres=rk, vaug=vaug, vres=rv) for h in range(6)]
                if DSA_MODE != 2:
                    masks(0)
                for sb in range(NG):
                    if sb + 1 < NG and DSA_MODE != 2:
                        masks(sb + 1)
                    if DSA_MODE == 1:
                        continue
                    i = sb % 2
                    rq = rs("ds_q", i)
                    dma(qs_t[i][:], qT_d[:, :, sb * 512:(sb + 1) * 512].rearrange("c p t -> p c t"), reads=[rs("qT_d")], writes=[rq])
                    qfn = lambda h, qs, i=i: qs_t[i][64 * (h % 2):64 * (h % 2) + 64, h // 2, qs]
                    att_main(T, sb, 6, 64.0 ** -0.5, heads, qfn, rq, nm_fn)
                    att_norm(T, sb, 6, gaT_d, "gaT_d", 0)
            S_.barrier()

        def mla_phase():
            with contextlib.ExitStack() as st:
                kT = sbt(st, "ml_kT", [96, 6, S], BF16)
                va = sbt(st, "ml_va", [128, 6, NB, 65], BF16)
                rk, rv = rs("ml_kT"), rs("ml_va")
                for h in range(6):
                    dma(kT[:, h, :], kc_d[h], reads=[rs("kc_d")], writes=[rk])
                    dma(va[:, h, :, :], vc_d[h].rearrange("t p c -> p t c"), reads=[rs("vc_d")], writes=[rv])
                T = att_tiles(st)
                qs_t = [sbt(st, f"ml_q{i}", [96, 6, 512], BF16) for i in range(2)]

                def nm_fn(b, kb):
                    if kb == b:
                        return (cb("nmd"), rs("cbf"))
                    return None
                heads = [dict(kT=(lambda ks, h=h: kT[:, h, ks]), kres=rk, vaug=va[:, h, :, :], vres=rv) for h in range(6)]
                for sb in range(NG):
                    i = sb % 2
                    rq = rs("ml_q", i)
                    dma(qs_t[i][:], qc_d[:, :, sb * 512:(sb + 1) * 512].rearrange("h p t -> p h t"), reads=[rs("qc_d")], writes=[rq])
                    qfn = lambda h, qs, i=i: qs_t[i][:, h, qs]
                    att_main(T, sb, 6, 96.0 ** -0.5, heads, qfn, rq, nm_fn)
                    att_norm(T, sb, 6, gcT_d, "gcT_d", 8)
            S_.barrier()

        def s5_consts(l):
            with contextlib.ExitStack() as st:
                ar = sbt(st, "s5_ar", [128, 16], F32)
                ai = sbt(st, "s5_ai", [128, 16], F32)
                stp = sbt(st, "s5_stp", [128, 16], F32)
                rp = rs("s5_par")
                for hh in range(2):
                    dma(ar[64 * hh:64 * hh + 64, :], a_re[l].rearrange("g p -> p g"), writes=[rp], allow_slow_non_contiguous=True)
                    dma(ai[64 * hh:64 * hh + 64, :], a_im[l].rearrange("g p -> p g"), writes=[rp], allow_slow_non_contiguous=True)
                dma(stp[:], log_step[l:l + 1, :].broadcast_to([128, 16]), writes=[rp])
                W = sbt(st, "s5_w", [128, 24, 16], F32)
                rw = rs("s5_w")

                def wv(k):
                    return W[:, k, :]

                def dv(fn, reads=(), writes=()):
                    op("dve", fn, reads=list(reads) + [rp, rw] + rC, writes=list(writes) + [rw])
                TWO_PI = 2.0 * np.pi

                def sincos(dst, ang, shift):
                    t, n_, m = wv(20), wv(21), wv(22)
                    ti = Wi[:, 0, :]
                    dv(lambda e: e.tensor_scalar(out=t, in0=ang, scalar1=shift, scalar2=1.0 / TWO_PI, op0=ALU.add, op1=ALU.mult))
                    dv(lambda e: e.tensor_copy(out=ti, in_=t))
                    dv(lambda e: e.tensor_copy(out=n_, in_=ti))
                    dv(lambda e: e.tensor_tensor(out=t, in0=t, in1=n_, op=ALU.subtract))
                    dv(lambda e: e.tensor_scalar(out=m, in0=t, scalar1=0.5, scalar2=None, op0=ALU.is_gt))
                    dv(lambda e: e.tensor_tensor(out=t, in0=t, in1=m, op=ALU.subtract))
                    dv(lambda e: e.tensor_scalar(out=m, in0=t, scalar1=-0.5, scalar2=None, op0=ALU.is_lt))
                    dv(lambda e: e.tensor_tensor(out=t, in0=t, in1=m, op=ALU.add))
                    op("act", lambda e: e.activation(out=dst, in_=t, func=AF.Sin, scale=TWO_PI), reads=[rw], writes=[rw])
                Wi = sbt(st, "s5_wi", [128, 1, 16], mybir.dt.int32)
                op("act", lambda e: e.activation(out=stp[:], in_=stp[:], func=AF.Exp), reads=[rp], writes=[rp])
                mag, ang, abr, abi, cs, sn = wv(0), wv(1), wv(2), wv(3), wv(4), wv(5)
                dv(lambda e: e.tensor_tensor(out=mag, in0=ar[:], in1=stp[:], op=ALU.mult))
                op("act", lambda e: e.activation(out=mag, in_=mag, func=AF.Exp), reads=[rw], writes=[rw])
                dv(lambda e: e.tensor_tensor(out=ang, in0=ai[:], in1=stp[:], op=ALU.mult))
                sincos(sn, ang, 0.0)
                sincos(cs, ang, np.pi / 2)
                dv(lambda e: e.tensor_tensor(out=abr, in0=mag, in1=cs, op=ALU.mult))
                dv(lambda e: e.tensor_tensor(out=abi, in0=mag, in1=sn, op=ALU.mult))
                den, nr, fr, fi, tA, tB = wv(6), wv(7), wv(8), wv(9), wv(10), wv(11)
                dv(lambda e: e.tensor_tensor(out=den, in0=ar[:], in1=ar[:], op=ALU.mult))
                dv(lambda e: e.tensor_tensor(out=tA, in0=ai[:], in1=ai[:], op=ALU.mult))
                dv(lambda e: e.tensor_tensor(out=den, in0=den, in1=tA, op=ALU.add))
                dv(lambda e: e.reciprocal(out=den, in_=den))
                dv(lambda e: e.tensor_scalar(out=nr, in0=abr, scalar1=-1.0, scalar2=None, op0=ALU.add))
                dv(lambda e: e.tensor_tensor(out=fr, in0=nr, in1=ar[:], op=ALU.mult))
                dv(lambda e: e.tensor_tensor(out=tA, in0=abi, in1=ai[:], op=ALU.mult))
                dv(lambda e: e.tensor_tensor(out=fr, in0=fr, in1=tA, op=ALU.add))
                dv(lambda e: e.tensor_tensor(out=fr, in0=fr, in1=den, op=ALU.mult))
                dv(lambda e: e.tensor_tensor(out=fi, in0=abi, in1=ar[:], op=ALU.mult))
                dv(lambda e: e.tensor_tensor(out=tA, in0=nr, in1=ai[:], op=ALU.mult))
                dv(lambda e: e.tensor_tensor(out=fi, in0=fi, in1=tA, op=ALU.subtract))
                dv(lambda e: e.tensor_tensor(out=fi, in0=fi, in1=den, op=ALU.mult))
                PW = sbt(st, "s5_pw", [128, 9, 2, 16], F32)
                MKp = sbt(st, "s5_mkp", [128, 9, 2, 16], F32)
                FN = sbt(st, "s5_fn", [128, 8, 2, 16], F32)
                CQc = sbt(st, "s5_cqc", [128, 9, 2, 16], F32)

                def cmul(o_re, o_im, x_re, x_im, y_re, y_im):
                    dv(lambda e: e.tensor_tensor(out=tA, in0=x_re, in1=y_re, op=ALU.mult))
                    dv(lambda e: e.tensor_tensor(out=tB, in0=x_im, in1=y_im, op=ALU.mult))
                    dv(lambda e: e.tensor_tensor(out=wv(12), in0=x_re, in1=y_im, op=ALU.mult))
                    dv(lambda e: e.tensor_tensor(out=wv(13), in0=x_im, in1=y_re, op=ALU.mult))
                    dv(lambda e: e.tensor_tensor(out=o_re, in0=tA, in1=tB, op=ALU.subtract))
                    dv(lambda e: e.tensor_tensor(out=o_im, in0=wv(12), in1=wv(13), op=ALU.add))
                dv(lambda e: e.memset(PW[:, 0, 0, :], 1.0))
                dv(lambda e: e.memset(PW[:, 0, 1, :], 0.0))
                for n_ in range(1, 9):
                    cmul(PW[:, n_, 0, :], PW[:, n_, 1, :], PW[:, n_ - 1, 0, :], PW[:, n_ - 1, 1, :], abr, abi)
                dv(lambda e: e.tensor_copy(out=MKp[:, 0, :, :], in_=PW[:, 8, :, :]))
                for k in range(1, 9):
                    cmul(MKp[:, k, 0, :], MKp[:, k, 1, :], MKp[:, k - 1, 0, :], MKp[:, k - 1, 1, :], MKp[:, k - 1, 0, :], MKp[:, k - 1, 1, :])
                sgB = cf[:, CF["sgB"]:CF["sgB"] + 1]
                sgQ = cf[:, CF["sgQ"]:CF["sgQ"] + 1]
                for n_ in range(8):
                    cmul(FN[:, n_, 0, :], FN[:, n_, 1, :], PW[:, n_, 0, :], PW[:, n_, 1, :], fr, fi)
                    dv(lambda e: e.tensor_scalar(out=FN[:, n_, 1, :], in0=FN[:, n_, 1, :], scalar1=sgB, scalar2=None, op0=ALU.mult))
                for n_ in range(9):
                    dv(lambda e: e.tensor_scalar(out=CQc[:, n_, 0, :], in0=PW[:, n_, 0, :], scalar1=sgQ, scalar2=None, op0=ALU.mult))
                    dv(lambda e: e.tensor_scalar(out=CQc[:, n_, 1, :], in0=PW[:, n_, 1, :], scalar1=-1.0, scalar2=None, op0=ALU.mult))
                for k in range(9):
                    dv(lambda e: e.tensor_scalar(out=MKp[:, k, 1, :], in0=MKp[:, k, 1, :], scalar1=sgQ, scalar2=None, op0=ALU.mult))
                bx = [sbt(st, f"s5_bx{i}", [128, 16], F32) for i in range(2)]
                by = [sbt(st, f"s5_by{i}", [128, 16], F32) for i in range(2)]
                cx = [sbt(st, f"s5_cx{i}", [128, 16], F32) for i in range(2)]
                cy = [sbt(st, f"s5_cy{i}", [128, 16], F32) for i in range(2)]
                dr = [sbt(st, f"s5_dr{i}", [128, 1], F32) for i in range(2)]
                Pm = [sbt(st, f"s5_Pm{i}", [128, 128], F32) for i in range(2)]
                CQ = [sbt(st, f"s5_CQ{i}", [128, 9, 16], F32) for i in range(2)]
                Btr = [sbt(st, f"s5_Btr{i}", [128, 8, 16], F32) for i in range(2)]
                KTp = [sbt(st, f"s5_KTp{i}", [128, 15 * 16], F32) for i in range(2)]
                OUTm = [sbt(st, f"s5_OUT{i}", [128, 12, 128], F32) for i in range(2)]
                pp = [pst(st, f"s5_pp{i}", [128, 128], F32) for i in range(2)]
                for i in range(2):
                    op("pool", lambda e: e.memset(KTp[i][:], 0.0), writes=[rs("s5_KTp", i)])
                for g in range(16):
                    i = g % 2
                    rin, rPm, rCQ, rBt, rKT, rOUT, rpp = rs("s5_in", i), rs("s5_Pm", i), rs("s5_CQ", i), rs("s5_Btr", i), rs("s5_KTp", i), rs("s5_OUT", i), rs("s5_pp", i)
                    dma(bx[i][0:64, :], b_re[l, g], writes=[rin])
                    dma(bx[i][64:128, :], b_im[l, g], writes=[rin])
                    dma(by[i][0:64, :], b_im[l, g], writes=[rin])
                    dma(by[i][64:128, :], b_re[l, g], writes=[rin])
                    dma(cx[i][0:64, :], c_re[l, g].rearrange("c p -> p c"), writes=[rin], allow_slow_non_contiguous=True)
                    dma(cx[i][64:128, :], c_im[l, g].rearrange("c p -> p c"), writes=[rin], allow_slow_non_contiguous=True)
                    dma(cy[i][0:64, :], c_im[l, g].rearrange("c p -> p c"), writes=[rin], allow_slow_non_contiguous=True)
                    dma(cy[i][64:128, :], c_re[l, g].rearrange("c p -> p c"), writes=[rin], allow_slow_non_contiguous=True)
                    for s_ in range(8):
                        dma(dr[i][16 * s_:16 * s_ + 16, :], d_skip[l, g].rearrange("(c o) -> c o", o=1), writes=[rin], allow_slow_non_contiguous=True)
                    for s_ in range(8):
                        n_ = 7 - s_
                        op("dve", lambda e: e.tensor_scalar(out=Pm[i][:, 16 * s_:16 * s_ + 16], in0=bx[i][:], scalar1=FN[:, n_, 0, g:g + 1], scalar2=None, op0=ALU.mult), reads=[rin, rw], writes=[rPm])
                        op("dve", lambda e: e.scalar_tensor_tensor(out=Pm[i][:, 16 * s_:16 * s_ + 16], in0=by[i][:], scalar=FN[:, n_, 1, g:g + 1], in1=Pm[i][:, 16 * s_:16 * s_ + 16], op0=ALU.mult, op1=ALU.add),
                           reads=[rin, rw], writes=[rPm])
                    for s_ in range(8):
                        op("dve", lambda e: e.tensor_copy(out=Btr[i][:, s_, :], in_=Pm[i][:, 112:128]), reads=[rPm], writes=[rBt])
                    for n_ in range(9):
                        op("dve", lambda e: e.tensor_scalar(out=CQ[i][:, n_, :], in0=cx[i][:], scalar1=CQc[:, n_, 0, g:g + 1], scalar2=None, op0=ALU.mult), reads=[rin, rw], writes=[rCQ])
                        op("dve", lambda e: e.scalar_tensor_tensor(out=CQ[i][:, n_, :], in0=cy[i][:], scalar=CQc[:, n_, 1, g:g + 1], in1=CQ[i][:, n_, :], op0=ALU.mult, op1=ALU.add),
                           reads=[rin, rw], writes=[rCQ])
                    op("pe", lambda e: e.transpose(out=pp[i][:], in_=Pm[i][:], identity=cfa("ident")), reads=[rPm] + rC, writes=[rpp])
                    op("act", lambda e: e.copy(out=OUTm[i][:, 0, :], in_=pp[i][:]), reads=[rpp], writes=[rOUT])
                    op("dve", lambda e: e.tensor_copy(out=OUTm[i][:, 1, :], in_=CQ[i][:, 1:9, :].rearrange("p n c -> p (n c)")), reads=[rCQ], writes=[rOUT])
                    op("pe", lambda e: e.matmul(pp[i][:], lhsT=Btr[i][:].rearrange("p s c -> p (s c)"), rhs=CQ[i][:, 0:8, :].rearrange("p n c -> p (n c)"), start=True, stop=True),
                       reads=[rBt, rCQ], writes=[rpp])
                    op("act", lambda e: e.copy(out=KTp[i][:, 112:240], in_=pp[i][:]), reads=[rpp], writes=[rKT])
                    op("dve", lambda e: e.scalar_tensor_tensor(out=KTp[i][:, 112:128], in0=cfa("i16", 128, 16), scalar=dr[i][:, 0:1], in1=KTp[i][:, 112:128], op0=ALU.mult, op1=ALU.add),
                       reads=[rin, rKT] + rC, writes=[rKT])
                    for s_ in range(8):
                        dma(s5c_d[g, 16 * s_:16 * s_ + 16, 2, :], KTp[i][16 * s_:16 * s_ + 16, (7 - s_) * 16:(7 - s_) * 16 + 128], reads=[rKT], writes=[rs("s5c_d")])
                    for k in range(9):
                        op("dve", lambda e: e.tensor_scalar(out=OUTm[i][:, 3 + k, :], in0=cfa("ident"), scalar1=MKp[:, k, 0, g:g + 1], scalar2=None, op0=ALU.mult), reads=[rw] + rC, writes=[rOUT])
                        op("dve", lambda e: e.scalar_tensor_tensor(out=OUTm[i][:, 3 + k, :], in0=cfa("icross"), scalar=MKp[:, k, 1, g:g + 1], in1=OUTm[i][:, 3 + k, :], op0=ALU.mult, op1=ALU.add),
                           reads=[rw] + rC, writes=[rOUT])
                    dma(s5c_d[g, :, 0:2, :], OUTm[i][:, 0:2, :], reads=[rOUT], writes=[rs("s5c_d")])
                    dma(s5c_d[g, :, 3:12, :], OUTm[i][:, 3:12, :], reads=[rOUT], writes=[rs("s5c_d")])
            S_.barrier()

        def s5_phase():
            with contextlib.ExitStack() as st:
                NPAR = 4
                assert J <= 512
                Cm = [sbt(st, f"s5r_C{i}", [128, 12, 128], F32) for i in range(NPAR)]
                Ut = [sbt(st, f"s5r_U{i}", [128, J], F32) for i in range(NPAR)]
                St = [sbt(st, f"s5r_S{i}", [128, J], F32) for i in range(NPAR)]
                Yt = [sbt(st, f"s5r_Y{i}", [128, J], F32) for i in range(NPAR)]
                pk = [pst(st, f"s5r_p{i}", [128, 512], F32) for i in range(8)]
                pc = [0]

                def nextp():
                    k = pc[0] % 8
                    pc[0] += 1
                    return k, rs("s5r_p", k)
                for g0 in range(0, 16, NPAR):
                    gs = list(range(g0, g0 + NPAR))
                    R_ = {}
                    for g in gs:
                        i = g % NPAR
                        R_[g] = (rs("s5r_C", i), rs("s5r_U", i), rs("s5r_S", i), rs("s5r_Y", i))
                        rCm, rU, rSt, rY = R_[g]
                        dma(Cm[i][:], s5c_d[g], reads=[rs("s5c_d")], writes=[rCm])
                        for s_ in range(8):
                            dma(Ut[i][16 * s_:16 * s_ + 16, :], U_d[g, s_, :, :], reads=[rs("U_d")], writes=[rU])
                    for g in gs:
                        i = g % NPAR
                        rCm, rU, rSt, rY = R_[g]
                        k, rpk = nextp()
                        op("pe", lambda e: e.matmul(pk[k][:, 0:J], lhsT=Cm[i][:, 0, :], rhs=Ut[i][:, 0:J], start=True, stop=True), reads=[rCm, rU], writes=[rpk])
                        op("act", lambda e: e.copy(out=St[i][:, 0:J], in_=pk[k][:, 0:J]), reads=[rpk], writes=[rSt])
                    k_ = 0
                    while (1 << k_) < J:
                        sh = 1 << k_
                        wc = J - sh
                        prods = []
                        for g in gs:
                            i = g % NPAR
                            rCm, rU, rSt, rY = R_[g]
                            k, rpk = nextp()
                            op("pe", lambda e: e.matmul(pk[k][:, 0:wc], lhsT=Cm[i][:, 3 + k_, :], rhs=St[i][:, 0:wc], start=True, stop=True), reads=[rCm, rSt], writes=[rpk])
                            prods.append((g, k, rpk))
                        for n_, (g, k, rpk) in enumerate(prods):
                            i = g % NPAR
                            rSt = R_[g][2]
                            op("dve", lambda e: e.tensor_tensor(out=St[i][:, sh:J], in0=pk[k][:, 0:wc], in1=St[i][:, sh:J], op=ALU.add), reads=[rpk, rSt], writes=[rSt])
                        k_ += 1
                    for g in gs:
                        i = g % NPAR
                        rCm, rU, rSt, rY = R_[g]
                        k, rpk = nextp()
                        op("pe", lambda e: e.matmul(pk[k][:, 0:J], lhsT=Cm[i][:, 2, :], rhs=Ut[i][:, 0:J], start=True, stop=False), reads=[rCm, rU], writes=[rpk])
                        op("pe", lambda e: e.matmul(pk[k][:, 1:J], lhsT=Cm[i][:, 1, :], rhs=St[i][:, 0:J - 1], start=False, stop=True), reads=[rCm, rSt], writes=[rpk])
                        op("act", lambda e: e.copy(out=Yt[i][:, 0:J], in_=pk[k][:, 0:J]), reads=[rpk], writes=[rY])
                        for s_ in range(8):
                            dma(Y_d[g, s_, :, :], Yt[i][16 * s_:16 * s_ + 16, :], reads=[rY], writes=[rs("Y_d")])
            S_.barrier()
            with contextlib.ExitStack() as st:
                yp = sbt(st, "s5p_yp", [128, 2, 8, 64], F32)
                y = sbt(st, "s5p_y", [128, 2, 512], F32)
                t = sbt(st, "s5p_t", [128, 2, 512], F32)
                gl = sbt(st, "s5p_g", [128, 2, 512], F32)
                gb16 = sbt(st, "s5p_gb", [128, 2, 512], BF16)
                sg = sbt(st, "s5p_sg", [128, 2, 512], F32)
                gate = sbt(st, "s5p_gate", [128, 2, 512], BF16)
                ob = sbt(st, "s5p_ob", [128, 2, 512], BF16)
                zp = [pst(st, f"s5p_z{i}", [128, 512], F32) for i in range(2)]
                ryp, ry, rt_, rgl, rgb, rsg, rgate, rob = (rs("s5p_" + n_) for n_ in ("yp", "y", "t", "g", "gb", "sg", "gate", "ob"))
                for tg in range(NG):
                    tsl = slice(tg * 512, tg * 512 + 512)
                    for c in range(2):
                        for g8 in range(8):
                            dma(yp[16 * g8:16 * g8 + 16, c, :, :], Y_d[8 * c + g8, :, :, tg * 64:(tg + 1) * 64].rearrange("s c j -> c s j"), reads=[rs("Y_d")], writes=[ryp])
                        dma(gate[:, c, :], gbT_d[c, :, tsl], reads=[rs("gbT_d")], writes=[rgate])
                    for c in range(2):
                        op("act", lambda e: e.copy(out=y[:, c, :].rearrange("p (j s) -> p s j", s=8), in_=yp[:, c, :, :]), reads=[ryp], writes=[ry])
                        op("act", lambda e: e.activation(out=t[:, c, :], in_=y[:, c, :], func=AF.Square), reads=[ry], writes=[rt_])
                        op("dve", lambda e: e.tensor_scalar(out=t[:, c, :], in0=t[:, c, :], scalar1=0.044715, scalar2=1.0, op0=ALU.mult, op1=ALU.add), reads=[rt_], writes=[rt_])
                        op("dve", lambda e: e.tensor_tensor(out=t[:, c, :], in0=t[:, c, :], in1=y[:, c, :], op=ALU.mult), reads=[rt_, ry], writes=[rt_])
                        op("act", lambda e: e.activation(out=t[:, c, :], in_=t[:, c, :], func=AF.Sigmoid, scale=2.0 * 0.7978845608028654), reads=[rt_], writes=[rt_])
                        op("dve", lambda e: e.tensor_tensor(out=gl[:, c, :], in0=t[:, c, :], in1=y[:, c, :], op=ALU.mult), reads=[rt_, ry], writes=[rgl])
                        op("dve", lambda e: e.tensor_copy(out=gb16[:, c, :], in_=gl[:, c, :]), reads=[rgl], writes=[rgb])
                    for co in range(2):
                        rz = rs("s5p_z", co)
                        for c in range(2):
                            op("pe", lambda e: e.matmul(zp[co][:], lhsT=Wglu[:, c, co * 128:(co + 1) * 128], rhs=gb16[:, c, :], start=(c == 0), stop=(c == 1)), reads=[rWsm, rgb], writes=[rz])
                        op("act", lambda e: e.activation(out=sg[:, co, :], in_=zp[co][:], func=AF.Sigmoid), reads=[rz], writes=[rsg])
                        op("dve", lambda e: e.tensor_tensor(out=sg[:, co, :], in0=sg[:, co, :], in1=gl[:, co, :], op=ALU.mult), reads=[rsg, rgl], writes=[rsg])
                        op("dve", lambda e: e.tensor_tensor(out=ob[:, co, :], in0=sg[:, co, :], in1=gate[:, co, :], op=ALU.mult), reads=[rsg, rgate], writes=[rob])
                        dma(mix_d[6 + co, :, tsl], ob[:, co, :], reads=[rob], writes=[rs("mix_d")])
            S_.barrier()

        def phase3(src_ap, src_res, dst_ap, dst_res):
            with contextlib.ExitStack() as st:
                Wo = sbt(st, "p3_Wo", [128, 14, D], BF16)
                rWo = rs("p3_Wo")
                for c in range(14):
                    nr = 128 if c in (6, 7) else 64
                    dma(Wo[0:nr, c, :], wout_d[0:nr, c, :], reads=[rs("wout_d")], writes=[rWo])
                mx = [sbt(st, f"p3_mx{i}", [128, 14, 512], BF16) for i in range(2)]
                xt = [sbt(st, f"p3_xt{i}", [128, 4, D], F32) for i in range(2)]
                pp = [pst(st, f"p3_pp{i}", [128, 512], F32) for i in range(4)]
                pc = 0
                for tg in range(NG):
                    i = tg % 2
                    rmx, rxt = rs("p3_mx", i), rs("p3_xt", i)
                    tsl = slice(tg * 512, tg * 512 + 512)
                    for c in range(14):
                        nr = 128 if c in (6, 7) else 64
                        dma(mx[i][0:nr, c, :], mix_d[c, 0:nr, tsl], reads=[rs("mix_d")], writes=[rmx])
                    dma(xt[i][:], src_ap[tg * 512:(tg + 1) * 512, :].rearrange("(j p) d -> p j d", p=128), reads=[src_res], writes=[rxt])
                    for j in range(4):
                        for half in range(2):
                            k = pc % 4
                            pc += 1
                            rpp = rs("p3_pp", k)
                            for c in range(14):
                                nr = 128 if c in (6, 7) else 64
                                op("pe", lambda e: e.matmul(pp[k][:], lhsT=mx[i][0:nr, c, j * 128:(j + 1) * 128], rhs=Wo[0:nr, c, half * 512:(half + 1) * 512], start=(c == 0), stop=(c == 13)),
                                   reads=[rmx, rWo], writes=[rpp])
                            op("dve", lambda e: e.tensor_tensor(out=xt[i][:, j, half * 512:(half + 1) * 512], in0=pp[k][:], in1=xt[i][:, j, half * 512:(half + 1) * 512], op=ALU.add), reads=[rpp, rxt], writes=[rxt])
                    dma(dst_ap[tg * 512:(tg + 1) * 512, :].rearrange("(j p) d -> p j d", p=128), xt[i][:], reads=[rxt], writes=[dst_res])
            S_.barrier()

        for l in range(NL):
            if "prep" in PH:
                prep_weights(l)
            if "s5c" in PH:
                s5_consts(l)
            for s in range(NSEQ):
                src = x_in[s] if l == 0 else xs_d[s]
                dst = out_d[s] if l == NL - 1 else xs_d[s]
                if "p1" in PH:
                    phase1(l, src, rs("xs", s))
                if "dsa" in PH:
                    dsa_phase()
                if "s5" in PH:
                    s5_phase()
                if "mla" in PH:
                    mla_phase()
                if "p3" in PH:
                    phase3(src, rs("xs", s), dst, rs("out_d") if l == NL - 1 else rs("xs", s))
        S_.barrier()
    return nc


xs_all = None


def _build(S, NL, NSEQ, KTOP):
    global xs_all
    return build_program(S, NL, NSEQ, KTOP)


_CACHE = {}


def kernel(**inputs):
    x = np.ascontiguousarray(np.asarray(inputs["x"], dtype=np.float32))
    B, S, _ = x.shape
    ncores = 8
    NSEQ = B // ncores
    KTOP = min(256, S // 4)
    nc = build_program(S, DEPTH, NSEQ, KTOP)
    consts = make_consts(S)
    names = ["norm_g", "w_in", "attn_q_norm", "attn_k_norm", "mla_q_lora_norm", "mla_kv_lora_norm", "mla_w_uq", "mla_w_ukv",
             "mla_q_norm", "mla_k_norm", "ssm_a_re", "ssm_a_im", "ssm_b_re", "ssm_b_im", "ssm_c_re", "ssm_c_im", "ssm_d",
             "ssm_log_step", "ssm_w_glu", "w_out"]
    shared = {n: np.ascontiguousarray(np.asarray(inputs[n], dtype=np.float32)) for n in names}
    in_maps = []
    for c in range(ncores):
        m = dict(shared)
        m["x"] = x[c * NSEQ:(c + 1) * NSEQ]
        m.update(consts)
        in_maps.append(m)
    res = run_bass_kernel_spmd(nc, in_maps, core_ids=list(range(ncores)))
    return np.concatenate([r["out"] for r in res.results], axis=0).astype(np.float32)
```

```python
import contextlib
import numpy as np
import ml_dtypes
import concourse.bass as bass
import concourse.mybir as mybir
from concourse.bass_utils import run_bass_kernel_spmd

F32 = mybir.dt.float32
BF16 = mybir.dt.bfloat16
AF = mybir.ActivationFunctionType
ALU = mybir.AluOpType
AX = mybir.AxisListType

D = 1024
DEPTH = 4
DIN = 2504
EPS = 1e-6
NCOL = 3240
C_IQ2 = 2728
C_QA, C_KA2, C_IQ, C_IK4, C_GA, C_U, C_GB, C_CQ, C_CKV, C_KPE, C_GC, C_VAIW = (
    0, 384, 512, 768, 896, 1280, 1536, 1792, 2048, 2176, 2272, 2656)
S_QA, S_KA, S_VA, S_IQ, S_IK, S_IW, S_GA, S_U, S_GB, S_CQ, S_CKV, S_KPE, S_GC = (
    0, 384, 448, 512, 768, 800, 808, 1192, 1448, 1704, 1960, 2088, 2120)
NEGM = -30000.0
NIT = 16
import os as _os5
P1ENG = _os5.environ.get('P1ENG', 'pool')
import os as _os4
DSA_MODE = int(_os4.environ.get('DSA_MODE', '0'))
import os as _os3
GV = int(_os3.environ.get('GV', '0'))
import os as _os2
GATE_DMA = int(_os2.environ.get('GATE_DMA', '2'))
import os as _os
P1CUT = int(_os.environ.get('P1CUT', '99'))
import os
PH = set(os.environ.get('PH', 'prep,s5c,p1,dsa,s5,mla,p3').split(','))


class Res:
    __slots__ = ("w", "r", "ps")

    def __init__(self, ps=False):
        self.w = {}
        self.r = {}
        self.ps = ps


class Sched:
    def __init__(self, nc, n_dma_sems=32):
        self.nc = nc
        self.engs = {}
        for name, e in (("pe", nc.tensor), ("act", nc.scalar), ("dve", nc.vector),
                        ("pool", nc.gpsimd), ("sp", nc.sync)):
            sem = nc.alloc_semaphore(name="sem_" + name)
            self.engs[name] = dict(e=e, sem=sem, cnt=0, waited={}, name=name)
        self.dma_sems = [nc.alloc_semaphore(name=f"dsem{i}") for i in range(n_dma_sems)]
        self.dma_cnt = [0] * n_dma_sems
        self.dma_i = 0
        self.nins = 0

    def _wait(self, en, deps, skip_self=False):
        E = self.engs[en]
        best = {}
        for sem, val in deps:
            k = id(sem)
            if k not in best or best[k][1] < val:
                best[k] = (sem, val)
        for k, (sem, val) in best.items():
            if skip_self and sem is E["sem"]:
                continue
            if E["waited"].get(k, 0) < val:
                E["e"].wait_ge(sem, val)
                E["waited"][k] = val

    def _deps(self, reads, writes, en=None):
        deps = []
        own = self.engs[en]["sem"] if en is not None else None
        for r in reads:
            deps.extend(r.w.values())
            if r.ps:
                deps.extend(ev for ev in r.r.values() if ev[0] is not own)
        for w in writes:
            deps.extend(w.w.values())
            deps.extend(w.r.values())
        return deps

    @staticmethod
    def _mark(ev, reads, writes):
        k = id(ev[0])
        for r in reads:
            r.r[k] = ev
        for w in writes:
            w.w[k] = ev

    def op(self, en, fn, reads=(), writes=()):
        E = self.engs[en]
        self._wait(en, self._deps(reads, writes, en), skip_self=(en == "pe"))
        ins = fn(E["e"])
        E["cnt"] += 1
        ins.then_inc(E["sem"], 1)
        ev = (E["sem"], E["cnt"])
        self._mark(ev, reads, writes)
        self.nins += 1
        return ev

    def dma(self, out, in_, reads=(), writes=(), en="sp", **kw):
        E = self.engs[en]
        i = self.dma_i % len(self.dma_sems)
        self.dma_i += 1
        sem = self.dma_sems[i]
        deps = self._deps(reads, writes)
        if self.dma_cnt[i] > 0:
            deps.append((sem, self.dma_cnt[i]))
        self._wait(en, deps)
        ins = E["e"].dma_start(out=out, in_=in_, **kw)
        self.dma_cnt[i] += 16
        ins.then_inc(sem, 16)
        ev = (sem, self.dma_cnt[i])
        self._mark(ev, reads, writes)
        self.nins += 1
        return ev

    def barrier(self):
        evs = [(E["sem"], E["cnt"]) for E in self.engs.values() if E["cnt"] > 0]
        evs += [(s, c) for s, c in zip(self.dma_sems, self.dma_cnt) if c > 0]
        for en in self.engs:
            self._wait(en, evs)


def rope_tab(S, d, rows):
    half = d // 2
    inv = (np.float32(10000.0) ** (-np.arange(half, dtype=np.float32) * np.float32(2.0) / np.float32(d))).astype(np.float32)
    ang = np.arange(S, dtype=np.float32)[:, None] * inv[None, :]
    c = np.cos(ang).astype(np.float32).T
    s = np.sin(ang).astype(np.float32).T
    idx = np.arange(rows) % half
    return c[idx], s[idx]


def make_consts(S):
    bf = ml_dtypes.bfloat16
    I = np.eye(128, dtype=np.float32)
    ones64 = np.zeros((128, 128), np.float32)
    ones64[:64, :64] = 1; ones64[64:, 64:] = 1
    ones96 = np.zeros((128, 128), np.float32); ones96[:96, :96] = 1
    ones128 = np.ones((128, 128), np.float32)

    def rot(M, hd, half, lo=0):
        R = np.zeros((128, 128), np.float32)
        for m in range(M):
            j = m % hd
            if j < lo:
                continue
            jj = j - lo
            if jj < half:
                R[m + half, m] = -1.0
            else:
                R[m - half, m] = 1.0
        return R
    R64 = rot(128, 64, 32)
    R32 = rot(128, 32, 16)
    R96 = rot(96, 96, 16, lo=64)
    nmd = np.zeros((128, 128), np.float32); nmd[:64, 64:] = NEGM
    cbf = np.concatenate([I, ones64, ones96, ones128, R64, R32, R96, nmd], axis=1).astype(bf)
    icross = np.zeros((128, 128), np.float32)
    for m in range(128):
        icross[m, (m + 64) % 128] = 1
    e65 = np.zeros((128, 64), np.float32); e65[64, :] = 1
    i16 = np.zeros((128, 16), np.float32)
    for p in range(128):
        i16[p, p % 16] = 1
    sgB = np.where(np.arange(128) < 64, -1.0, 1.0).astype(np.float32)[:, None]
    sgQ = -sgB
    epsc = np.full((128, 1), EPS, np.float32)
    pw = np.tile((2.0 ** -(np.arange(NIT + 1, dtype=np.float64) + 1.0)).astype(np.float32)[None, :], (128, 1))
    cf = np.concatenate([I, icross, e65, i16, sgB, sgQ, epsc, pw], axis=1).astype(np.float32)
    c64, s64 = rope_tab(S, 64, 128)
    c32, s32 = rope_tab(S, 32, 128)
    c96 = np.ones((128, S), np.float32); s96 = np.zeros((128, S), np.float32)
    c96[64:96] = c32[:32]; s96[64:96] = s32[:32]
    rt = np.stack([c64, s64, c32, s32, c96, s96]).astype(np.float32)
    return dict(cbf=cbf, cf=cf, rt=rt)


CB = dict(ident=0, ones64=128, ones96=256, ones128=384, R64=512, R32=640, R96=768, nmd=896)
CF = dict(ident=0, icross=128, e65=256, i16=320, sgB=336, sgQ=337, eps=338, pw=339)
NCF = 339 + NIT + 1


def build_program(S, NL, NSEQ, KTOP, dbg=None):
    nc = bass.Bass("TRN2", target_bir_lowering=False)
    NB = S // 128
    NG = S // 512
    J = S // 8
    dbg = dbg or {}

    def din(name, shape, dt=F32):
        return nc.dram_tensor(name, list(shape), dt, kind="ExternalInput").ap()

    def dscr(name, shape, dt):
        return nc.dram_tensor(name, list(shape), dt, kind="Internal").ap()

    x_in = din("x", [NSEQ, S, D])
    norm_g = din("norm_g", [DEPTH, D])
    w_in = din("w_in", [DEPTH, D, DIN])
    attn_q_norm = din("attn_q_norm", [DEPTH, 64])
    attn_k_norm = din("attn_k_norm", [DEPTH, 64])
    q_lora_g = din("mla_q_lora_norm", [DEPTH, 256])
    kv_lora_g = din("mla_kv_lora_norm", [DEPTH, 128])
    w_uq = din("mla_w_uq", [DEPTH, 256, 576])
    w_ukv = din("mla_w_ukv", [DEPTH, 128, 768])
    mla_q_norm = din("mla_q_norm", [DEPTH, 96])
    mla_k_norm = din("mla_k_norm", [DEPTH, 96])
    a_re = din("ssm_a_re", [DEPTH, 16, 64])
    a_im = din("ssm_a_im", [DEPTH, 16, 64])
    b_re = din("ssm_b_re", [DEPTH, 16, 64, 16])
    b_im = din("ssm_b_im", [DEPTH, 16, 64, 16])
    c_re = din("ssm_c_re", [DEPTH, 16, 16, 64])
    c_im = din("ssm_c_im", [DEPTH, 16, 16, 64])
    d_skip = din("ssm_d", [DEPTH, 16, 16])
    log_step = din("ssm_log_step", [DEPTH, 16])
    w_glu = din("ssm_w_glu", [DEPTH, 256, 256])
    w_out = din("w_out", [DEPTH, D, D])
    cbf_d = din("cbf", [128, 1024], BF16)
    cf_d = din("cf", [128, NCF])
    rt_d = din("rt", [6, 128, S])
    out_d = nc.dram_tensor("out", [NSEQ, S, D], F32, kind="ExternalOutput").ap()

    xs_d = dscr("xs", [NSEQ, S, D], F32)
    winp_d = dscr("winp", [128, 8, NCOL], BF16)
    wout_d = dscr("woutp", [128, 14, D], BF16)
    qT_d = dscr("qT", [3, 128, S], BF16)
    kT2_d = dscr("kT2", [128, S], BF16)
    va_d = dscr("va", [NB, 128, 65], BF16)
    iqT_d = dscr("iqT", [4, 128, S], BF16)
    ikT_d = dscr("ikT", [128, S], BF16)
    iw_d = dscr("iw", [NB, 128, 16], F32)
    gaT_d = dscr("gaT", [6, 64, S], BF16)
    gbT_d = dscr("gbT", [2, 128, S], BF16)
    gcT_d = dscr("gcT", [6, 64, S], BF16)
    U_d = dscr("U", [16, 8, 16, J], F32)
    Y_d = dscr("Y", [16, 8, 16, J], F32)
    qc_d = dscr("qc", [6, 96, S], BF16)
    kc_d = dscr("kc", [6, 96, S], BF16)
    vc_d = dscr("vc", [6, NB, 128, 65], BF16)
    mix_d = dscr("mix", [14, 128, S], BF16)
    s5c_d = dscr("s5c", [16, 128, 12, 128], F32)

    S_ = Sched(nc)
    op = S_.op
    dma = S_.dma
    RS = {}

    PSUM_KEYS = {"p1_tp", "p1_mm", "p1_aux", "at_ps", "at_o", "at_sbp", "ds_shp", "ds_scp", "s5_pp", "s5r_p", "s5p_z", "p3_pp"}

    def rs(*key):
        if key not in RS:
            RS[key] = Res(ps=(key[0] in PSUM_KEYS))
        return RS[key]

    with contextlib.ExitStack() as top:
        uid = [0]

        def sbt(st, name, shape, dt):
            uid[0] += 1
            return st.enter_context(nc.sbuf_tensor(f"sb{uid[0]}_{name}", list(shape), dt))

        def pst(st, name, shape, dt):
            uid[0] += 1
            return st.enter_context(nc.psum_tensor(f"ps{uid[0]}_{name}", list(shape), dt))

        cbf = sbt(top, "cbf", [128, 1024], BF16)
        cf = sbt(top, "cf", [128, NCF], F32)
        dma(cbf[:], cbf_d, writes=[rs("cbf")])
        dma(cf[:], cf_d, writes=[rs("cf")])
        rC = [rs("cbf"), rs("cf")]

        def cb(name, rows=128, cols=128):
            o = CB[name]
            return cbf[0:rows, o:o + cols]

        def cfa(name, rows=128, cols=128):
            o = CF[name]
            return cf[0:rows, o:o + cols]
        eps_ap = cf[:, CF["eps"]:CF["eps"] + 1]

        gsm = sbt(top, "gsm", [128, DEPTH, 16], F32)
        rg = rs("gsm")
        op("dve", lambda e: e.memset(gsm[:], 1.0), writes=[rg])
        for l in range(NL):
            dma(gsm[:, l, 0:8], norm_g[l].rearrange("(k p) -> p k", p=128), writes=[rg], allow_slow_non_contiguous=True)
            for hh in range(2):
                dma(gsm[64 * hh:64 * hh + 64, l, 8:9], attn_q_norm[l].rearrange("(p o) -> p o", o=1), writes=[rg], allow_slow_non_contiguous=True)
                dma(gsm[64 * hh:64 * hh + 64, l, 9:10], attn_k_norm[l].rearrange("(p o) -> p o", o=1), writes=[rg], allow_slow_non_contiguous=True)
            dma(gsm[:, l, 10:12], q_lora_g[l].rearrange("(k p) -> p k", p=128), writes=[rg], allow_slow_non_contiguous=True)
            dma(gsm[:, l, 12:13], kv_lora_g[l].rearrange("(p o) -> p o", o=1), writes=[rg], allow_slow_non_contiguous=True)
            dma(gsm[0:96, l, 13:14], mla_q_norm[l].rearrange("(p o) -> p o", o=1), writes=[rg], allow_slow_non_contiguous=True)
            dma(gsm[0:96, l, 14:15], mla_k_norm[l].rearrange("(p o) -> p o", o=1), writes=[rg], allow_slow_non_contiguous=True)

        def prep_weights(l):
            with contextlib.ExitStack() as st:
                stg = [sbt(st, f"stg{i}", [128, DIN], F32) for i in range(2)]
                wrow = [sbt(st, f"wrow{i}", [128, NCOL], BF16) for i in range(2)]
                for kc in range(8):
                    i = kc % 2
                    rst, rw = rs("stg", i), rs("wrow", i)
                    dma(stg[i][:], w_in[l, kc * 128:(kc + 1) * 128, :], writes=[rst])
                    g = gsm[:, l, kc:kc + 1]
                    segs = [(C_QA, S_QA, 384), (C_KA2, S_KA, 64), (C_KA2 + 64, S_KA, 64)] + [(C_IQ2 + (h_ // 2) * 128 + 64 * (h_ % 2), S_IQ + 32 * h_, 32) for h_ in range(8)] + \
                           [(C_IK4 + 32 * r, S_IK, 32) for r in range(4)] + \
                           [(C_GA, S_GA, 384), (C_U, S_U, 256), (C_GB, S_GB, 256), (C_CQ, S_CQ, 256), (C_CKV, S_CKV, 128),
                            (C_KPE + 64, S_KPE, 32), (C_GC, S_GC, 384), (C_VAIW, S_VA, 64), (C_VAIW + 64, S_IW, 8)]
                    op("pool", lambda e: e.memset(wrow[i][:, C_KPE:C_KPE + 64], 0.0), writes=[rw])
                    op("pool", lambda e: e.memset(wrow[i][:, C_IQ2:C_IQ2 + 512], 0.0), writes=[rw])
                    op("pool", lambda e: e.memset(wrow[i][:, C_IQ:C_IQ + 256], 0.0), writes=[rw])
                    for n_, (dc, sc_, w_) in enumerate(segs):
                        if n_ % 2 == 0:
                            op("dve", lambda e: e.tensor_scalar(out=wrow[i][:, dc:dc + w_], in0=stg[i][:, sc_:sc_ + w_], scalar1=g, scalar2=None, op0=ALU.mult),
                               reads=[rst, rg], writes=[rw])
                        else:
                            op("act", lambda e: e.activation(out=wrow[i][:, dc:dc + w_], in_=stg[i][:, sc_:sc_ + w_], func=AF.Copy, scale=g),
                               reads=[rst, rg], writes=[rw])
                    dma(winp_d[:, kc, :], wrow[i][:], reads=[rw], writes=[rs("winp_d")])
            with contextlib.ExitStack() as st:
                stg = [sbt(st, f"stgo{i}", [128, D], F32) for i in range(2)]
                wrow = [sbt(st, f"wrowo{i}", [128, D], BF16) for i in range(2)]
                for c in range(14):
                    i = c % 2
                    rst, rw = rs("stgo", i), rs("wrowo", i)
                    if c < 6:
                        r0, nr = 64 * c, 64
                    elif c < 8:
                        r0, nr = 384 + 128 * (c - 6), 128
                    else:
                        r0, nr = 640 + 64 * (c - 8), 64
                    dma(stg[i][0:nr, :], w_out[l, r0:r0 + nr, :], writes=[rst])
                    if c % 2 == 0:
                        op("dve", lambda e: e.tensor_copy(out=wrow[i][0:nr, :], in_=stg[i][0:nr, :]), reads=[rst], writes=[rw])
                    else:
                        op("act", lambda e: e.copy(out=wrow[i][0:nr, :], in_=stg[i][0:nr, :]), reads=[rst], writes=[rw])
                    dma(wout_d[0:nr, c, :], wrow[i][0:nr, :], reads=[rw], writes=[rs("wout_d")])
            with contextlib.ExitStack() as st:
                s1 = sbt(st, "s_uq", [128, 2, 576], F32)
                s2 = sbt(st, "s_ukv", [128, 768], F32)
                s3 = sbt(st, "s_glu", [128, 2, 256], F32)
                r1, r2, r3 = rs("s_uq"), rs("s_ukv"), rs("s_glu")
                dma(s1[:], w_uq[l].rearrange("(k p) n -> p k n", p=128), writes=[r1])
                dma(s2[:], w_ukv[l], writes=[r2])
                dma(s3[:], w_glu[l].rearrange("(k p) n -> p k n", p=128), writes=[r3])
                rw = rs("wsm")
                for k in range(2):
                    op("dve", lambda e: e.tensor_scalar(out=Wuq[:, k, :], in0=s1[:, k, :], scalar1=gsm[:, l, 10 + k:11 + k], scalar2=None, op0=ALU.mult),
                       reads=[r1, rg], writes=[rw])
                    op("dve", lambda e: e.tensor_copy(out=Wglu[:, k, :], in_=s3[:, k, :]), reads=[r3], writes=[rw])
                op("dve", lambda e: e.memset(WukT[:], 0.0), writes=[rw])
                for h in range(6):
                    op("dve", lambda e: e.tensor_scalar(out=WukT[:, h, 0:64], in0=s2[:, h * 128:h * 128 + 64], scalar1=gsm[:, l, 12:13], scalar2=None, op0=ALU.mult),
                       reads=[r2, rg], writes=[rw])
                    op("dve", lambda e: e.tensor_scalar(out=Wuv[:, h * 64:h * 64 + 64], in0=s2[:, h * 128 + 64:h * 128 + 128], scalar1=gsm[:, l, 12:13], scalar2=None, op0=ALU.mult),
                       reads=[r2, rg], writes=[rw])
            S_.barrier()

        Wuq = sbt(top, "Wuq", [128, 2, 576], BF16)
        WukT = sbt(top, "WukT", [128, 6, 96], BF16)
        Wuv = sbt(top, "Wuv", [128, 384], BF16)
        Wglu = sbt(top, "Wglu", [128, 2, 256], BF16)
        rWsm = rs("wsm")

        def phase1(l, src_ap, src_res):
            with contextlib.ExitStack() as st:
                Winp = sbt(st, "Winp", [128, 8, NCOL], BF16)
                rW = rs("Winp")
                for kc in range(8):
                    dma(Winp[:, kc, :], winp_d[:, kc, :], reads=[rs("winp_d")], writes=[rW])
                xt = sbt(st, "p1_xt", [128, 4, D], F32)
                rxt = rs("p1_xt")
                junk = sbt(st, "p1_junk", [128, D], BF16)
                xn = sbt(st, "p1_xn", [128, D], BF16)
                ssq = sbt(st, "p1_ssq", [128, 8], F32)
                hT = sbt(st, "p1_hT", [128, 8, 512], BF16)
                rhT = rs("p1_hT")
                tabs = sbt(st, "p1_tabs", [128, 6, 512], F32)
                rtab = rs("p1_tabs")
                tp = [pst(st, f"p1_tp{i}", [128, 1024], BF16) for i in range(2)]
                mm = [pst(st, f"p1_mm{i}", [128, 512], F32) for i in range(3)]
                aux = [pst(st, f"p1_aux{i}", [128, 512], F32) for i in range(3)]
                mmi = [0]
                NWK = 3
                xg = [sbt(st, f"p1_xg{i}", [128, 512], BF16) for i in range(NWK)]
                sq = [sbt(st, f"p1_sq{i}", [128, 512], BF16) for i in range(NWK)]
                sd = [sbt(st, f"p1_sd{i}", [128, 512], F32) for i in range(NWK)]
                t1 = [sbt(st, f"p1_t1{i}", [128, 512], F32) for i in range(NWK)]
                t2 = [sbt(st, f"p1_t2{i}", [128, 512], F32) for i in range(NWK)]
                ob = [sbt(st, f"p1_ob{i}", [128, 512], BF16) for i in range(NWK)]
                up = [sbt(st, f"p1_up{i}", [128, 8, 64], F32) for i in range(2)]
                cqn = sbt(st, "p1_cqn", [128, 2, 512], BF16)
                ckvn = sbt(st, "p1_ckvn", [128, 512], BF16)
                vcs = [sbt(st, f"p1_vcs{i}", [128, 6, 65], BF16) for i in range(2)]
                vas = [sbt(st, f"p1_vas{i}", [128, 65], BF16) for i in range(2)]
                iws = [sbt(st, f"p1_iws{i}", [128, 16], F32) for i in range(2)]
                for i in range(2):
                    op("pool", lambda e: e.memset(vcs[i][:], 1.0), writes=[rs("p1_vcs", i)])
                    op("pool", lambda e: e.memset(vas[i][:], 1.0), writes=[rs("p1_vas", i)])
                wk = [0]

                def proj(cols, M, rhs_fn=None):
                    k = mmi[0] % 3
                    mmi[0] += 1
                    r = rs("p1_mm", k)
                    for kc in range(8):
                        op("pe", lambda e: e.matmul(mm[k][0:M, :], lhsT=Winp[:, kc, cols:cols + M], rhs=hT[:, kc, :], start=(kc == 0), stop=(kc == 7)),
                           reads=[rW, rhT], writes=[r])
                    return mm[k], r

                def normrope(X, rX, M, gain, onesname, inv_dim, Rname, ci, dst_ap, dst_res):
                    i = wk[0] % NWK
                    wk[0] += 1
                    a0, a1 = aux[(2 * i) % 3], aux[(2 * i + 1) % 3]
                    ra0, ra1 = rs("p1_aux", (2 * i) % 3), rs("p1_aux", (2 * i + 1) % 3)
                    rxg, rsq, rsd, rt1, rt2, rob = (rs("p1_xg", i), rs("p1_sq", i), rs("p1_sd", i), rs("p1_t1", i), rs("p1_t2", i), rs("p1_ob", i))
                    Ct, St = tabs[0:M, ci, :], tabs[0:M, ci + 1, :]
                    if gain is not None:
                        op("act", lambda e: e.activation(out=xg[i][0:M, :], in_=X[0:M, :], func=AF.Copy, scale=gain), reads=[rX, rg], writes=[rxg])
                    else:
                        op("act", lambda e: e.copy(out=xg[i][0:M, :], in_=X[0:M, :]), reads=[rX], writes=[rxg])
                    op("pe", lambda e: e.matmul(a1[0:M, :], lhsT=cb(Rname, M, M), rhs=xg[i][0:M, :], start=True, stop=True), reads=[rxg] + rC, writes=[ra1])
                    if onesname is not None:
                        op("act", lambda e: e.activation(out=sq[i][0:M, :], in_=X[0:M, :], func=AF.Square), reads=[rX], writes=[rsq])
                        op("pe", lambda e: e.matmul(a0[0:M, :], lhsT=cb(onesname, M, M), rhs=sq[i][0:M, :], start=True, stop=True), reads=[rsq] + rC, writes=[ra0])
                        op("act", lambda e: e.activation(out=sd[i][0:M, :], in_=a0[0:M, :], func=AF.Ln, scale=inv_dim, bias=eps_ap[0:M, :]), reads=[ra0] + rC, writes=[rsd])
                        op("act", lambda e: e.activation(out=sd[i][0:M, :], in_=sd[i][0:M, :], func=AF.Exp, scale=-0.5), reads=[rsd], writes=[rsd])
                    if gain is not None:
                        op("dve", lambda e: e.scalar_tensor_tensor(out=t1[i][0:M, :], in0=X[0:M, :], scalar=gain, in1=Ct, op0=ALU.mult, op1=ALU.mult),
                           reads=[rX, rtab, rg], writes=[rt1])
                    else:
                        op("dve", lambda e: e.tensor_tensor(out=t1[i][0:M, :], in0=X[0:M, :], in1=Ct, op=ALU.mult), reads=[rX, rtab], writes=[rt1])
                    op("dve", lambda e: e.tensor_tensor(out=t2[i][0:M, :], in0=a1[0:M, :], in1=St, op=ALU.mult), reads=[ra1, rtab], writes=[rt2])
                    if onesname is not None:
                        op(P1ENG, lambda e: e.tensor_tensor(out=t1[i][0:M, :], in0=t1[i][0:M, :], in1=t2[i][0:M, :], op=ALU.add), reads=[rt1, rt2], writes=[rt1])
                        op(P1ENG, lambda e: e.tensor_tensor(out=ob[i][0:M, :], in0=t1[i][0:M, :], in1=sd[i][0:M, :], op=ALU.mult), reads=[rt1, rsd], writes=[rob])
                    else:
                        op("dve", lambda e: e.tensor_tensor(out=ob[i][0:M, :], in0=t1[i][0:M, :], in1=t2[i][0:M, :], op=ALU.add), reads=[rt1, rt2], writes=[rob])
                    dma(dst_ap, ob[i][0:M, :], reads=[rob], writes=[dst_res])

                def silu_out(X, rX, M, dsts):
                    i = wk[0] % NWK
                    wk[0] += 1
                    rob = rs("p1_ob", i)
                    rt1 = rs("p1_t1", i)
                    op("act", lambda e: e.activation(out=t1[i][0:M, :], in_=X[0:M, :], func=AF.Sigmoid), reads=[rX], writes=[rt1])
                    if GV != 3:
                        op("dve", lambda e: e.scalar_tensor_tensor(out=ob[i][0:M, :], in0=X[0:M, :], scalar=1.0, in1=t1[i][0:M, :], op0=ALU.mult, op1=ALU.mult), reads=[rX, rt1], writes=[rob])
                    for (p0, p1, dap, dres) in dsts:
                        if GATE_DMA == 0 or (GATE_DMA == 1 and p0 != 0):
                            continue
                        dma(dap, ob[i][p0:p1, :], reads=[rob], writes=[dres])

                for tg in range(NG):
                    t0 = tg * 512
                    tsl = slice(t0, t0 + 512)
                    dma(xt[:], src_ap[t0:t0 + 512, :].rearrange("(j p) d -> p j d", p=128), reads=[src_res], writes=[rxt])
                    dma(tabs[:], rt_d[:, :, tsl].rearrange("c p t -> p c t"), writes=[rtab])
                    rssq, rjunk, rxn = rs("p1_ssq"), rs("p1_junk"), rs("p1_xn")
                    for j in range(4):
                        op("act", lambda e: e.activation(out=junk[:], in_=xt[:, j, :], func=AF.Square, accum_out=ssq[:, j:j + 1]), reads=[rxt], writes=[rjunk, rssq])
                    op("act", lambda e: e.activation(out=ssq[:, 4:8], in_=ssq[:, 0:4], func=AF.Sqrt, scale=1.0 / D, bias=eps_ap), reads=[rssq] + rC, writes=[rssq])
                    op("dve", lambda e: e.reciprocal(out=ssq[:, 4:8], in_=ssq[:, 4:8]), reads=[rssq], writes=[rssq])
                    for j in range(4):
                        op("dve", lambda e: e.tensor_scalar(out=xn[:], in0=xt[:, j, :], scalar1=ssq[:, 4 + j:5 + j], scalar2=None, op0=ALU.mult), reads=[rxt, rssq], writes=[rxn])
                        for half in range(2):
                            k = half
                            rtp = rs("p1_tp", k)
                            for q in range(4):
                                kc = half * 4 + q
                                op("pe", lambda e: e.transpose(out=tp[k][:, q * 128:(q + 1) * 128], in_=xn[:, kc * 128:(kc + 1) * 128], identity=cb("ident")),
                                   reads=[rxn] + rC, writes=[rtp])
                            eng = "act" if half == 0 else "dve"
                            if eng == "act":
                                op("act", lambda e: e.copy(out=hT[:, half * 4:half * 4 + 4, j * 128:(j + 1) * 128], in_=tp[k][:, 0:512].rearrange("p (q t) -> p q t", q=4)), reads=[rtp], writes=[rhT])
                            else:
                                op("dve", lambda e: e.tensor_copy(out=hT[:, half * 4:half * 4 + 4, j * 128:(j + 1) * 128], in_=tp[k][:, 0:512].rearrange("p (q t) -> p q t", q=4)), reads=[rtp], writes=[rhT])
                    if P1CUT <= 1:
                        continue
                    for c in range(3):
                        X, rX = proj(C_QA + 128 * c, 128)
                        normrope(X, rX, 128, gsm[:, l, 8:9], "ones64", 1.0 / 64, "R64", 0, qT_d[c, :, tsl], rs("qT_d"))
                    X, rX = proj(C_KA2, 128)
                    normrope(X, rX, 128, gsm[:, l, 9:10], "ones64", 1.0 / 64, "R64", 0, kT2_d[:, tsl], rs("kT2_d"))
                    if P1CUT <= 2:
                        continue
                    for c in range(4):
                        X, rX = proj(C_IQ2 + 128 * c, 128)
                        normrope(X, rX, 128, None, None, None, "R32", 2, iqT_d[c, :, tsl], rs("iqT_d"))
                    X, rX = proj(C_IK4, 128)
                    normrope(X, rX, 128, None, None, None, "R32", 2, ikT_d[:, tsl], rs("ikT_d"))
                    if P1CUT <= 3:
                        continue
                    for c in range(3):
                        X, rX = proj(C_GA + 128 * c, 128)
                        silu_out(X, rX, 128, [(0, 128, gaT_d.rearrange("h p t -> (h p) t")[128 * c:128 * c + 128, tsl], rs("gaT_d"))])
                    for c in range(2):
                        X, rX = proj(C_GB + 128 * c, 128)
                        silu_out(X, rX, 128, [(0, 128, gbT_d[c, :, tsl], rs("gbT_d"))])
                    for c in range(3):
                        X, rX = proj(C_GC + 128 * c, 128)
                        silu_out(X, rX, 128, [(0, 128, gcT_d.rearrange("h p t -> (h p) t")[128 * c:128 * c + 128, tsl], rs("gcT_d"))])
                    if P1CUT <= 4:
                        continue
                    for c in range(2):
                        X, rX = proj(C_U + 128 * c, 128)
                        i = c
                        rup = rs("p1_up", i)
                        op("dve", lambda e: e.tensor_copy(out=up[i][:].rearrange("p s j -> p j s"), in_=X[:].rearrange("p (j s) -> p j s", s=8)), reads=[rX], writes=[rup])
                        for g8 in range(8):
                            g = 8 * c + g8
                            dma(U_d[g, :, :, tg * 64:(tg + 1) * 64].rearrange("s c j -> c s j"), up[i][16 * g8:16 * g8 + 16, :, :], reads=[rup], writes=[rs("U_d")])
                    if P1CUT <= 5:
                        continue
                    X0, rX0 = proj(C_CQ, 128)
                    X1, rX1 = proj(C_CQ + 128, 128)
                    i = wk[0] % NWK
                    wk[0] += 1
                    a0, ra0 = aux[(2 * i) % 3], rs("p1_aux", (2 * i) % 3)
                    rsq, rsd, rt1 = rs("p1_sq", i), rs("p1_sd", i), rs("p1_t1", i)
                    i2 = wk[0] % NWK
                    wk[0] += 1
                    rsq2 = rs("p1_sq", i2)
                    op("act", lambda e: e.activation(out=sq[i][:], in_=X0[:], func=AF.Square), reads=[rX0], writes=[rsq])
                    op("act", lambda e: e.activation(out=sq[i2][:], in_=X1[:], func=AF.Square), reads=[rX1], writes=[rsq2])
                    op("pe", lambda e: e.matmul(a0[:], lhsT=cb("ones128"), rhs=sq[i][:], start=True, stop=False), reads=[rsq] + rC, writes=[ra0])
                    op("pe", lambda e: e.matmul(a0[:], lhsT=cb("ones128"), rhs=sq[i2][:], start=False, stop=True), reads=[rsq2] + rC, writes=[ra0])
                    op("act", lambda e: e.activation(out=sd[i][:], in_=a0[:], func=AF.Ln, scale=1.0 / 256, bias=eps_ap), reads=[ra0] + rC, writes=[rsd])
                    op("act", lambda e: e.activation(out=sd[i][:], in_=sd[i][:], func=AF.Exp, scale=-0.5), reads=[rsd], writes=[rsd])
                    rcq = rs("p1_cqn")
                    op("dve", lambda e: e.tensor_tensor(out=cqn[:, 0, :], in0=X0[:], in1=sd[i][:], op=ALU.mult), reads=[rX0, rsd], writes=[rcq])
                    op("dve", lambda e: e.tensor_tensor(out=cqn[:, 1, :], in0=X1[:], in1=sd[i][:], op=ALU.mult), reads=[rX1, rsd], writes=[rcq])
                    X0, rX0 = proj(C_CKV, 128)
                    i = wk[0] % NWK
                    wk[0] += 1
                    a0, ra0 = aux[(2 * i) % 3], rs("p1_aux", (2 * i) % 3)
                    rsq, rsd = rs("p1_sq", i), rs("p1_sd", i)
                    op("act", lambda e: e.activation(out=sq[i][:], in_=X0[:], func=AF.Square), reads=[rX0], writes=[rsq])
                    op("pe", lambda e: e.matmul(a0[:], lhsT=cb("ones128"), rhs=sq[i][:], start=True, stop=True), reads=[rsq] + rC, writes=[ra0])
                    op("act", lambda e: e.activation(out=sd[i][:], in_=a0[:], func=AF.Ln, scale=1.0 / 128, bias=eps_ap), reads=[ra0] + rC, writes=[rsd])
                    op("act", lambda e: e.activation(out=sd[i][:], in_=sd[i][:], func=AF.Exp, scale=-0.5), reads=[rsd], writes=[rsd])
                    rckv = rs("p1_ckvn")
                    op("dve", lambda e: e.tensor_tensor(out=ckvn[:], in0=X0[:], in1=sd[i][:], op=ALU.mult), reads=[rX0, rsd], writes=[rckv])
                    if P1CUT <= 6:
                        continue
                    for h in range(6):
                        k = mmi[0] % 3
                        mmi[0] += 1
                        r = rs("p1_mm", k)
                        for c in range(2):
                            op("pe", lambda e: e.matmul(mm[k][0:96, :], lhsT=Wuq[:, c, h * 96:(h + 1) * 96], rhs=cqn[:, c, :], start=(c == 0), stop=(c == 1)),
                               reads=[rWsm, rcq], writes=[r])
                        normrope(mm[k], r, 96, gsm[0:96, l, 13:14], "ones96", 1.0 / 96, "R96", 4, qc_d[h, :, tsl], rs("qc_d"))
                    for h in range(6):
                        k = mmi[0] % 3
                        mmi[0] += 1
                        r = rs("p1_mm", k)
                        op("pe", lambda e: e.matmul(mm[k][0:96, :], lhsT=WukT[:, h, :], rhs=ckvn[:], start=True, stop=False), reads=[rWsm, rckv], writes=[r])
                        for kc in range(8):
                            op("pe", lambda e: e.matmul(mm[k][0:96, :], lhsT=Winp[:, kc, C_KPE:C_KPE + 96], rhs=hT[:, kc, :], start=False, stop=(kc == 7)),
                               reads=[rW, rhT], writes=[r])
                        normrope(mm[k], r, 96, gsm[0:96, l, 14:15], "ones96", 1.0 / 96, "R96", 4, kc_d[h, :, tsl], rs("kc_d"))
                    if P1CUT <= 7:
                        continue
                    for j in range(4):
                        tb = tg * 4 + j
                        k = mmi[0] % 3
                        mmi[0] += 1
                        r = rs("p1_mm", k)
                        op("pe", lambda e: e.matmul(mm[k][:, 0:384], lhsT=ckvn[:, j * 128:(j + 1) * 128], rhs=Wuv[:], start=True, stop=True), reads=[rWsm, rckv], writes=[r])
                        i = j % 2
                        rv = rs("p1_vcs", i)
                        op("act", lambda e: e.copy(out=vcs[i][:, :, 0:64], in_=mm[k][:, 0:384].rearrange("p (h d) -> p h d", h=6)), reads=[r], writes=[rv])
                        dma(vc_d[:, tb, :, :].rearrange("h p c -> p h c"), vcs[i][:], reads=[rv], writes=[rs("vc_d")])
                        k = mmi[0] % 3
                        mmi[0] += 1
                        r = rs("p1_mm", k)
                        for kc in range(8):
                            op("pe", lambda e: e.matmul(mm[k][:, 0:72], lhsT=hT[:, kc, j * 128:(j + 1) * 128], rhs=Winp[:, kc, C_VAIW:C_VAIW + 72], start=(kc == 0), stop=(kc == 7)),
                               reads=[rW, rhT], writes=[r])
                        rva, riw = rs("p1_vas", i), rs("p1_iws", i)
                        op("act", lambda e: e.copy(out=vas[i][:, 0:64], in_=mm[k][:, 0:64]), reads=[r], writes=[rva])
                        dma(va_d[tb, :, :], vas[i][:], reads=[rva], writes=[rs("va_d")])
                        sc_ = (8.0 ** -0.5) * (32.0 ** -0.5)
                        op("act", lambda e: e.activation(out=iws[i][:, 0:8], in_=mm[k][:, 64:72], func=AF.Abs, scale=sc_), reads=[r], writes=[riw])
                        op("act", lambda e: e.activation(out=iws[i][:, 8:16], in_=mm[k][:, 64:72], func=AF.Sign), reads=[r], writes=[riw])
                        dma(iw_d[tb, :, :], iws[i][:], reads=[riw], writes=[rs("iw_d")])
            S_.barrier()

        def att_tiles(st, n_ops=2):
            T = {}
            T["att"] = [pst(st, f"at_ps{i}", [128, 512], F32) for i in range(2)]
            T["Ops"] = [pst(st, f"at_o{i}", [128, 512], F32) for i in range(n_ops)]
            T["sbp"] = pst(st, "at_sb", [128, 512], F32)
            T["PT"] = [sbt(st, f"at_pt{i}", [128, 512], BF16) for i in range(3)]
            T["Osb"] = [sbt(st, f"at_osb{i}", [65, 6, 512], F32) for i in range(2)]
            T["rcp"] = [sbt(st, f"at_rcp{i}", [64, 512], F32) for i in range(2)]
            T["yb"] = [sbt(st, f"at_yb{i}", [64, 512], BF16) for i in range(2)]
            T["gt"] = [sbt(st, f"at_gt{i}", [64, 6, 512], BF16) for i in range(2)]
            T["cnt"] = [0, 0, 0]
            return T

        def att_main(T, sb, n_heads, scale, heads, qfn, qres, nm_fn):
            att, Ops, PT, Osb = T["att"], T["Ops"], T["PT"], T["Osb"]
            cnt = T["cnt"]
            ob = sb % 2
            nkb = 4 * sb + 4
            rosb = rs("at_osb", ob)
            n_ops = len(Ops)
            steps = [(h, kb) for h in range(n_heads) for kb in range(nkb)]
            pend = None
            ois = {}
            for stp in steps + [None]:
                cur = None
                if stp is not None:
                    h, kb = stp
                    H = heads[h]
                    if kb == 0:
                        ois[h] = cnt[1] % n_ops
                        cnt[1] += 1
                    qb0 = max(kb - 4 * sb, 0)
                    qs = slice(qb0 * 128, 512)
                    ai = cnt[0] % 2
                    pi = cnt[0] % 3
                    cnt[0] += 1
                    ra, rp = rs("at_ps", ai), rs("at_pt", pi)
                    masks = []
                    for qb in range(qb0, 4):
                        m = nm_fn(4 * sb + qb, kb)
                        if m is not None:
                            masks.append((qb, m))
                    op("pe", lambda e: e.matmul(att[ai][:, qs], lhsT=H["kT"](slice(kb * 128, kb * 128 + 128)), rhs=qfn(h, qs), start=True, stop=(len(masks) == 0)),
                       reads=[H["kres"], qres], writes=[ra])
                    for mi, (qb, (map_, mres)) in enumerate(masks):
                        op("pe", lambda e: e.matmul(att[ai][:, qb * 128:(qb + 1) * 128], lhsT=map_, rhs=cb("ident"), start=False, stop=(mi == len(masks) - 1)),
                           reads=[mres] + rC, writes=[ra])
                    op("act", lambda e: e.activation(out=PT[pi][:, qs], in_=att[ai][:, qs], func=AF.Exp, scale=scale), reads=[ra], writes=[rp])
                    cur = (h, kb, pi, qs)
                if pend is not None:
                    h2, kb2, pi2, qs2 = pend
                    H2 = heads[h2]
                    oi = ois[h2]
                    rO = rs("at_o", oi)
                    op("pe", lambda e: e.matmul(Ops[oi][0:65, qs2], lhsT=H2["vaug"][:, kb2, :], rhs=PT[pi2][:, qs2], start=(kb2 == 0), stop=(kb2 == nkb - 1)),
                       reads=[rs("at_pt", pi2), H2["vres"]], writes=[rO])
                    if kb2 == nkb - 1:
                        op("act", lambda e: e.copy(out=Osb[ob][:, h2, :], in_=Ops[oi][0:65, :]), reads=[rO], writes=[rosb])
                pend = cur

        def att_norm(T, sb, n_heads, gate_d, gate_res, mix_base):
            Osb, sbp, rcp, yb, gt = T["Osb"], T["sbp"], T["rcp"], T["yb"], T["gt"]
            cnt = T["cnt"]
            ob = sb % 2
            rosb, rgt, rsb = rs("at_osb", ob), rs("at_gt", ob), rs("at_sbp")
            ssl = slice(sb * 512, (sb + 1) * 512)
            dma(gt[ob][:], gate_d[:, :, ssl].rearrange("h p t -> p h t"), reads=[rs(gate_res)], writes=[rgt])
            for h in range(n_heads):
                ri = cnt[2] % 2
                cnt[2] += 1
                rrc, ryb = rs("at_rcp", ri), rs("at_yb", ri)
                op("pe", lambda e: e.matmul(sbp[0:64, :], lhsT=cfa("e65", 65, 64), rhs=Osb[ob][:, h, :], start=True, stop=True), reads=[rosb] + rC, writes=[rsb])
                op("act", lambda e: e.activation(out=rcp[ri][:], in_=sbp[0:64, :], func=AF.Ln), reads=[rsb], writes=[rrc])
                op("act", lambda e: e.activation(out=rcp[ri][:], in_=rcp[ri][:], func=AF.Exp, scale=-1.0), reads=[rrc], writes=[rrc])
                op("dve", lambda e: e.tensor_tensor(out=rcp[ri][:], in0=rcp[ri][:], in1=Osb[ob][0:64, h, :], op=ALU.mult), reads=[rrc, rosb], writes=[rrc])
                op(P1ENG, lambda e: e.tensor_tensor(out=yb[ri][:], in0=rcp[ri][:], in1=gt[ob][:, h, :], op=ALU.mult), reads=[rrc, rgt], writes=[ryb])
                dma(mix_d[mix_base + h, 0:64, ssl], yb[ri][:], reads=[ryb], writes=[rs("mix_d")])

        def dsa_phase():
            with contextlib.ExitStack() as st:
                kT2 = sbt(st, "ds_kT2", [128, S], BF16)
                vaug = sbt(st, "ds_vaug", [128, NB, 65], BF16)
                ikT = sbt(st, "ds_ikT", [128, S], BF16)
                rk, rv, rik = rs("ds_kT2"), rs("ds_vaug"), rs("ds_ikT")
                dma(kT2[:], kT2_d, reads=[rs("kT2_d")], writes=[rk])
                dma(vaug[:], va_d.rearrange("t p c -> p t c"), reads=[rs("va_d")], writes=[rv])
                dma(ikT[:], ikT_d, reads=[rs("ikT_d")], writes=[rik])
                NM = [sbt(st, f"ds_NM{i}", [128, 4, S], BF16) for i in range(2)]
                sc = [sbt(st, f"ds_sc{i}", [128, S], F32) for i in range(2)]
                junk = sbt(st, "ds_junk", [128, S], BF16)
                iq = [sbt(st, f"ds_iq{i}", [128, 4, 128], BF16) for i in range(2)]
                iw = [sbt(st, f"ds_iw{i}", [128, 16], F32) for i in range(2)]
                Dh = [sbt(st, f"ds_Dh{i}", [128, 8, 128], BF16) for i in range(2)]
                Th = [sbt(st, f"ds_Th{i}", [128, 512], BF16) for i in range(3)]
                bs = [sbt(st, f"ds_bs{i}", [128, 8 + NIT + 1], F32) for i in range(2)]
                shp = [pst(st, f"ds_shp{i}", [128, 512], F32) for i in range(2)]
                scp = [pst(st, f"ds_scp{i}", [128, 512], F32) for i in range(2)]
                T = att_tiles(st, n_ops=1)
                qs_t = [sbt(st, f"ds_q{i}", [128, 3, 512], BF16) for i in range(2)]
                c3 = [0, 0]

                def masks(sb):
                    for qb in range(4):
                        b = 4 * sb + qb
                        n = 128 * (b + 1)
                        if n <= KTOP:
                            continue
                        i = b % 2
                        rsc, riq, riw, rDh, rbs = rs("ds_sc", i), rs("ds_iq", i), rs("ds_iw", i), rs("ds_Dh", i), rs("ds_bs", i)
                        rNM = rs("ds_NM", sb % 2)
                        dma(iq[i][:], iqT_d[:, :, b * 128:(b + 1) * 128].rearrange("c p t -> p c t"), reads=[rs("iqT_d")], writes=[riq])
                        dma(iw[i][:], iw_d[b, :, :], reads=[rs("iw_d")], writes=[riw])
                        for h in range(8):
                            op("dve", lambda e: e.tensor_scalar(out=Dh[i][:, h, :], in0=cb("ident"), scalar1=iw[i][:, 8 + h:9 + h], scalar2=None, op0=ALU.mult),
                               reads=[riw] + rC, writes=[rDh])
                        chunks = [(c * 512, min(512, n - c * 512)) for c in range((n + 511) // 512)]
                        isteps = [(ci, h) for ci in range(len(chunks)) for h in range(8)]
                        ipend = None
                        for ist in isteps + [None]:
                            icur = None
                            if ist is not None:
                                ci, h = ist
                                k0, wc = chunks[ci]
                                hi_ = c3[0] % 2
                                ti_ = c3[0] % 3
                                c3[0] += 1
                                rsh, rth = rs("ds_shp", hi_), rs("ds_Th", ti_)
                                pb = 64 * (h % 2)
                                op("pe", lambda e: e.matmul(shp[hi_][:, 0:wc], lhsT=iq[i][pb:pb + 32, h // 2, :], rhs=ikT[pb:pb + 32, k0:k0 + wc], start=True, stop=True),
                                   reads=[riq, rik], writes=[rsh])
                                op("act", lambda e: e.activation(out=Th[ti_][:, 0:wc], in_=shp[hi_][:, 0:wc], func=AF.Relu, scale=iw[i][:, h:h + 1]), reads=[rsh, riw], writes=[rth])
                                if h == 0:
                                    c3[1] += 1
                                icur = (ci, h, ti_, c3[1] % 2)
                            if ipend is not None:
                                ci2, h2, ti2, si = ipend
                                k0, wc = chunks[ci2]
                                rscp = rs("ds_scp", si)
                                op("pe", lambda e: e.matmul(scp[si][:, 0:wc], lhsT=Dh[i][:, h2, :], rhs=Th[ti2][:, 0:wc], start=(h2 == 0), stop=(h2 == 7)), reads=[rs("ds_Th", ti2), rDh], writes=[rscp])
                                if h2 == 7:
                                    op("act", lambda e: e.copy(out=sc[i][:, k0:k0 + wc], in_=scp[si][:, 0:wc]), reads=[rscp], writes=[rsc])
                            ipend = icur
                        lo, hi, mid, cn, tt, rng = (bs[i][:, k:k + 1] for k in range(6))
                        Hc = lambda it: bs[i][:, 8 + it:9 + it]
                        op("dve", lambda e: e.tensor_reduce(out=hi, in_=sc[i][:, 0:n], axis=AX.X, op=ALU.max), reads=[rsc], writes=[rbs])
                        op("dve", lambda e: e.tensor_reduce(out=lo, in_=sc[i][:, 0:n], axis=AX.X, op=ALU.min), reads=[rsc], writes=[rbs])
                        op("dve", lambda e: e.scalar_tensor_tensor(out=rng, in0=hi, scalar=1.0, in1=lo, op0=ALU.add, op1=ALU.subtract), reads=[rbs], writes=[rbs])
                        op("dve", lambda e: e.tensor_scalar(out=bs[i][:, 8:8 + NIT + 1], in0=cf[:, CF["pw"]:CF["pw"] + NIT + 1], scalar1=rng, scalar2=None, op0=ALU.mult), reads=[rbs] + rC, writes=[rbs])
                        op("dve", lambda e: e.tensor_tensor(out=mid, in0=lo, in1=Hc(0), op=ALU.add), reads=[rbs], writes=[rbs])
                        op("dve", lambda e: e.memset(sc[i][0:64, n - 64:n], -1e30), writes=[rsc])
                        for it in range(NIT):
                            op("dve", lambda e: e.tensor_scalar(out=junk[:, 0:n], in0=sc[i][:, 0:n], scalar1=mid, scalar2=None, op0=ALU.is_ge, op1=ALU.add, accum_out=cn),
                               reads=[rsc, rbs], writes=[rbs, rs("ds_junk")])
                            op("dve", lambda e: e.tensor_scalar(out=tt, in0=cn, scalar1=float(KTOP) - 0.5, scalar2=-0.5, op0=ALU.is_ge, op1=ALU.add), reads=[rbs], writes=[rbs])
                            op("dve", lambda e: e.scalar_tensor_tensor(out=mid, in0=tt, scalar=Hc(it), in1=mid, op0=ALU.mult, op1=ALU.add), reads=[rbs], writes=[rbs])
                        op("dve", lambda e: e.tensor_tensor(out=lo, in0=mid, in1=Hc(NIT), op=ALU.subtract), reads=[rbs], writes=[rbs])
                        op("dve", lambda e: e.tensor_scalar(out=NM[sb % 2][:, qb, 0:n], in0=sc[i][:, 0:n], scalar1=lo, scalar2=NEGM, op0=ALU.is_lt, op1=ALU.mult), reads=[rsc, rbs], writes=[rNM])

                def nm_fn(b, kb):
                    n = 128 * (b + 1)
                    if n <= KTOP:
                        if kb == b:
                            return (cb("nmd"), rs("cbf"))
                        return None
                    return (NM[(b // 4) % 2][:, b % 4, kb * 128:(kb + 1) * 128], rs("ds_NM", (b // 4) % 2))

                heads = [dict(kT=(lambda ks, pb=64 * (h % 2): kT2[pb:pb + 64, ks]), kres=rk, vaug=vaug, vres=rv) for h in range(6)]
                if DSA_MODE != 2:
                    masks(0)
                for sb in range(NG):
                    if sb + 1 < NG and DSA_MODE != 2:
                        masks(sb + 1)
                    if DSA_MODE == 1:
                        continue
                    i = sb % 2
                    rq = rs("ds_q", i)
                    dma(qs_t[i][:], qT_d[:, :, sb * 512:(sb + 1) * 512].rearrange("c p t -> p c t"), reads=[rs("qT_d")], writes=[rq])
                    qfn = lambda h, qs, i=i: qs_t[i][64 * (h % 2):64 * (h % 2) + 64, h // 2, qs]
                    att_main(T, sb, 6, 64.0 ** -0.5, heads, qfn, rq, nm_fn)
                    att_norm(T, sb, 6, gaT_d, "gaT_d", 0)
            S_.barrier()

        def mla_phase():
            with contextlib.ExitStack() as st:
                kT = sbt(st, "ml_kT", [96, 6, S], BF16)
                va = sbt(st, "ml_va", [128, 6, NB, 65], BF16)
                rk, rv = rs("ml_kT"), rs("ml_va")
                for h in range(6):
                    dma(kT[:, h, :], kc_d[h], reads=[rs("kc_d")], writes=[rk])
                    dma(va[:, h, :, :], vc_d[h].rearrange("t p c -> p t c"), reads=[rs("vc_d")], writes=[rv])
                T = att_tiles(st)
                qs_t = [sbt(st, f"ml_q{i}", [96, 6, 512], BF16) for i in range(2)]

                def nm_fn(b, kb):
                    if kb == b:
                        return (cb("nmd"), rs("cbf"))
                    return None
                heads = [dict(kT=(lambda ks, h=h: kT[:, h, ks]), kres=rk, vaug=va[:, h, :, :], vres=rv) for h in range(6)]
                for sb in range(NG):
                    i = sb % 2
                    rq = rs("ml_q", i)
                    dma(qs_t[i][:], qc_d[:, :, sb * 512:(sb + 1) * 512].rearrange("h p t -> p h t"), reads=[rs("qc_d")], writes=[rq])
                    qfn = lambda h, qs, i=i: qs_t[i][:, h, qs]
                    att_main(T, sb, 6, 96.0 ** -0.5, heads, qfn, rq, nm_fn)
                    att_norm(T, sb, 6, gcT_d, "gcT_d", 8)
            S_.barrier()

        def s5_consts(l):
            with contextlib.ExitStack() as st:
                ar = sbt(st, "s5_ar", [128, 16], F32)
                ai = sbt(st, "s5_ai", [128, 16], F32)
                stp = sbt(st, "s5_stp", [128, 16], F32)
                rp = rs("s5_par")
                for hh in range(2):
                    dma(ar[64 * hh:64 * hh + 64, :], a_re[l].rearrange("g p -> p g"), writes=[rp], allow_slow_non_contiguous=True)
                    dma(ai[64 * hh:64 * hh + 64, :], a_im[l].rearrange("g p -> p g"), writes=[rp], allow_slow_non_contiguous=True)
                dma(stp[:], log_step[l:l + 1, :].broadcast_to([128, 16]), writes=[rp])
                W = sbt(st, "s5_w", [128, 24, 16], F32)
                rw = rs("s5_w")

                def wv(k):
                    return W[:, k, :]

                def dv(fn, reads=(), writes=()):
                    op("dve", fn, reads=list(reads) + [rp, rw] + rC, writes=list(writes) + [rw])
                TWO_PI = 2.0 * np.pi

                def sincos(dst, ang, shift):
                    t, n_, m = wv(20), wv(21), wv(22)
                    ti = Wi[:, 0, :]
                    dv(lambda e: e.tensor_scalar(out=t, in0=ang, scalar1=shift, scalar2=1.0 / TWO_PI, op0=ALU.add, op1=ALU.mult))
                    dv(lambda e: e.tensor_copy(out=ti, in_=t))
                    dv(lambda e: e.tensor_copy(out=n_, in_=ti))
                    dv(lambda e: e.tensor_tensor(out=t, in0=t, in1=n_, op=ALU.subtract))
                    dv(lambda e: e.tensor_scalar(out=m, in0=t, scalar1=0.5, scalar2=None, op0=ALU.is_gt))
                    dv(lambda e: e.tensor_tensor(out=t, in0=t, in1=m, op=ALU.subtract))
                    dv(lambda e: e.tensor_scalar(out=m, in0=t, scalar1=-0.5, scalar2=None, op0=ALU.is_lt))
                    dv(lambda e: e.tensor_tensor(out=t, in0=t, in1=m, op=ALU.add))
                    op("act", lambda e: e.activation(out=dst, in_=t, func=AF.Sin, scale=TWO_PI), reads=[rw], writes=[rw])
                Wi = sbt(st, "s5_wi", [128, 1, 16], mybir.dt.int32)
                op("act", lambda e: e.activation(out=stp[:], in_=stp[:], func=AF.Exp), reads=[rp], writes=[rp])
                mag, ang, abr, abi, cs, sn = wv(0), wv(1), wv(2), wv(3), wv(4), wv(5)
                dv(lambda e: e.tensor_tensor(out=mag, in0=ar[:], in1=stp[:], op=ALU.mult))
                op("act", lambda e: e.activation(out=mag, in_=mag, func=AF.Exp), reads=[rw], writes=[rw])
                dv(lambda e: e.tensor_tensor(out=ang, in0=ai[:], in1=stp[:], op=ALU.mult))
                sincos(sn, ang, 0.0)
                sincos(cs, ang, np.pi / 2)
                dv(lambda e: e.tensor_tensor(out=abr, in0=mag, in1=cs, op=ALU.mult))
                dv(lambda e: e.tensor_tensor(out=abi, in0=mag, in1=sn, op=ALU.mult))
                den, nr, fr, fi, tA, tB = wv(6), wv(7), wv(8), wv(9), wv(10), wv(11)
                dv(lambda e: e.tensor_tensor(out=den, in0=ar[:], in1=ar[:], op=ALU.mult))
                dv(lambda e: e.tensor_tensor(out=tA, in0=ai[:], in1=ai[:], op=ALU.mult))
                dv(lambda e: e.tensor_tensor(out=den, in0=den, in1=tA, op=ALU.add))
                dv(lambda e: e.reciprocal(out=den, in_=den))
                dv(lambda e: e.tensor_scalar(out=nr, in0=abr, scalar1=-1.0, scalar2=None, op0=ALU.add))
                dv(lambda e: e.tensor_tensor(out=fr, in0=nr, in1=ar[:], op=ALU.mult))
                dv(lambda e: e.tensor_tensor(out=tA, in0=abi, in1=ai[:], op=ALU.mult))
                dv(lambda e: e.tensor_tensor(out=fr, in0=fr, in1=tA, op=ALU.add))
                dv(lambda e: e.tensor_tensor(out=fr, in0=fr, in1=den, op=ALU.mult))
                dv(lambda e: e.tensor_tensor(out=fi, in0=abi, in1=ar[:], op=ALU.mult))
                dv(lambda e: e.tensor_tensor(out=tA, in0=nr, in1=ai[:], op=ALU.mult))
                dv(lambda e: e.tensor_tensor(out=fi, in0=fi, in1=tA, op=ALU.subtract))
                dv(lambda e: e.tensor_tensor(out=fi, in0=fi, in1=den, op=ALU.mult))
                PW = sbt(st, "s5_pw", [128, 9, 2, 16], F32)
                MKp = sbt(st, "s5_mkp", [128, 9, 2, 16], F32)
                FN = sbt(st, "s5_fn", [128, 8, 2, 16], F32)
                CQc = sbt(st, "s5_cqc", [128, 9, 2, 16], F32)

                def cmul(o_re, o_im, x_re, x_im, y_re, y_im):
                    dv(lambda e: e.tensor_tensor(out=tA, in0=x_re, in1=y_re, op=ALU.mult))
                    dv(lambda e: e.tensor_tensor(out=tB, in0=x_im, in1=y_im, op=ALU.mult))
                    dv(lambda e: e.tensor_tensor(out=wv(12), in0=x_re, in1=y_im, op=ALU.mult))
                    dv(lambda e: e.tensor_tensor(out=wv(13), in0=x_im, in1=y_re, op=ALU.mult))
                    dv(lambda e: e.tensor_tensor(out=o_re, in0=tA, in1=tB, op=ALU.subtract))
                    dv(lambda e: e.tensor_tensor(out=o_im, in0=wv(12), in1=wv(13), op=ALU.add))
                dv(lambda e: e.memset(PW[:, 0, 0, :], 1.0))
                dv(lambda e: e.memset(PW[:, 0, 1, :], 0.0))
                for n_ in range(1, 9):
                    cmul(PW[:, n_, 0, :], PW[:, n_, 1, :], PW[:, n_ - 1, 0, :], PW[:, n_ - 1, 1, :], abr, abi)
                dv(lambda e: e.tensor_copy(out=MKp[:, 0, :, :], in_=PW[:, 8, :, :]))
                for k in range(1, 9):
                    cmul(MKp[:, k, 0, :], MKp[:, k, 1, :], MKp[:, k - 1, 0, :], MKp[:, k - 1, 1, :], MKp[:, k - 1, 0, :], MKp[:, k - 1, 1, :])
                sgB = cf[:, CF["sgB"]:CF["sgB"] + 1]
                sgQ = cf[:, CF["sgQ"]:CF["sgQ"] + 1]
                for n_ in range(8):
                    cmul(FN[:, n_, 0, :], FN[:, n_, 1, :], PW[:, n_, 0, :], PW[:, n_, 1, :], fr, fi)
                    dv(lambda e: e.tensor_scalar(out=FN[:, n_, 1, :], in0=FN[:, n_, 1, :], scalar1=sgB, scalar2=None, op0=ALU.mult))
                for n_ in range(9):
                    dv(lambda e: e.tensor_scalar(out=CQc[:, n_, 0, :], in0=PW[:, n_, 0, :], scalar1=sgQ, scalar2=None, op0=ALU.mult))
                    dv(lambda e: e.tensor_scalar(out=CQc[:, n_, 1, :], in0=PW[:, n_, 1, :], scalar1=-1.0, scalar2=None, op0=ALU.mult))
                for k in range(9):
                    dv(lambda e: e.tensor_scalar(out=MKp[:, k, 1, :], in0=MKp[:, k, 1, :], scalar1=sgQ, scalar2=None, op0=ALU.mult))
                bx = [sbt(st, f"s5_bx{i}", [128, 16], F32) for i in range(2)]
                by = [sbt(st, f"s5_by{i}", [128, 16], F32) for i in range(2)]
                cx = [sbt(st, f"s5_cx{i}", [128, 16], F32) for i in range(2)]
                cy = [sbt(st, f"s5_cy{i}", [128, 16], F32) for i in range(2)]
                dr = [sbt(st, f"s5_dr{i}", [128, 1], F32) for i in range(2)]
                Pm = [sbt(st, f"s5_Pm{i}", [128, 128], F32) for i in range(2)]
                CQ = [sbt(st, f"s5_CQ{i}", [128, 9, 16], F32) for i in range(2)]
                Btr = [sbt(st, f"s5_Btr{i}", [128, 8, 16], F32) for i in range(2)]
                KTp = [sbt(st, f"s5_KTp{i}", [128, 15 * 16], F32) for i in range(2)]
                OUTm = [sbt(st, f"s5_OUT{i}", [128, 12, 128], F32) for i in range(2)]
                pp = [pst(st, f"s5_pp{i}", [128, 128], F32) for i in range(2)]
                for i in range(2):
                    op("pool", lambda e: e.memset(KTp[i][:], 0.0), writes=[rs("s5_KTp", i)])
                for g in range(16):
                    i = g % 2
                    rin, rPm, rCQ, rBt, rKT, rOUT, rpp = rs("s5_in", i), rs("s5_Pm", i), rs("s5_CQ", i), rs("s5_Btr", i), rs("s5_KTp", i), rs("s5_OUT", i), rs("s5_pp", i)
                    dma(bx[i][0:64, :], b_re[l, g], writes=[rin])
                    dma(bx[i][64:128, :], b_im[l, g], writes=[rin])
                    dma(by[i][0:64, :], b_im[l, g], writes=[rin])
                    dma(by[i][64:128, :], b_re[l, g], writes=[rin])
                    dma(cx[i][0:64, :], c_re[l, g].rearrange("c p -> p c"), writes=[rin], allow_slow_non_contiguous=True)
                    dma(cx[i][64:128, :], c_im[l, g].rearrange("c p -> p c"), writes=[rin], allow_slow_non_contiguous=True)
                    dma(cy[i][0:64, :], c_im[l, g].rearrange("c p -> p c"), writes=[rin], allow_slow_non_contiguous=True)
                    dma(cy[i][64:128, :], c_re[l, g].rearrange("c p -> p c"), writes=[rin], allow_slow_non_contiguous=True)
                    for s_ in range(8):
                        dma(dr[i][16 * s_:16 * s_ + 16, :], d_skip[l, g].rearrange("(c o) -> c o", o=1), writes=[rin], allow_slow_non_contiguous=True)
                    for s_ in range(8):
                        n_ = 7 - s_
                        op("dve", lambda e: e.tensor_scalar(out=Pm[i][:, 16 * s_:16 * s_ + 16], in0=bx[i][:], scalar1=FN[:, n_, 0, g:g + 1], scalar2=None, op0=ALU.mult), reads=[rin, rw], writes=[rPm])
                        op("dve", lambda e: e.scalar_tensor_tensor(out=Pm[i][:, 16 * s_:16 * s_ + 16], in0=by[i][:], scalar=FN[:, n_, 1, g:g + 1], in1=Pm[i][:, 16 * s_:16 * s_ + 16], op0=ALU.mult, op1=ALU.add),
                           reads=[rin, rw], writes=[rPm])
                    for s_ in range(8):
                        op("dve", lambda e: e.tensor_copy(out=Btr[i][:, s_, :], in_=Pm[i][:, 112:128]), reads=[rPm], writes=[rBt])
                    for n_ in range(9):
                        op("dve", lambda e: e.tensor_scalar(out=CQ[i][:, n_, :], in0=cx[i][:], scalar1=CQc[:, n_, 0, g:g + 1], scalar2=None, op0=ALU.mult), reads=[rin, rw], writes=[rCQ])
                        op("dve", lambda e: e.scalar_tensor_tensor(out=CQ[i][:, n_, :], in0=cy[i][:], scalar=CQc[:, n_, 1, g:g + 1], in1=CQ[i][:, n_, :], op0=ALU.mult, op1=ALU.add),
                           reads=[rin, rw], writes=[rCQ])
                    op("pe", lambda e: e.transpose(out=pp[i][:], in_=Pm[i][:], identity=cfa("ident")), reads=[rPm] + rC, writes=[rpp])
                    op("act", lambda e: e.copy(out=OUTm[i][:, 0, :], in_=pp[i][:]), reads=[rpp], writes=[rOUT])
                    op("dve", lambda e: e.tensor_copy(out=OUTm[i][:, 1, :], in_=CQ[i][:, 1:9, :].rearrange("p n c -> p (n c)")), reads=[rCQ], writes=[rOUT])
                    op("pe", lambda e: e.matmul(pp[i][:], lhsT=Btr[i][:].rearrange("p s c -> p (s c)"), rhs=CQ[i][:, 0:8, :].rearrange("p n c -> p (n c)"), start=True, stop=True),
                       reads=[rBt, rCQ], writes=[rpp])
                    op("act", lambda e: e.copy(out=KTp[i][:, 112:240], in_=pp[i][:]), reads=[rpp], writes=[rKT])
                    op("dve", lambda e: e.scalar_tensor_tensor(out=KTp[i][:, 112:128], in0=cfa("i16", 128, 16), scalar=dr[i][:, 0:1], in1=KTp[i][:, 112:128], op0=ALU.mult, op1=ALU.add),
                       reads=[rin, rKT] + rC, writes=[rKT])
                    for s_ in range(8):
                        dma(s5c_d[g, 16 * s_:16 * s_ + 16, 2, :], KTp[i][16 * s_:16 * s_ + 16, (7 - s_) * 16:(7 - s_) * 16 + 128], reads=[rKT], writes=[rs("s5c_d")])
                    for k in range(9):
                        op("dve", lambda e: e.tensor_scalar(out=OUTm[i][:, 3 + k, :], in0=cfa("ident"), scalar1=MKp[:, k, 0, g:g + 1], scalar2=None, op0=ALU.mult), reads=[rw] + rC, writes=[rOUT])
                        op("dve", lambda e: e.scalar_tensor_tensor(out=OUTm[i][:, 3 + k, :], in0=cfa("icross"), scalar=MKp[:, k, 1, g:g + 1], in1=OUTm[i][:, 3 + k, :], op0=ALU.mult, op1=ALU.add),
                           reads=[rw] + rC, writes=[rOUT])
                    dma(s5c_d[g, :, 0:2, :], OUTm[i][:, 0:2, :], reads=[rOUT], writes=[rs("s5c_d")])
                    dma(s5c_d[g, :, 3:12, :], OUTm[i][:, 3:12, :], reads=[rOUT], writes=[rs("s5c_d")])
            S_.barrier()

        def s5_phase():
            with contextlib.ExitStack() as st:
                NPAR = 4
                assert J <= 512
                Cm = [sbt(st, f"s5r_C{i}", [128, 12, 128], F32) for i in range(NPAR)]
                Ut = [sbt(st, f"s5r_U{i}", [128, J], F32) for i in range(NPAR)]
                St = [sbt(st, f"s5r_S{i}", [128, J], F32) for i in range(NPAR)]
                Yt = [sbt(st, f"s5r_Y{i}", [128, J], F32) for i in range(NPAR)]
                pk = [pst(st, f"s5r_p{i}", [128, 512], F32) for i in range(8)]
                pc = [0]

                def nextp():
                    k = pc[0] % 8
                    pc[0] += 1
                    return k, rs("s5r_p", k)
                for g0 in range(0, 16, NPAR):
                    gs = list(range(g0, g0 + NPAR))
                    R_ = {}
                    for g in gs:
                        i = g % NPAR
                        R_[g] = (rs("s5r_C", i), rs("s5r_U", i), rs("s5r_S", i), rs("s5r_Y", i))
                        rCm, rU, rSt, rY = R_[g]
                        dma(Cm[i][:], s5c_d[g], reads=[rs("s5c_d")], writes=[rCm])
                        for s_ in range(8):
                            dma(Ut[i][16 * s_:16 * s_ + 16, :], U_d[g, s_, :, :], reads=[rs("U_d")], writes=[rU])
                    for g in gs:
                        i = g % NPAR
                        rCm, rU, rSt, rY = R_[g]
                        k, rpk = nextp()
                        op("pe", lambda e: e.matmul(pk[k][:, 0:J], lhsT=Cm[i][:, 0, :], rhs=Ut[i][:, 0:J], start=True, stop=True), reads=[rCm, rU], writes=[rpk])
                        op("act", lambda e: e.copy(out=St[i][:, 0:J], in_=pk[k][:, 0:J]), reads=[rpk], writes=[rSt])
                    k_ = 0
                    while (1 << k_) < J:
                        sh = 1 << k_
                        wc = J - sh
                        prods = []
                        for g in gs:
                            i = g % NPAR
                            rCm, rU, rSt, rY = R_[g]
                            k, rpk = nextp()
                            op("pe", lambda e: e.matmul(pk[k][:, 0:wc], lhsT=Cm[i][:, 3 + k_, :], rhs=St[i][:, 0:wc], start=True, stop=True), reads=[rCm, rSt], writes=[rpk])
                            prods.append((g, k, rpk))
                        for n_, (g, k, rpk) in enumerate(prods):
                            i = g % NPAR
                            rSt = R_[g][2]
                            op("dve", lambda e: e.tensor_tensor(out=St[i][:, sh:J], in0=pk[k][:, 0:wc], in1=St[i][:, sh:J], op=ALU.add), reads=[rpk, rSt], writes=[rSt])
                        k_ += 1
                    for g in gs:
                        i = g % NPAR
                        rCm, rU, rSt, rY = R_[g]
                        k, rpk = nextp()
                        op("pe", lambda e: e.matmul(pk[k][:, 0:J], lhsT=Cm[i][:, 2, :], rhs=Ut[i][:, 0:J], start=True, stop=False), reads=[rCm, rU], writes=[rpk])
                        op("pe", lambda e: e.matmul(pk[k][:, 1:J], lhsT=Cm[i][:, 1, :], rhs=St[i][:, 0:J - 1], start=False, stop=True), reads=[rCm, rSt], writes=[rpk])
                        op("act", lambda e: e.copy(out=Yt[i][:, 0:J], in_=pk[k][:, 0:J]), reads=[rpk], writes=[rY])
                        for s_ in range(8):
                            dma(Y_d[g, s_, :, :], Yt[i][16 * s_:16 * s_ + 16, :], reads=[rY], writes=[rs("Y_d")])
            S_.barrier()
            with contextlib.ExitStack() as st:
                yp = sbt(st, "s5p_yp", [128, 2, 8, 64], F32)
                y = sbt(st, "s5p_y", [128, 2, 512], F32)
                t = sbt(st, "s5p_t", [128, 2, 512], F32)
                gl = sbt(st, "s5p_g", [128, 2, 512], F32)
                gb16 = sbt(st, "s5p_gb", [128, 2, 512], BF16)
                sg = sbt(st, "s5p_sg", [128, 2, 512], F32)
                gate = sbt(st, "s5p_gate", [128, 2, 512], BF16)
                ob = sbt(st, "s5p_ob", [128, 2, 512], BF16)
                zp = [pst(st, f"s5p_z{i}", [128, 512], F32) for i in range(2)]
                ryp, ry, rt_, rgl, rgb, rsg, rgate, rob = (rs("s5p_" + n_) for n_ in ("yp", "y", "t", "g", "gb", "sg", "gate", "ob"))
                for tg in range(NG):
                    tsl = slice(tg * 512, tg * 512 + 512)
                    for c in range(2):
                        for g8 in range(8):
                            dma(yp[16 * g8:16 * g8 + 16, c, :, :], Y_d[8 * c + g8, :, :, tg * 64:(tg + 1) * 64].rearrange("s c j -> c s j"), reads=[rs("Y_d")], writes=[ryp])
                        dma(gate[:, c, :], gbT_d[c, :, tsl], reads=[rs("gbT_d")], writes=[rgate])
                    for c in range(2):
                        op("act", lambda e: e.copy(out=y[:, c, :].rearrange("p (j s) -> p s j", s=8), in_=yp[:, c, :, :]), reads=[ryp], writes=[ry])
                        op("act", lambda e: e.activation(out=t[:, c, :], in_=y[:, c, :], func=AF.Square), reads=[ry], writes=[rt_])
                        op("dve", lambda e: e.tensor_scalar(out=t[:, c, :], in0=t[:, c, :], scalar1=0.044715, scalar2=1.0, op0=ALU.mult, op1=ALU.add), reads=[rt_], writes=[rt_])
                        op("dve", lambda e: e.tensor_tensor(out=t[:, c, :], in0=t[:, c, :], in1=y[:, c, :], op=ALU.mult), reads=[rt_, ry], writes=[rt_])
                        op("act", lambda e: e.activation(out=t[:, c, :], in_=t[:, c, :], func=AF.Sigmoid, scale=2.0 * 0.7978845608028654), reads=[rt_], writes=[rt_])
                        op("dve", lambda e: e.tensor_tensor(out=gl[:, c, :], in0=t[:, c, :], in1=y[:, c, :], op=ALU.mult), reads=[rt_, ry], writes=[rgl])
                        op("dve", lambda e: e.tensor_copy(out=gb16[:, c, :], in_=gl[:, c, :]), reads=[rgl], writes=[rgb])
                    for co in range(2):
                        rz = rs("s5p_z", co)
                        for c in range(2):
                            op("pe", lambda e: e.matmul(zp[co][:], lhsT=Wglu[:, c, co * 128:(co + 1) * 128], rhs=gb16[:, c, :], start=(c == 0), stop=(c == 1)), reads=[rWsm, rgb], writes=[rz])
                        op("act", lambda e: e.activation(out=sg[:, co, :], in_=zp[co][:], func=AF.Sigmoid), reads=[rz], writes=[rsg])
                        op("dve", lambda e: e.tensor_tensor(out=sg[:, co, :], in0=sg[:, co, :], in1=gl[:, co, :], op=ALU.mult), reads=[rsg, rgl], writes=[rsg])
                        op("dve", lambda e: e.tensor_tensor(out=ob[:, co, :], in0=sg[:, co, :], in1=gate[:, co, :], op=ALU.mult), reads=[rsg, rgate], writes=[rob])
                        dma(mix_d[6 + co, :, tsl], ob[:, co, :], reads=[rob], writes=[rs("mix_d")])
            S_.barrier()

        def phase3(src_ap, src_res, dst_ap, dst_res):
            with contextlib.ExitStack() as st:
                Wo = sbt(st, "p3_Wo", [128, 14, D], BF16)
                rWo = rs("p3_Wo")
                for c in range(14):
                    nr = 128 if c in (6, 7) else 64
                    dma(Wo[0:nr, c, :], wout_d[0:nr, c, :], reads=[rs("wout_d")], writes=[rWo])
                mx = [sbt(st, f"p3_mx{i}", [128, 14, 512], BF16) for i in range(2)]
                xt = [sbt(st, f"p3_xt{i}", [128, 4, D], F32) for i in range(2)]
                pp = [pst(st, f"p3_pp{i}", [128, 512], F32) for i in range(4)]
                pc = 0
                for tg in range(NG):
                    i = tg % 2
                    rmx, rxt = rs("p3_mx", i), rs("p3_xt", i)
                    tsl = slice(tg * 512, tg * 512 + 512)
                    for c in range(14):
                        nr = 128 if c in (6, 7) else 64
                        dma(mx[i][0:nr, c, :], mix_d[c, 0:nr, tsl], reads=[rs("mix_d")], writes=[rmx])
                    dma(xt[i][:], src_ap[tg * 512:(tg + 1) * 512, :].rearrange("(j p) d -> p j d", p=128), reads=[src_res], writes=[rxt])
                    for j in range(4):
                        for half in range(2):
                            k = pc % 4
                            pc += 1
                            rpp = rs("p3_pp", k)
                            for c in range(14):
                                nr = 128 if c in (6, 7) else 64
                                op("pe", lambda e: e.matmul(pp[k][:], lhsT=mx[i][0:nr, c, j * 128:(j + 1) * 128], rhs=Wo[0:nr, c, half * 512:(half + 1) * 512], start=(c == 0), stop=(c == 13)),
                                   reads=[rmx, rWo], writes=[rpp])
                            op("dve", lambda e: e.tensor_tensor(out=xt[i][:, j, half * 512:(half + 1) * 512], in0=pp[k][:], in1=xt[i][:, j, half * 512:(half + 1) * 512], op=ALU.add), reads=[rpp, rxt], writes=[rxt])
                    dma(dst_ap[tg * 512:(tg + 1) * 512, :].rearrange("(j p) d -> p j d", p=128), xt[i][:], reads=[rxt], writes=[dst_res])
            S_.barrier()

        for l in range(NL):
            if "prep" in PH:
                prep_weights(l)
            if "s5c" in PH:
                s5_consts(l)
            for s in range(NSEQ):
                src = x_in[s] if l == 0 else xs_d[s]
                dst = out_d[s] if l == NL - 1 else xs_d[s]
                if "p1" in PH:
                    phase1(l, src, rs("xs", s))
                if "dsa" in PH:
                    dsa_phase()
                if "s5" in PH:
                    s5_phase()
                if "mla" in PH:
                    mla_phase()
                if "p3" in PH:
                    phase3(src, rs("xs", s), dst, rs("out_d") if l == NL - 1 else rs("xs", s))
        S_.barrier()
    return nc


xs_all = None


def _build(S, NL, NSEQ, KTOP):
    global xs_all
    return build_program(S, NL, NSEQ, KTOP)


_CACHE = {}


def kernel(**inputs):
    x = np.ascontiguousarray(np.asarray(inputs["x"], dtype=np.float32))
    B, S, _ = x.shape
    ncores = 8
    NSEQ = B // ncores
    KTOP = min(256, S // 4)
    nc = build_program(S, DEPTH, NSEQ, KTOP)
    consts = make_consts(S)
    names = ["norm_g", "w_in", "attn_q_norm", "attn_k_norm", "mla_q_lora_norm", "mla_kv_lora_norm", "mla_w_uq", "mla_w_ukv",
             "mla_q_norm", "mla_k_norm", "ssm_a_re", "ssm_a_im", "ssm_b_re", "ssm_b_im", "ssm_c_re", "ssm_c_im", "ssm_d",
             "ssm_log_step", "ssm_w_glu", "w_out"]
    shared = {n: np.ascontiguousarray(np.asarray(inputs[n], dtype=np.float32)) for n in names}
    in_maps = []
    for c in range(ncores):
        m = dict(shared)
        m["x"] = x[c * NSEQ:(c + 1) * NSEQ]
        m.update(consts)
        in_maps.append(m)
    res = run_bass_kernel_spmd(nc, in_maps, core_ids=list(range(ncores)))
    return np.concatenate([r["out"] for r in res.results], axis=0).astype(np.float32)
```

```python
import contextlib
import numpy as np
import ml_dtypes
import concourse.bass as bass
import concourse.mybir as mybir
from concourse.bass_utils import run_bass_kernel_spmd

F32 = mybir.dt.float32
BF16 = mybir.dt.bfloat16
AF = mybir.ActivationFunctionType
ALU = mybir.AluOpType
AX = mybir.AxisListType

D = 1024
DEPTH = 4
DIN = 2504
EPS = 1e-6
NCOL = 3240
C_IQ2 = 2728
C_QA, C_KA2, C_IQ, C_IK4, C_GA, C_U, C_GB, C_CQ, C_CKV, C_KPE, C_GC, C_VAIW = (
    0, 384, 512, 768, 896, 1280, 1536, 1792, 2048, 2176, 2272, 2656)
S_QA, S_KA, S_VA, S_IQ, S_IK, S_IW, S_GA, S_U, S_GB, S_CQ, S_CKV, S_KPE, S_GC = (
    0, 384, 448, 512, 768, 800, 808, 1192, 1448, 1704, 1960, 2088, 2120)
NEGM = -30000.0
NIT = 16
import os as _os5
P1ENG = _os5.environ.get('P1ENG', 'pool')
import os as _os4
DSA_MODE = int(_os4.environ.get('DSA_MODE', '0'))
import os as _os3
GV = int(_os3.environ.get('GV', '0'))
import os as _os2
GATE_DMA = int(_os2.environ.get('GATE_DMA', '2'))
import os as _os
P1CUT = int(_os.environ.get('P1CUT', '99'))
import os
PH = set(os.environ.get('PH', 'prep,s5c,p1,dsa,s5,mla,p3').split(','))


class Res:
    __slots__ = ("w", "r", "ps")

    def __init__(self, ps=False):
        self.w = {}
        self.r = {}
        self.ps = ps


class Sched:
    def __init__(self, nc, n_dma_sems=32):
        self.nc = nc
        self.engs = {}
        for name, e in (("pe", nc.tensor), ("act", nc.scalar), ("dve", nc.vector),
                        ("pool", nc.gpsimd), ("sp", nc.sync)):
            sem = nc.alloc_semaphore(name="sem_" + name)
            self.engs[name] = dict(e=e, sem=sem, cnt=0, waited={}, name=name)
        self.dma_sems = [nc.alloc_semaphore(name=f"dsem{i}") for i in range(n_dma_sems)]
        self.dma_cnt = [0] * n_dma_sems
        self.dma_i = 0
        self.nins = 0

    def _wait(self, en, deps, skip_self=False):
        E = self.engs[en]
        best = {}
        for sem, val in deps:
            k = id(sem)
            if k not in best or best[k][1] < val:
                best[k] = (sem, val)
        for k, (sem, val) in best.items():
            if skip_self and sem is E["sem"]:
                continue
            if E["waited"].get(k, 0) < val:
                E["e"].wait_ge(sem, val)
                E["waited"][k] = val

    def _deps(self, reads, writes, en=None):
        deps = []
        own = self.engs[en]["sem"] if en is not None else None
        for r in reads:
            deps.extend(r.w.values())
            if r.ps:
                deps.extend(ev for ev in r.r.values() if ev[0] is not own)
        for w in writes:
            deps.extend(w.w.values())
            deps.extend(w.r.values())
        return deps

    @staticmethod
    def _mark(ev, reads, writes):
        k = id(ev[0])
        for r in reads:
            r.r[k] = ev
        for w in writes:
            w.w[k] = ev

    def op(self, en, fn, reads=(), writes=()):
        E = self.engs[en]
        self._wait(en, self._deps(reads, writes, en), skip_self=(en == "pe"))
        ins = fn(E["e"])
        E["cnt"] += 1
        ins.then_inc(E["sem"], 1)
        ev = (E["sem"], E["cnt"])
        self._mark(ev, reads, writes)
        self.nins += 1
        return ev

    def dma(self, out, in_, reads=(), writes=(), en="sp", **kw):
        E = self.engs[en]
        i = self.dma_i % len(self.dma_sems)
        self.dma_i += 1
        sem = self.dma_sems[i]
        deps = self._deps(reads, writes)
        if self.dma_cnt[i] > 0:
            deps.append((sem, self.dma_cnt[i]))
        self._wait(en, deps)
        ins = E["e"].dma_start(out=out, in_=in_, **kw)
        self.dma_cnt[i] += 16
        ins.then_inc(sem, 16)
        ev = (sem, self.dma_cnt[i])
        self._mark(ev, reads, writes)
        self.nins += 1
        return ev

    def barrier(self):
        evs = [(E["sem"], E["cnt"]) for E in self.engs.values() if E["cnt"] > 0]
        evs += [(s, c) for s, c in zip(self.dma_sems, self.dma_cnt) if c > 0]
        for en in self.engs:
            self._wait(en, evs)


def rope_tab(S, d, rows):
    half = d // 2
    inv = (np.float32(10000.0) ** (-np.arange(half, dtype=np.float32) * np.float32(2.0) / np.float32(d))).astype(np.float32)
    ang = np.arange(S, dtype=np.float32)[:, None] * inv[None, :]
    c = np.cos(ang).astype(np.float32).T
    s = np.sin(ang).astype(np.float32).T
    idx = np.arange(rows) % half
    return c[idx], s[idx]


def make_consts(S):
    bf = ml_dtypes.bfloat16
    I = np.eye(128, dtype=np.float32)
    ones64 = np.zeros((128, 128), np.float32)
    ones64[:64, :64] = 1; ones64[64:, 64:] = 1
    ones96 = np.zeros((128, 128), np.float32); ones96[:96, :96] = 1
    ones128 = np.ones((128, 128), np.float32)

    def rot(M, hd, half, lo=0):
        R = np.zeros((128, 128), np.float32)
        for m in range(M):
            j = m % hd
            if j < lo:
                continue
            jj = j - lo
            if jj < half:
                R[m + half, m] = -1.0
            else:
                R[m - half, m] = 1.0
        return R
    R64 = rot(128, 64, 32)
    R32 = rot(128, 32, 16)
    R96 = rot(96, 96, 16, lo=64)
    nmd = np.zeros((128, 128), np.float32); nmd[:64, 64:] = NEGM
    cbf = np.concatenate([I, ones64, ones96, ones128, R64, R32, R96, nmd], axis=1).astype(bf)
    icross = np.zeros((128, 128), np.float32)
    for m in range(128):
        icross[m, (m + 64) % 128] = 1
    e65 = np.zeros((128, 64), np.float32); e65[64, :] = 1
    i16 = np.zeros((128, 16), np.float32)
    for p in range(128):
        i16[p, p % 16] = 1
    sgB = np.where(np.arange(128) < 64, -1.0, 1.0).astype(np.float32)[:, None]
    sgQ = -sgB
    epsc = np.full((128, 1), EPS, np.float32)
    pw = np.tile((2.0 ** -(np.arange(NIT + 1, dtype=np.float64) + 1.0)).astype(np.float32)[None, :], (128, 1))
    cf = np.concatenate([I, icross, e65, i16, sgB, sgQ, epsc, pw], axis=1).astype(np.float32)
    c64, s64 = rope_tab(S, 64, 128)
    c32, s32 = rope_tab(S, 32, 128)
    c96 = np.ones((128, S), np.float32); s96 = np.zeros((128, S), np.float32)
    c96[64:96] = c32[:32]; s96[64:96] = s32[:32]
    rt = np.stack([c64, s64, c32, s32, c96, s96]).astype(np.float32)
    return dict(cbf=cbf, cf=cf, rt=rt)


CB = dict(ident=0, ones64=128, ones96=256, ones128=384, R64=512, R32=640, R96=768, nmd=896)
CF = dict(ident=0, icross=128, e65=256, i16=320, sgB=336, sgQ=337, eps=338, pw=339)
NCF = 339 + NIT + 1


def build_program(S, NL, NSEQ, KTOP, dbg=None):
    nc = bass.Bass("TRN2", target_bir_lowering=False)
    NB = S // 128
    NG = S // 512
    J = S // 8
    dbg = dbg or {}

    def din(name, shape, dt=F32):
        return nc.dram_tensor(name, list(shape), dt, kind="ExternalInput").ap()

    def dscr(name, shape, dt):
        return nc.dram_tensor(name, list(shape), dt, kind="Internal").ap()

    x_in = din("x", [NSEQ, S, D])
    norm_g = din("norm_g", [DEPTH, D])
    w_in = din("w_in", [DEPTH, D, DIN])
    attn_q_norm = din("attn_q_norm", [DEPTH, 64])
    attn_k_norm = din("attn_k_norm", [DEPTH, 64])
    q_lora_g = din("mla_q_lora_norm", [DEPTH, 256])
    kv_lora_g = din("mla_kv_lora_norm", [DEPTH, 128])
    w_uq = din("mla_w_uq", [DEPTH, 256, 576])
    w_ukv = din("mla_w_ukv", [DEPTH, 128, 768])
    mla_q_norm = din("mla_q_norm", [DEPTH, 96])
    mla_k_norm = din("mla_k_norm", [DEPTH, 96])
    a_re = din("ssm_a_re", [DEPTH, 16, 64])
    a_im = din("ssm_a_im", [DEPTH, 16, 64])
    b_re = din("ssm_b_re", [DEPTH, 16, 64, 16])
    b_im = din("ssm_b_im", [DEPTH, 16, 64, 16])
    c_re = din("ssm_c_re", [DEPTH, 16, 16, 64])
    c_im = din("ssm_c_im", [DEPTH, 16, 16, 64])
    d_skip = din("ssm_d", [DEPTH, 16, 16])
    log_step = din("ssm_log_step", [DEPTH, 16])
    w_glu = din("ssm_w_glu", [DEPTH, 256, 256])
    w_out = din("w_out", [DEPTH, D, D])
    cbf_d = din("cbf", [128, 1024], BF16)
    cf_d = din("cf", [128, NCF])
    rt_d = din("rt", [6, 128, S])
    out_d = nc.dram_tensor("out", [NSEQ, S, D], F32, kind="ExternalOutput").ap()

    xs_d = dscr("xs", [NSEQ, S, D], F32)
    winp_d = dscr("winp", [128, 8, NCOL], BF16)
    wout_d = dscr("woutp", [128, 14, D], BF16)
    qT_d = dscr("qT", [3, 128, S], BF16)
    kT2_d = dscr("kT2", [128, S], BF16)
    va_d = dscr("va", [NB, 128, 65], BF16)
    iqT_d = dscr("iqT", [4, 128, S], BF16)
    ikT_d = dscr("ikT", [128, S], BF16)
    iw_d = dscr("iw", [NB, 128, 16], F32)
    gaT_d = dscr("gaT", [6, 64, S], BF16)
    gbT_d = dscr("gbT", [2, 128, S], BF16)
    gcT_d = dscr("gcT", [6, 64, S], BF16)
    U_d = dscr("U", [16, 8, 16, J], F32)
    Y_d = dscr("Y", [16, 8, 16, J], F32)
    qc_d = dscr("qc", [6, 96, S], BF16)
    kc_d = dscr("kc", [6, 96, S], BF16)
    vc_d = dscr("vc", [6, NB, 128, 65], BF16)
    mix_d = dscr("mix", [14, 128, S], BF16)
    s5c_d = dscr("s5c", [16, 128, 12, 128], F32)

    S_ = Sched(nc)
    op = S_.op
    dma = S_.dma
    RS = {}

    PSUM_KEYS = {"p1_tp", "p1_mm", "p1_aux", "at_ps", "at_o", "at_sbp", "ds_shp", "ds_scp", "s5_pp", "s5r_p", "s5p_z", "p3_pp"}

    def rs(*key):
        if key not in RS:
            RS[key] = Res(ps=(key[0] in PSUM_KEYS))
        return RS[key]

    with contextlib.ExitStack() as top:
        uid = [0]

        def sbt(st, name, shape, dt):
            uid[0] += 1
            return st.enter_context(nc.sbuf_tensor(f"sb{uid[0]}_{name}", list(shape), dt))

        def pst(st, name, shape, dt):
            uid[0] += 1
            return st.enter_context(nc.psum_tensor(f"ps{uid[0]}_{name}", list(shape), dt))

        cbf = sbt(top, "cbf", [128, 1024], BF16)
        cf = sbt(top, "cf", [128, NCF], F32)
        dma(cbf[:], cbf_d, writes=[rs("cbf")])
        dma(cf[:], cf_d, writes=[rs("cf")])
        rC = [rs("cbf"), rs("cf")]

        def cb(name, rows=128, cols=128):
            o = CB[name]
            return cbf[0:rows, o:o + cols]

        def cfa(name, rows=128, cols=128):
            o = CF[name]
            return cf[0:rows, o:o + cols]
        eps_ap = cf[:, CF["eps"]:CF["eps"] + 1]

        gsm = sbt(top, "gsm", [128, DEPTH, 16], F32)
        rg = rs("gsm")
        op("dve", lambda e: e.memset(gsm[:], 1.0), writes=[rg])
        for l in range(NL):
            dma(gsm[:, l, 0:8], norm_g[l].rearrange("(k p) -> p k", p=128), writes=[rg], allow_slow_non_contiguous=True)
            for hh in range(2):
                dma(gsm[64 * hh:64 * hh + 64, l, 8:9], attn_q_norm[l].rearrange("(p o) -> p o", o=1), writes=[rg], allow_slow_non_contiguous=True)
                dma(gsm[64 * hh:64 * hh + 64, l, 9:10], attn_k_norm[l].rearrange("(p o) -> p o", o=1), writes=[rg], allow_slow_non_contiguous=True)
            dma(gsm[:, l, 10:12], q_lora_g[l].rearrange("(k p) -> p k", p=128), writes=[rg], allow_slow_non_contiguous=True)
            dma(gsm[:, l, 12:13], kv_lora_g[l].rearrange("(p o) -> p o", o=1), writes=[rg], allow_slow_non_contiguous=True)
            dma(gsm[0:96, l, 13:14], mla_q_norm[l].rearrange("(p o) -> p o", o=1), writes=[rg], allow_slow_non_contiguous=True)
            dma(gsm[0:96, l, 14:15], mla_k_norm[l].rearrange("(p o) -> p o", o=1), writes=[rg], allow_slow_non_contiguous=True)

        def prep_weights(l):
            with contextlib.ExitStack() as st:
                stg = [sbt(st, f"stg{i}", [128, DIN], F32) for i in range(2)]
                wrow = [sbt(st, f"wrow{i}", [128, NCOL], BF16) for i in range(2)]
                for kc in range(8):
                    i = kc % 2
                    rst, rw = rs("stg", i), rs("wrow", i)
                    dma(stg[i][:], w_in[l, kc * 128:(kc + 1) * 128, :], writes=[rst])
                    g = gsm[:, l, kc:kc + 1]
                    segs = [(C_QA, S_QA, 384), (C_KA2, S_KA, 64), (C_KA2 + 64, S_KA, 64)] + [(C_IQ2 + (h_ // 2) * 128 + 64 * (h_ % 2), S_IQ + 32 * h_, 32) for h_ in range(8)] + \
                           [(C_IK4 + 32 * r, S_IK, 32) for r in range(4)] + \
                           [(C_GA, S_GA, 384), (C_U, S_U, 256), (C_GB, S_GB, 256), (C_CQ, S_CQ, 256), (C_CKV, S_CKV, 128),
                            (C_KPE + 64, S_KPE, 32), (C_GC, S_GC, 384), (C_VAIW, S_VA, 64), (C_VAIW + 64, S_IW, 8)]
                    op("pool", lambda e: e.memset(wrow[i][:, C_KPE:C_KPE + 64], 0.0), writes=[rw])
                    op("pool", lambda e: e.memset(wrow[i][:, C_IQ2:C_IQ2 + 512], 0.0), writes=[rw])
                    op("pool", lambda e: e.memset(wrow[i][:, C_IQ:C_IQ + 256], 0.0), writes=[rw])
                    for n_, (dc, sc_, w_) in enumerate(segs):
                        if n_ % 2 == 0:
                            op("dve", lambda e: e.tensor_scalar(out=wrow[i][:, dc:dc + w_], in0=stg[i][:, sc_:sc_ + w_], scalar1=g, scalar2=None, op0=ALU.mult),
                               reads=[rst, rg], writes=[rw])
                        else:
                            op("act", lambda e: e.activation(out=wrow[i][:, dc:dc + w_], in_=stg[i][:, sc_:sc_ + w_], func=AF.Copy, scale=g),
                               reads=[rst, rg], writes=[rw])
                    dma(winp_d[:, kc, :], wrow[i][:], reads=[rw], writes=[rs("winp_d")])
            with contextlib.ExitStack() as st:
                stg = [sbt(st, f"stgo{i}", [128, D], F32) for i in range(2)]
                wrow = [sbt(st, f"wrowo{i}", [128, D], BF16) for i in range(2)]
                for c in range(14):
                    i = c % 2
                    rst, rw = rs("stgo", i), rs("wrowo", i)
                    if c < 6:
                        r0, nr = 64 * c, 64
                    elif c < 8:
                        r0, nr = 384 + 128 * (c - 6), 128
                    else:
                        r0, nr = 640 + 64 * (c - 8), 64
                    dma(stg[i][0:nr, :], w_out[l, r0:r0 + nr, :], writes=[rst])
                    if c % 2 == 0:
                        op("dve", lambda e: e.tensor_copy(out=wrow[i][0:nr, :], in_=stg[i][0:nr, :]), reads=[rst], writes=[rw])
                    else:
                        op("act", lambda e: e.copy(out=wrow[i][0:nr, :], in_=stg[i][0:nr, :]), reads=[rst], writes=[rw])
                    dma(wout_d[0:nr, c, :], wrow[i][0:nr, :], reads=[rw], writes=[rs("wout_d")])
            with contextlib.ExitStack() as st:
                s1 = sbt(st, "s_uq", [128, 2, 576], F32)
                s2 = sbt(st, "s_ukv", [128, 768], F32)
                s3 = sbt(st, "s_glu", [128, 2, 256], F32)
                r1, r2, r3 = rs("s_uq"), rs("s_ukv"), rs("s_glu")
                dma(s1[:], w_uq[l].rearrange("(k p) n -> p k n", p=128), writes=[r1])
                dma(s2[:], w_ukv[l], writes=[r2])
                dma(s3[:], w_glu[l].rearrange("(k p) n -> p k n", p=128), writes=[r3])
                rw = rs("wsm")
                for k in range(2):
                    op("dve", lambda e: e.tensor_scalar(out=Wuq[:, k, :], in0=s1[:, k, :], scalar1=gsm[:, l, 10 + k:11 + k], scalar2=None, op0=ALU.mult),
                       reads=[r1, rg], writes=[rw])
                    op("dve", lambda e: e.tensor_copy(out=Wglu[:, k, :], in_=s3[:, k, :]), reads=[r3], writes=[rw])
                op("dve", lambda e: e.memset(WukT[:], 0.0), writes=[rw])
                for h in range(6):
                    op("dve", lambda e: e.tensor_scalar(out=WukT[:, h, 0:64], in0=s2[:, h * 128:h * 128 + 64], scalar1=gsm[:, l, 12:13], scalar2=None, op0=ALU.mult),
                       reads=[r2, rg], writes=[rw])
                    op("dve", lambda e: e.tensor_scalar(out=Wuv[:, h * 64:h * 64 + 64], in0=s2[:, h * 128 + 64:h * 128 + 128], scalar1=gsm[:, l, 12:13], scalar2=None, op0=ALU.mult),
                       reads=[r2, rg], writes=[rw])
            S_.barrier()

        Wuq = sbt(top, "Wuq", [128, 2, 576], BF16)
        WukT = sbt(top, "WukT", [128, 6, 96], BF16)
        Wuv = sbt(top, "Wuv", [128, 384], BF16)
        Wglu = sbt(top, "Wglu", [128, 2, 256], BF16)
        rWsm = rs("wsm")

        def phase1(l, src_ap, src_res):
            with contextlib.ExitStack() as st:
                Winp = sbt(st, "Winp", [128, 8, NCOL], BF16)
                rW = rs("Winp")
                for kc in range(8):
                    dma(Winp[:, kc, :], winp_d[:, kc, :], reads=[rs("winp_d")], writes=[rW])
                xt = sbt(st, "p1_xt", [128, 4, D], F32)
                rxt = rs("p1_xt")
                junk = sbt(st, "p1_junk", [128, D], BF16)
                xn = sbt(st, "p1_xn", [128, D], BF16)
                ssq = sbt(st, "p1_ssq", [128, 8], F32)
                hT = sbt(st, "p1_hT", [128, 8, 512], BF16)
                rhT = rs("p1_hT")
                tabs = sbt(st, "p1_tabs", [128, 6, 512], F32)
                rtab = rs("p1_tabs")
                tp = [pst(st, f"p1_tp{i}", [128, 1024], BF16) for i in range(2)]
                mm = [pst(st, f"p1_mm{i}", [128, 512], F32) for i in range(3)]
                aux = [pst(st, f"p1_aux{i}", [128, 512], F32) for i in range(3)]
                mmi = [0]
                NWK = 3
                xg = [sbt(st, f"p1_xg{i}", [128, 512], BF16) for i in range(NWK)]
                sq = [sbt(st, f"p1_sq{i}", [128, 512], BF16) for i in range(NWK)]
                sd = [sbt(st, f"p1_sd{i}", [128, 512], F32) for i in range(NWK)]
                t1 = [sbt(st, f"p1_t1{i}", [128, 512], F32) for i in range(NWK)]
                t2 = [sbt(st, f"p1_t2{i}", [128, 512], F32) for i in range(NWK)]
                ob = [sbt(st, f"p1_ob{i}", [128, 512], BF16) for i in range(NWK)]
                up = [sbt(st, f"p1_up{i}", [128, 8, 64], F32) for i in range(2)]
                cqn = sbt(st, "p1_cqn", [128, 2, 512], BF16)
                ckvn = sbt(st, "p1_ckvn", [128, 512], BF16)
                vcs = [sbt(st, f"p1_vcs{i}", [128, 6, 65], BF16) for i in range(2)]
                vas = [sbt(st, f"p1_vas{i}", [128, 65], BF16) for i in range(2)]
                iws = [sbt(st, f"p1_iws{i}", [128, 16], F32) for i in range(2)]
                for i in range(2):
                    op("pool", lambda e: e.memset(vcs[i][:], 1.0), writes=[rs("p1_vcs", i)])
                    op("pool", lambda e: e.memset(vas[i][:], 1.0), writes=[rs("p1_vas", i)])
                wk = [0]

                def proj(cols, M, rhs_fn=None):
                    k = mmi[0] % 3
                    mmi[0] += 1
                    r = rs("p1_mm", k)
                    for kc in range(8):
                        op("pe", lambda e: e.matmul(mm[k][0:M, :], lhsT=Winp[:, kc, cols:cols + M], rhs=hT[:, kc, :], start=(kc == 0), stop=(kc == 7)),
                           reads=[rW, rhT], writes=[r])
                    return mm[k], r

                def normrope(X, rX, M, gain, onesname, inv_dim, Rname, ci, dst_ap, dst_res):
                    i = wk[0] % NWK
                    wk[0] += 1
                    a0, a1 = aux[(2 * i) % 3], aux[(2 * i + 1) % 3]
                    ra0, ra1 = rs("p1_aux", (2 * i) % 3), rs("p1_aux", (2 * i + 1) % 3)
                    rxg, rsq, rsd, rt1, rt2, rob = (rs("p1_xg", i), rs("p1_sq", i), rs("p1_sd", i), rs("p1_t1", i), rs("p1_t2", i), rs("p1_ob", i))
                    Ct, St = tabs[0:M, ci, :], tabs[0:M, ci + 1, :]
                    if gain is not None:
                        op("act", lambda e: e.activation(out=xg[i][0:M, :], in_=X[0:M, :], func=AF.Copy, scale=gain), reads=[rX, rg], writes=[rxg])
                    else:
                        op("act", lambda e: e.copy(out=xg[i][0:M, :], in_=X[0:M, :]), reads=[rX], writes=[rxg])
                    op("pe", lambda e: e.matmul(a1[0:M, :], lhsT=cb(Rname, M, M), rhs=xg[i][0:M, :], start=True, stop=True), reads=[rxg] + rC, writes=[ra1])
                    if onesname is not None:
                        op("act", lambda e: e.activation(out=sq[i][0:M, :], in_=X[0:M, :], func=AF.Square), reads=[rX], writes=[rsq])
                        op("pe", lambda e: e.matmul(a0[0:M, :], lhsT=cb(onesname, M, M), rhs=sq[i][0:M, :], start=True, stop=True), reads=[rsq] + rC, writes=[ra0])
                        op("act", lambda e: e.activation(out=sd[i][0:M, :], in_=a0[0:M, :], func=AF.Ln, scale=inv_dim, bias=eps_ap[0:M, :]), reads=[ra0] + rC, writes=[rsd])
                        op("act", lambda e: e.activation(out=sd[i][0:M, :], in_=sd[i][0:M, :], func=AF.Exp, scale=-0.5), reads=[rsd], writes=[rsd])
                    if gain is not None:
                        op("dve", lambda e: e.scalar_tensor_tensor(out=t1[i][0:M, :], in0=X[0:M, :], scalar=gain, in1=Ct, op0=ALU.mult, op1=ALU.mult),
                           reads=[rX, rtab, rg], writes=[rt1])
                    else:
                        op("dve", lambda e: e.tensor_tensor(out=t1[i][0:M, :], in0=X[0:M, :], in1=Ct, op=ALU.mult), reads=[rX, rtab], writes=[rt1])
                    op("dve", lambda e: e.tensor_tensor(out=t2[i][0:M, :], in0=a1[0:M, :], in1=St, op=ALU.mult), reads=[ra1, rtab], writes=[rt2])
                    if onesname is not None:
                        op(P1ENG, lambda e: e.tensor_tensor(out=t1[i][0:M, :], in0=t1[i][0:M, :], in1=t2[i][0:M, :], op=ALU.add), reads=[rt1, rt2], writes=[rt1])
                        op(P1ENG, lambda e: e.tensor_tensor(out=ob[i][0:M, :], in0=t1[i][0:M, :], in1=sd[i][0:M, :], op=ALU.mult), reads=[rt1, rsd], writes=[rob])
                    else:
                        op("dve", lambda e: e.tensor_tensor(out=ob[i][0:M, :], in0=t1[i][0:M, :], in1=t2[i][0:M, :], op=ALU.add), reads=[rt1, rt2], writes=[rob])
                    dma(dst_ap, ob[i][0:M, :], reads=[rob], writes=[dst_res])

                def silu_out(X, rX, M, dsts):
                    i = wk[0] % NWK
                    wk[0] += 1
                    rob = rs("p1_ob", i)
                    rt1 = rs("p1_t1", i)
                    op("act", lambda e: e.activation(out=t1[i][0:M, :], in_=X[0:M, :], func=AF.Sigmoid), reads=[rX], writes=[rt1])
                    if GV != 3:
                        op("dve", lambda e: e.scalar_tensor_tensor(out=ob[i][0:M, :], in0=X[0:M, :], scalar=1.0, in1=t1[i][0:M, :], op0=ALU.mult, op1=ALU.mult), reads=[rX, rt1], writes=[rob])
                    for (p0, p1, dap, dres) in dsts:
                        if GATE_DMA == 0 or (GATE_DMA == 1 and p0 != 0):
                            continue
                        dma(dap, ob[i][p0:p1, :], reads=[rob], writes=[dres])

                for tg in range(NG):
                    t0 = tg * 512
                    tsl = slice(t0, t0 + 512)
                    dma(xt[:], src_ap[t0:t0 + 512, :].rearrange("(j p) d -> p j d", p=128), reads=[src_res], writes=[rxt])
                    dma(tabs[:], rt_d[:, :, tsl].rearrange("c p t -> p c t"), writes=[rtab])
                    rssq, rjunk, rxn = rs("p1_ssq"), rs("p1_junk"), rs("p1_xn")
                    for j in range(4):
                        op("act", lambda e: e.activation(out=junk[:], in_=xt[:, j, :], func=AF.Square, accum_out=ssq[:, j:j + 1]), reads=[rxt], writes=[rjunk, rssq])
                    op("act", lambda e: e.activation(out=ssq[:, 4:8], in_=ssq[:, 0:4], func=AF.Sqrt, scale=1.0 / D, bias=eps_ap), reads=[rssq] + rC, writes=[rssq])
                    op("dve", lambda e: e.reciprocal(out=ssq[:, 4:8], in_=ssq[:, 4:8]), reads=[rssq], writes=[rssq])
                    for j in range(4):
                        op("dve", lambda e: e.tensor_scalar(out=xn[:], in0=xt[:, j, :], scalar1=ssq[:, 4 + j:5 + j], scalar2=None, op0=ALU.mult), reads=[rxt, rssq], writes=[rxn])
                        for half in range(2):
                            k = half
                            rtp = rs("p1_tp", k)
                            for q in range(4):
                                kc = half * 4 + q
                                op("pe", lambda e: e.transpose(out=tp[k][:, q * 128:(q + 1) * 128], in_=xn[:, kc * 128:(kc + 1) * 128], identity=cb("ident")),
                                   reads=[rxn] + rC, writes=[rtp])
                            eng = "act" if half == 0 else "dve"
                            if eng == "act":
                                op("act", lambda e: e.copy(out=hT[:, half * 4:half * 4 + 4, j * 128:(j + 1) * 128], in_=tp[k][:, 0:512].rearrange("p (q t) -> p q t", q=4)), reads=[rtp], writes=[rhT])
                            else:
                                op("dve", lambda e: e.tensor_copy(out=hT[:, half * 4:half * 4 + 4, j * 128:(j + 1) * 128], in_=tp[k][:, 0:512].rearrange("p (q t) -> p q t", q=4)), reads=[rtp], writes=[rhT])
                    if P1CUT <= 1:
                        continue
                    for c in range(3):
                        X, rX = proj(C_QA + 128 * c, 128)
                        normrope(X, rX, 128, gsm[:, l, 8:9], "ones64", 1.0 / 64, "R64", 0, qT_d[c, :, tsl], rs("qT_d"))
                    X, rX = proj(C_KA2, 128)
                    normrope(X, rX, 128, gsm[:, l, 9:10], "ones64", 1.0 / 64, "R64", 0, kT2_d[:, tsl], rs("kT2_d"))
                    if P1CUT <= 2:
                        continue
                    for c in range(4):
                        X, rX = proj(C_IQ2 + 128 * c, 128)
                        normrope(X, rX, 128, None, None, None, "R32", 2, iqT_d[c, :, tsl], rs("iqT_d"))
                    X, rX = proj(C_IK4, 128)
                    normrope(X, rX, 128, None, None, None, "R32", 2, ikT_d[:, tsl], rs("ikT_d"))
                    if P1CUT <= 3:
                        continue
                    for c in range(3):
                        X, rX = proj(C_GA + 128 * c, 128)
                        silu_out(X, rX, 128, [(0, 128, gaT_d.rearrange("h p t -> (h p) t")[128 * c:128 * c + 128, tsl], rs("gaT_d"))])
                    for c in range(2):
                        X, rX = proj(C_GB + 128 * c, 128)
                        silu_out(X, rX, 128, [(0, 128, gbT_d[c, :, tsl], rs("gbT_d"))])
                    for c in range(3):
                        X, rX = proj(C_GC + 128 * c, 128)
                        silu_out(X, rX, 128, [(0, 128, gcT_d.rearrange("h p t -> (h p) t")[128 * c:128 * c + 128, tsl], rs("gcT_d"))])
                    if P1CUT <= 4:
                        continue
                    for c in range(2):
                        X, rX = proj(C_U + 128 * c, 128)
                        i = c
                        rup = rs("p1_up", i)
                        op("dve", lambda e: e.tensor_copy(out=up[i][:].rearrange("p s j -> p j s"), in_=X[:].rearrange("p (j s) -> p j s", s=8)), reads=[rX], writes=[rup])
                        for g8 in range(8):
                            g = 8 * c + g8
                            dma(U_d[g, :, :, tg * 64:(tg + 1) * 64].rearrange("s c j -> c s j"), up[i][16 * g8:16 * g8 + 16, :, :], reads=[rup], writes=[rs("U_d")])
                    if P1CUT <= 5:
                        continue
                    X0, rX0 = proj(C_CQ, 128)
                    X1, rX1 = proj(C_CQ + 128, 128)
                    i = wk[0] % NWK
                    wk[0] += 1
                    a0, ra0 = aux[(2 * i) % 3], rs("p1_aux", (2 * i) % 3)
                    rsq, rsd, rt1 = rs("p1_sq", i), rs("p1_sd", i), rs("p1_t1", i)
                    i2 = wk[0] % NWK
                    wk[0] += 1
                    rsq2 = rs("p1_sq", i2)
                    op("act", lambda e: e.activation(out=sq[i][:], in_=X0[:], func=AF.Square), reads=[rX0], writes=[rsq])
                    op("act", lambda e: e.activation(out=sq[i2][:], in_=X1[:], func=AF.Square), reads=[rX1], writes=[rsq2])
                    op("pe", lambda e: e.matmul(a0[:], lhsT=cb("ones128"), rhs=sq[i][:], start=True, stop=False), reads=[rsq] + rC, writes=[ra0])
                    op("pe", lambda e: e.matmul(a0[:], lhsT=cb("ones128"), rhs=sq[i2][:], start=False, stop=True), reads=[rsq2] + rC, writes=[ra0])
                    op("act", lambda e: e.activation(out=sd[i][:], in_=a0[:], func=AF.Ln, scale=1.0 / 256, bias=eps_ap), reads=[ra0] + rC, writes=[rsd])
                    op("act", lambda e: e.activation(out=sd[i][:], in_=sd[i][:], func=AF.Exp, scale=-0.5), reads=[rsd], writes=[rsd])
                    rcq = rs("p1_cqn")
                    op("dve", lambda e: e.tensor_tensor(out=cqn[:, 0, :], in0=X0[:], in1=sd[i][:], op=ALU.mult), reads=[rX0, rsd], writes=[rcq])
                    op("dve", lambda e: e.tensor_tensor(out=cqn[:, 1, :], in0=X1[:], in1=sd[i][:], op=ALU.mult), reads=[rX1, rsd], writes=[rcq])
                    X0, rX0 = proj(C_CKV, 128)
                    i = wk[0] % NWK
                    wk[0] += 1
                    a0, ra0 = aux[(2 * i) % 3], rs("p1_aux", (2 * i) % 3)
                    rsq, rsd = rs("p1_sq", i), rs("p1_sd", i)
                    op("act", lambda e: e.activation(out=sq[i][:], in_=X0[:], func=AF.Square), reads=[rX0], writes=[rsq])
                    op("pe", lambda e: e.matmul(a0[:], lhsT=cb("ones128"), rhs=sq[i][:], start=True, stop=True), reads=[rsq] + rC, writes=[ra0])
                    op("act", lambda e: e.activation(out=sd[i][:], in_=a0[:], func=AF.Ln, scale=1.0 / 128, bias=eps_ap), reads=[ra0] + rC, writes=[rsd])
                    op("act", lambda e: e.activation(out=sd[i][:], in_=sd[i][:], func=AF.Exp, scale=-0.5), reads=[rsd], writes=[rsd])
                    rckv = rs("p1_ckvn")
                    op("dve", lambda e: e.tensor_tensor(out=ckvn[:], in0=X0[:], in1=sd[i][:], op=ALU.mult), reads=[rX0, rsd], writes=[rckv])
                    if P1CUT <= 6:
                        continue
                    for h in range(6):
                        k = mmi[0] % 3
                        mmi[0] += 1
                        r = rs("p1_mm", k)
                        for c in range(2):
                            op("pe", lambda e: e.matmul(mm[k][0:96, :], lhsT=Wuq[:, c, h * 96:(h + 1) * 96], rhs=cqn[:, c, :], start=(c == 0), stop=(c == 1)),
                               reads=[rWsm, rcq], writes=[r])
                        normrope(mm[k], r, 96, gsm[0:96, l, 13:14], "ones96", 1.0 / 96, "R96", 4, qc_d[h, :, tsl], rs("qc_d"))
                    for h in range(6):
                        k = mmi[0] % 3
                        mmi[0] += 1
                        r = rs("p1_mm", k)
                        op("pe", lambda e: e.matmul(mm[k][0:96, :], lhsT=WukT[:, h, :], rhs=ckvn[:], start=True, stop=False), reads=[rWsm, rckv], writes=[r])
                        for kc in range(8):
                            op("pe", lambda e: e.matmul(mm[k][0:96, :], lhsT=Winp[:, kc, C_KPE:C_KPE + 96], rhs=hT[:, kc, :], start=False, stop=(kc == 7)),
                               reads=[rW, rhT], writes=[r])
                        normrope(mm[k], r, 96, gsm[0:96, l, 14:15], "ones96", 1.0 / 96, "R96", 4, kc_d[h, :, tsl], rs("kc_d"))
                    if P1CUT <= 7:
                        continue
                    for j in range(4):
                        tb = tg * 4 + j
                        k = mmi[0] % 3
                        mmi[0] += 1
                        r = rs("p1_mm", k)
                        op("pe", lambda e: e.matmul(mm[k][:, 0:384], lhsT=ckvn[:, j * 128:(j + 1) * 128], rhs=Wuv[:], start=True, stop=True), reads=[rWsm, rckv], writes=[r])
                        i = j % 2
                        rv = rs("p1_vcs", i)
                        op("act", lambda e: e.copy(out=vcs[i][:, :, 0:64], in_=mm[k][:, 0:384].rearrange("p (h d) -> p h d", h=6)), reads=[r], writes=[rv])
                        dma(vc_d[:, tb, :, :].rearrange("h p c -> p h c"), vcs[i][:], reads=[rv], writes=[rs("vc_d")])
                        k = mmi[0] % 3
                        mmi[0] += 1
                        r = rs("p1_mm", k)
                        for kc in range(8):
                            op("pe", lambda e: e.matmul(mm[k][:, 0:72], lhsT=hT[:, kc, j * 128:(j + 1) * 128], rhs=Winp[:, kc, C_VAIW:C_VAIW + 72], start=(kc == 0), stop=(kc == 7)),
                               reads=[rW, rhT], writes=[r])
                        rva, riw = rs("p1_vas", i), rs("p1_iws", i)
                        op("act", lambda e: e.copy(out=vas[i][:, 0:64], in_=mm[k][:, 0:64]), reads=[r], writes=[rva])
                        dma(va_d[tb, :, :], vas[i][:], reads=[rva], writes=[rs("va_d")])
                        sc_ = (8.0 ** -0.5) * (32.0 ** -0.5)
                        op("act", lambda e: e.activation(out=iws[i][:, 0:8], in_=mm[k][:, 64:72], func=AF.Abs, scale=sc_), reads=[r], writes=[riw])
                        op("act", lambda e: e.activation(out=iws[i][:, 8:16], in_=mm[k][:, 64:72], func=AF.Sign), reads=[r], writes=[riw])
                        dma(iw_d[tb, :, :], iws[i][:], reads=[riw], writes=[rs("iw_d")])
            S_.barrier()

        def att_tiles(st, n_ops=2):
            T = {}
            T["att"] = [pst(st, f"at_ps{i}", [128, 512], F32) for i in range(2)]
            T["Ops"] = [pst(st, f"at_o{i}", [128, 512], F32) for i in range(n_ops)]
            T["sbp"] = pst(st, "at_sb", [128, 512], F32)
            T["PT"] = [sbt(st, f"at_pt{i}", [128, 512], BF16) for i in range(3)]
            T["Osb"] = [sbt(st, f"at_osb{i}", [65, 6, 512], F32) for i in range(2)]
            T["rcp"] = [sbt(st, f"at_rcp{i}", [64, 512], F32) for i in range(2)]
            T["yb"] = [sbt(st, f"at_yb{i}", [64, 512], BF16) for i in range(2)]
            T["gt"] = [sbt(st, f"at_gt{i}", [64, 6, 512], BF16) for i in range(2)]
            T["cnt"] = [0, 0, 0]
            return T

        def att_main(T, sb, n_heads, scale, heads, qfn, qres, nm_fn):
            att, Ops, PT, Osb = T["att"], T["Ops"], T["PT"], T["Osb"]
            cnt = T["cnt"]
            ob = sb % 2
            nkb = 4 * sb + 4
            rosb = rs("at_osb", ob)
            n_ops = len(Ops)
            steps = [(h, kb) for h in range(n_heads) for kb in range(nkb)]
            pend = None
            ois = {}
            for stp in steps + [None]:
                cur = None
                if stp is not None:
                    h, kb = stp
                    H = heads[h]
                    if kb == 0:
                        ois[h] = cnt[1] % n_ops
                        cnt[1] += 1
                    qb0 = max(kb - 4 * sb, 0)
                    qs = slice(qb0 * 128, 512)
                    ai = cnt[0] % 2
                    pi = cnt[0] % 3
                    cnt[0] += 1
                    ra, rp = rs("at_ps", ai), rs("at_pt", pi)
                    masks = []
                    for qb in range(qb0, 4):
                        m = nm_fn(4 * sb + qb, kb)
                        if m is not None:
                            masks.append((qb, m))
                    op("pe", lambda e: e.matmul(att[ai][:, qs], lhsT=H["kT"](slice(kb * 128, kb * 128 + 128)), rhs=qfn(h, qs), start=True, stop=(len(masks) == 0)),
                       reads=[H["kres"], qres], writes=[ra])
                    for mi, (qb, (map_, mres)) in enumerate(masks):
                        op("pe", lambda e: e.matmul(att[ai][:, qb * 128:(qb + 1) * 128], lhsT=map_, rhs=cb("ident"), start=False, stop=(mi == len(masks) - 1)),
                           reads=[mres] + rC, writes=[ra])
                    op("act", lambda e: e.activation(out=PT[pi][:, qs], in_=att[ai][:, qs], func=AF.Exp, scale=scale), reads=[ra], writes=[rp])
                    cur = (h, kb, pi, qs)
                if pend is not None:
                    h2, kb2, pi2, qs2 = pend
                    H2 = heads[h2]
                    oi = ois[h2]
                    rO = rs("at_o", oi)
                    op("pe", lambda e: e.matmul(Ops[oi][0:65, qs2], lhsT=H2["vaug"][:, kb2, :], rhs=PT[pi2][:, qs2], start=(kb2 == 0), stop=(kb2 == nkb - 1)),
                       reads=[rs("at_pt", pi2), H2["vres"]], writes=[rO])
                    if kb2 == nkb - 1:
                        op("act", lambda e: e.copy(out=Osb[ob][:, h2, :], in_=Ops[oi][0:65, :]), reads=[rO], writes=[rosb])
                pend = cur

        def att_norm(T, sb, n_heads, gate_d, gate_res, mix_base):
            Osb, sbp, rcp, yb, gt = T["Osb"], T["sbp"], T["rcp"], T["yb"], T["gt"]
            cnt = T["cnt"]
            ob = sb % 2
            rosb, rgt, rsb = rs("at_osb", ob), rs("at_gt", ob), rs("at_sbp")
            ssl = slice(sb * 512, (sb + 1) * 512)
            dma(gt[ob][:], gate_d[:, :, ssl].rearrange("h p t -> p h t"), reads=[rs(gate_res)], writes=[rgt])
            for h in range(n_heads):
                ri = cnt[2] % 2
                cnt[2] += 1
                rrc, ryb = rs("at_rcp", ri), rs("at_yb", ri)
                op("pe", lambda e: e.matmul(sbp[0:64, :], lhsT=cfa("e65", 65, 64), rhs=Osb[ob][:, h, :], start=True, stop=True), reads=[rosb] + rC, writes=[rsb])
                op("act", lambda e: e.activation(out=rcp[ri][:], in_=sbp[0:64, :], func=AF.Ln), reads=[rsb], writes=[rrc])
                op("act", lambda e: e.activation(out=rcp[ri][:], in_=rcp[ri][:], func=AF.Exp, scale=-1.0), reads=[rrc], writes=[rrc])
                op("dve", lambda e: e.tensor_tensor(out=rcp[ri][:], in0=rcp[ri][:], in1=Osb[ob][0:64, h, :], op=ALU.mult), reads=[rrc, rosb], writes=[rrc])
                op(P1ENG, lambda e: e.tensor_tensor(out=yb[ri][:], in0=rcp[ri][:], in1=gt[ob][:, h, :], op=ALU.mult), reads=[rrc, rgt], writes=[ryb])
                dma(mix_d[mix_base + h, 0:64, ssl], yb[ri][:], reads=[ryb], writes=[rs("mix_d")])

        def dsa_phase():
            with contextlib.ExitStack() as st:
                kT2 = sbt(st, "ds_kT2", [128, S], BF16)
                vaug = sbt(st, "ds_vaug", [128, NB, 65], BF16)
                ikT = sbt(st, "ds_ikT", [128, S], BF16)
                rk, rv, rik = rs("ds_kT2"), rs("ds_vaug"), rs("ds_ikT")
                dma(kT2[:], kT2_d, reads=[rs("kT2_d")], writes=[rk])
                dma(vaug[:], va_d.rearrange("t p c -> p t c"), reads=[rs("va_d")], writes=[rv])
                dma(ikT[:], ikT_d, reads=[rs("ikT_d")], writes=[rik])
                NM = [sbt(st, f"ds_NM{i}", [128, 4, S], BF16) for i in range(2)]
                sc = [sbt(st, f"ds_sc{i}", [128, S], F32) for i in range(2)]
                junk = sbt(st, "ds_junk", [128, S], BF16)
                iq = [sbt(st, f"ds_iq{i}", [128, 4, 128], BF16) for i in range(2)]
                iw = [sbt(st, f"ds_iw{i}", [128, 16], F32) for i in range(2)]
                Dh = [sbt(st, f"ds_Dh{i}", [128, 8, 128], BF16) for i in range(2)]
                Th = [sbt(st, f"ds_Th{i}", [128, 512], BF16) for i in range(3)]
                bs = [sbt(st, f"ds_bs{i}", [128, 8 + NIT + 1], F32) for i in range(2)]
                shp = [pst(st, f"ds_shp{i}", [128, 512], F32) for i in range(2)]
                scp = [pst(st, f"ds_scp{i}", [128, 512], F32) for i in range(2)]
                T = att_tiles(st, n_ops=1)
                qs_t = [sbt(st, f"ds_q{i}", [128, 3, 512], BF16) for i in range(2)]
                c3 = [0, 0]

                def masks(sb):
                    for qb in range(4):
                        b = 4 * sb + qb
                        n = 128 * (b + 1)
                        if n <= KTOP:
                            continue
                        i = b % 2
                        rsc, riq, riw, rDh, rbs = rs("ds_sc", i), rs("ds_iq", i), rs("ds_iw", i), rs("ds_Dh", i), rs("ds_bs", i)
                        rNM = rs("ds_NM", sb % 2)
                        dma(iq[i][:], iqT_d[:, :, b * 128:(b + 1) * 128].rearrange("c p t -> p c t"), reads=[rs("iqT_d")], writes=[riq])
                        dma(iw[i][:], iw_d[b, :, :], reads=[rs("iw_d")], writes=[riw])
                        for h in range(8):
                            op("dve", lambda e: e.tensor_scalar(out=Dh[i][:, h, :], in0=cb("ident"), scalar1=iw[i][:, 8 + h:9 + h], scalar2=None, op0=ALU.mult),
                               reads=[riw] + rC, writes=[rDh])
                        chunks = [(c * 512, min(512, n - c * 512)) for c in range((n + 511) // 512)]
                        isteps = [(ci, h) for ci in range(len(chunks)) for h in range(8)]
                        ipend = None
                        for ist in isteps + [None]:
                            icur = None
                            if ist is not None:
                                ci, h = ist
                                k0, wc = chunks[ci]
                                hi_ = c3[0] % 2
                                ti_ = c3[0] % 3
                                c3[0] += 1
                                rsh, rth = rs("ds_shp", hi_), rs("ds_Th", ti_)
                                pb = 64 * (h % 2)
                                op("pe", lambda e: e.matmul(shp[hi_][:, 0:wc], lhsT=iq[i][pb:pb + 32, h // 2, :], rhs=ikT[pb:pb + 32, k0:k0 + wc], start=True, stop=True),
                                   reads=[riq, rik], writes=[rsh])
                                op("act", lambda e: e.activation(out=Th[ti_][:, 0:wc], in_=shp[hi_][:, 0:wc], func=AF.Relu, scale=iw[i][:, h:h + 1]), reads=[rsh, riw], writes=[rth])
                                if h == 0:
                                    c3[1] += 1
                                icur = (ci, h, ti_, c3[1] % 2)
                            if ipend is not None:
                                ci2, h2, ti2, si = ipend
                                k0, wc = chunks[ci2]
                                rscp = rs("ds_scp", si)
                                op("pe", lambda e: e.matmul(scp[si][:, 0:wc], lhsT=Dh[i][:, h2, :], rhs=Th[ti2][:, 0:wc], start=(h2 == 0), stop=(h2 == 7)), reads=[rs("ds_Th", ti2), rDh], writes=[rscp])
                                if h2 == 7:
                                    op("act", lambda e: e.copy(out=sc[i][:, k0:k0 + wc], in_=scp[si][:, 0:wc]), reads=[rscp], writes=[rsc])
                            ipend = icur
                        lo, hi, mid, cn, tt, rng = (bs[i][:, k:k + 1] for k in range(6))
                        Hc = lambda it: bs[i][:, 8 + it:9 + it]
                        op("dve", lambda e: e.tensor_reduce(out=hi, in_=sc[i][:, 0:n], axis=AX.X, op=ALU.max), reads=[rsc], writes=[rbs])
                        op("dve", lambda e: e.tensor_reduce(out=lo, in_=sc[i][:, 0:n], axis=AX.X, op=ALU.min), reads=[rsc], writes=[rbs])
                        op("dve", lambda e: e.scalar_tensor_tensor(out=rng, in0=hi, scalar=1.0, in1=lo, op0=ALU.add, op1=ALU.subtract), reads=[rbs], writes=[rbs])
                        op("dve", lambda e: e.tensor_scalar(out=bs[i][:, 8:8 + NIT + 1], in0=cf[:, CF["pw"]:CF["pw"] + NIT + 1], scalar1=rng, scalar2=None, op0=ALU.mult), reads=[rbs] + rC, writes=[rbs])
                        op("dve", lambda e: e.tensor_tensor(out=mid, in0=lo, in1=Hc(0), op=ALU.add), reads=[rbs], writes=[rbs])
                        op("dve", lambda e: e.memset(sc[i][0:64, n - 64:n], -1e30), writes=[rsc])
                        for it in range(NIT):
                            op("dve", lambda e: e.tensor_scalar(out=junk[:, 0:n], in0=sc[i][:, 0:n], scalar1=mid, scalar2=None, op0=ALU.is_ge, op1=ALU.add, accum_out=cn),
                               reads=[rsc, rbs], writes=[rbs, rs("ds_junk")])
                            op("dve", lambda e: e.tensor_scalar(out=tt, in0=cn, scalar1=float(KTOP) - 0.5, scalar2=-0.5, op0=ALU.is_ge, op1=ALU.add), reads=[rbs], writes=[rbs])
                            op("dve", lambda e: e.scalar_tensor_tensor(out=mid, in0=tt, scalar=Hc(it), in1=mid, op0=ALU.mult, op1=ALU.add), reads=[rbs], writes=[rbs])
                        op("dve", lambda e: e.tensor_tensor(out=lo, in0=mid, in1=Hc(NIT), op=ALU.subtract), reads=[rbs], writes=[rbs])
                        op("dve", lambda e: e.tensor_scalar(out=NM[sb % 2][:, qb, 0:n], in0=sc[i][:, 0:n], scalar1=lo, scalar2=NEGM, op0=ALU.is_lt, op1=ALU.mult), reads=[rsc, rbs], writes=[rNM])

                def nm_fn(b, kb):
                    n = 128 * (b + 1)
                    if n <= KTOP:
                        if kb == b:
                            return (cb("nmd"), rs("cbf"))
                        return None
                    return (NM[(b // 4) % 2][:, b % 4, kb * 128:(kb + 1) * 128], rs("ds_NM", (b // 4) % 2))

                heads = [dict(kT=(lambda ks, pb=64 * (h % 2): kT2[pb:pb + 64, ks]), kres=rk, vaug=vaug, vres=rv) for h in range(6)]
                if DSA_MODE != 2:
                    masks(0)
                for sb in range(NG):
                    if sb + 1 < NG and DSA_MODE != 2:
                        masks(sb + 1)
                    if DSA_MODE == 1:
                        continue
                    i = sb % 2
                    rq = rs("ds_q", i)
                    dma(qs_t[i][:], qT_d[:, :, sb * 512:(sb + 1) * 512].rearrange("c p t -> p c t"), reads=[rs("qT_d")], writes=[rq])
                    qfn = lambda h, qs, i=i: qs_t[i][64 * (h % 2):64 * (h % 2) + 64, h // 2, qs]
                    att_main(T, sb, 6, 64.0 ** -0.5, heads, qfn, rq, nm_fn)
                    att_norm(T, sb, 6, gaT_d, "gaT_d", 0)
            S_.barrier()

        def mla_phase():
            with contextlib.ExitStack() as st:
                kT = sbt(st, "ml_kT", [96, 6, S], BF16)
                va = sbt(st, "ml_va", [128, 6, NB, 65], BF16)
                rk, rv = rs("ml_kT"), rs("ml_va")
                for h in range(6):
                    dma(kT[:, h, :], kc_d[h], reads=[rs("kc_d")], writes=[rk])
                    dma(va[:, h, :, :], vc_d[h].rearrange("t p c -> p t c"), reads=[rs("vc_d")], writes=[rv])
                T = att_tiles(st)
                qs_t = [sbt(st, f"ml_q{i}", [96, 6, 512], BF16) for i in range(2)]

                def nm_fn(b, kb):
                    if kb == b:
                        return (cb("nmd"), rs("cbf"))
                    return None
                heads = [dict(kT=(lambda ks, h=h: kT[:, h, ks]), kres=rk, vaug=va[:, h, :, :], vres=rv) for h in range(6)]
                for sb in range(NG):
                    i = sb % 2
                    rq = rs("ml_q", i)
                    dma(qs_t[i][:], qc_d[:, :, sb * 512:(sb + 1) * 512].rearrange("h p t -> p h t"), reads=[rs("qc_d")], writes=[rq])
                    qfn = lambda h, qs, i=i: qs_t[i][:, h, qs]
                    att_main(T, sb, 6, 96.0 ** -0.5, heads, qfn, rq, nm_fn)
                    att_norm(T, sb, 6, gcT_d, "gcT_d", 8)
            S_.barrier()

        def s5_consts(l):
            with contextlib.ExitStack() as st:
                ar = sbt(st, "s5_ar", [128, 16], F32)
                ai = sbt(st, "s5_ai", [128, 16], F32)
                stp = sbt(st, "s5_stp", [128, 16], F32)
                rp = rs("s5_par")
                for hh in range(2):
                    dma(ar[64 * hh:64 * hh + 64, :], a_re[l].rearrange("g p -> p g"), writes=[rp], allow_slow_non_contiguous=True)
                    dma(ai[64 * hh:64 * hh + 64, :], a_im[l].rearrange("g p -> p g"), writes=[rp], allow_slow_non_contiguous=True)
                dma(stp[:], log_step[l:l + 1, :].broadcast_to([128, 16]), writes=[rp])
                W = sbt(st, "s5_w", [128, 24, 16], F32)
                rw = rs("s5_w")

                def wv(k):
                    return W[:, k, :]

                def dv(fn, reads=(), writes=()):
                    op("dve", fn, reads=list(reads) + [rp, rw] + rC, writes=list(writes) + [rw])
                TWO_PI = 2.0 * np.pi

                def sincos(dst, ang, shift):
                    t, n_, m = wv(20), wv(21), wv(22)
                    ti = Wi[:, 0, :]
                    dv(lambda e: e.tensor_scalar(out=t, in0=ang, scalar1=shift, scalar2=1.0 / TWO_PI, op0=ALU.add, op1=ALU.mult))
                    dv(lambda e: e.tensor_copy(out=ti, in_=t))
                    dv(lambda e: e.tensor_copy(out=n_, in_=ti))
                    dv(lambda e: e.tensor_tensor(out=t, in0=t, in1=n_, op=ALU.subtract))
                    dv(lambda e: e.tensor_scalar(out=m, in0=t, scalar1=0.5, scalar2=None, op0=ALU.is_gt))
                    dv(lambda e: e.tensor_tensor(out=t, in0=t, in1=m, op=ALU.subtract))
                    dv(lambda e: e.tensor_scalar(out=m, in0=t, scalar1=-0.5, scalar2=None, op0=ALU.is_lt))
                    dv(lambda e: e.tensor_tensor(out=t, in0=t, in1=m, op=ALU.add))
                    op("act", lambda e: e.activation(out=dst, in_=t, func=AF.Sin, scale=TWO_PI), reads=[rw], writes=[rw])
                Wi = sbt(st, "s5_wi", [128, 1, 16], mybir.dt.int32)
                op("act", lambda e: e.activation(out=stp[:], in_=stp[:], func=AF.Exp), reads=[rp], writes=[rp])
                mag, ang, abr, abi, cs, sn = wv(0), wv(1), wv(2), wv(3), wv(4), wv(5)
                dv(lambda e: e.tensor_tensor(out=mag, in0=ar[:], in1=stp[:], op=ALU.mult))
                op("act", lambda e: e.activation(out=mag, in_=mag, func=AF.Exp), reads=[rw], writes=[rw])
                dv(lambda e: e.tensor_tensor(out=ang, in0=ai[:], in1=stp[:], op=ALU.mult))
                sincos(sn, ang, 0.0)
                sincos(cs, ang, np.pi / 2)
                dv(lambda e: e.tensor_tensor(out=abr, in0=mag, in1=cs, op=ALU.mult))
                dv(lambda e: e.tensor_tensor(out=abi, in0=mag, in1=sn, op=ALU.mult))
                den, nr, fr, fi, tA, tB = wv(6), wv(7), wv(8), wv(9), wv(10), wv(11)
                dv(lambda e: e.tensor_tensor(out=den, in0=ar[:], in1=ar[:], op=ALU.mult))
                dv(lambda e: e.tensor_tensor(out=tA, in0=ai[:], in1=ai[:], op=ALU.mult))
                dv(lambda e: e.tensor_tensor(out=den, in0=den, in1=tA, op=ALU.add))
                dv(lambda e: e.reciprocal(out=den, in_=den))
                dv(lambda e: e.tensor_scalar(out=nr, in0=abr, scalar1=-1.0, scalar2=None, op0=ALU.add))
                dv(lambda e: e.tensor_tensor(out=fr, in0=nr, in1=ar[:], op=ALU.mult))
                dv(lambda e: e.tensor_tensor(out=tA, in0=abi, in1=ai[:], op=ALU.mult))
                dv(lambda e: e.tensor_tensor(out=fr, in0=fr, in1=tA, op=ALU.add))
                dv(lambda e: e.tensor_tensor(out=fr, in0=fr, in1=den, op=ALU.mult))
                dv(lambda e: e.tensor_tensor(out=fi, in0=abi, in1=ar[:], op=ALU.mult))
                dv(lambda e: e.tensor_tensor(out=tA, in0=nr, in1=ai[:], op=ALU.mult))
                dv(lambda e: e.tensor_tensor(out=fi, in0=fi, in1=tA, op=ALU.subtract))
                dv(lambda e: e.tensor_tensor(out=fi, in0=fi, in1=den, op=ALU.mult))
                PW = sbt(st, "s5_pw", [128, 9, 2, 16], F32)
                MKp = sbt(st, "s5_mkp", [128, 9, 2, 16], F32)
                FN = sbt(st, "s5_fn", [128, 8, 2, 16], F32)
                CQc = sbt(st, "s5_cqc", [128, 9, 2, 16], F32)

                def cmul(o_re, o_im, x_re, x_im, y_re, y_im):
                    dv(lambda e: e.tensor_tensor(out=tA, in0=x_re, in1=y_re, op=ALU.mult))
                    dv(lambda e: e.tensor_tensor(out=tB, in0=x_im, in1=y_im, op=ALU.mult))
                    dv(lambda e: e.tensor_tensor(out=wv(12), in0=x_re, in1=y_im, op=ALU.mult))
                    dv(lambda e: e.tensor_tensor(out=wv(13), in0=x_im, in1=y_re, op=ALU.mult))
                    dv(lambda e: e.tensor_tensor(out=o_re, in0=tA, in1=tB, op=ALU.subtract))
                    dv(lambda e: e.tensor_tensor(out=o_im, in0=wv(12), in1=wv(13), op=ALU.add))
                dv(lambda e: e.memset(PW[:, 0, 0, :], 1.0))
                dv(lambda e: e.memset(PW[:, 0, 1, :], 0.0))
                for n_ in range(1, 9):
                    cmul(PW[:, n_, 0, :], PW[:, n_, 1, :], PW[:, n_ - 1, 0, :], PW[:, n_ - 1, 1, :], abr, abi)
                dv(lambda e: e.tensor_copy(out=MKp[:, 0, :, :], in_=PW[:, 8, :, :]))
                for k in range(1, 9):
                    cmul(MKp[:, k, 0, :], MKp[:, k, 1, :], MKp[:, k - 1, 0, :], MKp[:, k - 1, 1, :], MKp[:, k - 1, 0, :], MKp[:, k - 1, 1, :])
                sgB = cf[:, CF["sgB"]:CF["sgB"] + 1]
                sgQ = cf[:, CF["sgQ"]:CF["sgQ"] + 1]
                for n_ in range(8):
                    cmul(FN[:, n_, 0, :], FN[:, n_, 1, :], PW[:, n_, 0, :], PW[:, n_, 1, :], fr, fi)
                    dv(lambda e: e.tensor_scalar(out=FN[:, n_, 1, :], in0=FN[:, n_, 1, :], scalar1=sgB, scalar2=None, op0=ALU.mult))
                for n_ in range(9):
                    dv(lambda e: e.tensor_scalar(out=CQc[:, n_, 0, :], in0=PW[:, n_, 0, :], scalar1=sgQ, scalar2=None, op0=ALU.mult))
                    dv(lambda e: e.tensor_scalar(out=CQc[:, n_, 1, :], in0=PW[:, n_, 1, :], scalar1=-1.0, scalar2=None, op0=ALU.mult))
                for k in range(9):
                    dv(lambda e: e.tensor_scalar(out=MKp[:, k, 1, :], in0=MKp[:, k, 1, :], scalar1=sgQ, scalar2=None, op0=ALU.mult))
                bx = [sbt(st, f"s5_bx{i}", [128, 16], F32) for i in range(2)]
                by = [sbt(st, f"s5_by{i}", [128, 16], F32) for i in range(2)]
                cx = [sbt(st, f"s5_cx{i}", [128, 16], F32) for i in range(2)]
                cy = [sbt(st, f"s5_cy{i}", [128, 16], F32) for i in range(2)]
                dr = [sbt(st, f"s5_dr{i}", [128, 1], F32) for i in range(2)]
                Pm = [sbt(st, f"s5_Pm{i}", [128, 128], F32) for i in range(2)]
                CQ = [sbt(st, f"s5_CQ{i}", [128, 9, 16], F32) for i in range(2)]
                Btr = [sbt(st, f"s5_Btr{i}", [128, 8, 16], F32) for i in range(2)]
                KTp = [sbt(st, f"s5_KTp{i}", [128, 15 * 16], F32) for i in range(2)]
                OUTm = [sbt(st, f"s5_OUT{i}", [128, 12, 128], F32) for i in range(2)]
                pp = [pst(st, f"s5_pp{i}", [128, 128], F32) for i in range(2)]
                for i in range(2):
                    op("pool", lambda e: e.memset(KTp[i][:], 0.0), writes=[rs("s5_KTp", i)])
                bxa = sbt(st, "s5_bxa", [128, 16, 16], F32)
                bya = sbt(st, "s5_bya", [128, 16, 16], F32)
                dra = sbt(st, "s5_dra", [128, 16], F32)
                rina = rs("s5_inall")
                dma(bxa[0:64], b_re[l].rearrange("g p c -> p g c"), writes=[rina])
                dma(bxa[64:128], b_im[l].rearrange("g p c -> p g c"), writes=[rina])
                dma(bya[0:64], b_im[l].rearrange("g p c -> p g c"), writes=[rina])
                dma(bya[64:128], b_re[l].rearrange("g p c -> p g c"), writes=[rina])
                for s_ in range(8):
                    dma(dra[16 * s_:16 * s_ + 16, :], d_skip[l].rearrange("g c -> c g"), writes=[rina], allow_slow_non_contiguous=True)
                for g in range(16):
                    i = g % 2
                    rin, rPm, rCQ, rBt, rKT, rOUT, rpp = rs("s5_in", i), rs("s5_Pm", i), rs("s5_CQ", i), rs("s5_Btr", i), rs("s5_KTp", i), rs("s5_OUT", i), rs("s5_pp", i)
                    dma(cx[i][0:64, :], c_re[l, g].rearrange("c p -> p c"), writes=[rin], allow_slow_non_contiguous=True)
                    dma(cx[i][64:128, :], c_im[l, g].rearrange("c p -> p c"), writes=[rin], allow_slow_non_contiguous=True)
                    dma(cy[i][0:64, :], c_im[l, g].rearrange("c p -> p c"), writes=[rin], allow_slow_non_contiguous=True)
                    dma(cy[i][64:128, :], c_re[l, g].rearrange("c p -> p c"), writes=[rin], allow_slow_non_contiguous=True)
                    for s_ in range(8):
                        n_ = 7 - s_
                        op("dve", lambda e: e.tensor_scalar(out=Pm[i][:, 16 * s_:16 * s_ + 16], in0=bxa[:, g, :], scalar1=FN[:, n_, 0, g:g + 1], scalar2=None, op0=ALU.mult), reads=[rina, rw], writes=[rPm])
                        op("dve", lambda e: e.scalar_tensor_tensor(out=Pm[i][:, 16 * s_:16 * s_ + 16], in0=bya[:, g, :], scalar=FN[:, n_, 1, g:g + 1], in1=Pm[i][:, 16 * s_:16 * s_ + 16], op0=ALU.mult, op1=ALU.add),
                           reads=[rina, rw], writes=[rPm])
                    for s_ in range(8):
                        op("dve", lambda e: e.tensor_copy(out=Btr[i][:, s_, :], in_=Pm[i][:, 112:128]), reads=[rPm], writes=[rBt])
                    for n_ in range(9):
                        op("dve", lambda e: e.tensor_scalar(out=CQ[i][:, n_, :], in0=cx[i][:], scalar1=CQc[:, n_, 0, g:g + 1], scalar2=None, op0=ALU.mult), reads=[rin, rw], writes=[rCQ])
                        op("dve", lambda e: e.scalar_tensor_tensor(out=CQ[i][:, n_, :], in0=cy[i][:], scalar=CQc[:, n_, 1, g:g + 1], in1=CQ[i][:, n_, :], op0=ALU.mult, op1=ALU.add),
                           reads=[rin, rw], writes=[rCQ])
                    op("pe", lambda e: e.transpose(out=pp[i][:], in_=Pm[i][:], identity=cfa("ident")), reads=[rPm] + rC, writes=[rpp])
                    op("act", lambda e: e.copy(out=OUTm[i][:, 0, :], in_=pp[i][:]), reads=[rpp], writes=[rOUT])
                    op("dve", lambda e: e.tensor_copy(out=OUTm[i][:, 1, :], in_=CQ[i][:, 1:9, :].rearrange("p n c -> p (n c)")), reads=[rCQ], writes=[rOUT])
                    op("pe", lambda e: e.matmul(pp[i][:], lhsT=Btr[i][:].rearrange("p s c -> p (s c)"), rhs=CQ[i][:, 0:8, :].rearrange("p n c -> p (n c)"), start=True, stop=True),
                       reads=[rBt, rCQ], writes=[rpp])
                    op("act", lambda e: e.copy(out=KTp[i][:, 112:240], in_=pp[i][:]), reads=[rpp], writes=[rKT])
                    op("dve", lambda e: e.scalar_tensor_tensor(out=KTp[i][:, 112:128], in0=cfa("i16", 128, 16), scalar=dra[:, g:g + 1], in1=KTp[i][:, 112:128], op0=ALU.mult, op1=ALU.add),
                       reads=[rina, rKT] + rC, writes=[rKT])
                    for s_ in range(8):
                        dma(s5c_d[g, 16 * s_:16 * s_ + 16, 2, :], KTp[i][16 * s_:16 * s_ + 16, (7 - s_) * 16:(7 - s_) * 16 + 128], reads=[rKT], writes=[rs("s5c_d")])
                    for k in range(9):
                        op("dve", lambda e: e.tensor_scalar(out=OUTm[i][:, 3 + k, :], in0=cfa("ident"), scalar1=MKp[:, k, 0, g:g + 1], scalar2=None, op0=ALU.mult), reads=[rw] + rC, writes=[rOUT])
                        op("dve", lambda e: e.scalar_tensor_tensor(out=OUTm[i][:, 3 + k, :], in0=cfa("icross"), scalar=MKp[:, k, 1, g:g + 1], in1=OUTm[i][:, 3 + k, :], op0=ALU.mult, op1=ALU.add),
                           reads=[rw] + rC, writes=[rOUT])
                    dma(s5c_d[g, :, 0:2, :], OUTm[i][:, 0:2, :], reads=[rOUT], writes=[rs("s5c_d")])
                    dma(s5c_d[g, :, 3:12, :], OUTm[i][:, 3:12, :], reads=[rOUT], writes=[rs("s5c_d")])
            S_.barrier()

        def s5_phase():
            with contextlib.ExitStack() as st:
                NPAR = 4
                assert J <= 512
                Cm = [sbt(st, f"s5r_C{i}", [128, 12, 128], F32) for i in range(NPAR)]
                Ut = [sbt(st, f"s5r_U{i}", [128, J], F32) for i in range(NPAR)]
                St = [sbt(st, f"s5r_S{i}", [128, J], F32) for i in range(NPAR)]
                Yt = [sbt(st, f"s5r_Y{i}", [128, J], F32) for i in range(NPAR)]
                pk = [pst(st, f"s5r_p{i}", [128, 512], F32) for i in range(8)]
                pc = [0]

                def nextp():
                    k = pc[0] % 8
                    pc[0] += 1
                    return k, rs("s5r_p", k)
                for g0 in range(0, 16, NPAR):
                    gs = list(range(g0, g0 + NPAR))
                    R_ = {}
                    for g in gs:
                        i = g % NPAR
                        R_[g] = (rs("s5r_C", i), rs("s5r_U", i), rs("s5r_S", i), rs("s5r_Y", i))
                        rCm, rU, rSt, rY = R_[g]
                        dma(Cm[i][:], s5c_d[g], reads=[rs("s5c_d")], writes=[rCm])
                        dma(Ut[i][:], U_d[g].rearrange("s c j -> (s c) j"), reads=[rs("U_d")], writes=[rU])
                    for g in gs:
                        i = g % NPAR
                        rCm, rU, rSt, rY = R_[g]
                        k, rpk = nextp()
                        op("pe", lambda e: e.matmul(pk[k][:, 0:J], lhsT=Cm[i][:, 0, :], rhs=Ut[i][:, 0:J], start=True, stop=True), reads=[rCm, rU], writes=[rpk])
                        op("act", lambda e: e.copy(out=St[i][:, 0:J], in_=pk[k][:, 0:J]), reads=[rpk], writes=[rSt])
                    k_ = 0
                    while (1 << k_) < J:
                        sh = 1 << k_
                        wc = J - sh
                        prods = []
                        for g in gs:
                            i = g % NPAR
                            rCm, rU, rSt, rY = R_[g]
                            k, rpk = nextp()
                            op("pe", lambda e: e.matmul(pk[k][:, 0:wc], lhsT=Cm[i][:, 3 + k_, :], rhs=St[i][:, 0:wc], start=True, stop=True), reads=[rCm, rSt], writes=[rpk])
                            prods.append((g, k, rpk))
                        for n_, (g, k, rpk) in enumerate(prods):
                            i = g % NPAR
                            rSt = R_[g][2]
                            op("dve", lambda e: e.tensor_tensor(out=St[i][:, sh:J], in0=pk[k][:, 0:wc], in1=St[i][:, sh:J], op=ALU.add), reads=[rpk, rSt], writes=[rSt])
                        k_ += 1
                    for g in gs:
                        i = g % NPAR
                        rCm, rU, rSt, rY = R_[g]
                        k, rpk = nextp()
                        op("pe", lambda e: e.matmul(pk[k][:, 0:J], lhsT=Cm[i][:, 2, :], rhs=Ut[i][:, 0:J], start=True, stop=False), reads=[rCm, rU], writes=[rpk])
                        op("pe", lambda e: e.matmul(pk[k][:, 1:J], lhsT=Cm[i][:, 1, :], rhs=St[i][:, 0:J - 1], start=False, stop=True), reads=[rCm, rSt], writes=[rpk])
                        op("act", lambda e: e.copy(out=Yt[i][:, 0:J], in_=pk[k][:, 0:J]), reads=[rpk], writes=[rY])
                        dma(Y_d[g].rearrange("s c j -> (s c) j"), Yt[i][:], reads=[rY], writes=[rs("Y_d")])
            S_.barrier()
            with contextlib.ExitStack() as st:
                yp = sbt(st, "s5p_yp", [128, 2, 8, 64], F32)
                y = sbt(st, "s5p_y", [128, 2, 512], F32)
                t = sbt(st, "s5p_t", [128, 2, 512], F32)
                gl = sbt(st, "s5p_g", [128, 2, 512], F32)
                gb16 = sbt(st, "s5p_gb", [128, 2, 512], BF16)
                sg = sbt(st, "s5p_sg", [128, 2, 512], F32)
                gate = sbt(st, "s5p_gate", [128, 2, 512], BF16)
                ob = sbt(st, "s5p_ob", [128, 2, 512], BF16)
                zp = [pst(st, f"s5p_z{i}", [128, 512], F32) for i in range(2)]
                ryp, ry, rt_, rgl, rgb, rsg, rgate, rob = (rs("s5p_" + n_) for n_ in ("yp", "y", "t", "g", "gb", "sg", "gate", "ob"))
                for tg in range(NG):
                    tsl = slice(tg * 512, tg * 512 + 512)
                    for c in range(2):
                        for g8 in range(8):
                            dma(yp[16 * g8:16 * g8 + 16, c, :, :], Y_d[8 * c + g8, :, :, tg * 64:(tg + 1) * 64].rearrange("s c j -> c s j"), reads=[rs("Y_d")], writes=[ryp])
                        dma(gate[:, c, :], gbT_d[c, :, tsl], reads=[rs("gbT_d")], writes=[rgate])
                    for c in range(2):
                        op("act", lambda e: e.copy(out=y[:, c, :].rearrange("p (j s) -> p s j", s=8), in_=yp[:, c, :, :]), reads=[ryp], writes=[ry])
                        op("act", lambda e: e.activation(out=t[:, c, :], in_=y[:, c, :], func=AF.Square), reads=[ry], writes=[rt_])
                        op("dve", lambda e: e.tensor_scalar(out=t[:, c, :], in0=t[:, c, :], scalar1=0.044715, scalar2=1.0, op0=ALU.mult, op1=ALU.add), reads=[rt_], writes=[rt_])
                        op("dve", lambda e: e.tensor_tensor(out=t[:, c, :], in0=t[:, c, :], in1=y[:, c, :], op=ALU.mult), reads=[rt_, ry], writes=[rt_])
                        op("act", lambda e: e.activation(out=t[:, c, :], in_=t[:, c, :], func=AF.Sigmoid, scale=2.0 * 0.7978845608028654), reads=[rt_], writes=[rt_])
                        op("dve", lambda e: e.tensor_tensor(out=gl[:, c, :], in0=t[:, c, :], in1=y[:, c, :], op=ALU.mult), reads=[rt_, ry], writes=[rgl])
                        op("dve", lambda e: e.tensor_copy(out=gb16[:, c, :], in_=gl[:, c, :]), reads=[rgl], writes=[rgb])
                    for co in range(2):
                        rz = rs("s5p_z", co)
                        for c in range(2):
                            op("pe", lambda e: e.matmul(zp[co][:], lhsT=Wglu[:, c, co * 128:(co + 1) * 128], rhs=gb16[:, c, :], start=(c == 0), stop=(c == 1)), reads=[rWsm, rgb], writes=[rz])
                        op("act", lambda e: e.activation(out=sg[:, co, :], in_=zp[co][:], func=AF.Sigmoid), reads=[rz], writes=[rsg])
                        op("dve", lambda e: e.tensor_tensor(out=sg[:, co, :], in0=sg[:, co, :], in1=gl[:, co, :], op=ALU.mult), reads=[rsg, rgl], writes=[rsg])
                        op("dve", lambda e: e.tensor_tensor(out=ob[:, co, :], in0=sg[:, co, :], in1=gate[:, co, :], op=ALU.mult), reads=[rsg, rgate], writes=[rob])
                        dma(mix_d[6 + co, :, tsl], ob[:, co, :], reads=[rob], writes=[rs("mix_d")])
            S_.barrier()

        def phase3(src_ap, src_res, dst_ap, dst_res):
            with contextlib.ExitStack() as st:
                Wo = sbt(st, "p3_Wo", [128, 14, D], BF16)
                rWo = rs("p3_Wo")
                for c in range(14):
                    nr = 128 if c in (6, 7) else 64
                    dma(Wo[0:nr, c, :], wout_d[0:nr, c, :], reads=[rs("wout_d")], writes=[rWo])
                mx = [sbt(st, f"p3_mx{i}", [128, 14, 512], BF16) for i in range(2)]
                xt = [sbt(st, f"p3_xt{i}", [128, 4, D], F32) for i in range(2)]
                pp = [pst(st, f"p3_pp{i}", [128, 512], F32) for i in range(4)]
                pc = 0
                for tg in range(NG):
                    i = tg % 2
                    rmx, rxt = rs("p3_mx", i), rs("p3_xt", i)
                    tsl = slice(tg * 512, tg * 512 + 512)
                    for c in range(14):
                        nr = 128 if c in (6, 7) else 64
                        dma(mx[i][0:nr, c, :], mix_d[c, 0:nr, tsl], reads=[rs("mix_d")], writes=[rmx])
                    dma(xt[i][:], src_ap[tg * 512:(tg + 1) * 512, :].rearrange("(j p) d -> p j d", p=128), reads=[src_res], writes=[rxt])
                    for j in range(4):
                        for half in range(2):
                            k = pc % 4
                            pc += 1
                            rpp = rs("p3_pp", k)
                            for c in range(14):
                                nr = 128 if c in (6, 7) else 64
                                op("pe", lambda e: e.matmul(pp[k][:], lhsT=mx[i][0:nr, c, j * 128:(j + 1) * 128], rhs=Wo[0:nr, c, half * 512:(half + 1) * 512], start=(c == 0), stop=(c == 13)),
                                   reads=[rmx, rWo], writes=[rpp])
                            op("dve", lambda e: e.tensor_tensor(out=xt[i][:, j, half * 512:(half + 1) * 512], in0=pp[k][:], in1=xt[i][:, j, half * 512:(half + 1) * 512], op=ALU.add), reads=[rpp, rxt], writes=[rxt])
                    dma(dst_ap[tg * 512:(tg + 1) * 512, :].rearrange("(j p) d -> p j d", p=128), xt[i][:], reads=[rxt], writes=[dst_res])
            S_.barrier()

        for l in range(NL):
            if "prep" in PH:
                prep_weights(l)
            if "s5c" in PH:
                s5_consts(l)
            for s in range(NSEQ):
                src = x_in[s] if l == 0 else xs_d[s]
                dst = out_d[s] if l == NL - 1 else xs_d[s]
                if "p1" in PH:
                    phase1(l, src, rs("xs", s))
                if "dsa" in PH:
                    dsa_phase()
                if "s5" in PH:
                    s5_phase()
                if "mla" in PH:
                    mla_phase()
                if "p3" in PH:
                    phase3(src, rs("xs", s), dst, rs("out_d") if l == NL - 1 else rs("xs", s))
        S_.barrier()
    return nc


xs_all = None


def _build(S, NL, NSEQ, KTOP):
    global xs_all
    return build_program(S, NL, NSEQ, KTOP)


_CACHE = {}


def kernel(**inputs):
    x = np.ascontiguousarray(np.asarray(inputs["x"], dtype=np.float32))
    B, S, _ = x.shape
    ncores = 8
    NSEQ = B // ncores
    KTOP = min(256, S // 4)
    nc = build_program(S, DEPTH, NSEQ, KTOP)
    consts = make_consts(S)
    names = ["norm_g", "w_in", "attn_q_norm", "attn_k_norm", "mla_q_lora_norm", "mla_kv_lora_norm", "mla_w_uq", "mla_w_ukv",
             "mla_q_norm", "mla_k_norm", "ssm_a_re", "ssm_a_im", "ssm_b_re", "ssm_b_im", "ssm_c_re", "ssm_c_im", "ssm_d",
             "ssm_log_step", "ssm_w_glu", "w_out"]
    shared = {n: np.ascontiguousarray(np.asarray(inputs[n], dtype=np.float32)) for n in names}
    in_maps = []
    for c in range(ncores):
        m = dict(shared)
        m["x"] = x[c * NSEQ:(c + 1) * NSEQ]
        m.update(consts)
        in_maps.append(m)
    res = run_bass_kernel_spmd(nc, in_maps, core_ids=list(range(ncores)))
    return np.concatenate([r["out"] for r in res.results], axis=0).astype(np.float32)
```

```python
import contextlib
import numpy as np
import ml_dtypes
import concourse.bass as bass
import concourse.mybir as mybir
from concourse.bass_utils import run_bass_kernel_spmd

F32 = mybir.dt.float32
BF16 = mybir.dt.bfloat16
AF = mybir.ActivationFunctionType
ALU = mybir.AluOpType
AX = mybir.AxisListType

D = 1024
DEPTH = 4
DIN = 2504
EPS = 1e-6
NCOL = 3240
C_IQ2 = 2728
C_QA, C_KA2, C_IQ, C_IK4, C_GA, C_U, C_GB, C_CQ, C_CKV, C_KPE, C_GC, C_VAIW = (
    0, 384, 512, 768, 896, 1280, 1536, 1792, 2048, 2176, 2272, 2656)
S_QA, S_KA, S_VA, S_IQ, S_IK, S_IW, S_GA, S_U, S_GB, S_CQ, S_CKV, S_KPE, S_GC = (
    0, 384, 448, 512, 768, 800, 808, 1192, 1448, 1704, 1960, 2088, 2120)
NEGM = -30000.0
NIT = 16
import os as _os5
P1ENG = _os5.environ.get('P1ENG', 'pool')
import os as _os4
DSA_MODE = int(_os4.environ.get('DSA_MODE', '0'))
import os as _os3
GV = int(_os3.environ.get('GV', '0'))
import os as _os2
GATE_DMA = int(_os2.environ.get('GATE_DMA', '2'))
import os as _os
P1CUT = int(_os.environ.get('P1CUT', '99'))
import os
PH = set(os.environ.get('PH', 'prep,s5c,p1,dsa,s5,mla,p3').split(','))


class Res:
    __slots__ = ("w", "r", "ps")

    def __init__(self, ps=False):
        self.w = {}
        self.r = {}
        self.ps = ps


class Sched:
    def __init__(self, nc, n_dma_sems=32):
        self.nc = nc
        self.engs = {}
        for name, e in (("pe", nc.tensor), ("act", nc.scalar), ("dve", nc.vector),
                        ("pool", nc.gpsimd), ("sp", nc.sync)):
            sem = nc.alloc_semaphore(name="sem_" + name)
            self.engs[name] = dict(e=e, sem=sem, cnt=0, waited={}, name=name)
        self.dma_sems = [nc.alloc_semaphore(name=f"dsem{i}") for i in range(n_dma_sems)]
        self.dma_cnt = [0] * n_dma_sems
        self.dma_i = 0
        self.nins = 0

    def _wait(self, en, deps, skip_self=False):
        E = self.engs[en]
        best = {}
        for sem, val in deps:
            k = id(sem)
            if k not in best or best[k][1] < val:
                best[k] = (sem, val)
        for k, (sem, val) in best.items():
            if skip_self and sem is E["sem"]:
                continue
            if E["waited"].get(k, 0) < val:
                E["e"].wait_ge(sem, val)
                E["waited"][k] = val

    def _deps(self, reads, writes, en=None):
        deps = []
        own = self.engs[en]["sem"] if en is not None else None
        for r in reads:
            deps.extend(r.w.values())
            if r.ps:
                deps.extend(ev for ev in r.r.values() if ev[0] is not own)
        for w in writes:
            deps.extend(w.w.values())
            deps.extend(w.r.values())
        return deps

    @staticmethod
    def _mark(ev, reads, writes):
        k = id(ev[0])
        for r in reads:
            r.r[k] = ev
        for w in writes:
            w.w[k] = ev

    def op(self, en, fn, reads=(), writes=()):
        E = self.engs[en]
        self._wait(en, self._deps(reads, writes, en), skip_self=(en == "pe"))
        ins = fn(E["e"])
        E["cnt"] += 1
        ins.then_inc(E["sem"], 1)
        ev = (E["sem"], E["cnt"])
        self._mark(ev, reads, writes)
        self.nins += 1
        return ev

    def dma(self, out, in_, reads=(), writes=(), en="sp", **kw):
        E = self.engs[en]
        i = self.dma_i % len(self.dma_sems)
        self.dma_i += 1
        sem = self.dma_sems[i]
        deps = self._deps(reads, writes)
        if self.dma_cnt[i] > 0:
            deps.append((sem, self.dma_cnt[i]))
        self._wait(en, deps)
        ins = E["e"].dma_start(out=out, in_=in_, **kw)
        self.dma_cnt[i] += 16
        ins.then_inc(sem, 16)
        ev = (sem, self.dma_cnt[i])
        self._mark(ev, reads, writes)
        self.nins += 1
        return ev

    def barrier(self):
        evs = [(E["sem"], E["cnt"]) for E in self.engs.values() if E["cnt"] > 0]
        evs += [(s, c) for s, c in zip(self.dma_sems, self.dma_cnt) if c > 0]
        for en in self.engs:
            self._wait(en, evs)


def rope_tab(S, d, rows):
    half = d // 2
    inv = (np.float32(10000.0) ** (-np.arange(half, dtype=np.float32) * np.float32(2.0) / np.float32(d))).astype(np.float32)
    ang = np.arange(S, dtype=np.float32)[:, None] * inv[None, :]
    c = np.cos(ang).astype(np.float32).T
    s = np.sin(ang).astype(np.float32).T
    idx = np.arange(rows) % half
    return c[idx], s[idx]


def make_consts(S):
    bf = ml_dtypes.bfloat16
    I = np.eye(128, dtype=np.float32)
    ones64 = np.zeros((128, 128), np.float32)
    ones64[:64, :64] = 1; ones64[64:, 64:] = 1
    ones96 = np.zeros((128, 128), np.float32); ones96[:96, :96] = 1
    ones128 = np.ones((128, 128), np.float32)

    def rot(M, hd, half, lo=0):
        R = np.zeros((128, 128), np.float32)
        for m in range(M):
            j = m % hd
            if j < lo:
                continue
            jj = j - lo
            if jj < half:
                R[m + half, m] = -1.0
            else:
                R[m - half, m] = 1.0
        return R
    R64 = rot(128, 64, 32)
    R32 = rot(128, 32, 16)
    R96 = rot(96, 96, 16, lo=64)
    nmd = np.zeros((128, 128), np.float32); nmd[:64, 64:] = NEGM
    cbf = np.concatenate([I, ones64, ones96, ones128, R64, R32, R96, nmd], axis=1).astype(bf)
    icross = np.zeros((128, 128), np.float32)
    for m in range(128):
        icross[m, (m + 64) % 128] = 1
    e65 = np.zeros((128, 64), np.float32); e65[64, :] = 1
    i16 = np.zeros((128, 16), np.float32)
    for p in range(128):
        i16[p, p % 16] = 1
    sgB = np.where(np.arange(128) < 64, -1.0, 1.0).astype(np.float32)[:, None]
    sgQ = -sgB
    epsc = np.full((128, 1), EPS, np.float32)
    pw = np.tile((2.0 ** -(np.arange(NIT + 1, dtype=np.float64) + 1.0)).astype(np.float32)[None, :], (128, 1))
    cf = np.concatenate([I, icross, e65, i16, sgB, sgQ, epsc, pw], axis=1).astype(np.float32)
    c64, s64 = rope_tab(S, 64, 128)
    c32, s32 = rope_tab(S, 32, 128)
    c96 = np.ones((128, S), np.float32); s96 = np.zeros((128, S), np.float32)
    c96[64:96] = c32[:32]; s96[64:96] = s32[:32]
    rt = np.stack([c64, s64, c32, s32, c96, s96]).astype(np.float32)
    return dict(cbf=cbf, cf=cf, rt=rt)


CB = dict(ident=0, ones64=128, ones96=256, ones128=384, R64=512, R32=640, R96=768, nmd=896)
CF = dict(ident=0, icross=128, e65=256, i16=320, sgB=336, sgQ=337, eps=338, pw=339)
NCF = 339 + NIT + 1


def build_program(S, NL, NSEQ, KTOP, dbg=None):
    nc = bass.Bass("TRN2", target_bir_lowering=False)
    NB = S // 128
    NG = S // 512
    J = S // 8
    dbg = dbg or {}

    def din(name, shape, dt=F32):
        return nc.dram_tensor(name, list(shape), dt, kind="ExternalInput").ap()

    def dscr(name, shape, dt):
        return nc.dram_tensor(name, list(shape), dt, kind="Internal").ap()

    x_in = din("x", [NSEQ, S, D])
    norm_g = din("norm_g", [DEPTH, D])
    w_in = din("w_in", [DEPTH, D, DIN])
    attn_q_norm = din("attn_q_norm", [DEPTH, 64])
    attn_k_norm = din("attn_k_norm", [DEPTH, 64])
    q_lora_g = din("mla_q_lora_norm", [DEPTH, 256])
    kv_lora_g = din("mla_kv_lora_norm", [DEPTH, 128])
    w_uq = din("mla_w_uq", [DEPTH, 256, 576])
    w_ukv = din("mla_w_ukv", [DEPTH, 128, 768])
    mla_q_norm = din("mla_q_norm", [DEPTH, 96])
    mla_k_norm = din("mla_k_norm", [DEPTH, 96])
    a_re = din("ssm_a_re", [DEPTH, 16, 64])
    a_im = din("ssm_a_im", [DEPTH, 16, 64])
    b_re = din("ssm_b_re", [DEPTH, 16, 64, 16])
    b_im = din("ssm_b_im", [DEPTH, 16, 64, 16])
    c_re = din("ssm_c_re", [DEPTH, 16, 16, 64])
    c_im = din("ssm_c_im", [DEPTH, 16, 16, 64])
    d_skip = din("ssm_d", [DEPTH, 16, 16])
    log_step = din("ssm_log_step", [DEPTH, 16])
    w_glu = din("ssm_w_glu", [DEPTH, 256, 256])
    w_out = din("w_out", [DEPTH, D, D])
    cbf_d = din("cbf", [128, 1024], BF16)
    cf_d = din("cf", [128, NCF])
    rt_d = din("rt", [6, 128, S])
    out_d = nc.dram_tensor("out", [NSEQ, S, D], F32, kind="ExternalOutput").ap()

    xs_d = dscr("xs", [NSEQ, S, D], F32)
    winp_d = dscr("winp", [128, 8, NCOL], BF16)
    wout_d = dscr("woutp", [128, 14, D], BF16)
    qT_d = dscr("qT", [3, 128, S], BF16)
    kT2_d = dscr("kT2", [128, S], BF16)
    va_d = dscr("va", [NB, 128, 65], BF16)
    iqT_d = dscr("iqT", [4, 128, S], BF16)
    ikT_d = dscr("ikT", [128, S], BF16)
    iw_d = dscr("iw", [NB, 128, 16], F32)
    gaT_d = dscr("gaT", [6, 64, S], BF16)
    gbT_d = dscr("gbT", [2, 128, S], BF16)
    gcT_d = dscr("gcT", [6, 64, S], BF16)
    U_d = dscr("U", [16, 8, 16, J], F32)
    Y_d = dscr("Y", [16, 8, 16, J], F32)
    qc_d = dscr("qc", [6, 96, S], BF16)
    kc_d = dscr("kc", [6, 96, S], BF16)
    vc_d = dscr("vc", [6, NB, 128, 65], BF16)
    mix_d = dscr("mix", [14, 128, S], BF16)
    s5c_d = dscr("s5c", [16, 128, 12, 128], F32)

    S_ = Sched(nc)
    op = S_.op
    dma = S_.dma
    RS = {}

    PSUM_KEYS = {"p1_tp", "p1_mm", "p1_aux", "at_ps", "at_o", "at_sbp", "ds_shp", "ds_scp", "s5_pp", "s5r_p", "s5p_z", "p3_pp"}

    def rs(*key):
        if key not in RS:
            RS[key] = Res(ps=(key[0] in PSUM_KEYS))
        return RS[key]

    with contextlib.ExitStack() as top:
        uid = [0]

        def sbt(st, name, shape, dt):
            uid[0] += 1
            return st.enter_context(nc.sbuf_tensor(f"sb{uid[0]}_{name}", list(shape), dt))

        def pst(st, name, shape, dt):
            uid[0] += 1
            return st.enter_context(nc.psum_tensor(f"ps{uid[0]}_{name}", list(shape), dt))

        cbf = sbt(top, "cbf", [128, 1024], BF16)
        cf = sbt(top, "cf", [128, NCF], F32)
        dma(cbf[:], cbf_d, writes=[rs("cbf")])
        dma(cf[:], cf_d, writes=[rs("cf")])
        rC = [rs("cbf"), rs("cf")]

        def cb(name, rows=128, cols=128):
            o = CB[name]
            return cbf[0:rows, o:o + cols]

        def cfa(name, rows=128, cols=128):
            o = CF[name]
            return cf[0:rows, o:o + cols]
        eps_ap = cf[:, CF["eps"]:CF["eps"] + 1]

        gsm = sbt(top, "gsm", [128, DEPTH, 16], F32)
        rg = rs("gsm")
        op("dve", lambda e: e.memset(gsm[:], 1.0), writes=[rg])
        for l in range(NL):
            dma(gsm[:, l, 0:8], norm_g[l].rearrange("(k p) -> p k", p=128), writes=[rg], allow_slow_non_contiguous=True)
            for hh in range(2):
                dma(gsm[64 * hh:64 * hh + 64, l, 8:9], attn_q_norm[l].rearrange("(p o) -> p o", o=1), writes=[rg], allow_slow_non_contiguous=True)
                dma(gsm[64 * hh:64 * hh + 64, l, 9:10], attn_k_norm[l].rearrange("(p o) -> p o", o=1), writes=[rg], allow_slow_non_contiguous=True)
            dma(gsm[:, l, 10:12], q_lora_g[l].rearrange("(k p) -> p k", p=128), writes=[rg], allow_slow_non_contiguous=True)
            dma(gsm[:, l, 12:13], kv_lora_g[l].rearrange("(p o) -> p o", o=1), writes=[rg], allow_slow_non_contiguous=True)
            dma(gsm[0:96, l, 13:14], mla_q_norm[l].rearrange("(p o) -> p o", o=1), writes=[rg], allow_slow_non_contiguous=True)
            dma(gsm[0:96, l, 14:15], mla_k_norm[l].rearrange("(p o) -> p o", o=1), writes=[rg], allow_slow_non_contiguous=True)

        def prep_weights(l):
            with contextlib.ExitStack() as st:
                stg = [sbt(st, f"stg{i}", [128, DIN], F32) for i in range(2)]
                wrow = [sbt(st, f"wrow{i}", [128, NCOL], BF16) for i in range(2)]
                for kc in range(8):
                    i = kc % 2
                    rst, rw = rs("stg", i), rs("wrow", i)
                    dma(stg[i][:], w_in[l, kc * 128:(kc + 1) * 128, :], writes=[rst])
                    g = gsm[:, l, kc:kc + 1]
                    segs = [(C_QA, S_QA, 384), (C_KA2, S_KA, 64), (C_KA2 + 64, S_KA, 64)] + [(C_IQ2 + (h_ // 2) * 128 + 64 * (h_ % 2), S_IQ + 32 * h_, 32) for h_ in range(8)] + \
                           [(C_IK4 + 32 * r, S_IK, 32) for r in range(4)] + \
                           [(C_GA, S_GA, 384), (C_U, S_U, 256), (C_GB, S_GB, 256), (C_CQ, S_CQ, 256), (C_CKV, S_CKV, 128),
                            (C_KPE + 64, S_KPE, 32), (C_GC, S_GC, 384), (C_VAIW, S_VA, 64), (C_VAIW + 64, S_IW, 8)]
                    op("pool", lambda e: e.memset(wrow[i][:, C_KPE:C_KPE + 64], 0.0), writes=[rw])
                    op("pool", lambda e: e.memset(wrow[i][:, C_IQ2:C_IQ2 + 512], 0.0), writes=[rw])
                    op("pool", lambda e: e.memset(wrow[i][:, C_IQ:C_IQ + 256], 0.0), writes=[rw])
                    for n_, (dc, sc_, w_) in enumerate(segs):
                        if n_ % 2 == 0:
                            op("dve", lambda e: e.tensor_scalar(out=wrow[i][:, dc:dc + w_], in0=stg[i][:, sc_:sc_ + w_], scalar1=g, scalar2=None, op0=ALU.mult),
                               reads=[rst, rg], writes=[rw])
                        else:
                            op("act", lambda e: e.activation(out=wrow[i][:, dc:dc + w_], in_=stg[i][:, sc_:sc_ + w_], func=AF.Copy, scale=g),
                               reads=[rst, rg], writes=[rw])
                    dma(winp_d[:, kc, :], wrow[i][:], reads=[rw], writes=[rs("winp_d")])
            with contextlib.ExitStack() as st:
                stg = [sbt(st, f"stgo{i}", [128, D], F32) for i in range(2)]
                wrow = [sbt(st, f"wrowo{i}", [128, D], BF16) for i in range(2)]
                for c in range(14):
                    i = c % 2
                    rst, rw = rs("stgo", i), rs("wrowo", i)
                    if c < 6:
                        r0, nr = 64 * c, 64
                    elif c < 8:
                        r0, nr = 384 + 128 * (c - 6), 128
                    else:
                        r0, nr = 640 + 64 * (c - 8), 64
                    dma(stg[i][0:nr, :], w_out[l, r0:r0 + nr, :], writes=[rst])
                    if c % 2 == 0:
                        op("dve", lambda e: e.tensor_copy(out=wrow[i][0:nr, :], in_=stg[i][0:nr, :]), reads=[rst], writes=[rw])
                    else:
                        op("act", lambda e: e.copy(out=wrow[i][0:nr, :], in_=stg[i][0:nr, :]), reads=[rst], writes=[rw])
                    dma(wout_d[0:nr, c, :], wrow[i][0:nr, :], reads=[rw], writes=[rs("wout_d")])
            with contextlib.ExitStack() as st:
                s1 = sbt(st, "s_uq", [128, 2, 576], F32)
                s2 = sbt(st, "s_ukv", [128, 768], F32)
                s3 = sbt(st, "s_glu", [128, 2, 256], F32)
                r1, r2, r3 = rs("s_uq"), rs("s_ukv"), rs("s_glu")
                dma(s1[:], w_uq[l].rearrange("(k p) n -> p k n", p=128), writes=[r1])
                dma(s2[:], w_ukv[l], writes=[r2])
                dma(s3[:], w_glu[l].rearrange("(k p) n -> p k n", p=128), writes=[r3])
                rw = rs("wsm")
                for k in range(2):
                    op("dve", lambda e: e.tensor_scalar(out=Wuq[:, k, :], in0=s1[:, k, :], scalar1=gsm[:, l, 10 + k:11 + k], scalar2=None, op0=ALU.mult),
                       reads=[r1, rg], writes=[rw])
                    op("dve", lambda e: e.tensor_copy(out=Wglu[:, k, :], in_=s3[:, k, :]), reads=[r3], writes=[rw])
                op("dve", lambda e: e.memset(WukT[:], 0.0), writes=[rw])
                for h in range(6):
                    op("dve", lambda e: e.tensor_scalar(out=WukT[:, h, 0:64], in0=s2[:, h * 128:h * 128 + 64], scalar1=gsm[:, l, 12:13], scalar2=None, op0=ALU.mult),
                       reads=[r2, rg], writes=[rw])
                    op("dve", lambda e: e.tensor_scalar(out=Wuv[:, h * 64:h * 64 + 64], in0=s2[:, h * 128 + 64:h * 128 + 128], scalar1=gsm[:, l, 12:13], scalar2=None, op0=ALU.mult),
                       reads=[r2, rg], writes=[rw])
            S_.barrier()

        Wuq = sbt(top, "Wuq", [128, 2, 576], BF16)
        WukT = sbt(top, "WukT", [128, 6, 96], BF16)
        Wuv = sbt(top, "Wuv", [128, 384], BF16)
        Wglu = sbt(top, "Wglu", [128, 2, 256], BF16)
        rWsm = rs("wsm")

        def phase1(l, src_ap, src_res):
            with contextlib.ExitStack() as st:
                Winp = sbt(st, "Winp", [128, 8, NCOL], BF16)
                rW = rs("Winp")
                for kc in range(8):
                    dma(Winp[:, kc, :], winp_d[:, kc, :], reads=[rs("winp_d")], writes=[rW])
                xt = sbt(st, "p1_xt", [128, 4, D], F32)
                rxt = rs("p1_xt")
                junk = sbt(st, "p1_junk", [128, D], BF16)
                xn = sbt(st, "p1_xn", [128, D], BF16)
                ssq = sbt(st, "p1_ssq", [128, 8], F32)
                hT = sbt(st, "p1_hT", [128, 8, 512], BF16)
                rhT = rs("p1_hT")
                tabs = sbt(st, "p1_tabs", [128, 6, 512], F32)
                rtab = rs("p1_tabs")
                tp = [pst(st, f"p1_tp{i}", [128, 1024], BF16) for i in range(2)]
                mm = [pst(st, f"p1_mm{i}", [128, 512], F32) for i in range(3)]
                aux = [pst(st, f"p1_aux{i}", [128, 512], F32) for i in range(3)]
                mmi = [0]
                NWK = 3
                xg = [sbt(st, f"p1_xg{i}", [128, 512], BF16) for i in range(NWK)]
                sq = [sbt(st, f"p1_sq{i}", [128, 512], BF16) for i in range(NWK)]
                sd = [sbt(st, f"p1_sd{i}", [128, 512], F32) for i in range(NWK)]
                t1 = [sbt(st, f"p1_t1{i}", [128, 512], F32) for i in range(NWK)]
                t2 = [sbt(st, f"p1_t2{i}", [128, 512], F32) for i in range(NWK)]
                ob = [sbt(st, f"p1_ob{i}", [128, 512], BF16) for i in range(NWK)]
                up = [sbt(st, f"p1_up{i}", [128, 8, 64], F32) for i in range(2)]
                cqn = sbt(st, "p1_cqn", [128, 2, 512], BF16)
                ckvn = sbt(st, "p1_ckvn", [128, 512], BF16)
                vcs = [sbt(st, f"p1_vcs{i}", [128, 6, 65], BF16) for i in range(2)]
                vas = [sbt(st, f"p1_vas{i}", [128, 65], BF16) for i in range(2)]
                iws = [sbt(st, f"p1_iws{i}", [128, 16], F32) for i in range(2)]
                for i in range(2):
                    op("pool", lambda e: e.memset(vcs[i][:], 1.0), writes=[rs("p1_vcs", i)])
                    op("pool", lambda e: e.memset(vas[i][:], 1.0), writes=[rs("p1_vas", i)])
                wk = [0]

                def proj(cols, M, rhs_fn=None):
                    k = mmi[0] % 3
                    mmi[0] += 1
                    r = rs("p1_mm", k)
                    for kc in range(8):
                        op("pe", lambda e: e.matmul(mm[k][0:M, :], lhsT=Winp[:, kc, cols:cols + M], rhs=hT[:, kc, :], start=(kc == 0), stop=(kc == 7)),
                           reads=[rW, rhT], writes=[r])
                    return mm[k], r

                def normrope(X, rX, M, gain, onesname, inv_dim, Rname, ci, dst_ap, dst_res):
                    i = wk[0] % NWK
                    wk[0] += 1
                    a0, a1 = aux[(2 * i) % 3], aux[(2 * i + 1) % 3]
                    ra0, ra1 = rs("p1_aux", (2 * i) % 3), rs("p1_aux", (2 * i + 1) % 3)
                    rxg, rsq, rsd, rt1, rt2, rob = (rs("p1_xg", i), rs("p1_sq", i), rs("p1_sd", i), rs("p1_t1", i), rs("p1_t2", i), rs("p1_ob", i))
                    Ct, St = tabs[0:M, ci, :], tabs[0:M, ci + 1, :]
                    if gain is not None:
                        op("act", lambda e: e.activation(out=xg[i][0:M, :], in_=X[0:M, :], func=AF.Copy, scale=gain), reads=[rX, rg], writes=[rxg])
                    else:
                        op("act", lambda e: e.copy(out=xg[i][0:M, :], in_=X[0:M, :]), reads=[rX], writes=[rxg])
                    op("pe", lambda e: e.matmul(a1[0:M, :], lhsT=cb(Rname, M, M), rhs=xg[i][0:M, :], start=True, stop=True), reads=[rxg] + rC, writes=[ra1])
                    if onesname is not None:
                        op("act", lambda e: e.activation(out=sq[i][0:M, :], in_=X[0:M, :], func=AF.Square), reads=[rX], writes=[rsq])
                        op("pe", lambda e: e.matmul(a0[0:M, :], lhsT=cb(onesname, M, M), rhs=sq[i][0:M, :], start=True, stop=True), reads=[rsq] + rC, writes=[ra0])
                        op("act", lambda e: e.activation(out=sd[i][0:M, :], in_=a0[0:M, :], func=AF.Ln, scale=inv_dim, bias=eps_ap[0:M, :]), reads=[ra0] + rC, writes=[rsd])
                        op("act", lambda e: e.activation(out=sd[i][0:M, :], in_=sd[i][0:M, :], func=AF.Exp, scale=-0.5), reads=[rsd], writes=[rsd])
                    if gain is not None:
                        op("dve", lambda e: e.scalar_tensor_tensor(out=t1[i][0:M, :], in0=X[0:M, :], scalar=gain, in1=Ct, op0=ALU.mult, op1=ALU.mult),
                           reads=[rX, rtab, rg], writes=[rt1])
                    else:
                        op("dve", lambda e: e.tensor_tensor(out=t1[i][0:M, :], in0=X[0:M, :], in1=Ct, op=ALU.mult), reads=[rX, rtab], writes=[rt1])
                    op("dve", lambda e: e.tensor_tensor(out=t2[i][0:M, :], in0=a1[0:M, :], in1=St, op=ALU.mult), reads=[ra1, rtab], writes=[rt2])
                    if onesname is not None:
                        op(P1ENG, lambda e: e.tensor_tensor(out=t1[i][0:M, :], in0=t1[i][0:M, :], in1=t2[i][0:M, :], op=ALU.add), reads=[rt1, rt2], writes=[rt1])
                        op(P1ENG, lambda e: e.tensor_tensor(out=ob[i][0:M, :], in0=t1[i][0:M, :], in1=sd[i][0:M, :], op=ALU.mult), reads=[rt1, rsd], writes=[rob])
                    else:
                        op("dve", lambda e: e.tensor_tensor(out=ob[i][0:M, :], in0=t1[i][0:M, :], in1=t2[i][0:M, :], op=ALU.add), reads=[rt1, rt2], writes=[rob])
                    dma(dst_ap, ob[i][0:M, :], reads=[rob], writes=[dst_res])

                def silu_out(X, rX, M, dsts):
                    i = wk[0] % NWK
                    wk[0] += 1
                    rob = rs("p1_ob", i)
                    rt1 = rs("p1_t1", i)
                    op("act", lambda e: e.activation(out=t1[i][0:M, :], in_=X[0:M, :], func=AF.Sigmoid), reads=[rX], writes=[rt1])
                    if GV != 3:
                        op("dve", lambda e: e.scalar_tensor_tensor(out=ob[i][0:M, :], in0=X[0:M, :], scalar=1.0, in1=t1[i][0:M, :], op0=ALU.mult, op1=ALU.mult), reads=[rX, rt1], writes=[rob])
                    for (p0, p1, dap, dres) in dsts:
                        if GATE_DMA == 0 or (GATE_DMA == 1 and p0 != 0):
                            continue
                        dma(dap, ob[i][p0:p1, :], reads=[rob], writes=[dres])

                for tg in range(NG):
                    t0 = tg * 512
                    tsl = slice(t0, t0 + 512)
                    dma(xt[:], src_ap[t0:t0 + 512, :].rearrange("(j p) d -> p j d", p=128), reads=[src_res], writes=[rxt])
                    dma(tabs[:], rt_d[:, :, tsl].rearrange("c p t -> p c t"), writes=[rtab])
                    rssq, rjunk, rxn = rs("p1_ssq"), rs("p1_junk"), rs("p1_xn")
                    for j in range(4):
                        op("act", lambda e: e.activation(out=junk[:], in_=xt[:, j, :], func=AF.Square, accum_out=ssq[:, j:j + 1]), reads=[rxt], writes=[rjunk, rssq])
                    op("act", lambda e: e.activation(out=ssq[:, 4:8], in_=ssq[:, 0:4], func=AF.Sqrt, scale=1.0 / D, bias=eps_ap), reads=[rssq] + rC, writes=[rssq])
                    op("dve", lambda e: e.reciprocal(out=ssq[:, 4:8], in_=ssq[:, 4:8]), reads=[rssq], writes=[rssq])
                    for j in range(4):
                        op("dve", lambda e: e.tensor_scalar(out=xn[:], in0=xt[:, j, :], scalar1=ssq[:, 4 + j:5 + j], scalar2=None, op0=ALU.mult), reads=[rxt, rssq], writes=[rxn])
                        for half in range(2):
                            k = half
                            rtp = rs("p1_tp", k)
                            for q in range(4):
                                kc = half * 4 + q
                                op("pe", lambda e: e.transpose(out=tp[k][:, q * 128:(q + 1) * 128], in_=xn[:, kc * 128:(kc + 1) * 128], identity=cb("ident")),
                                   reads=[rxn] + rC, writes=[rtp])
                            eng = "act" if half == 0 else "dve"
                            if eng == "act":
                                op("act", lambda e: e.copy(out=hT[:, half * 4:half * 4 + 4, j * 128:(j + 1) * 128], in_=tp[k][:, 0:512].rearrange("p (q t) -> p q t", q=4)), reads=[rtp], writes=[rhT])
                            else:
                                op("dve", lambda e: e.tensor_copy(out=hT[:, half * 4:half * 4 + 4, j * 128:(j + 1) * 128], in_=tp[k][:, 0:512].rearrange("p (q t) -> p q t", q=4)), reads=[rtp], writes=[rhT])
                    if P1CUT <= 1:
                        continue
                    for c in range(3):
                        X, rX = proj(C_QA + 128 * c, 128)
                        normrope(X, rX, 128, gsm[:, l, 8:9], "ones64", 1.0 / 64, "R64", 0, qT_d[c, :, tsl], rs("qT_d"))
                    X, rX = proj(C_KA2, 128)
                    normrope(X, rX, 128, gsm[:, l, 9:10], "ones64", 1.0 / 64, "R64", 0, kT2_d[:, tsl], rs("kT2_d"))
                    if P1CUT <= 2:
                        continue
                    for c in range(4):
                        X, rX = proj(C_IQ2 + 128 * c, 128)
                        normrope(X, rX, 128, None, None, None, "R32", 2, iqT_d[c, :, tsl], rs("iqT_d"))
                    X, rX = proj(C_IK4, 128)
                    normrope(X, rX, 128, None, None, None, "R32", 2, ikT_d[:, tsl], rs("ikT_d"))
                    if P1CUT <= 3:
                        continue
                    for c in range(3):
                        X, rX = proj(C_GA + 128 * c, 128)
                        silu_out(X, rX, 128, [(0, 128, gaT_d.rearrange("h p t -> (h p) t")[128 * c:128 * c + 128, tsl], rs("gaT_d"))])
                    for c in range(2):
                        X, rX = proj(C_GB + 128 * c, 128)
                        silu_out(X, rX, 128, [(0, 128, gbT_d[c, :, tsl], rs("gbT_d"))])
                    for c in range(3):
                        X, rX = proj(C_GC + 128 * c, 128)
                        silu_out(X, rX, 128, [(0, 128, gcT_d.rearrange("h p t -> (h p) t")[128 * c:128 * c + 128, tsl], rs("gcT_d"))])
                    if P1CUT <= 4:
                        continue
                    for c in range(2):
                        X, rX = proj(C_U + 128 * c, 128)
                        i = c
                        rup = rs("p1_up", i)
                        op("dve", lambda e: e.tensor_copy(out=up[i][:].rearrange("p s j -> p j s"), in_=X[:].rearrange("p (j s) -> p j s", s=8)), reads=[rX], writes=[rup])
                        for g8 in range(8):
                            g = 8 * c + g8
                            dma(U_d[g, :, :, tg * 64:(tg + 1) * 64].rearrange("s c j -> c s j"), up[i][16 * g8:16 * g8 + 16, :, :], reads=[rup], writes=[rs("U_d")])
                    if P1CUT <= 5:
                        continue
                    X0, rX0 = proj(C_CQ, 128)
                    X1, rX1 = proj(C_CQ + 128, 128)
                    i = wk[0] % NWK
                    wk[0] += 1
                    a0, ra0 = aux[(2 * i) % 3], rs("p1_aux", (2 * i) % 3)
                    rsq, rsd, rt1 = rs("p1_sq", i), rs("p1_sd", i), rs("p1_t1", i)
                    i2 = wk[0] % NWK
                    wk[0] += 1
                    rsq2 = rs("p1_sq", i2)
                    op("act", lambda e: e.activation(out=sq[i][:], in_=X0[:], func=AF.Square), reads=[rX0], writes=[rsq])
                    op("act", lambda e: e.activation(out=sq[i2][:], in_=X1[:], func=AF.Square), reads=[rX1], writes=[rsq2])
                    op("pe", lambda e: e.matmul(a0[:], lhsT=cb("ones128"), rhs=sq[i][:], start=True, stop=False), reads=[rsq] + rC, writes=[ra0])
                    op("pe", lambda e: e.matmul(a0[:], lhsT=cb("ones128"), rhs=sq[i2][:], start=False, stop=True), reads=[rsq2] + rC, writes=[ra0])
                    op("act", lambda e: e.activation(out=sd[i][:], in_=a0[:], func=AF.Ln, scale=1.0 / 256, bias=eps_ap), reads=[ra0] + rC, writes=[rsd])
                    op("act", lambda e: e.activation(out=sd[i][:], in_=sd[i][:], func=AF.Exp, scale=-0.5), reads=[rsd], writes=[rsd])
                    rcq = rs("p1_cqn")
                    op("dve", lambda e: e.tensor_tensor(out=cqn[:, 0, :], in0=X0[:], in1=sd[i][:], op=ALU.mult), reads=[rX0, rsd], writes=[rcq])
                    op("dve", lambda e: e.tensor_tensor(out=cqn[:, 1, :], in0=X1[:], in1=sd[i][:], op=ALU.mult), reads=[rX1, rsd], writes=[rcq])
                    X0, rX0 = proj(C_CKV, 128)
                    i = wk[0] % NWK
                    wk[0] += 1
                    a0, ra0 = aux[(2 * i) % 3], rs("p1_aux", (2 * i) % 3)
                    rsq, rsd = rs("p1_sq", i), rs("p1_sd", i)
                    op("act", lambda e: e.activation(out=sq[i][:], in_=X0[:], func=AF.Square), reads=[rX0], writes=[rsq])
                    op("pe", lambda e: e.matmul(a0[:], lhsT=cb("ones128"), rhs=sq[i][:], start=True, stop=True), reads=[rsq] + rC, writes=[ra0])
                    op("act", lambda e: e.activation(out=sd[i][:], in_=a0[:], func=AF.Ln, scale=1.0 / 128, bias=eps_ap), reads=[ra0] + rC, writes=[rsd])
                    op("act", lambda e: e.activation(out=sd[i][:], in_=sd[i][:], func=AF.Exp, scale=-0.5), reads=[rsd], writes=[rsd])
                    rckv = rs("p1_ckvn")
                    op("dve", lambda e: e.tensor_tensor(out=ckvn[:], in0=X0[:], in1=sd[i][:], op=ALU.mult), reads=[rX0, rsd], writes=[rckv])
                    if P1CUT <= 6:
                        continue
                    for h in range(6):
                        k = mmi[0] % 3
                        mmi[0] += 1
                        r = rs("p1_mm", k)
                        for c in range(2):
                            op("pe", lambda e: e.matmul(mm[k][0:96, :], lhsT=Wuq[:, c, h * 96:(h + 1) * 96], rhs=cqn[:, c, :], start=(c == 0), stop=(c == 1)),
                               reads=[rWsm, rcq], writes=[r])
                        normrope(mm[k], r, 96, gsm[0:96, l, 13:14], "ones96", 1.0 / 96, "R96", 4, qc_d[h, :, tsl], rs("qc_d"))
                    for h in range(6):
                        k = mmi[0] % 3
                        mmi[0] += 1
                        r = rs("p1_mm", k)
                        op("pe", lambda e: e.matmul(mm[k][0:96, :], lhsT=WukT[:, h, :], rhs=ckvn[:], start=True, stop=False), reads=[rWsm, rckv], writes=[r])
                        for kc in range(8):
                            op("pe", lambda e: e.matmul(mm[k][0:96, :], lhsT=Winp[:, kc, C_KPE:C_KPE + 96], rhs=hT[:, kc, :], start=False, stop=(kc == 7)),
                               reads=[rW, rhT], writes=[r])
                        normrope(mm[k], r, 96, gsm[0:96, l, 14:15], "ones96", 1.0 / 96, "R96", 4, kc_d[h, :, tsl], rs("kc_d"))
                    if P1CUT <= 7:
                        continue
                    for j in range(4):
                        tb = tg * 4 + j
                        k = mmi[0] % 3
                        mmi[0] += 1
                        r = rs("p1_mm", k)
                        op("pe", lambda e: e.matmul(mm[k][:, 0:384], lhsT=ckvn[:, j * 128:(j + 1) * 128], rhs=Wuv[:], start=True, stop=True), reads=[rWsm, rckv], writes=[r])
                        i = j % 2
                        rv = rs("p1_vcs", i)
                        op("act", lambda e: e.copy(out=vcs[i][:, :, 0:64], in_=mm[k][:, 0:384].rearrange("p (h d) -> p h d", h=6)), reads=[r], writes=[rv])
                        dma(vc_d[:, tb, :, :].rearrange("h p c -> p h c"), vcs[i][:], reads=[rv], writes=[rs("vc_d")])
                        k = mmi[0] % 3
                        mmi[0] += 1
                        r = rs("p1_mm", k)
                        for kc in range(8):
                            op("pe", lambda e: e.matmul(mm[k][:, 0:72], lhsT=hT[:, kc, j * 128:(j + 1) * 128], rhs=Winp[:, kc, C_VAIW:C_VAIW + 72], start=(kc == 0), stop=(kc == 7)),
                               reads=[rW, rhT], writes=[r])
                        rva, riw = rs("p1_vas", i), rs("p1_iws", i)
                        op("act", lambda e: e.copy(out=vas[i][:, 0:64], in_=mm[k][:, 0:64]), reads=[r], writes=[rva])
                        dma(va_d[tb, :, :], vas[i][:], reads=[rva], writes=[rs("va_d")])
                        sc_ = (8.0 ** -0.5) * (32.0 ** -0.5)
                        op("act", lambda e: e.activation(out=iws[i][:, 0:8], in_=mm[k][:, 64:72], func=AF.Abs, scale=sc_), reads=[r], writes=[riw])
                        op("act", lambda e: e.activation(out=iws[i][:, 8:16], in_=mm[k][:, 64:72], func=AF.Sign), reads=[r], writes=[riw])
                        dma(iw_d[tb, :, :], iws[i][:], reads=[riw], writes=[rs("iw_d")])
            S_.barrier()

        def att_tiles(st, n_ops=2):
            T = {}
            T["att"] = [pst(st, f"at_ps{i}", [128, 512], F32) for i in range(2)]
            T["Ops"] = [pst(st, f"at_o{i}", [128, 512], F32) for i in range(n_ops)]
            T["sbp"] = pst(st, "at_sb", [128, 512], F32)
            T["PT"] = [sbt(st, f"at_pt{i}", [128, 512], BF16) for i in range(3)]
            T["Osb"] = [sbt(st, f"at_osb{i}", [65, 6, 512], F32) for i in range(2)]
            T["rcp"] = [sbt(st, f"at_rcp{i}", [64, 512], F32) for i in range(2)]
            T["yb"] = [sbt(st, f"at_yb{i}", [64, 512], BF16) for i in range(2)]
            T["gt"] = [sbt(st, f"at_gt{i}", [64, 6, 512], BF16) for i in range(2)]
            T["cnt"] = [0, 0, 0]
            return T

        def att_main(T, sb, n_heads, scale, heads, qfn, qres, nm_fn):
            att, Ops, PT, Osb = T["att"], T["Ops"], T["PT"], T["Osb"]
            cnt = T["cnt"]
            ob = sb % 2
            nkb = 4 * sb + 4
            rosb = rs("at_osb", ob)
            n_ops = len(Ops)
            steps = [(h, kb) for h in range(n_heads) for kb in range(nkb)]
            pend = None
            ois = {}
            for stp in steps + [None]:
                cur = None
                if stp is not None:
                    h, kb = stp
                    H = heads[h]
                    if kb == 0:
                        ois[h] = cnt[1] % n_ops
                        cnt[1] += 1
                    qb0 = max(kb - 4 * sb, 0)
                    qs = slice(qb0 * 128, 512)
                    ai = cnt[0] % 2
                    pi = cnt[0] % 3
                    cnt[0] += 1
                    ra, rp = rs("at_ps", ai), rs("at_pt", pi)
                    masks = []
                    for qb in range(qb0, 4):
                        m = nm_fn(4 * sb + qb, kb)
                        if m is not None:
                            masks.append((qb, m))
                    op("pe", lambda e: e.matmul(att[ai][:, qs], lhsT=H["kT"](slice(kb * 128, kb * 128 + 128)), rhs=qfn(h, qs), start=True, stop=(len(masks) == 0)),
                       reads=[H["kres"], qres], writes=[ra])
                    for mi, (qb, (map_, mres)) in enumerate(masks):
                        op("pe", lambda e: e.matmul(att[ai][:, qb * 128:(qb + 1) * 128], lhsT=map_, rhs=cb("ident"), start=False, stop=(mi == len(masks) - 1)),
                           reads=[mres] + rC, writes=[ra])
                    op("act", lambda e: e.activation(out=PT[pi][:, qs], in_=att[ai][:, qs], func=AF.Exp, scale=scale), reads=[ra], writes=[rp])
                    cur = (h, kb, pi, qs)
                if pend is not None:
                    h2, kb2, pi2, qs2 = pend
                    H2 = heads[h2]
                    oi = ois[h2]
                    rO = rs("at_o", oi)
                    op("pe", lambda e: e.matmul(Ops[oi][0:65, qs2], lhsT=H2["vaug"][:, kb2, :], rhs=PT[pi2][:, qs2], start=(kb2 == 0), stop=(kb2 == nkb - 1)),
                       reads=[rs("at_pt", pi2), H2["vres"]], writes=[rO])
                    if kb2 == nkb - 1:
                        op("act", lambda e: e.copy(out=Osb[ob][:, h2, :], in_=Ops[oi][0:65, :]), reads=[rO], writes=[rosb])
                pend = cur

        def att_norm(T, sb, n_heads, gate_d, gate_res, mix_base):
            Osb, sbp, rcp, yb, gt = T["Osb"], T["sbp"], T["rcp"], T["yb"], T["gt"]
            cnt = T["cnt"]
            ob = sb % 2
            rosb, rgt, rsb = rs("at_osb", ob), rs("at_gt", ob), rs("at_sbp")
            ssl = slice(sb * 512, (sb + 1) * 512)
            dma(gt[ob][:], gate_d[:, :, ssl].rearrange("h p t -> p h t"), reads=[rs(gate_res)], writes=[rgt])
            for h in range(n_heads):
                ri = cnt[2] % 2
                cnt[2] += 1
                rrc, ryb = rs("at_rcp", ri), rs("at_yb", ri)
                op("pe", lambda e: e.matmul(sbp[0:64, :], lhsT=cfa("e65", 65, 64), rhs=Osb[ob][:, h, :], start=True, stop=True), reads=[rosb] + rC, writes=[rsb])
                op("act", lambda e: e.activation(out=rcp[ri][:], in_=sbp[0:64, :], func=AF.Ln), reads=[rsb], writes=[rrc])
                op("act", lambda e: e.activation(out=rcp[ri][:], in_=rcp[ri][:], func=AF.Exp, scale=-1.0), reads=[rrc], writes=[rrc])
                op("dve", lambda e: e.tensor_tensor(out=rcp[ri][:], in0=rcp[ri][:], in1=Osb[ob][0:64, h, :], op=ALU.mult), reads=[rrc, rosb], writes=[rrc])
                op(P1ENG, lambda e: e.tensor_tensor(out=yb[ri][:], in0=rcp[ri][:], in1=gt[ob][:, h, :], op=ALU.mult), reads=[rrc, rgt], writes=[ryb])
                dma(mix_d[mix_base + h, 0:64, ssl], yb[ri][:], reads=[ryb], writes=[rs("mix_d")])

        def dsa_phase():
            with contextlib.ExitStack() as st:
                kT2 = sbt(st, "ds_kT2", [128, S], BF16)
                vaug = sbt(st, "ds_vaug", [128, NB, 65], BF16)
                ikT = sbt(st, "ds_ikT", [128, S], BF16)
                rk, rv, rik = rs("ds_kT2"), rs("ds_vaug"), rs("ds_ikT")
                dma(kT2[:], kT2_d, reads=[rs("kT2_d")], writes=[rk])
                dma(vaug[:], va_d.rearrange("t p c -> p t c"), reads=[rs("va_d")], writes=[rv])
                dma(ikT[:], ikT_d, reads=[rs("ikT_d")], writes=[rik])
                NM = [sbt(st, f"ds_NM{i}", [128, 4, S], BF16) for i in range(2)]
                sc = [sbt(st, f"ds_sc{i}", [128, S], F32) for i in range(2)]
                junk = sbt(st, "ds_junk", [128, S], BF16)
                iq = [sbt(st, f"ds_iq{i}", [128, 4, 128], BF16) for i in range(2)]
                iw = [sbt(st, f"ds_iw{i}", [128, 16], F32) for i in range(2)]
                Dh = [sbt(st, f"ds_Dh{i}", [128, 8, 128], BF16) for i in range(2)]
                Th = [sbt(st, f"ds_Th{i}", [128, 512], BF16) for i in range(3)]
                bs = [sbt(st, f"ds_bs{i}", [128, 8 + NIT + 1], F32) for i in range(2)]
                shp = [pst(st, f"ds_shp{i}", [128, 512], F32) for i in range(2)]
                scp = [pst(st, f"ds_scp{i}", [128, 512], F32) for i in range(2)]
                T = att_tiles(st, n_ops=1)
                qs_t = [sbt(st, f"ds_q{i}", [128, 3, 512], BF16) for i in range(2)]
                c3 = [0, 0]

                def masks(sb):
                    for qb in range(4):
                        b = 4 * sb + qb
                        n = 128 * (b + 1)
                        if n <= KTOP:
                            continue
                        i = b % 2
                        rsc, riq, riw, rDh, rbs = rs("ds_sc", i), rs("ds_iq", i), rs("ds_iw", i), rs("ds_Dh", i), rs("ds_bs", i)
                        rNM = rs("ds_NM", sb % 2)
                        dma(iq[i][:], iqT_d[:, :, b * 128:(b + 1) * 128].rearrange("c p t -> p c t"), reads=[rs("iqT_d")], writes=[riq])
                        dma(iw[i][:], iw_d[b, :, :], reads=[rs("iw_d")], writes=[riw])
                        for h in range(8):
                            op("dve", lambda e: e.tensor_scalar(out=Dh[i][:, h, :], in0=cb("ident"), scalar1=iw[i][:, 8 + h:9 + h], scalar2=None, op0=ALU.mult),
                               reads=[riw] + rC, writes=[rDh])
                        chunks = [(c * 512, min(512, n - c * 512)) for c in range((n + 511) // 512)]
                        isteps = [(ci, h) for ci in range(len(chunks)) for h in range(8)]
                        ipend = None
                        for ist in isteps + [None]:
                            icur = None
                            if ist is not None:
                                ci, h = ist
                                k0, wc = chunks[ci]
                                hi_ = c3[0] % 2
                                ti_ = c3[0] % 3
                                c3[0] += 1
                                rsh, rth = rs("ds_shp", hi_), rs("ds_Th", ti_)
                                pb = 64 * (h % 2)
                                op("pe", lambda e: e.matmul(shp[hi_][:, 0:wc], lhsT=iq[i][pb:pb + 32, h // 2, :], rhs=ikT[pb:pb + 32, k0:k0 + wc], start=True, stop=True),
                                   reads=[riq, rik], writes=[rsh])
                                op("act", lambda e: e.activation(out=Th[ti_][:, 0:wc], in_=shp[hi_][:, 0:wc], func=AF.Relu, scale=iw[i][:, h:h + 1]), reads=[rsh, riw], writes=[rth])
                                if h == 0:
                                    c3[1] += 1
                                icur = (ci, h, ti_, c3[1] % 2)
                            if ipend is not None:
                                ci2, h2, ti2, si = ipend
                                k0, wc = chunks[ci2]
                                rscp = rs("ds_scp", si)
                                op("pe", lambda e: e.matmul(scp[si][:, 0:wc], lhsT=Dh[i][:, h2, :], rhs=Th[ti2][:, 0:wc], start=(h2 == 0), stop=(h2 == 7)), reads=[rs("ds_Th", ti2), rDh], writes=[rscp])
                                if h2 == 7:
                                    op("act", lambda e: e.copy(out=sc[i][:, k0:k0 + wc], in_=scp[si][:, 0:wc]), reads=[rscp], writes=[rsc])
                            ipend = icur
                        lo, hi, mid, cn, tt, rng = (bs[i][:, k:k + 1] for k in range(6))
                        Hc = lambda it: bs[i][:, 8 + it:9 + it]
                        op("dve", lambda e: e.tensor_reduce(out=hi, in_=sc[i][:, 0:n], axis=AX.X, op=ALU.max), reads=[rsc], writes=[rbs])
                        op("dve", lambda e: e.tensor_reduce(out=lo, in_=sc[i][:, 0:n], axis=AX.X, op=ALU.min), reads=[rsc], writes=[rbs])
                        op("dve", lambda e: e.scalar_tensor_tensor(out=rng, in0=hi, scalar=1.0, in1=lo, op0=ALU.add, op1=ALU.subtract), reads=[rbs], writes=[rbs])
                        op("dve", lambda e: e.tensor_scalar(out=bs[i][:, 8:8 + NIT + 1], in0=cf[:, CF["pw"]:CF["pw"] + NIT + 1], scalar1=rng, scalar2=None, op0=ALU.mult), reads=[rbs] + rC, writes=[rbs])
                        op("dve", lambda e: e.tensor_tensor(out=mid, in0=lo, in1=Hc(0), op=ALU.add), reads=[rbs], writes=[rbs])
                        op("dve", lambda e: e.memset(sc[i][0:64, n - 64:n], -1e30), writes=[rsc])
                        for it in range(NIT):
                            op("dve", lambda e: e.tensor_scalar(out=junk[:, 0:n], in0=sc[i][:, 0:n], scalar1=mid, scalar2=None, op0=ALU.is_ge, op1=ALU.add, accum_out=cn),
                               reads=[rsc, rbs], writes=[rbs, rs("ds_junk")])
                            op("dve", lambda e: e.tensor_scalar(out=tt, in0=cn, scalar1=float(KTOP) - 0.5, scalar2=-0.5, op0=ALU.is_ge, op1=ALU.add), reads=[rbs], writes=[rbs])
                            op("dve", lambda e: e.scalar_tensor_tensor(out=mid, in0=tt, scalar=Hc(it), in1=mid, op0=ALU.mult, op1=ALU.add), reads=[rbs], writes=[rbs])
                        op("dve", lambda e: e.tensor_tensor(out=lo, in0=mid, in1=Hc(NIT), op=ALU.subtract), reads=[rbs], writes=[rbs])
                        op("dve", lambda e: e.tensor_scalar(out=NM[sb % 2][:, qb, 0:n], in0=sc[i][:, 0:n], scalar1=lo, scalar2=NEGM, op0=ALU.is_lt, op1=ALU.mult), reads=[rsc, rbs], writes=[rNM])

                def nm_fn(b, kb):
                    n = 128 * (b + 1)
                    if n <= KTOP:
                        if kb == b:
                            return (cb("nmd"), rs("cbf"))
                        return None
                    return (NM[(b // 4) % 2][:, b % 4, kb * 128:(kb + 1) * 128], rs("ds_NM", (b // 4) % 2))

                heads = [dict(kT=(lambda ks, pb=64 * (h % 2): kT2[pb:pb + 64, ks]), kres=rk, vaug=vaug, vres=rv) for h in range(6)]
                if DSA_MODE != 2:
                    masks(0)
                for sb in range(NG):
                    if sb + 1 < NG and DSA_MODE != 2:
                        masks(sb + 1)
                    if DSA_MODE == 1:
                        continue
                    i = sb % 2
                    rq = rs("ds_q", i)
                    dma(qs_t[i][:], qT_d[:, :, sb * 512:(sb + 1) * 512].rearrange("c p t -> p c t"), reads=[rs("qT_d")], writes=[rq])
                    qfn = lambda h, qs, i=i: qs_t[i][64 * (h % 2):64 * (h % 2) + 64, h // 2, qs]
                    att_main(T, sb, 6, 64.0 ** -0.5, heads, qfn, rq, nm_fn)
                    att_norm(T, sb, 6, gaT_d, "gaT_d", 0)
            S_.barrier()

        def mla_phase():
            with contextlib.ExitStack() as st:
                kT = sbt(st, "ml_kT", [96, 6, S], BF16)
                va = sbt(st, "ml_va", [128, 6, NB, 65], BF16)
                rk, rv = rs("ml_kT"), rs("ml_va")
                for h in range(6):
                    dma(kT[:, h, :], kc_d[h], reads=[rs("kc_d")], writes=[rk])
                    dma(va[:, h, :, :], vc_d[h].rearrange("t p c -> p t c"), reads=[rs("vc_d")], writes=[rv])
                T = att_tiles(st)
                qs_t = [sbt(st, f"ml_q{i}", [96, 6, 512], BF16) for i in range(2)]

                def nm_fn(b, kb):
                    if kb == b:
                        return (cb("nmd"), rs("cbf"))
                    return None
                heads = [dict(kT=(lambda ks, h=h: kT[:, h, ks]), kres=rk, vaug=va[:, h, :, :], vres=rv) for h in range(6)]
                for sb in range(NG):
                    i = sb % 2
                    rq = rs("ml_q", i)
                    dma(qs_t[i][:], qc_d[:, :, sb * 512:(sb + 1) * 512].rearrange("h p t -> p h t"), reads=[rs("qc_d")], writes=[rq])
                    qfn = lambda h, qs, i=i: qs_t[i][:, h, qs]
                    att_main(T, sb, 6, 96.0 ** -0.5, heads, qfn, rq, nm_fn)
                    att_norm(T, sb, 6, gcT_d, "gcT_d", 8)
            S_.barrier()

        def s5_consts(l):
            with contextlib.ExitStack() as st:
                ar = sbt(st, "s5_ar", [128, 16], F32)
                ai = sbt(st, "s5_ai", [128, 16], F32)
                stp = sbt(st, "s5_stp", [128, 16], F32)
                rp = rs("s5_par")
                for hh in range(2):
                    dma(ar[64 * hh:64 * hh + 64, :], a_re[l].rearrange("g p -> p g"), writes=[rp], allow_slow_non_contiguous=True)
                    dma(ai[64 * hh:64 * hh + 64, :], a_im[l].rearrange("g p -> p g"), writes=[rp], allow_slow_non_contiguous=True)
                dma(stp[:], log_step[l:l + 1, :].broadcast_to([128, 16]), writes=[rp])
                W = sbt(st, "s5_w", [128, 24, 16], F32)
                rw = rs("s5_w")

                def wv(k):
                    return W[:, k, :]

                def dv(fn, reads=(), writes=()):
                    op("dve", fn, reads=list(reads) + [rp, rw] + rC, writes=list(writes) + [rw])
                TWO_PI = 2.0 * np.pi

                def sincos(dst, ang, shift):
                    t, n_, m = wv(20), wv(21), wv(22)
                    ti = Wi[:, 0, :]
                    dv(lambda e: e.tensor_scalar(out=t, in0=ang, scalar1=shift, scalar2=1.0 / TWO_PI, op0=ALU.add, op1=ALU.mult))
                    dv(lambda e: e.tensor_copy(out=ti, in_=t))
                    dv(lambda e: e.tensor_copy(out=n_, in_=ti))
                    dv(lambda e: e.tensor_tensor(out=t, in0=t, in1=n_, op=ALU.subtract))
                    dv(lambda e: e.tensor_scalar(out=m, in0=t, scalar1=0.5, scalar2=None, op0=ALU.is_gt))
                    dv(lambda e: e.tensor_tensor(out=t, in0=t, in1=m, op=ALU.subtract))
                    dv(lambda e: e.tensor_scalar(out=m, in0=t, scalar1=-0.5, scalar2=None, op0=ALU.is_lt))
                    dv(lambda e: e.tensor_tensor(out=t, in0=t, in1=m, op=ALU.add))
                    op("act", lambda e: e.activation(out=dst, in_=t, func=AF.Sin, scale=TWO_PI), reads=[rw], writes=[rw])
                Wi = sbt(st, "s5_wi", [128, 1, 16], mybir.dt.int32)
                op("act", lambda e: e.activation(out=stp[:], in_=stp[:], func=AF.Exp), reads=[rp], writes=[rp])
                mag, ang, abr, abi, cs, sn = wv(0), wv(1), wv(2), wv(3), wv(4), wv(5)
                dv(lambda e: e.tensor_tensor(out=mag, in0=ar[:], in1=stp[:], op=ALU.mult))
                op("act", lambda e: e.activation(out=mag, in_=mag, func=AF.Exp), reads=[rw], writes=[rw])
                dv(lambda e: e.tensor_tensor(out=ang, in0=ai[:], in1=stp[:], op=ALU.mult))
                sincos(sn, ang, 0.0)
                sincos(cs, ang, np.pi / 2)
                dv(lambda e: e.tensor_tensor(out=abr, in0=mag, in1=cs, op=ALU.mult))
                dv(lambda e: e.tensor_tensor(out=abi, in0=mag, in1=sn, op=ALU.mult))
                den, nr, fr, fi, tA, tB = wv(6), wv(7), wv(8), wv(9), wv(10), wv(11)
                dv(lambda e: e.tensor_tensor(out=den, in0=ar[:], in1=ar[:], op=ALU.mult))
                dv(lambda e: e.tensor_tensor(out=tA, in0=ai[:], in1=ai[:], op=ALU.mult))
                dv(lambda e: e.tensor_tensor(out=den, in0=den, in1=tA, op=ALU.add))
                dv(lambda e: e.reciprocal(out=den, in_=den))
                dv(lambda e: e.tensor_scalar(out=nr, in0=abr, scalar1=-1.0, scalar2=None, op0=ALU.add))
                dv(lambda e: e.tensor_tensor(out=fr, in0=nr, in1=ar[:], op=ALU.mult))
                dv(lambda e: e.tensor_tensor(out=tA, in0=abi, in1=ai[:], op=ALU.mult))
                dv(lambda e: e.tensor_tensor(out=fr, in0=fr, in1=tA, op=ALU.add))
                dv(lambda e: e.tensor_tensor(out=fr, in0=fr, in1=den, op=ALU.mult))
                dv(lambda e: e.tensor_tensor(out=fi, in0=abi, in1=ar[:], op=ALU.mult))
                dv(lambda e: e.tensor_tensor(out=tA, in0=nr, in1=ai[:], op=ALU.mult))
                dv(lambda e: e.tensor_tensor(out=fi, in0=fi, in1=tA, op=ALU.subtract))
                dv(lambda e: e.tensor_tensor(out=fi, in0=fi, in1=den, op=ALU.mult))
                PW = sbt(st, "s5_pw", [128, 9, 2, 16], F32)
                MKp = sbt(st, "s5_mkp", [128, 9, 2, 16], F32)
                FN = sbt(st, "s5_fn", [128, 8, 2, 16], F32)
                CQc = sbt(st, "s5_cqc", [128, 9, 2, 16], F32)

                def cmul(o_re, o_im, x_re, x_im, y_re, y_im):
                    dv(lambda e: e.tensor_tensor(out=tA, in0=x_re, in1=y_re, op=ALU.mult))
                    dv(lambda e: e.tensor_tensor(out=tB, in0=x_im, in1=y_im, op=ALU.mult))
                    dv(lambda e: e.tensor_tensor(out=wv(12), in0=x_re, in1=y_im, op=ALU.mult))
                    dv(lambda e: e.tensor_tensor(out=wv(13), in0=x_im, in1=y_re, op=ALU.mult))
                    dv(lambda e: e.tensor_tensor(out=o_re, in0=tA, in1=tB, op=ALU.subtract))
                    dv(lambda e: e.tensor_tensor(out=o_im, in0=wv(12), in1=wv(13), op=ALU.add))
                dv(lambda e: e.memset(PW[:, 0, 0, :], 1.0))
                dv(lambda e: e.memset(PW[:, 0, 1, :], 0.0))
                for n_ in range(1, 9):
                    cmul(PW[:, n_, 0, :], PW[:, n_, 1, :], PW[:, n_ - 1, 0, :], PW[:, n_ - 1, 1, :], abr, abi)
                dv(lambda e: e.tensor_copy(out=MKp[:, 0, :, :], in_=PW[:, 8, :, :]))
                for k in range(1, 9):
                    cmul(MKp[:, k, 0, :], MKp[:, k, 1, :], MKp[:, k - 1, 0, :], MKp[:, k - 1, 1, :], MKp[:, k - 1, 0, :], MKp[:, k - 1, 1, :])
                sgB = cf[:, CF["sgB"]:CF["sgB"] + 1]
                sgQ = cf[:, CF["sgQ"]:CF["sgQ"] + 1]
                for n_ in range(8):
                    cmul(FN[:, n_, 0, :], FN[:, n_, 1, :], PW[:, n_, 0, :], PW[:, n_, 1, :], fr, fi)
                    dv(lambda e: e.tensor_scalar(out=FN[:, n_, 1, :], in0=FN[:, n_, 1, :], scalar1=sgB, scalar2=None, op0=ALU.mult))
                for n_ in range(9):
                    dv(lambda e: e.tensor_scalar(out=CQc[:, n_, 0, :], in0=PW[:, n_, 0, :], scalar1=sgQ, scalar2=None, op0=ALU.mult))
                    dv(lambda e: e.tensor_scalar(out=CQc[:, n_, 1, :], in0=PW[:, n_, 1, :], scalar1=-1.0, scalar2=None, op0=ALU.mult))
                for k in range(9):
                    dv(lambda e: e.tensor_scalar(out=MKp[:, k, 1, :], in0=MKp[:, k, 1, :], scalar1=sgQ, scalar2=None, op0=ALU.mult))
                bx = [sbt(st, f"s5_bx{i}", [128, 16], F32) for i in range(2)]
                by = [sbt(st, f"s5_by{i}", [128, 16], F32) for i in range(2)]
                cx = [sbt(st, f"s5_cx{i}", [128, 16], F32) for i in range(2)]
                cy = [sbt(st, f"s5_cy{i}", [128, 16], F32) for i in range(2)]
                dr = [sbt(st, f"s5_dr{i}", [128, 1], F32) for i in range(2)]
                Pm = [sbt(st, f"s5_Pm{i}", [128, 128], F32) for i in range(2)]
                CQ = [sbt(st, f"s5_CQ{i}", [128, 9, 16], F32) for i in range(2)]
                Btr = [sbt(st, f"s5_Btr{i}", [128, 8, 16], F32) for i in range(2)]
                KTp = [sbt(st, f"s5_KTp{i}", [128, 15 * 16], F32) for i in range(2)]
                OUTm = [sbt(st, f"s5_OUT{i}", [128, 12, 128], F32) for i in range(2)]
                pp = [pst(st, f"s5_pp{i}", [128, 128], F32) for i in range(2)]
                for i in range(2):
                    op("pool", lambda e: e.memset(KTp[i][:], 0.0), writes=[rs("s5_KTp", i)])
                bxa = sbt(st, "s5_bxa", [128, 16, 16], F32)
                bya = sbt(st, "s5_bya", [128, 16, 16], F32)
                dra = sbt(st, "s5_dra", [128, 16], F32)
                rina = rs("s5_inall")
                dma(bxa[0:64], b_re[l].rearrange("g p c -> p g c"), writes=[rina])
                dma(bxa[64:128], b_im[l].rearrange("g p c -> p g c"), writes=[rina])
                dma(bya[0:64], b_im[l].rearrange("g p c -> p g c"), writes=[rina])
                dma(bya[64:128], b_re[l].rearrange("g p c -> p g c"), writes=[rina])
                for s_ in range(8):
                    dma(dra[16 * s_:16 * s_ + 16, :], d_skip[l].rearrange("g c -> c g"), writes=[rina], allow_slow_non_contiguous=True)
                for g in range(16):
                    i = g % 2
                    rin, rPm, rCQ, rBt, rKT, rOUT, rpp = rs("s5_in", i), rs("s5_Pm", i), rs("s5_CQ", i), rs("s5_Btr", i), rs("s5_KTp", i), rs("s5_OUT", i), rs("s5_pp", i)
                    dma(cx[i][0:64, :], c_re[l, g].rearrange("c p -> p c"), writes=[rin], allow_slow_non_contiguous=True)
                    dma(cx[i][64:128, :], c_im[l, g].rearrange("c p -> p c"), writes=[rin], allow_slow_non_contiguous=True)
                    dma(cy[i][0:64, :], c_im[l, g].rearrange("c p -> p c"), writes=[rin], allow_slow_non_contiguous=True)
                    dma(cy[i][64:128, :], c_re[l, g].rearrange("c p -> p c"), writes=[rin], allow_slow_non_contiguous=True)
                    for s_ in range(8):
                        n_ = 7 - s_
                        op("dve", lambda e: e.tensor_scalar(out=Pm[i][:, 16 * s_:16 * s_ + 16], in0=bxa[:, g, :], scalar1=FN[:, n_, 0, g:g + 1], scalar2=None, op0=ALU.mult), reads=[rina, rw], writes=[rPm])
                        op("dve", lambda e: e.scalar_tensor_tensor(out=Pm[i][:, 16 * s_:16 * s_ + 16], in0=bya[:, g, :], scalar=FN[:, n_, 1, g:g + 1], in1=Pm[i][:, 16 * s_:16 * s_ + 16], op0=ALU.mult, op1=ALU.add),
                           reads=[rina, rw], writes=[rPm])
                    for s_ in range(8):
                        op("dve", lambda e: e.tensor_copy(out=Btr[i][:, s_, :], in_=Pm[i][:, 112:128]), reads=[rPm], writes=[rBt])
                    for n_ in range(9):
                        op("dve", lambda e: e.tensor_scalar(out=CQ[i][:, n_, :], in0=cx[i][:], scalar1=CQc[:, n_, 0, g:g + 1], scalar2=None, op0=ALU.mult), reads=[rin, rw], writes=[rCQ])
                        op("dve", lambda e: e.scalar_tensor_tensor(out=CQ[i][:, n_, :], in0=cy[i][:], scalar=CQc[:, n_, 1, g:g + 1], in1=CQ[i][:, n_, :], op0=ALU.mult, op1=ALU.add),
                           reads=[rin, rw], writes=[rCQ])
                    op("pe", lambda e: e.transpose(out=pp[i][:], in_=Pm[i][:], identity=cfa("ident")), reads=[rPm] + rC, writes=[rpp])
                    op("act", lambda e: e.copy(out=OUTm[i][:, 0, :], in_=pp[i][:]), reads=[rpp], writes=[rOUT])
                    op("dve", lambda e: e.tensor_copy(out=OUTm[i][:, 1, :], in_=CQ[i][:, 1:9, :].rearrange("p n c -> p (n c)")), reads=[rCQ], writes=[rOUT])
                    op("pe", lambda e: e.matmul(pp[i][:], lhsT=Btr[i][:].rearrange("p s c -> p (s c)"), rhs=CQ[i][:, 0:8, :].rearrange("p n c -> p (n c)"), start=True, stop=True),
                       reads=[rBt, rCQ], writes=[rpp])
                    op("act", lambda e: e.copy(out=KTp[i][:, 112:240], in_=pp[i][:]), reads=[rpp], writes=[rKT])
                    op("dve", lambda e: e.scalar_tensor_tensor(out=KTp[i][:, 112:128], in0=cfa("i16", 128, 16), scalar=dra[:, g:g + 1], in1=KTp[i][:, 112:128], op0=ALU.mult, op1=ALU.add),
                       reads=[rina, rKT] + rC, writes=[rKT])
                    for s_ in range(8):
                        dma(s5c_d[g, 16 * s_:16 * s_ + 16, 2, :], KTp[i][16 * s_:16 * s_ + 16, (7 - s_) * 16:(7 - s_) * 16 + 128], reads=[rKT], writes=[rs("s5c_d")])
                    for k in range(9):
                        op("dve", lambda e: e.tensor_scalar(out=OUTm[i][:, 3 + k, :], in0=cfa("ident"), scalar1=MKp[:, k, 0, g:g + 1], scalar2=None, op0=ALU.mult), reads=[rw] + rC, writes=[rOUT])
                        op("dve", lambda e: e.scalar_tensor_tensor(out=OUTm[i][:, 3 + k, :], in0=cfa("icross"), scalar=MKp[:, k, 1, g:g + 1], in1=OUTm[i][:, 3 + k, :], op0=ALU.mult, op1=ALU.add),
                           reads=[rw] + rC, writes=[rOUT])
                    dma(s5c_d[g, :, 0:2, :], OUTm[i][:, 0:2, :], reads=[rOUT], writes=[rs("s5c_d")])
                    dma(s5c_d[g, :, 3:12, :], OUTm[i][:, 3:12, :], reads=[rOUT], writes=[rs("s5c_d")])
            S_.barrier()

        def s5_phase():
            with contextlib.ExitStack() as st:
                NPAR = 4
                assert J <= 512
                Cm = [sbt(st, f"s5r_C{i}", [128, 12, 128], F32) for i in range(NPAR)]
                Ut = [sbt(st, f"s5r_U{i}", [128, J], F32) for i in range(NPAR)]
                St = [sbt(st, f"s5r_S{i}", [128, J], F32) for i in range(NPAR)]
                Yt = [sbt(st, f"s5r_Y{i}", [128, J], F32) for i in range(NPAR)]
                pk = [pst(st, f"s5r_p{i}", [128, 512], F32) for i in range(8)]
                pc = [0]

                def nextp():
                    k = pc[0] % 8
                    pc[0] += 1
                    return k, rs("s5r_p", k)
                for g0 in range(0, 16, NPAR):
                    gs = list(range(g0, g0 + NPAR))
                    R_ = {}
                    for g in gs:
                        i = g % NPAR
                        R_[g] = (rs("s5r_C", i), rs("s5r_U", i), rs("s5r_S", i), rs("s5r_Y", i))
                        rCm, rU, rSt, rY = R_[g]
                        dma(Cm[i][:], s5c_d[g], reads=[rs("s5c_d")], writes=[rCm])
                        dma(Ut[i][:], U_d[g].rearrange("s c j -> (s c) j"), reads=[rs("U_d")], writes=[rU])
                    for g in gs:
                        i = g % NPAR
                        rCm, rU, rSt, rY = R_[g]
                        k, rpk = nextp()
                        op("pe", lambda e: e.matmul(pk[k][:, 0:J], lhsT=Cm[i][:, 0, :], rhs=Ut[i][:, 0:J], start=True, stop=True), reads=[rCm, rU], writes=[rpk])
                        op("act", lambda e: e.copy(out=St[i][:, 0:J], in_=pk[k][:, 0:J]), reads=[rpk], writes=[rSt])
                    k_ = 0
                    while (1 << k_) < J:
                        sh = 1 << k_
                        wc = J - sh
                        prods = []
                        for g in gs:
                            i = g % NPAR
                            rCm, rU, rSt, rY = R_[g]
                            k, rpk = nextp()
                            op("pe", lambda e: e.matmul(pk[k][:, 0:wc], lhsT=Cm[i][:, 3 + k_, :], rhs=St[i][:, 0:wc], start=True, stop=True), reads=[rCm, rSt], writes=[rpk])
                            prods.append((g, k, rpk))
                        for n_, (g, k, rpk) in enumerate(prods):
                            i = g % NPAR
                            rSt = R_[g][2]
                            op("dve", lambda e: e.tensor_tensor(out=St[i][:, sh:J], in0=pk[k][:, 0:wc], in1=St[i][:, sh:J], op=ALU.add), reads=[rpk, rSt], writes=[rSt])
                        k_ += 1
                    for g in gs:
                        i = g % NPAR
                        rCm, rU, rSt, rY = R_[g]
                        k, rpk = nextp()
                        op("pe", lambda e: e.matmul(pk[k][:, 0:J], lhsT=Cm[i][:, 2, :], rhs=Ut[i][:, 0:J], start=True, stop=False), reads=[rCm, rU], writes=[rpk])
                        op("pe", lambda e: e.matmul(pk[k][:, 1:J], lhsT=Cm[i][:, 1, :], rhs=St[i][:, 0:J - 1], start=False, stop=True), reads=[rCm, rSt], writes=[rpk])
                        op("act", lambda e: e.copy(out=Yt[i][:, 0:J], in_=pk[k][:, 0:J]), reads=[rpk], writes=[rY])
                        dma(Y_d[g].rearrange("s c j -> (s c) j"), Yt[i][:], reads=[rY], writes=[rs("Y_d")])
            S_.barrier()
            with contextlib.ExitStack() as st:
                yp = sbt(st, "s5p_yp", [128, 2, 8, 64], F32)
                y = sbt(st, "s5p_y", [128, 2, 512], F32)
                t = sbt(st, "s5p_t", [128, 2, 512], F32)
                gl = sbt(st, "s5p_g", [128, 2, 512], F32)
                gb16 = sbt(st, "s5p_gb", [128, 2, 512], BF16)
                sg = sbt(st, "s5p_sg", [128, 2, 512], F32)
                gate = sbt(st, "s5p_gate", [128, 2, 512], BF16)
                ob = sbt(st, "s5p_ob", [128, 2, 512], BF16)
                zp = [pst(st, f"s5p_z{i}", [128, 512], F32) for i in range(2)]
                ryp, ry, rt_, rgl, rgb, rsg, rgate, rob = (rs("s5p_" + n_) for n_ in ("yp", "y", "t", "g", "gb", "sg", "gate", "ob"))
                for tg in range(NG):
                    tsl = slice(tg * 512, tg * 512 + 512)
                    for c in range(2):
                        for g8 in range(8):
                            dma(yp[16 * g8:16 * g8 + 16, c, :, :], Y_d[8 * c + g8, :, :, tg * 64:(tg + 1) * 64].rearrange("s c j -> c s j"), reads=[rs("Y_d")], writes=[ryp])
                        dma(gate[:, c, :], gbT_d[c, :, tsl], reads=[rs("gbT_d")], writes=[rgate])
                    for c in range(2):
                        op("act", lambda e: e.copy(out=y[:, c, :].rearrange("p (j s) -> p s j", s=8), in_=yp[:, c, :, :]), reads=[ryp], writes=[ry])
                        op("act", lambda e: e.activation(out=t[:, c, :], in_=y[:, c, :], func=AF.Square), reads=[ry], writes=[rt_])
                        op("dve", lambda e: e.tensor_scalar(out=t[:, c, :], in0=t[:, c, :], scalar1=0.044715, scalar2=1.0, op0=ALU.mult, op1=ALU.add), reads=[rt_], writes=[rt_])
                        op("dve", lambda e: e.tensor_tensor(out=t[:, c, :], in0=t[:, c, :], in1=y[:, c, :], op=ALU.mult), reads=[rt_, ry], writes=[rt_])
                        op("act", lambda e: e.activation(out=t[:, c, :], in_=t[:, c, :], func=AF.Sigmoid, scale=2.0 * 0.7978845608028654), reads=[rt_], writes=[rt_])
                        op("dve", lambda e: e.tensor_tensor(out=gl[:, c, :], in0=t[:, c, :], in1=y[:, c, :], op=ALU.mult), reads=[rt_, ry], writes=[rgl])
                        op("dve", lambda e: e.tensor_copy(out=gb16[:, c, :], in_=gl[:, c, :]), reads=[rgl], writes=[rgb])
                    for co in range(2):
                        rz = rs("s5p_z", co)
                        for c in range(2):
                            op("pe", lambda e: e.matmul(zp[co][:], lhsT=Wglu[:, c, co * 128:(co + 1) * 128], rhs=gb16[:, c, :], start=(c == 0), stop=(c == 1)), reads=[rWsm, rgb], writes=[rz])
                        op("act", lambda e: e.activation(out=sg[:, co, :], in_=zp[co][:], func=AF.Sigmoid), reads=[rz], writes=[rsg])
                        op("dve", lambda e: e.tensor_tensor(out=sg[:, co, :], in0=sg[:, co, :], in1=gl[:, co, :], op=ALU.mult), reads=[rsg, rgl], writes=[rsg])
                        op("dve", lambda e: e.tensor_tensor(out=ob[:, co, :], in0=sg[:, co, :], in1=gate[:, co, :], op=ALU.mult), reads=[rsg, rgate], writes=[rob])
                        dma(mix_d[6 + co, :, tsl], ob[:, co, :], reads=[rob], writes=[rs("mix_d")])
            S_.barrier()

        def phase3(src_ap, src_res, dst_ap, dst_res):
            with contextlib.ExitStack() as st:
                Wo = sbt(st, "p3_Wo", [128, 14, D], BF16)
                rWo = rs("p3_Wo")
                for c in range(14):
                    nr = 128 if c in (6, 7) else 64
                    dma(Wo[0:nr, c, :], wout_d[0:nr, c, :], reads=[rs("wout_d")], writes=[rWo])
                mx = [sbt(st, f"p3_mx{i}", [128, 14, 512], BF16) for i in range(2)]
                xt = [sbt(st, f"p3_xt{i}", [128, 4, D], F32) for i in range(2)]
                pp = [pst(st, f"p3_pp{i}", [128, 512], F32) for i in range(4)]
                pc = 0
                for tg in range(NG):
                    i = tg % 2
                    rmx, rxt = rs("p3_mx", i), rs("p3_xt", i)
                    tsl = slice(tg * 512, tg * 512 + 512)
                    dma(mx[i][0:64, 0:6, :], mix_d[0:6, 0:64, tsl].rearrange("c p t -> p c t"), reads=[rs("mix_d")], writes=[rmx])
                    dma(mx[i][:, 6:8, :], mix_d[6:8, :, tsl].rearrange("c p t -> p c t"), reads=[rs("mix_d")], writes=[rmx])
                    dma(mx[i][0:64, 8:14, :], mix_d[8:14, 0:64, tsl].rearrange("c p t -> p c t"), reads=[rs("mix_d")], writes=[rmx])
                    dma(xt[i][:], src_ap[tg * 512:(tg + 1) * 512, :].rearrange("(j p) d -> p j d", p=128), reads=[src_res], writes=[rxt])
                    for j in range(4):
                        for half in range(2):
                            k = pc % 4
                            pc += 1
                            rpp = rs("p3_pp", k)
                            for c in range(14):
                                nr = 128 if c in (6, 7) else 64
                                op("pe", lambda e: e.matmul(pp[k][:], lhsT=mx[i][0:nr, c, j * 128:(j + 1) * 128], rhs=Wo[0:nr, c, half * 512:(half + 1) * 512], start=(c == 0), stop=(c == 13)),
                                   reads=[rmx, rWo], writes=[rpp])
                            op("dve", lambda e: e.tensor_tensor(out=xt[i][:, j, half * 512:(half + 1) * 512], in0=pp[k][:], in1=xt[i][:, j, half * 512:(half + 1) * 512], op=ALU.add), reads=[rpp, rxt], writes=[rxt])
                    dma(dst_ap[tg * 512:(tg + 1) * 512, :].rearrange("(j p) d -> p j d", p=128), xt[i][:], reads=[rxt], writes=[dst_res])
            S_.barrier()

        for l in range(NL):
            if "prep" in PH:
                prep_weights(l)
            if "s5c" in PH:
                s5_consts(l)
            for s in range(NSEQ):
                src = x_in[s] if l == 0 else xs_d[s]
                dst = out_d[s] if l == NL - 1 else xs_d[s]
                if "p1" in PH:
                    phase1(l, src, rs("xs", s))
                if "dsa" in PH:
                    dsa_phase()
                if "s5" in PH:
                    s5_phase()
                if "mla" in PH:
                    mla_phase()
                if "p3" in PH:
                    phase3(src, rs("xs", s), dst, rs("out_d") if l == NL - 1 else rs("xs", s))
        S_.barrier()
    return nc


xs_all = None


def _build(S, NL, NSEQ, KTOP):
    global xs_all
    return build_program(S, NL, NSEQ, KTOP)


_CACHE = {}


def kernel(**inputs):
    x = np.ascontiguousarray(np.asarray(inputs["x"], dtype=np.float32))
    B, S, _ = x.shape
    ncores = 8
    NSEQ = B // ncores
    KTOP = min(256, S // 4)
    nc = build_program(S, DEPTH, NSEQ, KTOP)
    consts = make_consts(S)
    names = ["norm_g", "w_in", "attn_q_norm", "attn_k_norm", "mla_q_lora_norm", "mla_kv_lora_norm", "mla_w_uq", "mla_w_ukv",
             "mla_q_norm", "mla_k_norm", "ssm_a_re", "ssm_a_im", "ssm_b_re", "ssm_b_im", "ssm_c_re", "ssm_c_im", "ssm_d",
             "ssm_log_step", "ssm_w_glu", "w_out"]
    shared = {n: np.ascontiguousarray(np.asarray(inputs[n], dtype=np.float32)) for n in names}
    in_maps = []
    for c in range(ncores):
        m = dict(shared)
        m["x"] = x[c * NSEQ:(c + 1) * NSEQ]
        m.update(consts)
        in_maps.append(m)
    res = run_bass_kernel_spmd(nc, in_maps, core_ids=list(range(ncores)))
    return np.concatenate([r["out"] for r in res.results], axis=0).astype(np.float32)
```
